# Optimizing a Trainium2 kernel written in Bass

```python
import numpy as np
import jax
import jax.numpy as jnp
from jax import lax

D_MODEL = 1024
BATCH = 8
SEQ = 4096
DEPTH = 2

HEAD_DIM = 64
ROPE_THETA = 10000.0
RMS_EPS = 1e-6
GN_EPS = 1e-5
RWKV_LN_EPS = 64e-5
D_FF = 2816
N_BRANCH = 4
BAND_BLOCK = 128
NEG_INF = -1e30
FORCED_SCORE = 1e9

NSA_HEADS = 4
NSA_KV_DIM = HEAD_DIM
NSA_CMP_LEN = 32
NSA_CMP_STRIDE = 16
NSA_CMP_HIDDEN = 256
NSA_SEL_BLOCK = 64
NSA_TOP_N = 16
NSA_WINDOW = 512

RET_HEADS = 4
RET_DK = HEAD_DIM
RET_DV = 2 * HEAD_DIM
RET_CHUNK = 128

RWKV_HEADS = 4
RWKV_W_LORA = 64
RWKV_A_LORA = 64
RWKV_G_LORA = 128

SWA_HEADS = 4
SWA_KV_HEADS = 2
SWA_WINDOW = 128

NSA_WIDTH = NSA_HEADS * HEAD_DIM
RET_WIDTH = RET_HEADS * RET_DV
RWKV_WIDTH = RWKV_HEADS * HEAD_DIM
SWA_WIDTH = SWA_HEADS * HEAD_DIM
RWKV_MIX_WIDTH = 3 * RWKV_WIDTH + RWKV_W_LORA + RWKV_A_LORA + RWKV_G_LORA

kernel_name = 'hybrid_nsa_retnet_rwkv7_swa_macaron'


def _column_layout():
    spec = (
        ('nsa_q', NSA_WIDTH), ('nsa_k_cmp', NSA_KV_DIM), ('nsa_v_cmp', NSA_KV_DIM),
        ('nsa_k_slc', NSA_KV_DIM), ('nsa_v_slc', NSA_KV_DIM),
        ('nsa_k_win', NSA_KV_DIM), ('nsa_v_win', NSA_KV_DIM), ('nsa_gate', 3 * NSA_HEADS),
        ('ret_q', RET_HEADS * RET_DK), ('ret_k', RET_HEADS * RET_DK),
        ('ret_v', RET_WIDTH), ('ret_g', RET_WIDTH),
        ('rwkv', RWKV_MIX_WIDTH),
        ('swa_q', SWA_WIDTH), ('swa_k', SWA_KV_HEADS * HEAD_DIM), ('swa_v', SWA_KV_HEADS * HEAD_DIM),
        ('branch_gate', N_BRANCH * D_MODEL),
    )
    layout, start = {}, 0
    for name, width in spec:
        layout[name] = (start, start + width)
        start += width
    return layout, start


def _cols(p, layout, name):
    s, e = layout[name]
    return p[..., s:e]


def rms_norm(x, g):
    xf = x.astype(jnp.float32)
    y = xf * lax.rsqrt(jnp.mean(xf * xf, axis=-1, keepdims=True) + RMS_EPS)
    return (y * g.astype(jnp.float32)).astype(x.dtype)


def swiglu(x, w_gate, w_up, w_down):
    return (jax.nn.silu(x @ w_gate) * (x @ w_up)) @ w_down


def rope(t, pos):
    d = t.shape[-1]
    half = d // 2
    inv_freq = jnp.power(ROPE_THETA, -jnp.arange(half, dtype=jnp.float32) * 2.0 / d)
    ang = pos.astype(jnp.float32)[:, None] * inv_freq[None, :]
    cos = jnp.cos(ang)[:, None, :]
    sin = jnp.sin(ang)[:, None, :]
    tf = t.astype(jnp.float32)
    t1, t2 = tf[..., :half], tf[..., half:]
    return jnp.concatenate([t1 * cos - t2 * sin, t2 * cos + t1 * sin], axis=-1).astype(t.dtype)


def masked_softmax(s, mask):
    s = jnp.where(mask, s.astype(jnp.float32), NEG_INF)
    m = jnp.max(s, axis=-1, keepdims=True)
    p = jnp.exp(s - m) * mask
    return p / jnp.maximum(jnp.sum(p, axis=-1, keepdims=True), 1e-30)


def banded_attention(q, k, v, window, sinks=None):
    B, S, Hk, G, D = q.shape
    blk = BAND_BLOCK
    nb = S // blk
    n_prev = -(-(window - 1) // blk)
    pad = n_prev * blk
    n_keys = (n_prev + 1) * blk
    kp = jnp.pad(k, ((0, 0), (pad, 0), (0, 0), (0, 0))).reshape(B, nb + n_prev, blk, Hk, D)
    vp = jnp.pad(v, ((0, 0), (pad, 0), (0, 0), (0, 0))).reshape(B, nb + n_prev, blk, Hk, D)
    kb = jnp.concatenate([kp[:, j:j + nb] for j in range(n_prev + 1)], axis=2)
    vb = jnp.concatenate([vp[:, j:j + nb] for j in range(n_prev + 1)], axis=2)
    qb = q.reshape(B, nb, blk, Hk, G, D)
    s = jnp.einsum('bnqhgd,bnkhd->bnhgqk', qb, kb).astype(jnp.float32) * (D ** -0.5)
    qpos = jnp.arange(nb)[:, None] * blk + jnp.arange(blk)[None, :]
    kpos = jnp.arange(nb)[:, None] * blk - pad + jnp.arange(n_keys)[None, :]
    diff = qpos[:, :, None] - kpos[:, None, :]
    mask = ((diff >= 0) & (diff < window) & (kpos[:, None, :] >= 0))[None, :, None, None]
    s = jnp.where(mask, s, NEG_INF)
    m = jnp.max(s, axis=-1, keepdims=True)
    if sinks is not None:
        sk = sinks.astype(jnp.float32)[None, None, :, :, None, None]
        m = jnp.maximum(m, sk)
    p = jnp.exp(s - m) * mask
    denom = jnp.sum(p, axis=-1, keepdims=True)
    if sinks is not None:
        denom = denom + jnp.exp(sk - m)
    p = (p / denom).astype(v.dtype)
    o = jnp.einsum('bnhgqk,bnkhd->bnqhgd', p, vb)
    return o.reshape(B, S, Hk * G * D)


def nsa_mixer(q, k_cmp, v_cmp, k_slc, v_slc, k_win, v_win, gate_logits,
              pos_k, pos_v, ck_w1, ck_w2, cv_w1, cv_w2, pos):
    B, S, H, D = q.shape
    scale = D ** -0.5
    n_cmp = (S - NSA_CMP_LEN) // NSA_CMP_STRIDE + 1
    cmp_start = jnp.arange(n_cmp) * NSA_CMP_STRIDE
    cmp_idx = cmp_start[:, None] + jnp.arange(NSA_CMP_LEN)[None, :]

    def compress(t, pos_emb, w1, w2):
        blocks = (t[:, cmp_idx] + pos_emb).reshape(B, n_cmp, NSA_CMP_LEN * D)
        return jax.nn.gelu(blocks @ w1) @ w2

    kc = compress(k_cmp, pos_k, ck_w1, ck_w2)
    vc = compress(v_cmp, pos_v, cv_w1, cv_w2)
    cmp_mask = (cmp_start + NSA_CMP_LEN - 1)[None, :] <= pos[:, None]
    p_cmp = masked_softmax(jnp.einsum('bshd,bnd->bhsn', q, kc) * scale, cmp_mask)
    o_cmp = jnp.einsum('bhsn,bnd->bshd', p_cmp.astype(vc.dtype), vc)
    n_sel = S // NSA_SEL_BLOCK
    sel_start = jnp.arange(n_sel) * NSA_SEL_BLOCK
    overlap = ((cmp_start[:, None] <= sel_start[None, :] + NSA_SEL_BLOCK - 1)
               & (cmp_start[:, None] + NSA_CMP_LEN - 1 >= sel_start[None, :])).astype(jnp.float32)
    importance = jnp.einsum('bhsn,nj->bsj', p_cmp, overlap)
    cur = pos // NSA_SEL_BLOCK
    blk_id = jnp.arange(n_sel)
    valid = blk_id[None, :] <= cur[:, None]
    forced = (blk_id[None, :] == 0) | (blk_id[None, :] == cur[:, None]) | (blk_id[None, :] == cur[:, None] - 1)
    score = jnp.where(forced, FORCED_SCORE, jnp.where(valid, importance, -FORCED_SCORE))
    n_top = min(NSA_TOP_N, n_sel)
    _, sel = lax.top_k(score, n_top)
    sel_valid = sel <= cur[None, :, None]
    q_rot = rope(q, pos)
    k_slc = rope(k_slc[:, :, None], pos)[:, :, 0]
    k_win = rope(k_win[:, :, None], pos)[:, :, 0]
    blk = BAND_BLOCK
    nqb = S // blk

    def sel_block(args):
        qb, selb, validb, posb = args
        tok4 = selb[..., None] * NSA_SEL_BLOCK + jnp.arange(NSA_SEL_BLOCK)
        mask = (validb[..., None] & (tok4 <= posb[None, :, None, None])).reshape(B, blk, -1)
        tok = tok4.reshape(B, blk, -1)
        ks = jax.vmap(lambda t, i: t[i])(k_slc, tok)
        vs = jax.vmap(lambda t, i: t[i])(v_slc, tok)
        p = masked_softmax(jnp.einsum('bqhd,bqtd->bhqt', qb, ks) * scale, mask[:, None])
        return jnp.einsum('bhqt,bqtd->bqhd', p.astype(vs.dtype), vs)

    def to_blocks(t):
        return jnp.swapaxes(t.reshape((B, nqb, blk) + t.shape[2:]), 0, 1)

    o_slc = lax.map(sel_block, (to_blocks(q_rot), to_blocks(sel), to_blocks(sel_valid), pos.reshape(nqb, blk)))
    o_slc = jnp.swapaxes(o_slc, 0, 1).reshape(B, S, H, D)
    o_win = banded_attention(q_rot.reshape(B, S, 1, H, D), k_win[:, :, None], v_win[:, :, None],
                             NSA_WINDOW).reshape(B, S, H, D)
    g = jax.nn.sigmoid(gate_logits.reshape(B, S, H, 3))
    o = g[..., 0:1] * o_cmp + g[..., 1:2] * o_slc + g[..., 2:3] * o_win
    return o.reshape(B, S, H * D)


def retention_mixer(q, k, v, g_in, gn_g, pos):
    B, S, H, Dk = q.shape
    Dv = v.shape[-1]
    dt = q.dtype
    C = RET_CHUNK
    n = S // C
    log_g = jnp.log(1.0 - jnp.power(2.0, -5.0 - jnp.arange(H, dtype=jnp.float32)))
    q = rope(q, pos)
    k = rope(k, pos) * (Dk ** -0.5)
    qc = q.reshape(B, n, C, H, Dk)
    kc = k.reshape(B, n, C, H, Dk)
    vc = v.reshape(B, n, C, H, Dv)
    i = jnp.arange(C, dtype=jnp.float32)
    diff = i[:, None] - i[None, :]
    decay_mask = jnp.where(diff >= 0, jnp.exp(log_g[:, None, None] * jnp.maximum(diff, 0.0)), 0.0).astype(dt)
    zeta = jnp.exp(log_g[:, None] * (C - 1 - i)[None, :]).astype(dt)
    xi = jnp.exp(log_g[:, None] * (i + 1.0)[None, :]).astype(dt)
    g_chunk = jnp.exp(log_g * C).astype(dt)
    inner = jnp.einsum('bnchd,bnmhd->bnhcm', qc, kc) * decay_mask[None, None]
    o_inner = jnp.einsum('bnhcm,bnmhe->bnche', inner, vc)
    kv = jnp.einsum('bnmhd,hm,bnmhe->bnhde', kc, zeta, vc)

    def step(state, kv_n):
        return state * g_chunk[None, :, None, None] + kv_n, state

    _, r_prev = lax.scan(step, jnp.zeros((B, H, Dk, Dv), dt), jnp.swapaxes(kv, 0, 1))
    r_prev = jnp.swapaxes(r_prev, 0, 1)
    o_cross = jnp.einsum('bnchd,hc,bnhde->bnche', qc, xi, r_prev)
    o = (o_inner + o_cross).reshape(B, S, H, Dv).astype(jnp.float32)
    mu = jnp.mean(o, axis=-1, keepdims=True)
    var = jnp.mean(jnp.square(o - mu), axis=-1, keepdims=True)
    o = ((o - mu) * lax.rsqrt(var + GN_EPS)).reshape(B, S, H * Dv) * gn_g.astype(jnp.float32)
    return jax.nn.silu(g_in) * o.astype(dt)


def rwkv7_mixer(p, mu, w0, w2, a0, a2, g2, k_k, k_a, r_k, ln_g, ln_b):
    B, S, _ = p.shape
    H, N = RWKV_HEADS, HEAD_DIM
    f32 = lambda t: t.astype(jnp.float32)
    prev = jnp.pad(p, ((0, 0), (1, 0), (0, 0)))[:, :-1]
    xm = f32(p + mu * (prev - p))
    splits = [int(s) for s in np.cumsum([RWKV_WIDTH, RWKV_WIDTH, RWKV_WIDTH, RWKV_W_LORA, RWKV_A_LORA])]
    r, k, v, wl, al, gl = jnp.split(xm, splits, axis=-1)
    w = -jax.nn.softplus(-(f32(w0) + jnp.tanh(wl) @ f32(w2))) - 0.5
    decay = jnp.exp(-jnp.exp(w))
    a = jax.nn.sigmoid(f32(a0) + al @ f32(a2))
    g = jax.nn.sigmoid(gl) @ f32(g2)
    kk = (k * f32(k_k)).reshape(B, S, H, N)
    kk = kk / jnp.maximum(jnp.sqrt(jnp.sum(kk * kk, axis=-1, keepdims=True)), 1e-12)
    k = k * (1.0 + (a - 1.0) * f32(k_a))
    r, k, v, a, decay = [t.reshape(B, S, H, N) for t in (r, k, v, a, decay)]

    def step(state, inp):
        r_t, k_t, v_t, kk_t, a_t, w_t = inp
        sa = jnp.einsum('bhvk,bhk->bhv', state, -kk_t)
        state = (state * w_t[:, :, None, :] + sa[..., None] * (kk_t * a_t)[:, :, None, :]
                 + v_t[..., None] * k_t[:, :, None, :])
        return state, jnp.einsum('bhvk,bhk->bhv', state, r_t)

    xs = tuple(jnp.swapaxes(t, 0, 1) for t in (r, k, v, kk, a, decay))
    _, y = lax.scan(step, jnp.zeros((B, H, N, N), jnp.float32), xs)
    y = jnp.swapaxes(y, 0, 1)
    y_mu = jnp.mean(y, axis=-1, keepdims=True)
    y_var = jnp.mean(jnp.square(y - y_mu), axis=-1, keepdims=True)
    yn = ((y - y_mu) * lax.rsqrt(y_var + RWKV_LN_EPS)).reshape(B, S, H * N) * f32(ln_g) + f32(ln_b)
    bonus = (jnp.sum(r * k * f32(r_k), axis=-1, keepdims=True) * v).reshape(B, S, H * N)
    return ((yn + bonus) * g).astype(p.dtype)


def swa_sink_mixer(q, k, v, sinks, pos):
    B, S, Hq, D = q.shape
    Hkv = k.shape[2]
    G = Hq // Hkv
    q = rope(q, pos).reshape(B, S, Hkv, G, D)
    k = rope(k, pos)
    return banded_attention(q, k, v, SWA_WINDOW, sinks=sinks.reshape(Hkv, G))


def setup_inputs(seed: int = 0) -> dict:
    key = jax.random.key(seed)
    keys = iter(jax.random.split(key, 48))
    L = DEPTH
    _, n_cols = _column_layout()

    def dense(shape, fan_in):
        return jax.random.normal(next(keys), shape, jnp.float32) * (fan_in ** -0.5)

    def gain(shape):
        return 1.0 + 0.1 * jax.random.normal(next(keys), shape, jnp.float32)

    def small(shape, scale):
        return scale * jax.random.normal(next(keys), shape, jnp.float32)

    return {
        'x': jax.random.normal(next(keys), (BATCH, SEQ, D_MODEL), jnp.float32),
        'ffn1_pre_g': gain((L, D_MODEL)),
        'ffn1_post_g': gain((L, D_MODEL)),
        'ffn1_w_gate': dense((L, D_MODEL, D_FF), D_MODEL),
        'ffn1_w_up': dense((L, D_MODEL, D_FF), D_MODEL),
        'ffn1_w_down': dense((L, D_FF, D_MODEL), D_FF),
        'mix_pre_g': gain((L, D_MODEL)),
        'mix_post_g': gain((L, D_MODEL)),
        'w_in': dense((L, D_MODEL, n_cols), D_MODEL),
        'nsa_cmp_pos_k': small((L, NSA_CMP_LEN, NSA_KV_DIM), 0.5),
        'nsa_cmp_pos_v': small((L, NSA_CMP_LEN, NSA_KV_DIM), 0.5),
        'nsa_cmp_k_w1': dense((L, NSA_CMP_LEN * NSA_KV_DIM, NSA_CMP_HIDDEN), NSA_CMP_LEN * NSA_KV_DIM),
        'nsa_cmp_k_w2': dense((L, NSA_CMP_HIDDEN, NSA_KV_DIM), NSA_CMP_HIDDEN),
        'nsa_cmp_v_w1': dense((L, NSA_CMP_LEN * NSA_KV_DIM, NSA_CMP_HIDDEN), NSA_CMP_LEN * NSA_KV_DIM),
        'nsa_cmp_v_w2': dense((L, NSA_CMP_HIDDEN, NSA_KV_DIM), NSA_CMP_HIDDEN),
        'ret_gn_g': gain((L, RET_WIDTH)),
        'rwkv_mu': jax.random.uniform(next(keys), (L, RWKV_MIX_WIDTH), jnp.float32, 0.0, 1.0),
        'rwkv_w0': jax.random.uniform(next(keys), (L, RWKV_WIDTH), jnp.float32, -6.0, -1.0),
        'rwkv_w2': dense((L, RWKV_W_LORA, RWKV_WIDTH), RWKV_W_LORA),
        'rwkv_a0': small((L, RWKV_WIDTH), 0.1),
        'rwkv_a2': dense((L, RWKV_A_LORA, RWKV_WIDTH), RWKV_A_LORA),
        'rwkv_g2': dense((L, RWKV_G_LORA, RWKV_WIDTH), RWKV_G_LORA),
        'rwkv_k_k': 0.85 + small((L, RWKV_WIDTH), 0.1),
        'rwkv_k_a': gain((L, RWKV_WIDTH)),
        'rwkv_r_k': small((L, RWKV_HEADS, HEAD_DIM), 0.3),
        'rwkv_ln_g': gain((L, RWKV_WIDTH)),
        'rwkv_ln_b': small((L, RWKV_WIDTH), 0.02),
        'swa_sinks': small((L, SWA_HEADS), 1.0),
        'w_br_nsa': dense((L, NSA_WIDTH, D_MODEL), NSA_WIDTH),
        'w_br_ret': dense((L, RET_WIDTH, D_MODEL), RET_WIDTH),
        'w_br_rwkv': dense((L, RWKV_WIDTH, D_MODEL), RWKV_WIDTH),
        'w_br_swa': dense((L, SWA_WIDTH, D_MODEL), SWA_WIDTH),
        'w_out': dense((L, D_MODEL, D_MODEL), D_MODEL),
        'ffn2_pre_g': gain((L, D_MODEL)),
        'ffn2_post_g': gain((L, D_MODEL)),
        'ffn2_w_gate': dense((L, D_MODEL, D_FF), D_MODEL),
        'ffn2_w_up': dense((L, D_MODEL, D_FF), D_MODEL),
        'ffn2_w_down': dense((L, D_FF, D_MODEL), D_FF),
    }


def reference(x, ffn1_pre_g, ffn1_post_g, ffn1_w_gate, ffn1_w_up, ffn1_w_down,
              mix_pre_g, mix_post_g, w_in,
              nsa_cmp_pos_k, nsa_cmp_pos_v, nsa_cmp_k_w1, nsa_cmp_k_w2, nsa_cmp_v_w1, nsa_cmp_v_w2,
              ret_gn_g,
              rwkv_mu, rwkv_w0, rwkv_w2, rwkv_a0, rwkv_a2, rwkv_g2, rwkv_k_k, rwkv_k_a, rwkv_r_k,
              rwkv_ln_g, rwkv_ln_b,
              swa_sinks,
              w_br_nsa, w_br_ret, w_br_rwkv, w_br_swa, w_out,
              ffn2_pre_g, ffn2_post_g, ffn2_w_gate, ffn2_w_up, ffn2_w_down):
    B, S, D = x.shape
    pos = jnp.arange(S, dtype=jnp.int32)
    layout, _ = _column_layout()
    for l in range(DEPTH):
        f = swiglu(rms_norm(x, ffn1_pre_g[l]), ffn1_w_gate[l], ffn1_w_up[l], ffn1_w_down[l])
        x = x + 0.5 * rms_norm(f, ffn1_post_g[l])
        h = rms_norm(x, mix_pre_g[l])
        p = h @ w_in[l]
        c = lambda name: _cols(p, layout, name)
        y_nsa = nsa_mixer(
            c('nsa_q').reshape(B, S, NSA_HEADS, HEAD_DIM), c('nsa_k_cmp'), c('nsa_v_cmp'),
            c('nsa_k_slc'), c('nsa_v_slc'), c('nsa_k_win'), c('nsa_v_win'), c('nsa_gate'),
            nsa_cmp_pos_k[l], nsa_cmp_pos_v[l], nsa_cmp_k_w1[l], nsa_cmp_k_w2[l],
            nsa_cmp_v_w1[l], nsa_cmp_v_w2[l], pos)
        y_ret = retention_mixer(
            c('ret_q').reshape(B, S, RET_HEADS, RET_DK), c('ret_k').reshape(B, S, RET_HEADS, RET_DK),
            c('ret_v').reshape(B, S, RET_HEADS, RET_DV), c('ret_g'), ret_gn_g[l], pos)
        y_rwkv = rwkv7_mixer(c('rwkv'), rwkv_mu[l], rwkv_w0[l], rwkv_w2[l], rwkv_a0[l], rwkv_a2[l],
                             rwkv_g2[l], rwkv_k_k[l], rwkv_k_a[l], rwkv_r_k[l], rwkv_ln_g[l], rwkv_ln_b[l])
        y_swa = swa_sink_mixer(
            c('swa_q').reshape(B, S, SWA_HEADS, HEAD_DIM), c('swa_k').reshape(B, S, SWA_KV_HEADS, HEAD_DIM),
            c('swa_v').reshape(B, S, SWA_KV_HEADS, HEAD_DIM), swa_sinks[l], pos)
        gates = jax.nn.sigmoid(c('branch_gate').reshape(B, S, N_BRANCH, D))
        merged = (gates[:, :, 0] * (y_nsa @ w_br_nsa[l]) + gates[:, :, 1] * (y_ret @ w_br_ret[l])
                  + gates[:, :, 2] * (y_rwkv @ w_br_rwkv[l]) + gates[:, :, 3] * (y_swa @ w_br_swa[l]))
        x = x + rms_norm(merged @ w_out[l], mix_post_g[l])
        f = swiglu(rms_norm(x, ffn2_pre_g[l]), ffn2_w_gate[l], ffn2_w_up[l], ffn2_w_down[l])
        x = x + 0.5 * rms_norm(f, ffn2_post_g[l])
    return x
```

```python
import numpy as np
from contextlib import ExitStack
import concourse.bass as bass
import concourse.mybir as mybir
from concourse.bass_utils import run_bass_kernel_spmd

F32 = mybir.dt.float32
BF16 = mybir.dt.bfloat16
AF = mybir.ActivationFunctionType
ALU = mybir.AluOpType
AX = mybir.AxisListType

S_LEN = 4096
D = 1024
DFF = 2816
NT = S_LEN // 128
DEPTH = 2
RMS_EPS = 1e-6

ENGS = ("pe", "act", "dve", "pool", "sp")
DMA_K = 8


def _kref(x):
    if isinstance(x, tuple):
        return x[0], tuple(x[1:])
    return x, (x.tensor.name,)


class Sched:
    def __init__(self, nc, stack):
        self.nc = nc
        self.gstack = stack
        self.esem = {e: stack.enter_context(nc.semaphore("es_" + e)) for e in ENGS if e != "sp"}
        self.dsem = {q: [stack.enter_context(nc.semaphore("ds_%s%d" % (q, i))) for i in range(DMA_K)]
                     for q in ("sp", "act", "pool")}
        self.ccount = {e: 0 for e in ENGS}
        self.qcount = {q: 0 for q in ("sp", "act", "pool")}
        self.wm = {e: {} for e in ENGS}
        self.pending = {e: [] for e in ENGS}
        self.keys = {}
        self.post_barrier = {e: set() for e in ENGS}
        self.stack = None
        self.uid = 0

    def sb(self, name, shape, dtype):
        self.uid += 1
        return self.stack.enter_context(self.nc.sbuf_tensor("%s_%d" % (name, self.uid), list(shape), dtype))

    def ps(self, name, shape, dtype=F32):
        self.uid += 1
        return self.stack.enter_context(self.nc.psum_tensor("%s_%d" % (name, self.uid), list(shape), dtype))

    def _st(self, key):
        st = self.keys.get(key)
        if st is None:
            st = {"W": {}, "R": {}, "Wd": {}, "Rd": {}}
            self.keys[key] = st
        return st

    def op(self, eng, fn, reads=(), writes=(), dma=False):
        deps = set()
        rkeys, wkeys = [], []
        for r in reads:
            if r is None:
                continue
            _, ks = _kref(r)
            rkeys.extend(ks)
        for w in writes:
            if w is None:
                continue
            _, ks = _kref(w)
            wkeys.extend(ks)
        for k in rkeys:
            st = self._st(k)
            for e, c in st["W"].items():
                deps.add((e, c))
            for q, js in st["Wd"].items():
                for j in js:
                    deps.add(("dma", q, j))
        for k in wkeys:
            st = self._st(k)
            for e, c in st["W"].items():
                deps.add((e, c))
            for e, c in st["R"].items():
                if e == eng and not dma:
                    continue
                deps.add((e, c))
            for q, js in st["Wd"].items():
                for j in js:
                    deps.add(("dma", q, j))
            for q, js in st["Rd"].items():
                for j in js:
                    deps.add(("dma", q, j))
        if eng == "pe":
            deps = {d for d in deps if d[0] != "pe"}
        deps |= self.post_barrier[eng]
        self.post_barrier[eng] = set()
        if dma:
            j = self.qcount[eng]
            self.qcount[eng] += 1
            rec = ("dma", eng, j)
            for k in rkeys:
                l = self._st(k)["Rd"].setdefault(eng, [])
                l.append(j)
                if len(l) > DMA_K:
                    del l[0]
            for k in wkeys:
                l = self._st(k)["Wd"].setdefault(eng, [])
                l.append(j)
                if len(l) > DMA_K:
                    del l[0]
            self.pending[eng].append((fn, deps, True, j))
        else:
            self.ccount[eng] += 1
            c = self.ccount[eng]
            for k in rkeys:
                self._st(k)["R"][eng] = c
            for k in wkeys:
                self._st(k)["W"][eng] = c
            self.pending[eng].append((fn, deps, False, c))

    def barrier(self):
        allc = set()
        for e in ENGS:
            if e != "sp" and self.ccount[e] > 0:
                allc.add((e, self.ccount[e]))
        for q in ("sp", "act", "pool"):
            n = self.qcount[q]
            for j in range(max(0, n - DMA_K), n):
                allc.add(("dma", q, j))
        for e in ENGS:
            self.post_barrier[e] |= allc
        self.keys = {}

    def _emit_engine(self, ename, eng, final=False):
        wm = self.wm[ename]
        for fn, deps, is_dma, idx in self.pending[ename]:
            waits = {}
            dmax = {}
            for d in deps:
                if d[0] == "dma":
                    dmax[d[1]] = max(dmax.get(d[1], -1), d[2])
            for d in deps:
                if d[0] == "dma":
                    q, j = d[1], d[2]
                    if j <= dmax[q] - DMA_K:
                        continue
                    sem = self.dsem[q][j % DMA_K]
                    val = 16 * (j // DMA_K + 1)
                else:
                    sem = self.esem[d[0]]
                    val = d[1]
                key = id(sem)
                if key not in waits or waits[key][1] < val:
                    waits[key] = (sem, val)
            if is_dma and idx >= DMA_K:
                sem = self.dsem[ename][idx % DMA_K]
                val = 16 * (idx // DMA_K)
                key = id(sem)
                if key not in waits or waits[key][1] < val:
                    waits[key] = (sem, val)
            for key, (sem, val) in waits.items():
                if wm.get(key, 0) < val:
                    eng.wait_ge(sem, val)
                    wm[key] = val
            ins = fn(eng)
            if is_dma:
                ins.then_inc(self.dsem[ename][idx % DMA_K], 16)
            else:
                ins.then_inc(self.esem[ename], 1)
        self.pending[ename] = []
        if final and ename in self.dsem:
            n = self.qcount[ename]
            for j in range(max(0, n - DMA_K), n):
                sem = self.dsem[ename][j % DMA_K]
                val = 16 * (j // DMA_K + 1)
                if wm.get(id(sem), 0) < val:
                    eng.wait_ge(sem, val)
                    wm[id(sem)] = val

    def emit(self, final=False):
        with self.nc.Block() as block:
            @block.tensor
            def _(e):
                self._emit_engine("pe", e, final)

            @block.scalar
            def _(e):
                self._emit_engine("act", e, final)

            @block.vector
            def _(e):
                self._emit_engine("dve", e, final)

            @block.gpsimd
            def _(e):
                self._emit_engine("pool", e, final)

            @block.sync
            def _(e):
                self._emit_engine("sp", e, final)

    def dma(self, out, in_, q="sp"):
        o, i = _kref(out)[0], _kref(in_)[0]
        self.op(q, lambda e: e.dma_start(out=o, in_=i), reads=[in_], writes=[out], dma=True)

    def mm(self, out, lhsT, rhs, start=True, stop=True):
        o, l, r = _kref(out)[0], _kref(lhsT)[0], _kref(rhs)[0]
        self.op("pe", lambda e: e.matmul(o, l, r, start=start, stop=stop), reads=[lhsT, rhs], writes=[out])

    def tr(self, out, in_, ident):
        o, i, d = _kref(out)[0], _kref(in_)[0], _kref(ident)[0]
        self.op("pe", lambda e: e.transpose(o, i, d), reads=[in_, ident], writes=[out])

    def act(self, out, in_, func, bias=None, scale=None, accum_out=None):
        o, i = _kref(out)[0], _kref(in_)[0]
        kw = {}
        rd = [in_]
        wr = [out]
        if bias is not None:
            if isinstance(bias, (int, float)):
                kw["bias"] = bias
            else:
                kw["bias"] = _kref(bias)[0]
                rd.append(bias)
        if scale is not None:
            if isinstance(scale, (int, float)):
                kw["scale"] = scale
            else:
                kw["scale"] = _kref(scale)[0]
                rd.append(scale)
        if accum_out is not None:
            kw["accum_out"] = _kref(accum_out)[0]
            wr.append(accum_out)
        self.op("act", lambda e: e.activation(o, i, func, **kw), reads=rd, writes=wr)

    def tt(self, out, in0, in1, op, eng="dve"):
        o, a, b = _kref(out)[0], _kref(in0)[0], _kref(in1)[0]
        self.op(eng, lambda e: e.tensor_tensor(o, a, b, op), reads=[in0, in1], writes=[out])

    def ts(self, out, in0, s1, s2, op0, op1=None, eng="dve", accum_out=None):
        o, a = _kref(out)[0], _kref(in0)[0]
        rd = [in0]
        wr = [out]

        def sc(s):
            if s is None or isinstance(s, (int, float)):
                return s
            rd.append(s)
            return _kref(s)[0]
        v1, v2 = sc(s1), sc(s2)
        kw = {}
        if op1 is not None:
            kw["op1"] = op1
        if accum_out is not None:
            kw["accum_out"] = _kref(accum_out)[0]
            wr.append(accum_out)
        self.op(eng, lambda e: e.tensor_scalar(o, a, v1, v2, op0, **kw), reads=rd, writes=wr)

    def stt(self, out, in0, scalar, in1, op0, op1, accum_out=None):
        o, a, b = _kref(out)[0], _kref(in0)[0], _kref(in1)[0]
        rd = [in0, in1]
        wr = [out]
        if isinstance(scalar, (int, float)):
            s = scalar
        else:
            s = _kref(scalar)[0]
            rd.append(scalar)
        kw = {}
        if accum_out is not None:
            kw["accum_out"] = _kref(accum_out)[0]
            wr.append(accum_out)
        self.op("dve", lambda e: e.scalar_tensor_tensor(o, a, s, b, op0, op1, **kw), reads=rd, writes=wr)

    def copy(self, out, in_, eng="dve"):
        o, i = _kref(out)[0], _kref(in_)[0]
        if eng == "act":
            self.op("act", lambda e: e.copy(o, i), reads=[in_], writes=[out])
        else:
            self.op(eng, lambda e: e.tensor_copy(o, i), reads=[in_], writes=[out])

    def recip(self, out, in_):
        o, i = _kref(out)[0], _kref(in_)[0]
        self.op("dve", lambda e: e.reciprocal(o, i), reads=[in_], writes=[out])

    def memset(self, out, val, eng="dve"):
        o = _kref(out)[0]
        self.op(eng, lambda e: e.memset(o, val), reads=[], writes=[out])

    def reduce(self, out, in_, op, axis=None, eng="dve"):
        o, i = _kref(out)[0], _kref(in_)[0]
        ax = AX.X if axis is None else axis
        self.op(eng, lambda e: e.tensor_reduce(o, i, ax, op), reads=[in_], writes=[out])


def load_w_bf16(S, dst, src_dram, nchunk, q="pool"):
    v = src_dram.rearrange("(c p) f -> p c f", p=128)
    for c in range(nchunk):
        S.dma(dst[:, c, :], v[:, c, :], q=q)


def bcast_row(S, dst, src_row_ap, q="sp"):
    S.dma(dst, src_row_ap.partition_broadcast(128), q=q)


def phase_ffn(S, nc, xres, w, l, pre, ident_d):
    TB = 256
    NSUB = TB // 128
    NFC = DFF // 128
    wg = S.sb("wg", [128, 8, DFF], BF16)
    wu = S.sb("wu", [128, 8, DFF], BF16)
    wd = S.sb("wd", [128, NFC, D], BF16)
    gpre = S.sb("gpre", [128, D], F32)
    gpost = S.sb("gpost", [128, D], F32)
    ident = S.sb("ident", [128, 128], BF16)
    S.dma(ident[:], ident_d[:, :], q="pool")
    bcast_row(S, gpre[:], w[pre + "_pre_g"][l:l + 1, :])
    bcast_row(S, gpost[:], w[pre + "_post_g"][l:l + 1, :])
    S.ts(gpost[:], gpost[:], 0.5, None, ALU.mult, eng="pool")
    load_w_bf16(S, wg, w[pre + "_w_gate"][l], 8)
    load_w_bf16(S, wu, w[pre + "_w_up"][l], 8)
    load_w_bf16(S, wd, w[pre + "_w_down"][l], NFC)

    xb = [S.sb("xb", [128, D], F32) for _ in range(NSUB * 2)]
    hb = [S.sb("hb", [128, D], BF16) for _ in range(2)]
    junk = S.sb("junk", [128, D], BF16)
    hT = [S.sb("hT", [128, 8, TB], BF16) for _ in range(2)]
    actT = S.sb("actT", [128, NFC, TB], BF16)
    sg = [S.sb("sg", [128, TB], F32) for _ in range(2)]
    fsb = [S.sb("fsb", [128, D], F32) for _ in range(2)]
    st = [S.sb("st", [128, 8], F32) for _ in range(4)]
    ptr = [S.ps("ptr", [128, 8, 128], BF16) for _ in range(2)]
    pg = [S.ps("pg", [128, 512], F32) for _ in range(2)]
    po = [S.ps("po", [128, 512], F32) for _ in range(2)]

    nblk = S_LEN // TB
    cnt = 0
    for b in range(nblk):
        hTb = hT[b % 2]
        for s in range(NSUB):
            t = b * NSUB + s
            x = xb[(b % 2) * NSUB + s]
            stt_ = st[cnt % 4]
            h = hb[cnt % 2]
            p = ptr[cnt % 2]
            cnt += 1
            S.dma(x[:], (xres[t * 128:(t + 1) * 128, :], ("xres", t)))
            S.act(junk[:], x[:], AF.Square, accum_out=stt_[:, 0:1])
            S.ts(stt_[:, 1:2], stt_[:, 0:1], 1.0 / D, RMS_EPS, ALU.mult, ALU.add)
            S.act(stt_[:, 2:3], stt_[:, 1:2], AF.Sqrt)
            S.recip(stt_[:, 3:4], stt_[:, 2:3])
            S.stt(h[:], x[:], stt_[:, 3:4], gpre[:], ALU.mult, ALU.mult)
            for dc in range(8):
                S.tr(p[:, dc, :], h[:, dc * 128:(dc + 1) * 128], ident[:])
            S.copy(hTb[:, :, s * 128:(s + 1) * 128], p[:, :, :], eng="act")
        for fc in range(NFC):
            pgt = pg[fc % 2]
            for dc in range(8):
                S.mm(pgt[:, 0:TB], wg[:, dc, fc * 128:(fc + 1) * 128], hTb[:, dc, :],
                     start=(dc == 0), stop=(dc == 7))
            for dc in range(8):
                S.mm(pgt[:, TB:2 * TB], wu[:, dc, fc * 128:(fc + 1) * 128], hTb[:, dc, :],
                     start=(dc == 0), stop=(dc == 7))
            sgt = sg[fc % 2]
            S.act(sgt[:], pgt[:, 0:TB], AF.Silu)
            S.tt(actT[:, fc, :], sgt[:], pgt[:, TB:2 * TB], ALU.mult)
        for s in range(NSUB):
            t = b * NSUB + s
            x = xb[(b % 2) * NSUB + s]
            f = fsb[s % 2]
            stt_ = st[cnt % 4]
            cnt += 1
            for dh in range(2):
                pot = po[dh]
                for fc in range(NFC):
                    S.mm(pot[:], actT[:, fc, s * 128:(s + 1) * 128], wd[:, fc, dh * 512:(dh + 1) * 512],
                         start=(fc == 0), stop=(fc == NFC - 1))
                S.act(f[:, dh * 512:(dh + 1) * 512], pot[:], AF.Copy)
                S.act(junk[:, dh * 512:(dh + 1) * 512], pot[:], AF.Square, accum_out=stt_[:, dh:dh + 1])
            S.tt(stt_[:, 2:3], stt_[:, 0:1], stt_[:, 1:2], ALU.add)
            S.ts(stt_[:, 3:4], stt_[:, 2:3], 1.0 / D, RMS_EPS, ALU.mult, ALU.add)
            S.act(stt_[:, 4:5], stt_[:, 3:4], AF.Sqrt)
            S.recip(stt_[:, 5:6], stt_[:, 4:5])
            S.stt(f[:], f[:], stt_[:, 5:6], gpost[:], ALU.mult, ALU.mult)
            S.tt(x[:], x[:], f[:], ALU.add, eng="pool")
            S.dma((xres[t * 128:(t + 1) * 128, :], ("xres", t)), x[:], q="act")


HD = 64


def col_layout():
    spec = (('nsa_q', 256), ('nsa_k_cmp', 64), ('nsa_v_cmp', 64), ('nsa_k_slc', 64), ('nsa_v_slc', 64),
            ('nsa_k_win', 64), ('nsa_v_win', 64), ('nsa_gate', 12), ('ret_q', 256), ('ret_k', 256),
            ('ret_v', 512), ('ret_g', 512), ('rwkv', 1024), ('swa_q', 256), ('swa_k', 128), ('swa_v', 128),
            ('branch_gate', 4096))
    lay, s = {}, 0
    for n, wd_ in spec:
        lay[n] = (s, s + wd_)
        s += wd_
    return lay, s


def _partner(cols):
    out = []
    for i in range(0, len(cols), 64):
        blk = cols[i:i + 64]
        out.extend(blk[32:64])
        out.extend(blk[0:32])
    return out


def w_in_index_sets():
    lay, _ = col_layout()
    r = lambda n: list(range(*lay[n]))
    idx = {}
    q = r('swa_q')
    k = r('swa_k')
    kk0 = k[0:64] + k[0:64]
    kk1 = k[64:128] + k[64:128]
    idx['swa'] = q + _partner(q) + kk0 + kk1 + _partner(kk0) + _partner(kk1) + r('swa_v')
    nq, ksl, kwi = r('nsa_q'), r('nsa_k_slc'), r('nsa_k_win')
    idx['nsa'] = (nq + _partner(nq) + r('nsa_k_cmp') + r('nsa_v_cmp') + ksl + _partner(ksl) + kwi + _partner(kwi)
                  + r('nsa_v_slc') + r('nsa_v_win') + r('nsa_gate'))
    idx['r'] = r('rwkv')
    idx['gate'] = r('branch_gate')
    rq, rk = r('ret_q'), r('ret_k')
    idx['ret'] = rq + _partner(rq) + rk + _partner(rk) + r('ret_v') + r('ret_g')
    return idx


def phase_mixpre(S, nc, xres, w, l, hT_d, ident_d):
    g = S.sb("g", [128, D], F32)
    ident = S.sb("ident", [128, 128], BF16)
    S.dma(ident[:], ident_d[:, :], q="pool")
    bcast_row(S, g[:], w["mix_pre_g"][l:l + 1, :])
    xb = [S.sb("xb", [128, D], F32) for _ in range(3)]
    hb = [S.sb("hb", [128, D], BF16) for _ in range(2)]
    junk = S.sb("junk", [128, D], BF16)
    hT = [S.sb("hT", [128, 8, 128], BF16) for _ in range(3)]
    st = [S.sb("st", [128, 8], F32) for _ in range(3)]
    ptr = [S.ps("ptr", [128, 8, 128], BF16) for _ in range(2)]
    hv = hT_d.rearrange("(c p) t -> p c t", p=128)
    for t in range(NT):
        x = xb[t % 3]
        s_ = st[t % 3]
        h = hb[t % 2]
        p = ptr[t % 2]
        o = hT[t % 3]
        S.dma(x[:], (xres[t * 128:(t + 1) * 128, :], ("xres", t)))
        S.act(junk[:], x[:], AF.Square, accum_out=s_[:, 0:1])
        S.ts(s_[:, 1:2], s_[:, 0:1], 1.0 / D, RMS_EPS, ALU.mult, ALU.add)
        S.act(s_[:, 2:3], s_[:, 1:2], AF.Sqrt)
        S.recip(s_[:, 3:4], s_[:, 2:3])
        S.stt(h[:], x[:], s_[:, 3:4], g[:], ALU.mult, ALU.mult)
        for dc in range(8):
            S.tr(p[:, dc, :], h[:, dc * 128:(dc + 1) * 128], ident[:])
        S.copy(o[:], p[:], eng="act")
        S.dma((hv[:, :, t * 128:(t + 1) * 128], ("hT", t)), o[:], q="act")


def proj_feat(S, dst, wsb, col0, hT, pps, rope=None, npart=128):
    for tb in range(8):
        tsl = slice(tb * 512, (tb + 1) * 512)
        p0 = pps[tb % 2]
        for dc in range(8):
            S.mm(p0[0:npart, 0:512], wsb[:, dc, col0:col0 + npart], hT[:, dc, tsl],
                 start=(dc == 0), stop=(dc == 7))
        if rope is None:
            S.copy(dst[0:npart, tsl], p0[0:npart, 0:512], eng="act")
        else:
            pc, cos_d, sin_d, cst, tmp, pps2 = rope
            p1 = pps2[tb % 2]
            for dc in range(8):
                S.mm(p1[0:npart, 0:512], wsb[:, dc, pc:pc + npart], hT[:, dc, tsl],
                     start=(dc == 0), stop=(dc == 7))
            cs = cst[tb % 2]
            S.dma(cs[:, 0, :], cos_d[:, tsl])
            S.dma(cs[:, 1, :], sin_d[:, tsl])
            t1 = tmp[tb % 2]
            S.tt(t1[0:npart, 0, :], p0[0:npart, 0:512], cs[0:npart, 0, :], ALU.mult)
            S.tt(t1[0:npart, 1, :], p1[0:npart, 0:512], cs[0:npart, 1, :], ALU.mult)
            S.tt(dst[0:npart, tsl], t1[0:npart, 0, :], t1[0:npart, 1, :], ALU.add, eng="pool")


def proj_tok(S, dst_fn, wsb, col0, ncol, hT, pps, eng="act", view=None):
    for t in range(NT):
        p0 = pps[t % 2]
        for dc in range(8):
            S.mm(p0[:, 0:ncol], hT[:, dc, t * 128:(t + 1) * 128], wsb[:, dc, col0:col0 + ncol],
                 start=(dc == 0), stop=(dc == 7))
        src = p0[:, 0:ncol]
        if view is not None:
            src = view(src)
        S.copy(dst_fn(t), src, eng=eng)


def attn_qblock(S, nheads, q_of, k_of, v_of, tiles, scale, pss, pacc, pbuf, cnt):
    for i, (kt, mask_fn) in enumerate(tiles):
        mask = mask_fn() if mask_fn is not None else None
        for h in range(nheads):
            ps = pss[cnt % 2]
            pb = pbuf[cnt % 3]
            cnt += 1
            S.mm(ps[:, 0:512], k_of(h, kt), q_of(h), start=True, stop=True)
            S.act(pb[:], ps[:, 0:512], AF.Exp, scale=scale)
            if mask is not None:
                S.tt(pb[:], pb[:], mask, ALU.mult)
            for sub in range(4):
                S.mm(pacc[h][:, sub, :], pb[:, sub * 128:(sub + 1) * 128], v_of(h, kt),
                     start=(i == 0 and sub == 0), stop=(i == len(tiles) - 1))
    return cnt


def attn_finish(S, nheads, pacc, ybuf, st, zextra=None, gate_of=None, accumulate=False):
    for h in range(nheads):
        s_ = st[h % len(st)]
        if zextra is not None:
            S.ts(s_[:, 0:4], pacc[h][:, :, 64], zextra(h), None, ALU.add)
        else:
            S.ts(s_[:, 0:4], pacc[h][:, :, 64], 1e-30, None, ALU.max)
        S.recip(s_[:, 4:8], s_[:, 0:4])
        if gate_of is not None:
            S.tt(s_[:, 4:8], s_[:, 4:8], gate_of(h), ALU.mult)
        for sub in range(4):
            yo = ybuf[:, sub, h * 64:(h + 1) * 64]
            if not accumulate:
                S.ts(yo, pacc[h][:, sub, 0:64], s_[:, 4 + sub:5 + sub], None, ALU.mult)
            else:
                S.stt(yo, pacc[h][:, sub, 0:64], s_[:, 4 + sub:5 + sub], yo, ALU.mult, ALU.add)


def phase_swa(S, nc, w, l, hT_d, wswa_d, cst, y_d):
    NCOL = 8 * 128 + 128
    wsb = S.sb("wsb", [128, 8, NCOL], BF16)
    load_w_bf16(S, wsb, wswa_d[l], 8)
    hT = S.sb("hTall", [128, 8, S_LEN], BF16)
    hv = hT_d.rearrange("(c p) t -> p c t", p=128)
    for c in range(8):
        S.dma(hT[:, c, :], (hv[:, c, :],) + tuple(("hT", t) for t in range(NT)))
    qT = [S.sb("qT", [128, S_LEN], BF16) for _ in range(2)]
    kT = [S.sb("kT", [128, S_LEN], BF16) for _ in range(2)]
    vext = S.sb("vext", [128, NT, 2, 65], BF16)
    pps = [S.ps("pp", [128, 512], F32) for _ in range(2)]
    pps2 = [S.ps("pp2", [128, 512], F32) for _ in range(2)]
    cstt = [S.sb("cs", [128, 2, 512], F32) for _ in range(2)]
    tmp = [S.sb("tmp", [128, 2, 512], F32) for _ in range(2)]
    rope = lambda pc: (pc, cst["cosT"], cst["sinT"], cstt, tmp, pps2)
    proj_feat(S, qT[0], wsb, 0, hT, pps, rope(256))
    proj_feat(S, qT[1], wsb, 128, hT, pps, rope(384))
    proj_feat(S, kT[0], wsb, 512, hT, pps, rope(768))
    proj_feat(S, kT[1], wsb, 640, hT, pps, rope(896))
    S.memset(vext[:, :, :, 64:65], 1.0, eng="pool")
    proj_tok(S, lambda t: vext[:, t, :, 0:64], wsb, 1024, 128, hT, pps,
             view=lambda a: a.rearrange("p (a b) -> p a b", a=2))
    masks = S.sb("masks", [128, 5, 512], BF16)
    for r in range(5):
        S.dma(masks[:, r, :], cst["mask_swa"][r], q="pool")
    sk = S.sb("sk", [128, 4], F32)
    S.dma(sk[:], w["swa_sinks"][l:l + 1, :].partition_broadcast(128))
    S.act(sk[:], sk[:], AF.Exp)
    pacc = [S.ps("pacc", [128, 4, 65], F32) for _ in range(4)]
    pbuf = [S.sb("pbuf", [128, 512], BF16) for _ in range(3)]
    ybuf = [S.sb("ybuf", [128, 4, 256], BF16) for _ in range(2)]
    st = [S.sb("st", [128, 8], F32) for _ in range(4)]
    cnt = 0
    for qb in range(8):
        qs = slice(qb * 512, (qb + 1) * 512)
        tiles = [(4 * qb + r, (lambda r=r: masks[:, r + 1, :])) for r in range(-1, 4) if 4 * qb + r >= 0]
        yb = ybuf[qb % 2]
        cnt = attn_qblock(
            S, 4,
            lambda h: qT[h // 2][(h % 2) * 64:(h % 2) * 64 + 64, qs],
            lambda h, kt: kT[h // 2][(h % 2) * 64:(h % 2) * 64 + 64, kt * 128:(kt + 1) * 128],
            lambda h, kt: vext[:, kt, h // 2, :],
            tiles, 0.125, pps, pacc, pbuf, cnt)
        attn_finish(S, 4, pacc, yb, st, zextra=lambda h: sk[:, h:h + 1])
        S.dma(y_d[qb * 512:(qb + 1) * 512, :].rearrange("(s p) f -> p s f", p=128), yb[:], q="act")


def phase_ret(S, nc, w, l, hT_d, wret_d, cst, y_d, ident_d):
    wsb = S.sb("wsb", [128, 8, 2048], BF16)
    load_w_bf16(S, wsb, wret_d[l], 8)
    hT = S.sb("hTall", [128, 8, S_LEN], BF16)
    hv = hT_d.rearrange("(c p) t -> p c t", p=128)
    for c in range(8):
        S.dma(hT[:, c, :], (hv[:, c, :],) + tuple(("hT", t) for t in range(NT)))
    ident = S.sb("ident", [128, 128], BF16)
    S.dma(ident[:], ident_d[:, :], q="pool")
    qT = [S.sb("qT", [64, S_LEN], BF16) for _ in range(4)]
    kT = [S.sb("kT", [64, S_LEN], BF16) for _ in range(4)]
    pps = [S.ps("pp", [128, 512], F32) for _ in range(2)]
    pps2 = [S.ps("pp2", [128, 512], F32) for _ in range(2)]
    cstt = [S.sb("cs", [128, 2, 512], F32) for _ in range(2)]
    tmp = [S.sb("tmp", [128, 2, 512], F32) for _ in range(2)]
    rope = lambda pc: (pc, cst["cosT"], cst["sinT"], cstt, tmp, pps2)
    for h in range(4):
        proj_feat(S, qT[h], wsb, h * 64, hT, pps, rope(256 + h * 64), npart=64)
        proj_feat(S, kT[h], wsb, 512 + h * 64, hT, pps, rope(768 + h * 64), npart=64)
    dmask = S.sb("dmask", [128, 4, 128], F32)
    S.dma(dmask[:], cst["ret_dmaskT"][:, :, :])
    zt = S.sb("zt", [128, 4], F32)
    S.dma(zt[:], cst["ret_zeta"][:, :])
    xiT = S.sb("xiT", [64, 4, 128], F32)
    S.dma(xiT[:], cst["ret_xiT"][:, :, :])
    gch = S.sb("gch", [64, 4], F32)
    S.dma(gch[:], cst["ret_gch"][:, :])
    gng = S.sb("gng", [128, 512], F32)
    bcast_row(S, gng[:], w["ret_gn_g"][l:l + 1, :])
    R = S.sb("R", [64, 4, 128], F32)
    Rbf = S.sb("Rbf", [64, 4, 128], BF16)
    S.memset(R[:], 0.0)
    S.memset(Rbf[:], 0.0)
    po = [S.ps("po", [128, 4, 128], F32) for _ in range(2)]
    ptk = S.ps("ptk", [128, 4, 64], BF16)
    pin = pps2[0][:, :].rearrange("p (h c) -> p h c", h=4)
    pkv = pps2[1][:, :].rearrange("p (h e) -> p h e", h=4)
    vb = [S.sb("vb", [128, 512], BF16) for _ in range(2)]
    sgb = [S.sb("sgb", [128, 512], F32) for _ in range(2)]
    qx = [S.sb("qx", [64, 4, 128], BF16) for _ in range(2)]
    kz = [S.sb("kz", [128, 4, 64], BF16) for _ in range(2)]
    inm = [S.sb("inm", [128, 4, 128], BF16) for _ in range(2)]
    osb = [S.sb("osb", [128, 4, 128], F32) for _ in range(2)]
    sq = [S.sb("sq", [128, 4, 128], F32) for _ in range(2)]
    st = [S.sb("st", [128, 8, 4], F32) for _ in range(2)]
    yb = [S.sb("yb", [128, 512], BF16) for _ in range(2)]
    for t in range(NT):
        ts_ = slice(t * 128, (t + 1) * 128)
        b = t % 2
        for dc in range(8):
            S.mm(pps[0][:, :], hT[:, dc, ts_], wsb[:, dc, 1024:1536], start=(dc == 0), stop=(dc == 7))
        S.copy(vb[b][:], pps[0][:, :], eng="act")
        for dc in range(8):
            S.mm(pps[1][:, :], hT[:, dc, ts_], wsb[:, dc, 1536:2048], start=(dc == 0), stop=(dc == 7))
        S.act(sgb[b][:], pps[1][:, :], AF.Silu)
        for h in range(4):
            S.tr(ptk[:, h, :], kT[h][:, ts_], ident[0:64, 0:64])
        S.tt(kz[b][:], ptk[:, :, :], zt[:, :].unsqueeze(2).to_broadcast([128, 4, 64]), ALU.mult)
        for h in range(4):
            S.tt(qx[b][:, h, :], qT[h][:, ts_], xiT[:, h, :], ALU.mult, eng="pool")
        for h in range(4):
            S.mm(pin[:, h, :], kT[h][:, ts_], qT[h][:, ts_], start=True, stop=True)
        S.tt(inm[b][:], pin, dmask[:], ALU.mult)
        pot = po[b]
        for h in range(4):
            S.mm(pot[:, h, :], inm[b][:, h, :], vb[b][:, h * 128:(h + 1) * 128], start=True, stop=False)
            S.mm(pot[:, h, :], qx[b][:, h, :], Rbf[:, h, :], start=False, stop=True)
        for h in range(4):
            S.mm(pkv[0:64, h, :], kz[b][:, h, :], vb[b][:, h * 128:(h + 1) * 128], start=True, stop=True)
        for h in range(4):
            S.stt(R[:, h, :], R[:, h, :], gch[:, h:h + 1], pkv[0:64, h, :], ALU.mult, ALU.add)
        S.copy(Rbf[:], R[:], eng="act")
        o = osb[b]
        s_ = st[b]
        S.copy(o[:], pot[:], eng="act")
        S.reduce(s_[:, 0, :], o[:], ALU.add)
        S.tt(sq[b][:], o[:], o[:], ALU.mult, eng="pool")
        S.reduce(s_[:, 1, :], sq[b][:], ALU.add)
        S.ts(s_[:, 2, :], s_[:, 0, :], 1.0 / 128, None, ALU.mult)
        S.tt(s_[:, 3, :], s_[:, 2, :], s_[:, 2, :], ALU.mult)
        S.stt(s_[:, 4, :], s_[:, 1, :], 1.0 / 128, s_[:, 3, :], ALU.mult, ALU.subtract)
        S.ts(s_[:, 5, :], s_[:, 4, :], 1e-5, None, ALU.add)
        S.act(s_[:, 6, :], s_[:, 5, :], AF.Sqrt)
        S.recip(s_[:, 7, :], s_[:, 6, :])
        S.tt(o[:], o[:], s_[:, 2, :].unsqueeze(2).to_broadcast([128, 4, 128]), ALU.subtract)
        S.tt(o[:], o[:], s_[:, 7, :].unsqueeze(2).to_broadcast([128, 4, 128]), ALU.mult)
        of = o[:].rearrange("p h e -> p (h e)")
        S.tt(of, of, gng[:], ALU.mult, eng="pool")
        S.tt(yb[b][:], of, sgb[b][:], ALU.mult)
        S.dma(y_d[ts_, :], yb[b][:], q="act")


def load_hT(S, hT_d):
    hT = S.sb("hTall", [128, 8, S_LEN], BF16)
    hv = hT_d.rearrange("(c p) t -> p c t", p=128)
    for c in range(8):
        S.dma(hT[:, c, :], (hv[:, c, :],) + tuple(("hT", t) for t in range(NT)))
    return hT


def phase_nsa_a(S, nc, w, l, hT_d, wnsa_d, cst, scr):
    wsb = S.sb("wsb", [128, 8, 1036], BF16)
    load_w_bf16(S, wsb, wnsa_d[l], 8)
    hT = load_hT(S, hT_d)
    pps = [S.ps("pp", [128, 512], F32) for _ in range(2)]
    pps2 = [S.ps("pp2", [128, 512], F32) for _ in range(2)]
    cstt = [S.sb("cs", [128, 2, 512], F32) for _ in range(2)]
    tmp = [S.sb("tmp", [128, 2, 512], F32) for _ in range(2)]
    rope = lambda pc: (pc, cst["cosT"], cst["sinT"], cstt, tmp, pps2)
    stg = [S.sb("stg", [64, S_LEN], BF16) for _ in range(2)]
    outs = [(h, h * 64, None) for h in range(4)] + [(4 + h, h * 64, 256 + h * 64) for h in range(4)]
    outs += [(8, 512, None), (9, 576, None), (10, 640, 704), (11, 768, 832)]
    for n, (idx, col0, pc) in enumerate(outs):
        st_ = stg[n % 2]
        proj_feat(S, st_, wsb, col0, hT, pps, rope(pc) if pc is not None else None, npart=64)
        S.dma((scr["nsaT_d"][idx], ("nsaT", idx)), st_[:, :], q="act")
    vb = [S.sb("vb", [128, 128], BF16) for _ in range(2)]
    gb = [S.sb("gb", [128, 12], F32) for _ in range(2)]
    for t in range(NT):
        p0 = pps[t % 2]
        for dc in range(8):
            S.mm(p0[:, 0:140], hT[:, dc, t * 128:(t + 1) * 128], wsb[:, dc, 896:1036],
                 start=(dc == 0), stop=(dc == 7))
        S.copy(vb[t % 2][:], p0[:, 0:128], eng="dve")
        S.act(gb[t % 2][:], p0[:, 128:140], AF.Sigmoid)
        S.dma(scr["nsa_v_d"][t * 128:(t + 1) * 128, :], vb[t % 2][:], q="act")
        S.dma(scr["nsa_g_d"][t * 128:(t + 1) * 128, :], gb[t % 2][:], q="act")


def phase_nsa_b(S, nc, w, l, cst, scr):
    kvT = S.sb("kvT", [64, 2, S_LEN], BF16)
    S.dma(kvT[:, 0, :], scr["nsaT_d"][8])
    S.dma(kvT[:, 1, :], scr["nsaT_d"][9])
    posT = S.sb("posT", [64, 2, 32], F32)
    S.dma(posT[:], w["nsa_posT"][l].rearrange("a d l -> d a l"))
    ident = S.sb("ident", [128, 128], BF16)
    S.dma(ident[:], cst["ident"][:, :], q="pool")
    w1 = [S.sb("w1", [64, 32, 256], BF16) for _ in range(2)]
    w2 = [S.sb("w2", [128, 2, 64], BF16) for _ in range(2)]
    for a, nm in enumerate(("k", "v")):
        S.dma(w1[a][:], w["nsa_cmp_%s_w1" % nm][l].rearrange("(l d) j -> d l j", d=64), q="pool")
        S.dma(w2[a][:], w["nsa_cmp_%s_w2" % nm][l].rearrange("(c p) d -> p c d", p=128), q="pool")
    X = S.sb("X", [64, 32, 256], BF16)
    gT = [S.sb("gT", [128, 2, 256], BF16) for _ in range(2)]
    kcT = S.sb("kcT", [64, 256], BF16)
    vcx = S.sb("vcx", [128, 2, 65], BF16)
    ov = S.sb("ov", [128, 2, 64], BF16)
    S.dma(ov[:], cst["overlap"].rearrange("(c p) j -> p c j", p=128), q="pool")
    S.memset(kcT[:], 0.0)
    S.memset(vcx[:], 0.0)
    S.memset(vcx[:, :, 64:65], 1.0)
    for a in range(2):
        S.memset(gT[a][:], 0.0)
    pps = [S.ps("pp", [128, 512], F32) for _ in range(2)]
    hs = [S.sb("hs", [128, 3, 256], F32) for _ in range(2)]
    for a in range(2):
        for l_ in range(32):
            S.ts(X[:, l_, 0:255], kvT[:, a, l_:l_ + 16 * 254 + 1:16], posT[:, a, l_:l_ + 1], None, ALU.add)
        for jc in range(2):
            ph = pps[jc]
            for l_ in range(32):
                S.mm(ph[:, 0:255], w1[a][:, l_, jc * 128:(jc + 1) * 128], X[:, l_, 0:255],
                     start=(l_ == 0), stop=(l_ == 31))
            h_ = hs[jc]
            S.act(h_[:, 0, 0:255], ph[:, 0:255], AF.Square)
            S.ts(h_[:, 0, 0:255], h_[:, 0, 0:255], 0.044715, 1.0, ALU.mult, ALU.add)
            S.tt(h_[:, 1, 0:255], h_[:, 0, 0:255], ph[:, 0:255], ALU.mult)
            S.act(h_[:, 2, 0:255], h_[:, 1, 0:255], AF.Sigmoid, scale=1.5957691216057308)
            S.tt(gT[a][:, jc, 0:255], h_[:, 2, 0:255], ph[:, 0:255], ALU.mult)
    for jc in range(2):
        S.mm(pps[0][0:64, 0:256], w2[0][:, jc, :], gT[0][:, jc, :], start=(jc == 0), stop=(jc == 1))
    S.copy(kcT[:, :], pps[0][0:64, 0:256], eng="act")
    for ic in range(2):
        for jc in range(2):
            S.mm(pps[1][:, ic * 64:(ic + 1) * 64], gT[1][:, jc, ic * 128:(ic + 1) * 128], w2[1][:, jc, :],
                 start=(jc == 0), stop=(jc == 1))
        S.copy(vcx[:, ic, 0:64], pps[1][:, ic * 64:(ic + 1) * 64], eng="act")
    qT = S.sb("qT", [64, 4, S_LEN], BF16)
    for h in range(4):
        S.dma(qT[:, h, :], scr["nsaT_d"][h])
    cmask = S.sb("cmask", [128, 2, S_LEN], BF16)
    S.dma(cmask[:], cst["cmpmaskT"].rearrange("(c p) q -> p c q", p=128), q="pool")
    pss = [S.ps("pss", [128, 512], F32) for _ in range(2)]
    pao = [S.ps("pao", [128, 4, 65], F32) for _ in range(2)]
    pai = S.ps("pai", [128, 4, 64], F32)
    pT = S.ps("pT", [64, 4, 128], BF16)
    pbuf = [S.sb("pbuf", [128, 512], BF16) for _ in range(3)]
    ocb = [S.sb("ocb", [128, 4, 256], BF16) for _ in range(2)]
    imp = [S.sb("imp", [128, 4, 64], F32) for _ in range(2)]
    sbt = [S.sb("sbt", [128, 4, 64], F32) for _ in range(2)]
    score = [S.sb("score", [128, 4, 64], F32) for _ in range(2)]
    work = S.sb("work", [128, 4, 64], F32)
    m8 = S.sb("m8", [128, 4, 8], F32)
    m8b = S.sb("m8b", [128, 4, 8], F32)
    selm = [S.sb("selm", [128, 4, 64], BF16) for _ in range(2)]
    selT = S.sb("selT", [64, S_LEN], BF16)
    st = [S.sb("st", [128, 8], F32) for _ in range(4)]
    cnt = 0
    for qb in range(8):
        qs = slice(qb * 512, (qb + 1) * 512)
        b = qb % 2
        for h in range(4):
            po_ = pao[h % 2]
            for ic in range(2):
                ps = pss[cnt % 2]
                pb = pbuf[cnt % 3]
                cnt += 1
                S.mm(ps[:, 0:512], kcT[:, ic * 128:(ic + 1) * 128], qT[:, h, qs], start=True, stop=True)
                S.act(pb[:], ps[:, 0:512], AF.Exp, scale=0.125)
                S.tt(pb[:], pb[:], cmask[:, ic, qs], ALU.mult)
                for sub in range(4):
                    S.mm(po_[:, sub, :], pb[:, sub * 128:(sub + 1) * 128], vcx[:, ic, :],
                         start=(ic == 0 and sub == 0), stop=(ic == 1))
                for sub in range(4):
                    S.mm(pai[:, sub, :], pb[:, sub * 128:(sub + 1) * 128], ov[:, ic, :],
                         start=(ic == 0 and sub == 0), stop=(ic == 1))
            s_ = st[h]
            S.ts(s_[:, 0:4], po_[:, :, 64], 1e-30, None, ALU.max)
            S.recip(s_[:, 4:8], s_[:, 0:4])
            for sub in range(4):
                S.ts(ocb[b][:, sub, h * 64:(h + 1) * 64], po_[:, sub, 0:64], s_[:, 4 + sub:5 + sub], None, ALU.mult)
                if h == 0:
                    S.ts(imp[b][:, sub, :], pai[:, sub, :], s_[:, 4 + sub:5 + sub], None, ALU.mult)
                else:
                    S.stt(imp[b][:, sub, :], pai[:, sub, :], s_[:, 4 + sub:5 + sub], imp[b][:, sub, :],
                          ALU.mult, ALU.add)
        S.dma(scr["ocmp_d"][qs, :].rearrange("(s p) f -> p s f", p=128), ocb[b][:], q="act")
        S.dma(sbt[b][:], cst["selbias"][qs, :].rearrange("(s p) j -> p s j", p=128))
        sc = score[b]
        S.tt(sc[:], imp[b][:], sbt[b][:], ALU.add)
        for sub in range(4):
            a_sc, a_m8, a_wk, a_m8b = sc[:, sub, :], m8[:, sub, :], work[:, sub, :], m8b[:, sub, :]
            S.op("dve", lambda e, o=a_m8, i=a_sc: e.max(o, i), reads=[sc[:]], writes=[m8[:]])
            S.op("dve", lambda e, o=a_wk, r=a_m8, i=a_sc: e.match_replace(o, r, i, -3.0e9),
                 reads=[sc[:], m8[:]], writes=[work[:]])
            S.op("dve", lambda e, o=a_m8b, i=a_wk: e.max(o, i), reads=[work[:]], writes=[m8b[:]])
            S.ts(selm[b][:, sub, :], sc[:, sub, :], m8b[:, sub, 7:8], None, ALU.is_ge)
        for sub in range(4):
            S.tr(pT[:, sub, :], selm[b][:, sub, :], ident[:])
        S.copy(selT[:, qs].rearrange("j (s p) -> j s p", s=4), pT[:, :, :], eng="act")
    S.dma(scr["selT_d"][:, :], selT[:, :], q="act")


def phase_nsa_c(S, nc, w, l, cst, scr, y_d):
    qrT = S.sb("qrT", [64, 4, S_LEN], BF16)
    for h in range(4):
        S.dma(qrT[:, h, :], scr["nsaT_d"][4 + h])
    kT = S.sb("kT", [64, 2, S_LEN], BF16)
    S.dma(kT[:, 0, :], scr["nsaT_d"][10])
    S.dma(kT[:, 1, :], scr["nsaT_d"][11])
    selT = S.sb("selT", [64, S_LEN], BF16)
    S.dma(selT[:, :], scr["selT_d"][:, :])
    vext = S.sb("vext", [128, NT, 2, 65], BF16)
    S.memset(vext[:, :, :, 64:65], 1.0, eng="pool")
    vv = scr["nsa_v_d"].rearrange("(t p) f -> p t f", p=128)
    for a in range(2):
        S.dma(vext[:, :, a, 0:64], vv[:, :, a * 64:(a + 1) * 64])
    gsb = S.sb("gsb", [128, NT, 12], F32)
    S.dma(gsb[:], scr["nsa_g_d"].rearrange("(t p) c -> p t c", p=128))
    mwin = S.sb("mwin", [128, 8, 512], BF16)
    for r in range(8):
        S.dma(mwin[:, r, :], cst["mask_win"][r], q="pool")
    mcau = S.sb("mcau", [128, 4, 512], BF16)
    for r in range(4):
        S.dma(mcau[:, r, :], cst["mask_causal"][r], q="pool")
    Eall = S.sb("Eall", [64, NT, 128], BF16)
    S.dma(Eall[:], cst["Eall"][:, :, :], q="pool")
    pss = [S.ps("pss", [128, 512], F32) for _ in range(2)]
    pacc = [S.ps("pacc", [128, 4, 65], F32) for _ in range(4)]
    pm = [S.ps("pm", [128, 512], F32) for _ in range(2)]
    pbuf = [S.sb("pbuf", [128, 512], BF16) for _ in range(3)]
    mbuf = [S.sb("mbuf", [128, 512], BF16) for _ in range(3)]
    ocb = [S.sb("ocb", [128, 4, 256], BF16) for _ in range(2)]
    ybuf = [S.sb("ybuf", [128, 4, 256], F32) for _ in range(2)]
    yb16 = [S.sb("yb16", [128, 4, 256], BF16) for _ in range(2)]
    st = [S.sb("st", [128, 8], F32) for _ in range(4)]
    cnt = 0
    mcnt = [0]
    for qb in range(8):
        qs = slice(qb * 512, (qb + 1) * 512)
        b = qb % 2
        yb = ybuf[b]
        gq = gsb[:, 4 * qb:4 * qb + 4, :]
        S.dma(ocb[b][:], scr["ocmp_d"][qs, :].rearrange("(s p) f -> p s f", p=128))
        for h in range(4):
            S.tt(yb[:, :, h * 64:(h + 1) * 64], ocb[b][:, :, h * 64:(h + 1) * 64],
                 gq[:, :, 3 * h:3 * h + 1].to_broadcast([128, 4, 64]), ALU.mult)
        tiles = [(4 * qb + r, (lambda r=r: mwin[:, r + 4, :])) for r in range(-4, 4) if 4 * qb + r >= 0]
        cnt = attn_qblock(S, 4, lambda h: qrT[:, h, qs], lambda h, kt: kT[:, 1, kt * 128:(kt + 1) * 128],
                          lambda h, kt: vext[:, kt, 1, :], tiles, 0.125, pss, pacc, pbuf, cnt)
        attn_finish(S, 4, pacc, yb, st, gate_of=lambda h: gq[:, :, 3 * h + 2], accumulate=True)

        def mk_mask(kt):
            def f():
                i = mcnt[0]
                mcnt[0] += 1
                p_ = pm[i % 2]
                m_ = mbuf[i % 3]
                S.mm(p_[:, 0:512], Eall[:, kt, :], selT[:, qs], start=True, stop=True)
                r = kt - 4 * qb
                if r >= 0:
                    S.tt(m_[:], p_[:, 0:512], mcau[:, r, :], ALU.mult)
                else:
                    S.copy(m_[:], p_[:, 0:512], eng="pool" if False else "dve")
                return m_[:]
            return f
        tiles = [(kt, mk_mask(kt)) for kt in range(0, 4 * qb + 4)]
        cnt = attn_qblock(S, 4, lambda h: qrT[:, h, qs], lambda h, kt: kT[:, 0, kt * 128:(kt + 1) * 128],
                          lambda h, kt: vext[:, kt, 0, :], tiles, 0.125, pss, pacc, pbuf, cnt)
        attn_finish(S, 4, pacc, yb, st, gate_of=lambda h: gq[:, :, 3 * h + 1], accumulate=True)
        S.copy(yb16[b][:], yb[:], eng="act")
        S.dma(y_d[qs, :].rearrange("(s p) f -> p s f", p=128), yb16[b][:], q="act")


NEG_EH = -0.6065306597126334


def phase_rwkv_a(S, nc, w, l, hT_d, wr_d, cst, scr):
    mub = S.sb("mub", [128, 1024], F32)
    omub = S.sb("omub", [128, 1024], F32)
    bcast_row(S, mub[:], w["rwkv_mu"][l:l + 1, :])
    S.ts(omub[:], mub[:], -1.0, 1.0, ALU.mult, ALU.add)
    W1 = S.sb("W1", [128, 8, 1024], BF16)
    W2 = S.sb("W2", [128, 8, 1024], BF16)
    wst = [S.sb("wst", [128, 1024], F32) for _ in range(2)]
    wv = wr_d[l].rearrange("(c p) f -> p c f", p=128)
    for c in range(8):
        S.dma(wst[c % 2][:], wv[:, c, :])
        S.tt(W1[:, c, :], wst[c % 2][:], omub[:], ALU.mult)
        S.tt(W2[:, c, :], wst[c % 2][:], mub[:], ALU.mult, eng="pool")
    hTp = S.sb("hTp", [128, 8, S_LEN + 1], BF16)
    S.memset(hTp[:, :, 0:1], 0.0)
    hv = hT_d.rearrange("(c p) t -> p c t", p=128)
    for c in range(8):
        S.dma(hTp[:, c, 1:S_LEN + 1], (hv[:, c, :],) + tuple(("hT", t) for t in range(NT)))
    cols = S.sb("cols", [64, 20], F32)
    S.dma(cols[:], w["rwkv_cols"][l])
    omka = S.sb("omka", [64, 4], F32)
    S.ts(omka[:], cols[:, 12:16], -1.0, 1.0, ALU.mult, ALU.add)
    rkc = S.sb("rkc", [64, 4], BF16)
    S.copy(rkc[:], cols[:, 16:20])
    w2sb = S.sb("w2sb", [64, 256], BF16)
    a2sb = S.sb("a2sb", [64, 256], BF16)
    g2sb = S.sb("g2sb", [128, 256], BF16)
    S.dma(w2sb[:], w["rwkv_w2"][l], q="pool")
    S.dma(a2sb[:], w["rwkv_a2"][l], q="pool")
    S.dma(g2sb[:], w["rwkv_g2"][l], q="pool")
    ones64 = S.sb("ones64", [64, 64], BF16)
    S.memset(ones64[:], 1.0)
    rmask = S.sb("rmask", [64, 512], F32)
    S.dma(rmask[:], cst["rw_reset"][:, :])
    gCs = S.sb("gCs", [64, 4, 64], F32)
    pp = [S.ps("pp", [128, 512], F32) for _ in range(7)]
    pbn = S.ps("pbn", [128, 4, 4], F32)
    pc = [0]

    def nextp():
        p = pp[pc[0] % 7]
        pc[0] += 1
        return p

    def xmT(c0, m, t0, n=512):
        p = nextp()
        for dc in range(8):
            S.mm(p[0:m, 0:n], W1[:, dc, c0:c0 + m], hTp[:, dc, 1 + t0:1 + t0 + n], start=(dc == 0), stop=False)
        for dc in range(8):
            S.mm(p[0:m, 0:n], W2[:, dc, c0:c0 + m], hTp[:, dc, t0:t0 + n], start=False, stop=(dc == 7))
        return p

    twl = [S.sb("twl", [64, 512], BF16) for _ in range(2)]
    tal = [S.sb("tal", [64, 512], BF16) for _ in range(2)]
    sgl = [S.sb("sgl", [128, 512], BF16) for _ in range(2)]
    vtok = [S.sb("vtok", [128, 256], BF16) for _ in range(2)]
    gtok = [S.sb("gtok", [128, 256], F32) for _ in range(2)]
    bon = [S.sb("bon", [128, 4, 4], F32) for _ in range(2)]
    NF = 14
    f32t = [[S.sb("f%d" % i, [64, 512], F32) for i in range(NF)] for _ in range(2)]
    sqb = [S.sb("sqb", [64, 512], BF16) for _ in range(2)]
    rkb = [S.sb("rkb", [64, 512], BF16) for _ in range(2)]
    out6 = [S.sb("out6", [64, 6, 512], BF16) for _ in range(2)]
    for tb in range(8):
        t0 = tb * 512
        b = tb % 2
        p = xmT(768, 64, t0)
        S.act(twl[b][:], p[0:64, :], AF.Tanh)
        p = xmT(832, 64, t0)
        S.copy(tal[b][:], p[0:64, :], eng="dve")
        p = xmT(896, 128, t0)
        S.act(sgl[b][:], p[:, :], AF.Sigmoid)
        for sub in range(4):
            tt0 = t0 + sub * 128
            p = nextp()
            for dc in range(8):
                S.mm(p[:, 0:256], hTp[:, dc, 1 + tt0:1 + tt0 + 128], W1[:, dc, 512:768], start=(dc == 0), stop=False)
            for dc in range(8):
                S.mm(p[:, 0:256], hTp[:, dc, tt0:tt0 + 128], W2[:, dc, 512:768], start=False, stop=(dc == 7))
            vt = vtok[sub % 2]
            S.copy(vt[:], p[:, 0:256], eng="act")
            S.dma(scr["rw_v_d"][tt0:tt0 + 128, :], vt[:], q="act")
            p = nextp()
            S.mm(p[:, 0:256], sgl[b][:, sub * 128:(sub + 1) * 128], g2sb[:, :], start=True, stop=True)
            gt = gtok[sub % 2]
            S.copy(gt[:], p[:, 0:256], eng="dve")
            S.dma(scr["rw_g_d"][tt0:tt0 + 128, :], gt[:], q="act")
        for h in range(4):
            F = f32t[h % 2]
            lw, cs, Ep, En, cse, Epe, EC, ag, kkr, nrm, kk, tf, k2, bv = F
            o6 = out6[h % 2]
            hs = slice(h * 64, (h + 1) * 64)
            p = nextp()
            S.mm(p[0:64, :], w2sb[:, hs], twl[b][:], start=True, stop=True)
            S.act(lw[:], p[0:64, :], AF.Sigmoid, bias=cols[:, h:h + 1])
            a_cs, a_rm, a_lw = cs[:], rmask[:], lw[:]
            S.op("dve", lambda e, o=a_cs, d0=a_rm, d1=a_lw: e.tensor_tensor_scan(o, d0, d1, 0.0, ALU.mult, ALU.add),
                 reads=[rmask[:], lw[:]], writes=[cs[:]])
            S.act(Ep[:], cs[:], AF.Exp, scale=NEG_EH)
            S.act(En[:], cs[:], AF.Exp, scale=-NEG_EH)
            S.tt(cse[:], cs[:], lw[:], ALU.subtract, eng="pool")
            S.act(Epe[:], cse[:], AF.Exp, scale=NEG_EH)
            S.tt(EC[:].rearrange("p (c t) -> p c t", c=8), En[:].rearrange("p (c t) -> p c t", c=8),
                 Ep[:, 63::64].unsqueeze(2).to_broadcast([64, 8, 64]), ALU.mult, eng="pool")
            S.copy(gCs[:, h, tb * 8:(tb + 1) * 8], Ep[:, 63::64], eng="dve")
            p = nextp()
            S.mm(p[0:64, :], a2sb[:, hs], tal[b][:], start=True, stop=True)
            S.act(ag[:], p[0:64, :], AF.Sigmoid, bias=cols[:, 4 + h:5 + h])
            pk = xmT(256 + h * 64, 64, t0)
            S.ts(kkr[:], pk[0:64, :], cols[:, 8 + h:9 + h], None, ALU.mult)
            S.act(sqb[h % 2][:], kkr[:], AF.Square)
            p = nextp()
            S.mm(p[0:64, :], ones64[:], sqb[h % 2][:], start=True, stop=True)
            S.act(nrm[:], p[0:64, :], AF.Sqrt)
            S.ts(nrm[:], nrm[:], 1e-12, None, ALU.max)
            S.recip(nrm[:], nrm[:])
            S.tt(kk[:], kkr[:], nrm[:], ALU.mult, eng="pool")
            S.ts(tf[:], ag[:], cols[:, 12 + h:13 + h], omka[:, h:h + 1], ALU.mult, ALU.add)
            S.tt(k2[:], tf[:], pk[0:64, :], ALU.mult)
            S.tt(bv[:], kk[:], ag[:], ALU.mult, eng="pool")
            pr = xmT(h * 64, 64, t0)
            S.stt(o6[:, 0, :], kk[:], -1.0, Epe[:], ALU.mult, ALU.mult)
            S.tt(o6[:, 1, :], bv[:], En[:], ALU.mult, eng="pool")
            S.tt(o6[:, 2, :], k2[:], En[:], ALU.mult, eng="pool")
            S.tt(o6[:, 3, :], pr[0:64, :], Ep[:], ALU.mult)
            S.tt(o6[:, 4, :], bv[:], EC[:], ALU.mult, eng="pool")
            S.tt(o6[:, 5, :], k2[:], EC[:], ALU.mult, eng="pool")
            S.tt(rkb[h % 2][:], pr[0:64, :], k2[:], ALU.mult)
            for sub in range(4):
                S.mm(pbn[:, sub, h:h + 1], rkb[h % 2][:, sub * 128:(sub + 1) * 128], rkc[:, h:h + 1],
                     start=True, stop=True)
            S.dma(scr["rwT_d"][h].rearrange("q k t -> k q t")[:, :, t0:t0 + 512], o6[:], q="act")
        S.copy(bon[b][:], pbn[:], eng="act")
        S.dma(scr["rw_b_d"][t0:t0 + 512, :].rearrange("(s p) h -> p s h", p=128), bon[b][:], q="act")
    S.dma(scr["rw_gC_d"][:, :, :], gCs[:], q="act")


def phase_rwkv_b(S, nc, w, l, cst, scr, y_d):
    ident = S.sb("ident", [128, 128], BF16)
    S.dma(ident[:], cst["ident"][:, :], q="pool")
    mlo = S.sb("mlo", [64, 4, 64], F32)
    mup = S.sb("mup", [64, 4, 64], F32)
    mupi = S.sb("mupi", [64, 4, 64], F32)
    I4 = S.sb("I4", [64, 4, 64], F32)
    S.dma(mlo[:], cst["rw_mlo"][:, :, :])
    S.dma(mup[:], cst["rw_mup"][:, :, :])
    S.dma(mupi[:], cst["rw_mupi"][:, :, :])
    S.dma(I4[:], cst["rw_I4"][:, :, :])
    gC = S.sb("gC", [64, 4, 64], F32)
    S.dma(gC[:], scr["rw_gC_d"][:, :, :])
    lng = S.sb("lng", [64, 256], F32)
    lnb = S.sb("lnb", [64, 256], F32)
    S.dma(lng[:], w["rwkv_ln_g"][l:l + 1, :].partition_broadcast(64))
    S.dma(lnb[:], w["rwkv_ln_b"][l:l + 1, :].partition_broadcast(64))
    M = S.sb("M", [64, 4, 64], F32)
    Mbf = S.sb("Mbf", [64, 4, 64], BF16)
    S.memset(M[:], 0.0)
    S.memset(Mbf[:], 0.0)
    NB = 4
    pp_full = [S.ps("pp", [128, 512], F32) for _ in range(7)]
    pp = [p_[0:64, :] for p_ in pp_full]
    ptr_full = S.ps("ptr", [128, 4, 3, 64], BF16)
    ptr = ptr_full[0:64]
    pc = [0]

    def nextp():
        p = pp[pc[0] % 7]
        pc[0] += 1
        return p

    def v4(p, half):
        return p[:, half * 256:(half + 1) * 256].rearrange("p (h s) -> p h s", h=4)

    def mk(name, shape, dt):
        return [S.sb(name, shape, dt) for _ in range(NB)]
    feat = [S.sb("feat", [64, 4, 6, 256], BF16) for _ in range(2)]
    vch = [S.sb("vch", [64, 4, 256], BF16) for _ in range(2)]
    bonb = [S.sb("bonb", [64, 4, 4], F32) for _ in range(2)]
    gtk = [S.sb("gtk", [64, 4, 256], F32) for _ in range(2)]
    tokM = mk("tokM", [64, 4, 2, 64], BF16)
    WZin = mk("WZin", [64, 4, 128], BF16)
    Lb = [mk("L0", [64, 4, 64], BF16), mk("L1", [64, 4, 64], BF16)]
    LTb = [mk("LT0", [64, 4, 64], BF16), mk("LT1", [64, 4, 64], BF16)]
    ILb = [mk("IL0", [64, 4, 64], BF16), mk("IL1", [64, 4, 64], BF16)]
    PTb = [mk("PT0", [64, 4, 64], BF16), mk("PT1", [64, 4, 64], BF16)]
    AakT = mk("AakT", [64, 4, 64], BF16)
    ArbT = mk("ArbT", [64, 4, 64], BF16)
    ArkT = mk("ArkT", [64, 4, 64], BF16)
    WZ = mk("WZ", [64, 4, 128], BF16)
    GT = mk("GT", [64, 4, 64], BF16)
    Dg = mk("Dg", [64, 4, 64], F32)
    Nsb = mk("Nsb", [64, 4, 64], F32)
    QeT = mk("QeT", [64, 4, 64], BF16)
    Ol = mk("Ol", [64, 4, 64], F32)
    osb = mk("osb", [64, 4, 64], F32)
    sqs = mk("sqs", [64, 4, 64], F32)
    bvt = mk("bvt", [64, 4, 64], F32)
    stt_ = mk("stt", [64, 8, 4], F32)
    yb = mk("yb", [64, 256], BF16)
    NBATCH = S_LEN // 256
    import os
    VAR = os.environ.get("RWB_VAR", "Z")
    for bt in range(NBATCH):
        if VAR == "A":
            break
        t0 = bt * 256
        fb = feat[bt % 2]
        vb = vch[bt % 2]
        for h in range(4):
            S.dma(fb[:, h, :, :], scr["rwT_d"][h].rearrange("q k t -> k q t")[:, :, t0:t0 + 256])
        S.dma(vb[:], scr["rw_v_d"][t0:t0 + 256, :].rearrange("(c p) f -> p c f", p=64))
        S.dma(bonb[bt % 2][:], scr["rw_b_d"][t0:t0 + 256, :].rearrange("(c p) f -> p c f", p=64))
        S.dma(gtk[bt % 2][:], scr["rw_g_d"][t0:t0 + 256, :].rearrange("(c p) f -> p c f", p=64))

        def F(h, q, c):
            return fb[:, h, q, c * 64:(c + 1) * 64]

        def V(h, c):
            return vb[:, c, h * 64:(h + 1) * 64]
        if VAR == "B":
            continue
        for c in range(NB):
            for h in range(4):
                for j, q in enumerate((0, 4, 5)):
                    if VAR == "D":
                        continue
                    S.tr(ptr[:, h, j, :], F(h, q, c), ident[0:64, 0:64])
            if VAR != "E":
                S.copy(WZin[c][:, :, 0:64], ptr[:, :, 0, :], eng="dve")
            if VAR != "F":
                S.copy(tokM[c][:], ptr[:, :, 1:3, :], eng="dve")
        import os
        STOP = int(os.environ.get("RWB_STOP", "9"))
        if STOP < 1:
            continue
        for c in range(NB):
            p1, p2, p3 = nextp(), nextp(), nextp()
            for h in range(4):
                S.mm(v4(p1, 0)[:, h, :], F(h, 0, c), F(h, 1, c), start=True, stop=True)
                S.mm(v4(p1, 1)[:, h, :], F(h, 1, c), F(h, 0, c), start=True, stop=True)
                S.mm(v4(p2, 0)[:, h, :], F(h, 2, c), F(h, 0, c), start=True, stop=True)
                S.mm(v4(p2, 1)[:, h, :], F(h, 1, c), F(h, 3, c), start=True, stop=True)
                S.mm(v4(p3, 0)[:, h, :], F(h, 2, c), F(h, 3, c), start=True, stop=True)
            S.tt(Lb[0][c][:], v4(p1, 0), mlo[:], ALU.mult)
            S.tt(LTb[0][c][:], v4(p1, 1), mup[:], ALU.mult)
            S.tt(PTb[0][c][:], LTb[0][c][:], I4[:], ALU.add, eng="pool")
            S.tt(AakT[c][:], v4(p2, 0), mup[:], ALU.mult)
            S.tt(ArbT[c][:], v4(p2, 1), mupi[:], ALU.mult)
            S.tt(ArkT[c][:], v4(p3, 0), mupi[:], ALU.mult)
        if STOP < 2:
            continue
        for c in range(NB):
            p1 = nextp()
            for h in range(4):
                S.mm(v4(p1, 0)[:, h, :], AakT[c][:, h, :], V(h, c), start=True, stop=True)
            S.copy(WZin[c][:, :, 64:128], v4(p1, 0), eng="dve")
        if STOP < 3:
            continue
        for i in range(1, 7):
            cur, prv = i % 2, (i - 1) % 2
            for c in range(NB):
                p1 = nextp()
                p2 = nextp() if i >= 2 else None
                for h in range(4):
                    if i <= 5:
                        S.mm(v4(p1, 0)[:, h, :], LTb[prv][c][:, h, :], Lb[prv][c][:, h, :], start=True, stop=True)
                    if i <= 4:
                        S.mm(v4(p1, 1)[:, h, :], Lb[prv][c][:, h, :], LTb[prv][c][:, h, :], start=True, stop=True)
                    if i >= 2:
                        S.mm(v4(p2, 0)[:, h, :], ILb[prv][c][:, h, :], PTb[i % 2][c][:, h, :], start=True, stop=True)
                if i <= 5:
                    S.copy(Lb[cur][c][:], v4(p1, 0), eng="dve")
                    S.tt(ILb[cur][c][:], v4(p1, 0), I4[:], ALU.add)
                if i <= 4:
                    S.copy(LTb[cur][c][:], v4(p1, 1), eng="dve")
                if i >= 2:
                    S.copy(PTb[(i - 1) % 2][c][:], v4(p2, 0), eng="dve")
        TT = PTb[1]
        if STOP < 4:
            continue
        for c in range(NB):
            p1 = nextp()
            pw = p1.rearrange("p (h s) -> p h s", h=4)
            for h in range(4):
                S.mm(pw[:, h, :], TT[c][:, h, :], WZin[c][:, h, :], start=True, stop=True)
            S.copy(WZ[c][:], pw, eng="dve")
        if STOP < 5:
            continue
        for c in range(NB):
            n = bt * NB + c
            p1, p2 = nextp(), nextp()
            for h in range(4):
                S.mm(v4(p1, 0)[:, h, :], WZ[c][:, h, 0:64], tokM[c][:, h, 0, :], start=True, stop=True)
                S.mm(v4(p1, 1)[:, h, :], tokM[c][:, h, 0, :], WZ[c][:, h, 64:128], start=True, stop=False)
                S.mm(v4(p1, 1)[:, h, :], tokM[c][:, h, 1, :], V(h, c), start=False, stop=True)
            for h in range(4):
                S.mm(v4(p2, 0)[:, h, :], WZ[c][:, h, 0:64], ArbT[c][:, h, :], start=True, stop=True)
                S.mm(v4(p2, 1)[:, h, :], ArbT[c][:, h, :], WZ[c][:, h, 64:128], start=True, stop=False)
                S.mm(v4(p2, 1)[:, h, :], ArkT[c][:, h, :], V(h, c), start=False, stop=True)
            S.tt(Dg[c][:], I4[:], gC[:, :, n:n + 1].to_broadcast([64, 4, 64]), ALU.mult, eng="pool")
            S.tt(GT[c][:], v4(p1, 0), Dg[c][:], ALU.add)
            S.copy(Nsb[c][:], v4(p1, 1), eng="dve")
            S.tt(QeT[c][:], v4(p2, 0), fb[:, :, 3, c * 64:(c + 1) * 64], ALU.add)
            S.copy(Ol[c][:], v4(p2, 1), eng="dve")
        if STOP < 6:
            continue
        for c in range(NB):
            p1 = nextp()
            for h in range(4):
                S.mm(v4(p1, 0)[:, h, :], QeT[c][:, h, :], Mbf[:, h, :], start=True, stop=True)
            for h in range(4):
                S.mm(v4(p1, 1)[:, h, :], GT[c][:, h, :], Mbf[:, h, :], start=True, stop=True)
            S.tt(M[:], v4(p1, 1), Nsb[c][:], ALU.add)
            S.copy(Mbf[:], M[:], eng="act")
            o = osb[c]
            s_ = stt_[c]
            S.tt(o[:], v4(p1, 0), Ol[c][:], ALU.add)
            S.reduce(s_[:, 0, :], o[:], ALU.add)
            S.tt(sqs[c][:], o[:], o[:], ALU.mult, eng="pool")
            S.reduce(s_[:, 1, :], sqs[c][:], ALU.add)
            S.ts(s_[:, 2, :], s_[:, 0, :], 1.0 / 64, None, ALU.mult)
            S.tt(s_[:, 3, :], s_[:, 2, :], s_[:, 2, :], ALU.mult)
            S.stt(s_[:, 4, :], s_[:, 1, :], 1.0 / 64, s_[:, 3, :], ALU.mult, ALU.subtract)
            S.ts(s_[:, 5, :], s_[:, 4, :], 64e-5, None, ALU.add)
            S.act(s_[:, 6, :], s_[:, 5, :], AF.Sqrt)
            S.recip(s_[:, 7, :], s_[:, 6, :])
            S.tt(o[:], o[:], s_[:, 2, :].unsqueeze(2).to_broadcast([64, 4, 64]), ALU.subtract)
            S.tt(o[:], o[:], s_[:, 7, :].unsqueeze(2).to_broadcast([64, 4, 64]), ALU.mult)
            of = o[:].rearrange("p h e -> p (h e)")
            S.tt(of, of, lng[:], ALU.mult, eng="pool")
            S.tt(of, of, lnb[:], ALU.add, eng="pool")
            S.tt(bvt[c][:], vb[:, c, :].rearrange("p (h e) -> p h e", h=4),
                 bonb[bt % 2][:, c, :].unsqueeze(2).to_broadcast([64, 4, 64]), ALU.mult)
            S.tt(o[:], o[:], bvt[c][:], ALU.add)
            S.tt(yb[c][:], of, gtk[bt % 2][:, c, :], ALU.mult)
            S.dma(y_d[t0 + c * 64:t0 + (c + 1) * 64, :], yb[c][:], q="act")


def phase_merge(S, nc, xres, w, l, hT_d, wgate_d, cst, scr):
    Wg = S.sb("Wg", [128, 8, 4096], BF16)
    load_w_bf16(S, Wg, wgate_d[l], 8)
    Wbr = S.sb("Wbr", [128, 10, D], BF16)
    off = 0
    for nm, nch in (("w_br_nsa", 2), ("w_br_ret", 4), ("w_br_rwkv", 2), ("w_br_swa", 2)):
        v = w[nm][l].rearrange("(c p) f -> p c f", p=128)
        for c in range(nch):
            S.dma(Wbr[:, off + c, :], v[:, c, :], q="pool")
        off += nch
    Wo = S.sb("Wo", [128, 8, D], BF16)
    load_w_bf16(S, Wo, w["w_out"][l], 8)
    gpost = S.sb("gpost", [128, D], F32)
    bcast_row(S, gpost[:], w["mix_post_g"][l:l + 1, :])
    ident = S.sb("ident", [128, 128], BF16)
    S.dma(ident[:], cst["ident"][:, :], q="pool")
    ycat = [S.sb("ycat", [128, 1280], BF16) for _ in range(2)]
    yT = [S.sb("yT", [128, 10, 128], BF16) for _ in range(2)]
    hTt = [S.sb("hTt", [128, 8, 128], BF16) for _ in range(2)]
    xb = [S.sb("xb", [128, D], F32) for _ in range(2)]
    sg = [S.sb("sg", [128, 512], F32) for _ in range(2)]
    tmpb = [S.sb("tmpb", [128, 512], F32) for _ in range(2)]
    merged = [S.sb("merged", [128, D], F32) for _ in range(2)]
    mbf = [S.sb("mbf", [128, D], BF16) for _ in range(2)]
    mT = [S.sb("mT", [128, 8, 128], BF16) for _ in range(2)]
    fsb = [S.sb("fsb", [128, D], F32) for _ in range(2)]
    junk = S.sb("junk", [128, D], BF16)
    st = [S.sb("st", [128, 8], F32) for _ in range(2)]
    ptrA = S.ps("ptrA", [128, 5, 128], BF16)
    ptrB = S.ps("ptrB", [128, 5, 128], BF16)
    ptm = S.ps("ptm", [128, 8, 128], BF16)
    pg = [S.ps("pg", [128, 512], F32) for _ in range(2)]
    po = [S.ps("po", [128, 512], F32) for _ in range(2)]
    hv = hT_d.rearrange("(c p) t -> p c t", p=128)
    brch = ((0, 2), (2, 4), (6, 2), (8, 2))
    cnt = 0
    for t in range(NT):
        b2 = t % 2
        rows = slice(t * 128, (t + 1) * 128)
        yc = ycat[b2]
        S.dma(yc[:, 0:256], scr["y_nsa"][rows, :])
        S.dma(yc[:, 256:768], scr["y_ret"][rows, :])
        S.dma(yc[:, 768:1024], scr["y_rwkv"][rows, :])
        S.dma(yc[:, 1024:1280], scr["y_swa"][rows, :])
        S.dma(hTt[b2][:], (hv[:, :, rows], ("hT", t)))
        S.dma(xb[b2][:], (xres[rows, :], ("xres", t)))
        for fc in range(10):
            pt_ = ptrA if fc < 5 else ptrB
            S.tr(pt_[:, fc % 5, :], yc[:, fc * 128:(fc + 1) * 128], ident[:])
        S.copy(yT[b2][:, 0:5, :], ptrA[:], eng="act")
        S.copy(yT[b2][:, 5:10, :], ptrB[:], eng="dve")
        mg = merged[b2]
        for br in range(4):
            f0, nf = brch[br]
            for half in range(2):
                pgt = pg[cnt % 2]
                pot = po[cnt % 2]
                sgt = sg[cnt % 2]
                tb_ = tmpb[cnt % 2]
                cnt += 1
                c0 = br * 1024 + half * 512
                for dc in range(8):
                    S.mm(pgt[:, :], hTt[b2][:, dc, :], Wg[:, dc, c0:c0 + 512], start=(dc == 0), stop=(dc == 7))
                for k in range(nf):
                    S.mm(pot[:, :], yT[b2][:, f0 + k, :], Wbr[:, f0 + k, half * 512:(half + 1) * 512],
                         start=(k == 0), stop=(k == nf - 1))
                S.act(sgt[:], pgt[:, :], AF.Sigmoid)
                mslice = mg[:, half * 512:(half + 1) * 512]
                if br == 0:
                    S.tt(mslice, sgt[:], pot[:, :], ALU.mult)
                else:
                    S.tt(tb_[:], sgt[:], pot[:, :], ALU.mult)
                    S.tt(mslice, mslice, tb_[:], ALU.add, eng="pool")
        S.copy(mbf[b2][:], mg[:], eng="act")
        for dc in range(8):
            S.tr(ptm[:, dc, :], mbf[b2][:, dc * 128:(dc + 1) * 128], ident[:])
        S.copy(mT[b2][:], ptm[:], eng="act")
        f = fsb[b2]
        s_ = st[b2]
        for half in range(2):
            pgt = pg[half]
            for dc in range(8):
                S.mm(pgt[:, :], mT[b2][:, dc, :], Wo[:, dc, half * 512:(half + 1) * 512],
                     start=(dc == 0), stop=(dc == 7))
            S.act(f[:, half * 512:(half + 1) * 512], pgt[:, :], AF.Copy)
            S.act(junk[:, half * 512:(half + 1) * 512], pgt[:, :], AF.Square, accum_out=s_[:, half:half + 1])
        S.tt(s_[:, 2:3], s_[:, 0:1], s_[:, 1:2], ALU.add)
        S.ts(s_[:, 3:4], s_[:, 2:3], 1.0 / D, RMS_EPS, ALU.mult, ALU.add)
        S.act(s_[:, 4:5], s_[:, 3:4], AF.Sqrt)
        S.recip(s_[:, 5:6], s_[:, 4:5])
        S.stt(f[:], f[:], s_[:, 5:6], gpost[:], ALU.mult, ALU.mult)
        S.tt(xb[b2][:], xb[b2][:], f[:], ALU.add, eng="pool")
        S.dma((xres[rows, :], ("xres", t)), xb[b2][:], q="act")


WNAMES = ['ffn1_pre_g', 'ffn1_post_g', 'ffn1_w_gate', 'ffn1_w_up', 'ffn1_w_down', 'mix_pre_g', 'mix_post_g',
          'w_in', 'nsa_cmp_pos_k', 'nsa_cmp_pos_v', 'nsa_cmp_k_w1', 'nsa_cmp_k_w2', 'nsa_cmp_v_w1',
          'nsa_cmp_v_w2', 'ret_gn_g', 'rwkv_mu', 'rwkv_w0', 'rwkv_w2', 'rwkv_a0', 'rwkv_a2', 'rwkv_g2',
          'rwkv_k_k', 'rwkv_k_a', 'rwkv_r_k', 'rwkv_ln_g', 'rwkv_ln_b', 'swa_sinks', 'w_br_nsa', 'w_br_ret',
          'w_br_rwkv', 'w_br_swa', 'w_out', 'ffn2_pre_g', 'ffn2_post_g', 'ffn2_w_gate', 'ffn2_w_up',
          'ffn2_w_down']

_CONSTS = None


def band_masks(window, rels):
    p = np.arange(128)[:, None]
    ql = np.arange(512)[None, :]
    out = []
    for r in rels:
        d = ql - (r * 128 + p)
        out.append(((d >= 0) & (d < window)).astype(np.float32))
    return np.stack(out, 0)


def host_consts():
    global _CONSTS
    if _CONSTS is not None:
        return _CONSTS
    c = {}
    c["ident"] = np.eye(128, dtype=np.float32)
    pos = np.arange(S_LEN, dtype=np.float32)
    inv = np.power(np.float32(10000.0), -np.arange(32, dtype=np.float32) * 2.0 / 64).astype(np.float32)
    ang = pos[None, :] * inv[:, None]
    cos = np.cos(ang).astype(np.float32)
    sin = np.sin(ang).astype(np.float32)
    c["cosT"] = np.ascontiguousarray(np.concatenate([cos, cos, cos, cos], 0))
    c["sinT"] = np.ascontiguousarray(np.concatenate([-sin, sin, -sin, sin], 0))
    c["mask_swa"] = band_masks(128, range(-1, 4))
    ii = np.arange(256)
    qq = np.arange(S_LEN)
    c["cmpmaskT"] = (((16 * ii[:, None] + 31) <= qq[None, :]) & (ii[:, None] < 255)).astype(np.float32)
    jj = np.arange(64)
    c["overlap"] = (((16 * ii[:, None]) <= (64 * jj[None, :] + 63)) & ((16 * ii[:, None] + 31) >= 64 * jj[None, :])
                    & (ii[:, None] < 255)).astype(np.float32)
    cur = (qq // 64)[:, None]
    forced = (jj[None, :] == 0) | (jj[None, :] == cur) | (jj[None, :] == cur - 1)
    valid = jj[None, :] <= cur
    c["selbias"] = np.where(forced, 1e9, np.where(valid, 0.0, -1e9)).astype(np.float32)
    c["mask_win"] = band_masks(512, range(-4, 4))
    c["mask_causal"] = band_masks(10 ** 7, range(0, 4))
    kt_ = np.arange(NT)[None, :, None]
    pp = np.arange(128)[None, None, :]
    c["Eall"] = (jj[:, None, None] == (2 * kt_ + pp // 64)).astype(np.float32)
    c["rw_reset"] = np.ascontiguousarray(np.broadcast_to((np.arange(512) % 64 != 0).astype(np.float32)[None, :], (64, 512)))
    tt_ = np.arange(64)[:, None, None]
    ss_ = np.arange(64)[None, None, :]
    one4 = np.ones((1, 4, 1), dtype=np.float32)
    c["rw_mlo"] = np.ascontiguousarray((ss_ < tt_).astype(np.float32) * one4)
    c["rw_mup"] = np.ascontiguousarray((ss_ > tt_).astype(np.float32) * one4)
    c["rw_mupi"] = np.ascontiguousarray((ss_ >= tt_).astype(np.float32) * one4)
    c["rw_I4"] = np.ascontiguousarray((ss_ == tt_).astype(np.float32) * one4)
    gam = (1.0 - np.power(2.0, -5.0 - np.arange(4, dtype=np.float64)))
    m = np.arange(128)[:, None, None]
    cc = np.arange(128)[None, None, :]
    gg = gam[None, :, None]
    dm = np.where(cc >= m, np.power(gg, np.maximum(cc - m, 0)), 0.0) * 0.125
    c["ret_dmaskT"] = np.ascontiguousarray(dm.astype(np.float32))
    c["ret_zeta"] = np.ascontiguousarray((np.power(gam[None, :], 127 - np.arange(128)[:, None]) * 0.125).astype(np.float32))
    xi = np.power(gam[None, :, None], np.arange(128)[None, None, :] + 1.0)
    c["ret_xiT"] = np.ascontiguousarray(np.broadcast_to(xi, (64, 4, 128)).astype(np.float32))
    c["ret_gch"] = np.ascontiguousarray(np.broadcast_to(np.power(gam, 128.0)[None, :], (64, 4)).astype(np.float32))
    _CONSTS = c
    return c


def derived_weights(inputs):
    idx = w_in_index_sets()
    out = {}
    w_in = np.asarray(inputs["w_in"], dtype=np.float32)
    for n, ix in idx.items():
        out["w" + n] = np.ascontiguousarray(w_in[:, :, ix])
    pk = np.asarray(inputs["nsa_cmp_pos_k"], dtype=np.float32)
    pv = np.asarray(inputs["nsa_cmp_pos_v"], dtype=np.float32)
    t64 = lambda n: np.asarray(inputs[n], dtype=np.float32).reshape(DEPTH, 4, 64).transpose(0, 2, 1)
    out["rwkv_cols"] = np.ascontiguousarray(np.concatenate(
        [t64("rwkv_w0"), t64("rwkv_a0"), t64("rwkv_k_k"), t64("rwkv_k_a"), t64("rwkv_r_k")], axis=2))
    out["nsa_posT"] = np.ascontiguousarray(np.stack([pk.transpose(0, 2, 1), pv.transpose(0, 2, 1)], axis=1))
    return out


SCRATCH = {
    "hT_d": ([D, S_LEN], BF16),
    "y_swa": ([S_LEN, 256], BF16),
    "y_ret": ([S_LEN, 512], BF16),
    "y_nsa": ([S_LEN, 256], BF16),
    "y_rwkv": ([S_LEN, 256], BF16),
    "rwT_d": ([4, 6, 64, S_LEN], BF16),
    "rw_gC_d": ([64, 4, 64], F32),
    "rw_v_d": ([S_LEN, 256], BF16),
    "rw_g_d": ([S_LEN, 256], F32),
    "rw_b_d": ([S_LEN, 4], F32),
    "nsaT_d": ([12, 64, S_LEN], BF16),
    "nsa_v_d": ([S_LEN, 128], BF16),
    "nsa_g_d": ([S_LEN, 12], F32),
    "ocmp_d": ([S_LEN, 256], BF16),
    "selT_d": ([64, S_LEN], BF16),
}


def default_phases():
    pl = [("copyin", None)]
    for l in range(DEPTH):
        pl += [("ffn1", l), ("mixpre", l), ("swa", l), ("ret", l), ("nsa_a", l), ("nsa_b", l), ("nsa_c", l),
               ("rwkv_a", l), ("rwkv_b", l), ("merge", l), ("ffn2", l)]
    return pl


def build(shapes, phases=None, dbg=()):
    nc = bass.Bass("TRN2", target_bir_lowering=False)
    x_in = nc.dram_tensor("x", [S_LEN, D], F32, kind="ExternalInput").ap()
    w = {}
    for n in shapes:
        w[n] = nc.dram_tensor(n, list(shapes[n]), F32, kind="ExternalInput").ap()
    cst = {}
    for n, a in host_consts().items():
        cst[n] = nc.dram_tensor("c_" + n, list(a.shape), F32, kind="ExternalInput").ap()
    y = nc.dram_tensor("y", [S_LEN, D], F32, kind="ExternalOutput").ap()
    scr = {}
    for n, (shp, dt_) in SCRATCH.items():
        if n in dbg:
            scr[n] = nc.dram_tensor(n, shp, dt_, kind="ExternalOutput").ap()
        else:
            scr[n] = nc.dram_tensor(n, shp, dt_).ap()
    xres = y
    plist = default_phases() if phases is None else phases
    with ExitStack() as gst:
        S = Sched(nc, gst)
        for pi, (pn, l) in enumerate(plist):
            with ExitStack() as pst:
                S.stack = pst
                if pn == "copyin":
                    for t in range(0, NT, 4):
                        S.dma((xres[t * 128:(t + 4) * 128, :], ("xres", t), ("xres", t + 1), ("xres", t + 2),
                               ("xres", t + 3)), x_in[t * 128:(t + 4) * 128, :], q="sp")
                elif pn in ("ffn1", "ffn2"):
                    phase_ffn(S, nc, xres, w, l, pn, cst["ident"])
                elif pn == "mixpre":
                    phase_mixpre(S, nc, xres, w, l, scr["hT_d"], cst["ident"])
                elif pn == "swa":
                    phase_swa(S, nc, w, l, scr["hT_d"], w["wswa"], cst, scr["y_swa"])
                elif pn == "nsa_a":
                    phase_nsa_a(S, nc, w, l, scr["hT_d"], w["wnsa"], cst, scr)
                elif pn == "nsa_b":
                    phase_nsa_b(S, nc, w, l, cst, scr)
                elif pn == "nsa_c":
                    phase_nsa_c(S, nc, w, l, cst, scr, scr["y_nsa"])
                elif pn == "rwkv_a":
                    phase_rwkv_a(S, nc, w, l, scr["hT_d"], w["wr"], cst, scr)
                elif pn == "rwkv_b":
                    phase_rwkv_b(S, nc, w, l, cst, scr, scr["y_rwkv"])
                elif pn == "merge":
                    phase_merge(S, nc, xres, w, l, scr["hT_d"], w["wgate"], cst, scr)
                elif pn == "ret":
                    phase_ret(S, nc, w, l, scr["hT_d"], w["wret"], cst, scr["y_ret"], cst["ident"])
                else:
                    raise ValueError(pn)
                S.barrier()
                S.emit(final=(pi == len(plist) - 1))
    return nc


def make_in_maps(inputs, cores):
    base = {k: np.ascontiguousarray(inputs[k], dtype=np.float32) for k in WNAMES}
    base.update(derived_weights(inputs))
    shapes = {k: v.shape for k, v in base.items()}
    for k, a in host_consts().items():
        base["c_" + k] = a
    x = np.asarray(inputs["x"], dtype=np.float32)
    in_maps = []
    for i in cores:
        m = dict(base)
        m["x"] = np.ascontiguousarray(x[i])
        in_maps.append(m)
    return shapes, in_maps


def kernel(**inputs):
    n = 8
    shapes, in_maps = make_in_maps(inputs, list(range(n)))
    nc = build(shapes)
    res = run_bass_kernel_spmd(nc, in_maps, core_ids=list(range(n)))
    return np.stack([r["y"] for r in res.results], axis=0)
```

```python
import numpy as np
from contextlib import ExitStack
import concourse.bass as bass
import concourse.mybir as mybir
from concourse.bass_utils import run_bass_kernel_spmd

F32 = mybir.dt.float32
BF16 = mybir.dt.bfloat16
AF = mybir.ActivationFunctionType
ALU = mybir.AluOpType
AX = mybir.AxisListType

S_LEN = 4096
D = 1024
DFF = 2816
NT = S_LEN // 128
DEPTH = 2
RMS_EPS = 1e-6

ENGS = ("pe", "act", "dve", "pool", "sp")
DMA_K = 8


def _kref(x):
    if isinstance(x, tuple):
        return x[0], tuple(x[1:])
    return x, (x.tensor.name,)


class Sched:
    def __init__(self, nc, stack):
        self.nc = nc
        self.gstack = stack
        self.esem = {e: stack.enter_context(nc.semaphore("es_" + e)) for e in ENGS if e != "sp"}
        self.dsem = {q: [stack.enter_context(nc.semaphore("ds_%s%d" % (q, i))) for i in range(DMA_K)]
                     for q in ("sp", "act", "pool")}
        self.ccount = {e: 0 for e in ENGS}
        self.qcount = {q: 0 for q in ("sp", "act", "pool")}
        self.wm = {e: {} for e in ENGS}
        self.pending = {e: [] for e in ENGS}
        self.keys = {}
        self.post_barrier = {e: set() for e in ENGS}
        self.stack = None
        self.uid = 0

    def sb(self, name, shape, dtype):
        self.uid += 1
        return self.stack.enter_context(self.nc.sbuf_tensor("%s_%d" % (name, self.uid), list(shape), dtype))

    def ps(self, name, shape, dtype=F32):
        self.uid += 1
        return self.stack.enter_context(self.nc.psum_tensor("%s_%d" % (name, self.uid), list(shape), dtype))

    def _st(self, key):
        st = self.keys.get(key)
        if st is None:
            st = {"W": {}, "R": {}, "Wd": {}, "Rd": {}}
            self.keys[key] = st
        return st

    def op(self, eng, fn, reads=(), writes=(), dma=False):
        deps = set()
        rkeys, wkeys = [], []
        for r in reads:
            if r is None:
                continue
            _, ks = _kref(r)
            rkeys.extend(ks)
        for w in writes:
            if w is None:
                continue
            _, ks = _kref(w)
            wkeys.extend(ks)
        for k in rkeys:
            st = self._st(k)
            for e, c in st["W"].items():
                deps.add((e, c))
            for q, js in st["Wd"].items():
                for j in js:
                    deps.add(("dma", q, j))
        for k in wkeys:
            st = self._st(k)
            for e, c in st["W"].items():
                deps.add((e, c))
            for e, c in st["R"].items():
                if e == eng and not dma:
                    continue
                deps.add((e, c))
            for q, js in st["Wd"].items():
                for j in js:
                    deps.add(("dma", q, j))
            for q, js in st["Rd"].items():
                for j in js:
                    deps.add(("dma", q, j))
        if eng == "pe":
            deps = {d for d in deps if d[0] != "pe"}
        deps |= self.post_barrier[eng]
        self.post_barrier[eng] = set()
        if dma:
            j = self.qcount[eng]
            self.qcount[eng] += 1
            rec = ("dma", eng, j)
            for k in rkeys:
                l = self._st(k)["Rd"].setdefault(eng, [])
                l.append(j)
                if len(l) > DMA_K:
                    del l[0]
            for k in wkeys:
                l = self._st(k)["Wd"].setdefault(eng, [])
                l.append(j)
                if len(l) > DMA_K:
                    del l[0]
            self.pending[eng].append((fn, deps, True, j))
        else:
            self.ccount[eng] += 1
            c = self.ccount[eng]
            for k in rkeys:
                self._st(k)["R"][eng] = c
            for k in wkeys:
                self._st(k)["W"][eng] = c
            self.pending[eng].append((fn, deps, False, c))

    def barrier(self):
        allc = set()
        for e in ENGS:
            if e != "sp" and self.ccount[e] > 0:
                allc.add((e, self.ccount[e]))
        for q in ("sp", "act", "pool"):
            n = self.qcount[q]
            for j in range(max(0, n - DMA_K), n):
                allc.add(("dma", q, j))
        for e in ENGS:
            self.post_barrier[e] |= allc
        self.keys = {}

    def _emit_engine(self, ename, eng, final=False):
        wm = self.wm[ename]
        for fn, deps, is_dma, idx in self.pending[ename]:
            waits = {}
            dmax = {}
            for d in deps:
                if d[0] == "dma":
                    dmax[d[1]] = max(dmax.get(d[1], -1), d[2])
            for d in deps:
                if d[0] == "dma":
                    q, j = d[1], d[2]
                    if j <= dmax[q] - DMA_K:
                        continue
                    sem = self.dsem[q][j % DMA_K]
                    val = 16 * (j // DMA_K + 1)
                else:
                    sem = self.esem[d[0]]
                    val = d[1]
                key = id(sem)
                if key not in waits or waits[key][1] < val:
                    waits[key] = (sem, val)
            if is_dma and idx >= DMA_K:
                sem = self.dsem[ename][idx % DMA_K]
                val = 16 * (idx // DMA_K)
                key = id(sem)
                if key not in waits or waits[key][1] < val:
                    waits[key] = (sem, val)
            for key, (sem, val) in waits.items():
                if wm.get(key, 0) < val:
                    eng.wait_ge(sem, val)
                    wm[key] = val
            ins = fn(eng)
            if is_dma:
                ins.then_inc(self.dsem[ename][idx % DMA_K], 16)
            else:
                ins.then_inc(self.esem[ename], 1)
        self.pending[ename] = []
        if final and ename in self.dsem:
            n = self.qcount[ename]
            for j in range(max(0, n - DMA_K), n):
                sem = self.dsem[ename][j % DMA_K]
                val = 16 * (j // DMA_K + 1)
                if wm.get(id(sem), 0) < val:
                    eng.wait_ge(sem, val)
                    wm[id(sem)] = val

    def emit(self, final=False):
        with self.nc.Block() as block:
            @block.tensor
            def _(e):
                self._emit_engine("pe", e, final)

            @block.scalar
            def _(e):
                self._emit_engine("act", e, final)

            @block.vector
            def _(e):
                self._emit_engine("dve", e, final)

            @block.gpsimd
            def _(e):
                self._emit_engine("pool", e, final)

            @block.sync
            def _(e):
                self._emit_engine("sp", e, final)

    def dma(self, out, in_, q="sp"):
        o, i = _kref(out)[0], _kref(in_)[0]
        self.op(q, lambda e: e.dma_start(out=o, in_=i), reads=[in_], writes=[out], dma=True)

    def mm(self, out, lhsT, rhs, start=True, stop=True):
        o, l, r = _kref(out)[0], _kref(lhsT)[0], _kref(rhs)[0]
        self.op("pe", lambda e: e.matmul(o, l, r, start=start, stop=stop), reads=[lhsT, rhs], writes=[out])

    def tr(self, out, in_, ident):
        o, i, d = _kref(out)[0], _kref(in_)[0], _kref(ident)[0]
        self.op("pe", lambda e: e.transpose(o, i, d), reads=[in_, ident], writes=[out])

    def act(self, out, in_, func, bias=None, scale=None, accum_out=None):
        o, i = _kref(out)[0], _kref(in_)[0]
        kw = {}
        rd = [in_]
        wr = [out]
        if bias is not None:
            if isinstance(bias, (int, float)):
                kw["bias"] = bias
            else:
                kw["bias"] = _kref(bias)[0]
                rd.append(bias)
        if scale is not None:
            if isinstance(scale, (int, float)):
                kw["scale"] = scale
            else:
                kw["scale"] = _kref(scale)[0]
                rd.append(scale)
        if accum_out is not None:
            kw["accum_out"] = _kref(accum_out)[0]
            wr.append(accum_out)
        self.op("act", lambda e: e.activation(o, i, func, **kw), reads=rd, writes=wr)

    def tt(self, out, in0, in1, op, eng="dve"):
        o, a, b = _kref(out)[0], _kref(in0)[0], _kref(in1)[0]
        self.op(eng, lambda e: e.tensor_tensor(o, a, b, op), reads=[in0, in1], writes=[out])

    def ts(self, out, in0, s1, s2, op0, op1=None, eng="dve", accum_out=None):
        o, a = _kref(out)[0], _kref(in0)[0]
        rd = [in0]
        wr = [out]

        def sc(s):
            if s is None or isinstance(s, (int, float)):
                return s
            rd.append(s)
            return _kref(s)[0]
        v1, v2 = sc(s1), sc(s2)
        kw = {}
        if op1 is not None:
            kw["op1"] = op1
        if accum_out is not None:
            kw["accum_out"] = _kref(accum_out)[0]
            wr.append(accum_out)
        self.op(eng, lambda e: e.tensor_scalar(o, a, v1, v2, op0, **kw), reads=rd, writes=wr)

    def stt(self, out, in0, scalar, in1, op0, op1, accum_out=None):
        o, a, b = _kref(out)[0], _kref(in0)[0], _kref(in1)[0]
        rd = [in0, in1]
        wr = [out]
        if isinstance(scalar, (int, float)):
            s = scalar
        else:
            s = _kref(scalar)[0]
            rd.append(scalar)
        kw = {}
        if accum_out is not None:
            kw["accum_out"] = _kref(accum_out)[0]
            wr.append(accum_out)
        self.op("dve", lambda e: e.scalar_tensor_tensor(o, a, s, b, op0, op1, **kw), reads=rd, writes=wr)

    def copy(self, out, in_, eng="dve"):
        o, i = _kref(out)[0], _kref(in_)[0]
        if eng == "act":
            self.op("act", lambda e: e.copy(o, i), reads=[in_], writes=[out])
        else:
            self.op(eng, lambda e: e.tensor_copy(o, i), reads=[in_], writes=[out])

    def recip(self, out, in_):
        o, i = _kref(out)[0], _kref(in_)[0]
        self.op("dve", lambda e: e.reciprocal(o, i), reads=[in_], writes=[out])

    def memset(self, out, val, eng="dve"):
        o = _kref(out)[0]
        self.op(eng, lambda e: e.memset(o, val), reads=[], writes=[out])

    def reduce(self, out, in_, op, axis=None, eng="dve"):
        o, i = _kref(out)[0], _kref(in_)[0]
        ax = AX.X if axis is None else axis
        self.op(eng, lambda e: e.tensor_reduce(o, i, ax, op), reads=[in_], writes=[out])


def load_w_bf16(S, dst, src_dram, nchunk, q="pool"):
    v = src_dram.rearrange("(c p) f -> p c f", p=128)
    for c in range(nchunk):
        S.dma(dst[:, c, :], v[:, c, :], q=q)


def bcast_row(S, dst, src_row_ap, q="sp"):
    S.dma(dst, src_row_ap.partition_broadcast(128), q=q)


def phase_ffn(S, nc, xres, w, l, pre, ident_d):
    TB = 256
    NSUB = TB // 128
    NFC = DFF // 128
    wg = S.sb("wg", [128, 8, DFF], BF16)
    wu = S.sb("wu", [128, 8, DFF], BF16)
    wd = S.sb("wd", [128, NFC, D], BF16)
    gpre = S.sb("gpre", [128, D], F32)
    gpost = S.sb("gpost", [128, D], F32)
    ident = S.sb("ident", [128, 128], BF16)
    S.dma(ident[:], ident_d[:, :], q="pool")
    bcast_row(S, gpre[:], w[pre + "_pre_g"][l:l + 1, :])
    bcast_row(S, gpost[:], w[pre + "_post_g"][l:l + 1, :])
    S.ts(gpost[:], gpost[:], 0.5, None, ALU.mult, eng="pool")
    load_w_bf16(S, wg, w[pre + "_w_gate"][l], 8)
    load_w_bf16(S, wu, w[pre + "_w_up"][l], 8)
    load_w_bf16(S, wd, w[pre + "_w_down"][l], NFC)

    xb = [S.sb("xb", [128, D], F32) for _ in range(NSUB * 2)]
    hb = [S.sb("hb", [128, D], BF16) for _ in range(2)]
    junk = S.sb("junk", [128, D], BF16)
    hT = [S.sb("hT", [128, 8, TB], BF16) for _ in range(2)]
    actT = S.sb("actT", [128, NFC, TB], BF16)
    sg = [S.sb("sg", [128, TB], F32) for _ in range(2)]
    fsb = [S.sb("fsb", [128, D], F32) for _ in range(2)]
    st = [S.sb("st", [128, 8], F32) for _ in range(4)]
    ptr = [S.ps("ptr", [128, 8, 128], BF16) for _ in range(2)]
    pg = [S.ps("pg", [128, 512], F32) for _ in range(2)]
    po = [S.ps("po", [128, 512], F32) for _ in range(2)]

    nblk = S_LEN // TB
    cnt = 0
    for b in range(nblk):
        hTb = hT[b % 2]
        for s in range(NSUB):
            t = b * NSUB + s
            x = xb[(b % 2) * NSUB + s]
            stt_ = st[cnt % 4]
            h = hb[cnt % 2]
            p = ptr[cnt % 2]
            cnt += 1
            S.dma(x[:], (xres[t * 128:(t + 1) * 128, :], ("xres", t)))
            S.act(junk[:], x[:], AF.Square, accum_out=stt_[:, 0:1])
            S.ts(stt_[:, 1:2], stt_[:, 0:1], 1.0 / D, RMS_EPS, ALU.mult, ALU.add)
            S.act(stt_[:, 2:3], stt_[:, 1:2], AF.Sqrt)
            S.recip(stt_[:, 3:4], stt_[:, 2:3])
            S.stt(h[:], x[:], stt_[:, 3:4], gpre[:], ALU.mult, ALU.mult)
            for dc in range(8):
                S.tr(p[:, dc, :], h[:, dc * 128:(dc + 1) * 128], ident[:])
            S.copy(hTb[:, :, s * 128:(s + 1) * 128], p[:, :, :], eng="act")
        for fc in range(NFC):
            pgt = pg[fc % 2]
            for dc in range(8):
                S.mm(pgt[:, 0:TB], wg[:, dc, fc * 128:(fc + 1) * 128], hTb[:, dc, :],
                     start=(dc == 0), stop=(dc == 7))
            for dc in range(8):
                S.mm(pgt[:, TB:2 * TB], wu[:, dc, fc * 128:(fc + 1) * 128], hTb[:, dc, :],
                     start=(dc == 0), stop=(dc == 7))
            sgt = sg[fc % 2]
            S.act(sgt[:], pgt[:, 0:TB], AF.Silu)
            S.tt(actT[:, fc, :], sgt[:], pgt[:, TB:2 * TB], ALU.mult)
        for s in range(NSUB):
            t = b * NSUB + s
            x = xb[(b % 2) * NSUB + s]
            f = fsb[s % 2]
            stt_ = st[cnt % 4]
            cnt += 1
            for dh in range(2):
                pot = po[dh]
                for fc in range(NFC):
                    S.mm(pot[:], actT[:, fc, s * 128:(s + 1) * 128], wd[:, fc, dh * 512:(dh + 1) * 512],
                         start=(fc == 0), stop=(fc == NFC - 1))
                S.act(f[:, dh * 512:(dh + 1) * 512], pot[:], AF.Copy)
                S.act(junk[:, dh * 512:(dh + 1) * 512], pot[:], AF.Square, accum_out=stt_[:, dh:dh + 1])
            S.tt(stt_[:, 2:3], stt_[:, 0:1], stt_[:, 1:2], ALU.add)
            S.ts(stt_[:, 3:4], stt_[:, 2:3], 1.0 / D, RMS_EPS, ALU.mult, ALU.add)
            S.act(stt_[:, 4:5], stt_[:, 3:4], AF.Sqrt)
            S.recip(stt_[:, 5:6], stt_[:, 4:5])
            S.stt(f[:], f[:], stt_[:, 5:6], gpost[:], ALU.mult, ALU.mult)
            S.tt(x[:], x[:], f[:], ALU.add, eng="pool")
            S.dma((xres[t * 128:(t + 1) * 128, :], ("xres", t)), x[:], q="act")


HD = 64


def col_layout():
    spec = (('nsa_q', 256), ('nsa_k_cmp', 64), ('nsa_v_cmp', 64), ('nsa_k_slc', 64), ('nsa_v_slc', 64),
            ('nsa_k_win', 64), ('nsa_v_win', 64), ('nsa_gate', 12), ('ret_q', 256), ('ret_k', 256),
            ('ret_v', 512), ('ret_g', 512), ('rwkv', 1024), ('swa_q', 256), ('swa_k', 128), ('swa_v', 128),
            ('branch_gate', 4096))
    lay, s = {}, 0
    for n, wd_ in spec:
        lay[n] = (s, s + wd_)
        s += wd_
    return lay, s


def _partner(cols):
    out = []
    for i in range(0, len(cols), 64):
        blk = cols[i:i + 64]
        out.extend(blk[32:64])
        out.extend(blk[0:32])
    return out


def w_in_index_sets():
    lay, _ = col_layout()
    r = lambda n: list(range(*lay[n]))
    idx = {}
    q = r('swa_q')
    k = r('swa_k')
    kk0 = k[0:64] + k[0:64]
    kk1 = k[64:128] + k[64:128]
    idx['swa'] = q + _partner(q) + kk0 + kk1 + _partner(kk0) + _partner(kk1) + r('swa_v')
    nq, ksl, kwi = r('nsa_q'), r('nsa_k_slc'), r('nsa_k_win')
    idx['nsa'] = (nq + _partner(nq) + r('nsa_k_cmp') + r('nsa_v_cmp') + ksl + _partner(ksl) + kwi + _partner(kwi)
                  + r('nsa_v_slc') + r('nsa_v_win') + r('nsa_gate'))
    idx['r'] = r('rwkv')
    idx['gate'] = r('branch_gate')
    rq, rk = r('ret_q'), r('ret_k')
    idx['ret'] = rq + _partner(rq) + rk + _partner(rk) + r('ret_v') + r('ret_g')
    return idx


def phase_mixpre(S, nc, xres, w, l, hT_d, ident_d):
    g = S.sb("g", [128, D], F32)
    ident = S.sb("ident", [128, 128], BF16)
    S.dma(ident[:], ident_d[:, :], q="pool")
    bcast_row(S, g[:], w["mix_pre_g"][l:l + 1, :])
    xb = [S.sb("xb", [128, D], F32) for _ in range(3)]
    hb = [S.sb("hb", [128, D], BF16) for _ in range(2)]
    junk = S.sb("junk", [128, D], BF16)
    hT = [S.sb("hT", [128, 8, 128], BF16) for _ in range(3)]
    st = [S.sb("st", [128, 8], F32) for _ in range(3)]
    ptr = [S.ps("ptr", [128, 8, 128], BF16) for _ in range(2)]
    hv = hT_d.rearrange("(c p) t -> p c t", p=128)
    for t in range(NT):
        x = xb[t % 3]
        s_ = st[t % 3]
        h = hb[t % 2]
        p = ptr[t % 2]
        o = hT[t % 3]
        S.dma(x[:], (xres[t * 128:(t + 1) * 128, :], ("xres", t)))
        S.act(junk[:], x[:], AF.Square, accum_out=s_[:, 0:1])
        S.ts(s_[:, 1:2], s_[:, 0:1], 1.0 / D, RMS_EPS, ALU.mult, ALU.add)
        S.act(s_[:, 2:3], s_[:, 1:2], AF.Sqrt)
        S.recip(s_[:, 3:4], s_[:, 2:3])
        S.stt(h[:], x[:], s_[:, 3:4], g[:], ALU.mult, ALU.mult)
        for dc in range(8):
            S.tr(p[:, dc, :], h[:, dc * 128:(dc + 1) * 128], ident[:])
        S.copy(o[:], p[:], eng="act")
        S.dma((hv[:, :, t * 128:(t + 1) * 128], ("hT", t)), o[:], q="act")


def proj_feat(S, dst, wsb, col0, hT, pps, rope=None, npart=128):
    for tb in range(8):
        tsl = slice(tb * 512, (tb + 1) * 512)
        p0 = pps[tb % 2]
        for dc in range(8):
            S.mm(p0[0:npart, 0:512], wsb[:, dc, col0:col0 + npart], hT[:, dc, tsl],
                 start=(dc == 0), stop=(dc == 7))
        if rope is None:
            S.copy(dst[0:npart, tsl], p0[0:npart, 0:512], eng="act")
        else:
            pc, cos_d, sin_d, cst, tmp, pps2 = rope
            p1 = pps2[tb % 2]
            for dc in range(8):
                S.mm(p1[0:npart, 0:512], wsb[:, dc, pc:pc + npart], hT[:, dc, tsl],
                     start=(dc == 0), stop=(dc == 7))
            cs = cst[tb % 2]
            S.dma(cs[:, 0, :], cos_d[:, tsl])
            S.dma(cs[:, 1, :], sin_d[:, tsl])
            t1 = tmp[tb % 2]
            S.tt(t1[0:npart, 0, :], p0[0:npart, 0:512], cs[0:npart, 0, :], ALU.mult)
            S.tt(t1[0:npart, 1, :], p1[0:npart, 0:512], cs[0:npart, 1, :], ALU.mult)
            S.tt(dst[0:npart, tsl], t1[0:npart, 0, :], t1[0:npart, 1, :], ALU.add, eng="pool")


def proj_tok(S, dst_fn, wsb, col0, ncol, hT, pps, eng="act", view=None):
    for t in range(NT):
        p0 = pps[t % 2]
        for dc in range(8):
            S.mm(p0[:, 0:ncol], hT[:, dc, t * 128:(t + 1) * 128], wsb[:, dc, col0:col0 + ncol],
                 start=(dc == 0), stop=(dc == 7))
        src = p0[:, 0:ncol]
        if view is not None:
            src = view(src)
        S.copy(dst_fn(t), src, eng=eng)


def attn_qblock(S, nheads, q_of, k_of, v_of, tiles, scale, pss, pacc, pbuf, cnt, skew=2):
    items = [(i, kt, h, mf) for i, (kt, mf) in enumerate(tiles) for h in range(nheads)]
    n = len(items)
    masks = {}
    pbs = {}
    skew = min(skew, len(pss) - 1)

    def front(j):
        i, kt, h, mf = items[j]
        if h == 0:
            masks[i] = mf() if mf is not None else None
        ps = pss[(cnt + j) % len(pss)]
        pb = pbuf[(cnt + j) % len(pbuf)]
        pbs[j] = pb
        S.mm(ps[:, 0:512], k_of(h, kt), q_of(h), start=True, stop=True)
        S.act(pb[:], ps[:, 0:512], AF.Exp, scale=scale)
        if masks[i] is not None:
            S.tt(pb[:], pb[:], masks[i], ALU.mult)

    def back(j):
        i, kt, h, mf = items[j]
        pb = pbs.pop(j)
        for sub in range(4):
            S.mm(pacc[h][:, sub, :], pb[:, sub * 128:(sub + 1) * 128], v_of(h, kt),
                 start=(i == 0 and sub == 0), stop=(i == len(tiles) - 1))

    for j in range(n + skew):
        if j < n:
            front(j)
        if j - skew >= 0:
            back(j - skew)
    return cnt + n


def attn_finish(S, nheads, pacc, ybuf, st, zextra=None, gate_of=None, accumulate=False):
    for h in range(nheads):
        s_ = st[h % len(st)]
        if zextra is not None:
            S.ts(s_[:, 0:4], pacc[h][:, :, 64], zextra(h), None, ALU.add)
        else:
            S.ts(s_[:, 0:4], pacc[h][:, :, 64], 1e-30, None, ALU.max)
        S.recip(s_[:, 4:8], s_[:, 0:4])
        if gate_of is not None:
            S.tt(s_[:, 4:8], s_[:, 4:8], gate_of(h), ALU.mult)
        for sub in range(4):
            yo = ybuf[:, sub, h * 64:(h + 1) * 64]
            if not accumulate:
                S.ts(yo, pacc[h][:, sub, 0:64], s_[:, 4 + sub:5 + sub], None, ALU.mult)
            else:
                S.stt(yo, pacc[h][:, sub, 0:64], s_[:, 4 + sub:5 + sub], yo, ALU.mult, ALU.add)


def phase_swa(S, nc, w, l, hT_d, wswa_d, cst, y_d):
    NCOL = 8 * 128 + 128
    wsb = S.sb("wsb", [128, 8, NCOL], BF16)
    load_w_bf16(S, wsb, wswa_d[l], 8)
    hT = S.sb("hTall", [128, 8, S_LEN], BF16)
    hv = hT_d.rearrange("(c p) t -> p c t", p=128)
    for c in range(8):
        S.dma(hT[:, c, :], (hv[:, c, :],) + tuple(("hT", t) for t in range(NT)))
    qT = [S.sb("qT", [128, S_LEN], BF16) for _ in range(2)]
    kT = [S.sb("kT", [128, S_LEN], BF16) for _ in range(2)]
    vext = S.sb("vext", [128, NT, 2, 65], BF16)
    pps = [S.ps("pp", [128, 512], F32) for _ in range(2)]
    pps2 = [S.ps("pp2", [128, 512], F32) for _ in range(2)]
    cstt = [S.sb("cs", [128, 2, 512], F32) for _ in range(2)]
    tmp = [S.sb("tmp", [128, 2, 512], F32) for _ in range(2)]
    rope = lambda pc: (pc, cst["cosT"], cst["sinT"], cstt, tmp, pps2)
    proj_feat(S, qT[0], wsb, 0, hT, pps, rope(256))
    proj_feat(S, qT[1], wsb, 128, hT, pps, rope(384))
    proj_feat(S, kT[0], wsb, 512, hT, pps, rope(768))
    proj_feat(S, kT[1], wsb, 640, hT, pps, rope(896))
    S.memset(vext[:, :, :, 64:65], 1.0, eng="pool")
    proj_tok(S, lambda t: vext[:, t, :, 0:64], wsb, 1024, 128, hT, pps,
             view=lambda a: a.rearrange("p (a b) -> p a b", a=2))
    masks = S.sb("masks", [128, 5, 512], BF16)
    for r in range(5):
        S.dma(masks[:, r, :], cst["mask_swa"][r], q="pool")
    sk = S.sb("sk", [128, 4], F32)
    S.dma(sk[:], w["swa_sinks"][l:l + 1, :].partition_broadcast(128))
    S.act(sk[:], sk[:], AF.Exp)
    pacc = [S.ps("pacc", [128, 4, 65], F32) for _ in range(4)]
    pbuf = [S.sb("pbuf", [128, 512], BF16) for _ in range(5)]
    ybuf = [S.sb("ybuf", [128, 4, 256], BF16) for _ in range(2)]
    st = [S.sb("st", [128, 8], F32) for _ in range(4)]
    cnt = 0
    for qb in range(8):
        qs = slice(qb * 512, (qb + 1) * 512)
        tiles = [(4 * qb + r, (lambda r=r: masks[:, r + 1, :])) for r in range(-1, 4) if 4 * qb + r >= 0]
        yb = ybuf[qb % 2]
        cnt = attn_qblock(
            S, 4,
            lambda h: qT[h // 2][(h % 2) * 64:(h % 2) * 64 + 64, qs],
            lambda h, kt: kT[h // 2][(h % 2) * 64:(h % 2) * 64 + 64, kt * 128:(kt + 1) * 128],
            lambda h, kt: vext[:, kt, h // 2, :],
            tiles, 0.125, pps + pps2, pacc, pbuf, cnt, skew=3)
        attn_finish(S, 4, pacc, yb, st, zextra=lambda h: sk[:, h:h + 1])
        S.dma(y_d[qb * 512:(qb + 1) * 512, :].rearrange("(s p) f -> p s f", p=128), yb[:], q="act")


def phase_ret(S, nc, w, l, hT_d, wret_d, cst, y_d, ident_d):
    wsb = S.sb("wsb", [128, 8, 2048], BF16)
    load_w_bf16(S, wsb, wret_d[l], 8)
    hT = S.sb("hTall", [128, 8, S_LEN], BF16)
    hv = hT_d.rearrange("(c p) t -> p c t", p=128)
    for c in range(8):
        S.dma(hT[:, c, :], (hv[:, c, :],) + tuple(("hT", t) for t in range(NT)))
    ident = S.sb("ident", [128, 128], BF16)
    S.dma(ident[:], ident_d[:, :], q="pool")
    qT = [S.sb("qT", [64, S_LEN], BF16) for _ in range(4)]
    kT = [S.sb("kT", [64, S_LEN], BF16) for _ in range(4)]
    pps = [S.ps("pp", [128, 512], F32) for _ in range(2)]
    pps2 = [S.ps("pp2", [128, 512], F32) for _ in range(2)]
    cstt = [S.sb("cs", [128, 2, 512], F32) for _ in range(2)]
    tmp = [S.sb("tmp", [128, 2, 512], F32) for _ in range(2)]
    rope = lambda pc: (pc, cst["cosT"], cst["sinT"], cstt, tmp, pps2)
    for h in range(4):
        proj_feat(S, qT[h], wsb, h * 64, hT, pps, rope(256 + h * 64), npart=64)
        proj_feat(S, kT[h], wsb, 512 + h * 64, hT, pps, rope(768 + h * 64), npart=64)
    dmask = S.sb("dmask", [128, 4, 128], F32)
    S.dma(dmask[:], cst["ret_dmaskT"][:, :, :])
    zt = S.sb("zt", [128, 4], F32)
    S.dma(zt[:], cst["ret_zeta"][:, :])
    xiT = S.sb("xiT", [64, 4, 128], F32)
    S.dma(xiT[:], cst["ret_xiT"][:, :, :])
    gch = S.sb("gch", [64, 4], F32)
    S.dma(gch[:], cst["ret_gch"][:, :])
    gng = S.sb("gng", [128, 512], F32)
    bcast_row(S, gng[:], w["ret_gn_g"][l:l + 1, :])
    R = S.sb("R", [64, 4, 128], F32)
    Rbf = S.sb("Rbf", [64, 4, 128], BF16)
    S.memset(R[:], 0.0)
    S.memset(Rbf[:], 0.0)
    po = [S.ps("po", [128, 4, 128], F32) for _ in range(2)]
    ptk = S.ps("ptk", [128, 4, 64], BF16)
    pin = pps2[0][:, :].rearrange("p (h c) -> p h c", h=4)
    pkv = pps2[1][:, :].rearrange("p (h e) -> p h e", h=4)
    vb = [S.sb("vb", [128, 512], BF16) for _ in range(2)]
    sgb = [S.sb("sgb", [128, 512], F32) for _ in range(2)]
    qx = [S.sb("qx", [64, 4, 128], BF16) for _ in range(2)]
    kz = [S.sb("kz", [128, 4, 64], BF16) for _ in range(2)]
    inm = [S.sb("inm", [128, 4, 128], BF16) for _ in range(2)]
    osb = [S.sb("osb", [128, 4, 128], F32) for _ in range(2)]
    sq = [S.sb("sq", [128, 4, 128], F32) for _ in range(2)]
    st = [S.sb("st", [128, 8, 4], F32) for _ in range(2)]
    yb = [S.sb("yb", [128, 512], BF16) for _ in range(2)]
    for t in range(NT):
        ts_ = slice(t * 128, (t + 1) * 128)
        b = t % 2
        for dc in range(8):
            S.mm(pps[0][:, :], hT[:, dc, ts_], wsb[:, dc, 1024:1536], start=(dc == 0), stop=(dc == 7))
        S.copy(vb[b][:], pps[0][:, :], eng="act")
        for dc in range(8):
            S.mm(pps[1][:, :], hT[:, dc, ts_], wsb[:, dc, 1536:2048], start=(dc == 0), stop=(dc == 7))
        S.act(sgb[b][:], pps[1][:, :], AF.Silu)
        for h in range(4):
            S.tr(ptk[:, h, :], kT[h][:, ts_], ident[0:64, 0:64])
        S.tt(kz[b][:], ptk[:, :, :], zt[:, :].unsqueeze(2).to_broadcast([128, 4, 64]), ALU.mult)
        for h in range(4):
            S.tt(qx[b][:, h, :], qT[h][:, ts_], xiT[:, h, :], ALU.mult, eng="pool")
        for h in range(4):
            S.mm(pin[:, h, :], kT[h][:, ts_], qT[h][:, ts_], start=True, stop=True)
        S.tt(inm[b][:], pin, dmask[:], ALU.mult)
        pot = po[b]
        for h in range(4):
            S.mm(pot[:, h, :], inm[b][:, h, :], vb[b][:, h * 128:(h + 1) * 128], start=True, stop=False)
            S.mm(pot[:, h, :], qx[b][:, h, :], Rbf[:, h, :], start=False, stop=True)
        for h in range(4):
            S.mm(pkv[0:64, h, :], kz[b][:, h, :], vb[b][:, h * 128:(h + 1) * 128], start=True, stop=True)
        for h in range(4):
            S.stt(R[:, h, :], R[:, h, :], gch[:, h:h + 1], pkv[0:64, h, :], ALU.mult, ALU.add)
        S.copy(Rbf[:], R[:], eng="act")
        o = osb[b]
        s_ = st[b]
        S.copy(o[:], pot[:], eng="act")
        S.reduce(s_[:, 0, :], o[:], ALU.add)
        S.tt(sq[b][:], o[:], o[:], ALU.mult, eng="pool")
        S.reduce(s_[:, 1, :], sq[b][:], ALU.add)
        S.ts(s_[:, 2, :], s_[:, 0, :], 1.0 / 128, None, ALU.mult)
        S.tt(s_[:, 3, :], s_[:, 2, :], s_[:, 2, :], ALU.mult)
        S.stt(s_[:, 4, :], s_[:, 1, :], 1.0 / 128, s_[:, 3, :], ALU.mult, ALU.subtract)
        S.ts(s_[:, 5, :], s_[:, 4, :], 1e-5, None, ALU.add)
        S.act(s_[:, 6, :], s_[:, 5, :], AF.Sqrt)
        S.recip(s_[:, 7, :], s_[:, 6, :])
        S.tt(o[:], o[:], s_[:, 2, :].unsqueeze(2).to_broadcast([128, 4, 128]), ALU.subtract)
        S.tt(o[:], o[:], s_[:, 7, :].unsqueeze(2).to_broadcast([128, 4, 128]), ALU.mult)
        of = o[:].rearrange("p h e -> p (h e)")
        S.tt(of, of, gng[:], ALU.mult, eng="pool")
        S.tt(yb[b][:], of, sgb[b][:], ALU.mult)
        S.dma(y_d[ts_, :], yb[b][:], q="act")


def load_hT(S, hT_d):
    hT = S.sb("hTall", [128, 8, S_LEN], BF16)
    hv = hT_d.rearrange("(c p) t -> p c t", p=128)
    for c in range(8):
        S.dma(hT[:, c, :], (hv[:, c, :],) + tuple(("hT", t) for t in range(NT)))
    return hT


def phase_nsa_a(S, nc, w, l, hT_d, wnsa_d, cst, scr):
    wsb = S.sb("wsb", [128, 8, 1036], BF16)
    load_w_bf16(S, wsb, wnsa_d[l], 8)
    hT = load_hT(S, hT_d)
    pps = [S.ps("pp", [128, 512], F32) for _ in range(2)]
    pps2 = [S.ps("pp2", [128, 512], F32) for _ in range(2)]
    cstt = [S.sb("cs", [128, 2, 512], F32) for _ in range(2)]
    tmp = [S.sb("tmp", [128, 2, 512], F32) for _ in range(2)]
    rope = lambda pc: (pc, cst["cosT"], cst["sinT"], cstt, tmp, pps2)
    stg = [S.sb("stg", [64, S_LEN], BF16) for _ in range(2)]
    outs = [(h, h * 64, None) for h in range(4)] + [(4 + h, h * 64, 256 + h * 64) for h in range(4)]
    outs += [(8, 512, None), (9, 576, None), (10, 640, 704), (11, 768, 832)]
    for n, (idx, col0, pc) in enumerate(outs):
        st_ = stg[n % 2]
        proj_feat(S, st_, wsb, col0, hT, pps, rope(pc) if pc is not None else None, npart=64)
        S.dma((scr["nsaT_d"][idx], ("nsaT", idx)), st_[:, :], q="act")
    vb = [S.sb("vb", [128, 128], BF16) for _ in range(2)]
    gb = [S.sb("gb", [128, 12], F32) for _ in range(2)]
    for t in range(NT):
        p0 = pps[t % 2]
        for dc in range(8):
            S.mm(p0[:, 0:140], hT[:, dc, t * 128:(t + 1) * 128], wsb[:, dc, 896:1036],
                 start=(dc == 0), stop=(dc == 7))
        S.copy(vb[t % 2][:], p0[:, 0:128], eng="dve")
        S.act(gb[t % 2][:], p0[:, 128:140], AF.Sigmoid)
        S.dma(scr["nsa_v_d"][t * 128:(t + 1) * 128, :], vb[t % 2][:], q="act")
        S.dma(scr["nsa_g_d"][t * 128:(t + 1) * 128, :], gb[t % 2][:], q="act")


def phase_nsa_b(S, nc, w, l, cst, scr):
    kvT = S.sb("kvT", [64, 2, S_LEN], BF16)
    S.dma(kvT[:, 0, :], scr["nsaT_d"][8])
    S.dma(kvT[:, 1, :], scr["nsaT_d"][9])
    posT = S.sb("posT", [64, 2, 32], F32)
    S.dma(posT[:], w["nsa_posT"][l].rearrange("a d l -> d a l"))
    ident = S.sb("ident", [128, 128], BF16)
    S.dma(ident[:], cst["ident"][:, :], q="pool")
    w1 = [S.sb("w1", [64, 32, 256], BF16) for _ in range(2)]
    w2 = [S.sb("w2", [128, 2, 64], BF16) for _ in range(2)]
    for a, nm in enumerate(("k", "v")):
        S.dma(w1[a][:], w["nsa_cmp_%s_w1" % nm][l].rearrange("(l d) j -> d l j", d=64), q="pool")
        S.dma(w2[a][:], w["nsa_cmp_%s_w2" % nm][l].rearrange("(c p) d -> p c d", p=128), q="pool")
    X = S.sb("X", [64, 32, 256], BF16)
    gT = [S.sb("gT", [128, 2, 256], BF16) for _ in range(2)]
    kcT = S.sb("kcT", [64, 256], BF16)
    vcx = S.sb("vcx", [128, 2, 65], BF16)
    ov = S.sb("ov", [128, 2, 64], BF16)
    S.dma(ov[:], cst["overlap"].rearrange("(c p) j -> p c j", p=128), q="pool")
    S.memset(kcT[:], 0.0)
    S.memset(vcx[:], 0.0)
    S.memset(vcx[:, :, 64:65], 1.0)
    for a in range(2):
        S.memset(gT[a][:], 0.0)
    pps = [S.ps("pp", [128, 512], F32) for _ in range(2)]
    hs = [S.sb("hs", [128, 3, 256], F32) for _ in range(2)]
    for a in range(2):
        for l_ in range(32):
            S.ts(X[:, l_, 0:255], kvT[:, a, l_:l_ + 16 * 254 + 1:16], posT[:, a, l_:l_ + 1], None, ALU.add)
        for jc in range(2):
            ph = pps[jc]
            for l_ in range(32):
                S.mm(ph[:, 0:255], w1[a][:, l_, jc * 128:(jc + 1) * 128], X[:, l_, 0:255],
                     start=(l_ == 0), stop=(l_ == 31))
            h_ = hs[jc]
            S.act(h_[:, 0, 0:255], ph[:, 0:255], AF.Square)
            S.ts(h_[:, 0, 0:255], h_[:, 0, 0:255], 0.044715, 1.0, ALU.mult, ALU.add)
            S.tt(h_[:, 1, 0:255], h_[:, 0, 0:255], ph[:, 0:255], ALU.mult)
            S.act(h_[:, 2, 0:255], h_[:, 1, 0:255], AF.Sigmoid, scale=1.5957691216057308)
            S.tt(gT[a][:, jc, 0:255], h_[:, 2, 0:255], ph[:, 0:255], ALU.mult)
    for jc in range(2):
        S.mm(pps[0][0:64, 0:256], w2[0][:, jc, :], gT[0][:, jc, :], start=(jc == 0), stop=(jc == 1))
    S.copy(kcT[:, :], pps[0][0:64, 0:256], eng="act")
    for ic in range(2):
        for jc in range(2):
            S.mm(pps[1][:, ic * 64:(ic + 1) * 64], gT[1][:, jc, ic * 128:(ic + 1) * 128], w2[1][:, jc, :],
                 start=(jc == 0), stop=(jc == 1))
        S.copy(vcx[:, ic, 0:64], pps[1][:, ic * 64:(ic + 1) * 64], eng="act")
    qT = S.sb("qT", [64, 4, S_LEN], BF16)
    for h in range(4):
        S.dma(qT[:, h, :], scr["nsaT_d"][h])
    cmask = S.sb("cmask", [128, 2, S_LEN], BF16)
    S.dma(cmask[:], cst["cmpmaskT"].rearrange("(c p) q -> p c q", p=128), q="pool")
    pss = [S.ps("pss", [128, 512], F32) for _ in range(2)]
    pao = [S.ps("pao", [128, 4, 65], F32) for _ in range(2)]
    pai = S.ps("pai", [128, 4, 64], F32)
    pT = S.ps("pT", [64, 4, 128], BF16)
    pbuf = [S.sb("pbuf", [128, 512], BF16) for _ in range(6)]
    pss4 = pss + pps
    ocb = [S.sb("ocb", [128, 4, 256], BF16) for _ in range(2)]
    imp = [S.sb("imp", [128, 4, 64], F32) for _ in range(2)]
    sbt = [S.sb("sbt", [128, 4, 64], F32) for _ in range(2)]
    score = [S.sb("score", [128, 4, 64], F32) for _ in range(2)]
    work = S.sb("work", [128, 4, 64], F32)
    m8 = S.sb("m8", [128, 4, 8], F32)
    m8b = S.sb("m8b", [128, 4, 8], F32)
    selm = [S.sb("selm", [128, 4, 64], BF16) for _ in range(2)]
    selT = S.sb("selT", [64, S_LEN], BF16)
    st = [S.sb("st", [128, 8], F32) for _ in range(4)]
    cnt = 0
    for qb in range(8):
        qs = slice(qb * 512, (qb + 1) * 512)
        b = qb % 2
        pend = []

        def nsab_back(item, b=b):
            h, po_, pbl = item
            for ic in range(2):
                pb = pbl[ic]
                for sub in range(4):
                    S.mm(po_[:, sub, :], pb[:, sub * 128:(sub + 1) * 128], vcx[:, ic, :],
                         start=(ic == 0 and sub == 0), stop=(ic == 1))
                for sub in range(4):
                    S.mm(pai[:, sub, :], pb[:, sub * 128:(sub + 1) * 128], ov[:, ic, :],
                         start=(ic == 0 and sub == 0), stop=(ic == 1))
            s_ = st[h]
            S.ts(s_[:, 0:4], po_[:, :, 64], 1e-30, None, ALU.max)
            S.recip(s_[:, 4:8], s_[:, 0:4])
            for sub in range(4):
                S.ts(ocb[b][:, sub, h * 64:(h + 1) * 64], po_[:, sub, 0:64], s_[:, 4 + sub:5 + sub], None, ALU.mult)
                if h == 0:
                    S.ts(imp[b][:, sub, :], pai[:, sub, :], s_[:, 4 + sub:5 + sub], None, ALU.mult)
                else:
                    S.stt(imp[b][:, sub, :], pai[:, sub, :], s_[:, 4 + sub:5 + sub], imp[b][:, sub, :],
                          ALU.mult, ALU.add)

        for h in range(4):
            po_ = pao[h % 2]
            pbl = []
            for ic in range(2):
                ps = pss4[cnt % 4]
                pb = pbuf[cnt % 6]
                cnt += 1
                pbl.append(pb)
                S.mm(ps[:, 0:512], kcT[:, ic * 128:(ic + 1) * 128], qT[:, h, qs], start=True, stop=True)
                S.act(pb[:], ps[:, 0:512], AF.Exp, scale=0.125)
                S.tt(pb[:], pb[:], cmask[:, ic, qs], ALU.mult)
            pend.append((h, po_, pbl))
            if len(pend) > 1:
                nsab_back(pend.pop(0))
        nsab_back(pend.pop(0))
        S.dma(scr["ocmp_d"][qs, :].rearrange("(s p) f -> p s f", p=128), ocb[b][:], q="act")
        S.dma(sbt[b][:], cst["selbias"][qs, :].rearrange("(s p) j -> p s j", p=128))
        sc = score[b]
        S.tt(sc[:], imp[b][:], sbt[b][:], ALU.add)
        for sub in range(4):
            a_sc, a_m8, a_wk, a_m8b = sc[:, sub, :], m8[:, sub, :], work[:, sub, :], m8b[:, sub, :]
            S.op("dve", lambda e, o=a_m8, i=a_sc: e.max(o, i), reads=[sc[:]], writes=[m8[:]])
            S.op("dve", lambda e, o=a_wk, r=a_m8, i=a_sc: e.match_replace(o, r, i, -3.0e9),
                 reads=[sc[:], m8[:]], writes=[work[:]])
            S.op("dve", lambda e, o=a_m8b, i=a_wk: e.max(o, i), reads=[work[:]], writes=[m8b[:]])
            S.ts(selm[b][:, sub, :], sc[:, sub, :], m8b[:, sub, 7:8], None, ALU.is_ge)
        for sub in range(4):
            S.tr(pT[:, sub, :], selm[b][:, sub, :], ident[:])
        S.copy(selT[:, qs].rearrange("j (s p) -> j s p", s=4), pT[:, :, :], eng="act")
    S.dma(scr["selT_d"][:, :], selT[:, :], q="act")


def phase_nsa_c(S, nc, w, l, cst, scr, y_d):
    qrT = S.sb("qrT", [64, 4, S_LEN], BF16)
    for h in range(4):
        S.dma(qrT[:, h, :], scr["nsaT_d"][4 + h])
    kT = S.sb("kT", [64, 2, S_LEN], BF16)
    S.dma(kT[:, 0, :], scr["nsaT_d"][10])
    S.dma(kT[:, 1, :], scr["nsaT_d"][11])
    selT = S.sb("selT", [64, S_LEN], BF16)
    S.dma(selT[:, :], scr["selT_d"][:, :])
    vext = S.sb("vext", [128, NT, 2, 65], BF16)
    S.memset(vext[:, :, :, 64:65], 1.0, eng="pool")
    vv = scr["nsa_v_d"].rearrange("(t p) f -> p t f", p=128)
    for a in range(2):
        S.dma(vext[:, :, a, 0:64], vv[:, :, a * 64:(a + 1) * 64])
    gsb = S.sb("gsb", [128, NT, 12], F32)
    S.dma(gsb[:], scr["nsa_g_d"].rearrange("(t p) c -> p t c", p=128))
    mwin = S.sb("mwin", [128, 8, 512], BF16)
    for r in range(8):
        S.dma(mwin[:, r, :], cst["mask_win"][r], q="pool")
    mcau = S.sb("mcau", [128, 4, 512], BF16)
    for r in range(4):
        S.dma(mcau[:, r, :], cst["mask_causal"][r], q="pool")
    Eall = S.sb("Eall", [64, NT, 128], BF16)
    S.dma(Eall[:], cst["Eall"][:, :, :], q="pool")
    pss = [S.ps("pss", [128, 512], F32) for _ in range(3)]
    pacc = [S.ps("pacc", [128, 4, 65], F32) for _ in range(4)]
    pm = [S.ps("pm", [128, 512], F32) for _ in range(1)]
    pbuf = [S.sb("pbuf", [128, 512], BF16) for _ in range(4)]
    mbuf = [S.sb("mbuf", [128, 512], BF16) for _ in range(3)]
    ocb = [S.sb("ocb", [128, 4, 256], BF16) for _ in range(2)]
    ybuf = [S.sb("ybuf", [128, 4, 256], F32) for _ in range(2)]
    yb16 = [S.sb("yb16", [128, 4, 256], BF16) for _ in range(2)]
    st = [S.sb("st", [128, 8], F32) for _ in range(4)]
    cnt = 0
    mcnt = [0]
    for qb in range(8):
        qs = slice(qb * 512, (qb + 1) * 512)
        b = qb % 2
        yb = ybuf[b]
        gq = gsb[:, 4 * qb:4 * qb + 4, :]
        S.dma(ocb[b][:], scr["ocmp_d"][qs, :].rearrange("(s p) f -> p s f", p=128))
        for h in range(4):
            S.tt(yb[:, :, h * 64:(h + 1) * 64], ocb[b][:, :, h * 64:(h + 1) * 64],
                 gq[:, :, 3 * h:3 * h + 1].to_broadcast([128, 4, 64]), ALU.mult)
        tiles = [(4 * qb + r, (lambda r=r: mwin[:, r + 4, :])) for r in range(-4, 4) if 4 * qb + r >= 0]
        cnt = attn_qblock(S, 4, lambda h: qrT[:, h, qs], lambda h, kt: kT[:, 1, kt * 128:(kt + 1) * 128],
                          lambda h, kt: vext[:, kt, 1, :], tiles, 0.125, pss, pacc, pbuf, cnt)
        attn_finish(S, 4, pacc, yb, st, gate_of=lambda h: gq[:, :, 3 * h + 2], accumulate=True)

        def mk_mask(kt):
            def f():
                i = mcnt[0]
                mcnt[0] += 1
                p_ = pm[0]
                m_ = mbuf[i % 3]
                S.mm(p_[:, 0:512], Eall[:, kt, :], selT[:, qs], start=True, stop=True)
                r = kt - 4 * qb
                if r >= 0:
                    S.tt(m_[:], p_[:, 0:512], mcau[:, r, :], ALU.mult)
                else:
                    S.copy(m_[:], p_[:, 0:512], eng="pool" if False else "dve")
                return m_[:]
            return f
        tiles = [(kt, mk_mask(kt)) for kt in range(0, 4 * qb + 4)]
        cnt = attn_qblock(S, 4, lambda h: qrT[:, h, qs], lambda h, kt: kT[:, 0, kt * 128:(kt + 1) * 128],
                          lambda h, kt: vext[:, kt, 0, :], tiles, 0.125, pss, pacc, pbuf, cnt)
        attn_finish(S, 4, pacc, yb, st, gate_of=lambda h: gq[:, :, 3 * h + 1], accumulate=True)
        S.copy(yb16[b][:], yb[:], eng="act")
        S.dma(y_d[qs, :].rearrange("(s p) f -> p s f", p=128), yb16[b][:], q="act")


NEG_EH = -0.6065306597126334


def phase_rwkv_a(S, nc, w, l, hT_d, wr_d, cst, scr):
    mub = S.sb("mub", [128, 1024], F32)
    omub = S.sb("omub", [128, 1024], F32)
    bcast_row(S, mub[:], w["rwkv_mu"][l:l + 1, :])
    S.ts(omub[:], mub[:], -1.0, 1.0, ALU.mult, ALU.add)
    W1 = S.sb("W1", [128, 8, 1024], BF16)
    W2 = S.sb("W2", [128, 8, 1024], BF16)
    wst = [S.sb("wst", [128, 1024], F32) for _ in range(2)]
    wv = wr_d[l].rearrange("(c p) f -> p c f", p=128)
    for c in range(8):
        S.dma(wst[c % 2][:], wv[:, c, :])
        S.tt(W1[:, c, :], wst[c % 2][:], omub[:], ALU.mult)
        S.tt(W2[:, c, :], wst[c % 2][:], mub[:], ALU.mult, eng="pool")
    hTp = S.sb("hTp", [128, 8, S_LEN + 1], BF16)
    S.memset(hTp[:, :, 0:1], 0.0)
    hv = hT_d.rearrange("(c p) t -> p c t", p=128)
    for c in range(8):
        S.dma(hTp[:, c, 1:S_LEN + 1], (hv[:, c, :],) + tuple(("hT", t) for t in range(NT)))
    cols = S.sb("cols", [64, 20], F32)
    S.dma(cols[:], w["rwkv_cols"][l])
    omka = S.sb("omka", [64, 4], F32)
    S.ts(omka[:], cols[:, 12:16], -1.0, 1.0, ALU.mult, ALU.add)
    rkc = S.sb("rkc", [64, 4], BF16)
    S.copy(rkc[:], cols[:, 16:20])
    w2sb = S.sb("w2sb", [64, 256], BF16)
    a2sb = S.sb("a2sb", [64, 256], BF16)
    g2sb = S.sb("g2sb", [128, 256], BF16)
    S.dma(w2sb[:], w["rwkv_w2"][l], q="pool")
    S.dma(a2sb[:], w["rwkv_a2"][l], q="pool")
    S.dma(g2sb[:], w["rwkv_g2"][l], q="pool")
    ones64 = S.sb("ones64", [64, 64], BF16)
    S.memset(ones64[:], 1.0)
    rmask = S.sb("rmask", [64, 512], F32)
    S.dma(rmask[:], cst["rw_reset"][:, :])
    gCs = S.sb("gCs", [64, 4, 64], F32)
    pp = [S.ps("pp", [128, 512], F32) for _ in range(7)]
    pbn = S.ps("pbn", [128, 4, 4], F32)
    pc = [0]

    def nextp():
        p = pp[pc[0] % 7]
        pc[0] += 1
        return p

    def xmT(c0, m, t0, n=512):
        p = nextp()
        for dc in range(8):
            S.mm(p[0:m, 0:n], W1[:, dc, c0:c0 + m], hTp[:, dc, 1 + t0:1 + t0 + n], start=(dc == 0), stop=False)
        for dc in range(8):
            S.mm(p[0:m, 0:n], W2[:, dc, c0:c0 + m], hTp[:, dc, t0:t0 + n], start=False, stop=(dc == 7))
        return p

    twl = [S.sb("twl", [64, 512], BF16) for _ in range(2)]
    tal = [S.sb("tal", [64, 512], BF16) for _ in range(2)]
    sgl = [S.sb("sgl", [128, 512], BF16) for _ in range(2)]
    vtok = [S.sb("vtok", [128, 256], BF16) for _ in range(2)]
    gtok = [S.sb("gtok", [128, 256], F32) for _ in range(2)]
    bon = [S.sb("bon", [128, 4, 4], F32) for _ in range(2)]
    NF = 14
    f32t = [[S.sb("f%d" % i, [64, 512], F32) for i in range(NF)] for _ in range(2)]
    sqb = [S.sb("sqb", [64, 512], BF16) for _ in range(2)]
    rkb = [S.sb("rkb", [64, 512], BF16) for _ in range(2)]
    out6 = [S.sb("out6", [64, 6, 512], BF16) for _ in range(2)]
    for tb in range(8):
        t0 = tb * 512
        b = tb % 2
        p = xmT(768, 64, t0)
        S.act(twl[b][:], p[0:64, :], AF.Tanh)
        p = xmT(832, 64, t0)
        S.copy(tal[b][:], p[0:64, :], eng="dve")
        p = xmT(896, 128, t0)
        S.act(sgl[b][:], p[:, :], AF.Sigmoid)
        for sub in range(4):
            tt0 = t0 + sub * 128
            p = nextp()
            for dc in range(8):
                S.mm(p[:, 0:256], hTp[:, dc, 1 + tt0:1 + tt0 + 128], W1[:, dc, 512:768], start=(dc == 0), stop=False)
            for dc in range(8):
                S.mm(p[:, 0:256], hTp[:, dc, tt0:tt0 + 128], W2[:, dc, 512:768], start=False, stop=(dc == 7))
            vt = vtok[sub % 2]
            S.copy(vt[:], p[:, 0:256], eng="act")
            S.dma(scr["rw_v_d"][tt0:tt0 + 128, :], vt[:], q="act")
            p = nextp()
            S.mm(p[:, 0:256], sgl[b][:, sub * 128:(sub + 1) * 128], g2sb[:, :], start=True, stop=True)
            gt = gtok[sub % 2]
            S.copy(gt[:], p[:, 0:256], eng="dve")
            S.dma(scr["rw_g_d"][tt0:tt0 + 128, :], gt[:], q="act")
        for h in range(4):
            F = f32t[h % 2]
            lw, cs, Ep, En, cse, Epe, EC, ag, kkr, nrm, kk, tf, k2, bv = F
            o6 = out6[h % 2]
            hs = slice(h * 64, (h + 1) * 64)
            p = nextp()
            S.mm(p[0:64, :], w2sb[:, hs], twl[b][:], start=True, stop=True)
            S.act(lw[:], p[0:64, :], AF.Sigmoid, bias=cols[:, h:h + 1])
            a_cs, a_rm, a_lw = cs[:], rmask[:], lw[:]
            S.op("dve", lambda e, o=a_cs, d0=a_rm, d1=a_lw: e.tensor_tensor_scan(o, d0, d1, 0.0, ALU.mult, ALU.add),
                 reads=[rmask[:], lw[:]], writes=[cs[:]])
            S.act(Ep[:], cs[:], AF.Exp, scale=NEG_EH)
            S.act(En[:], cs[:], AF.Exp, scale=-NEG_EH)
            S.tt(cse[:], cs[:], lw[:], ALU.subtract, eng="pool")
            S.act(Epe[:], cse[:], AF.Exp, scale=NEG_EH)
            S.tt(EC[:].rearrange("p (c t) -> p c t", c=8), En[:].rearrange("p (c t) -> p c t", c=8),
                 Ep[:, 63::64].unsqueeze(2).to_broadcast([64, 8, 64]), ALU.mult, eng="pool")
            S.copy(gCs[:, h, tb * 8:(tb + 1) * 8], Ep[:, 63::64], eng="dve")
            p = nextp()
            S.mm(p[0:64, :], a2sb[:, hs], tal[b][:], start=True, stop=True)
            S.act(ag[:], p[0:64, :], AF.Sigmoid, bias=cols[:, 4 + h:5 + h])
            pk = xmT(256 + h * 64, 64, t0)
            S.ts(kkr[:], pk[0:64, :], cols[:, 8 + h:9 + h], None, ALU.mult)
            S.act(sqb[h % 2][:], kkr[:], AF.Square)
            p = nextp()
            S.mm(p[0:64, :], ones64[:], sqb[h % 2][:], start=True, stop=True)
            S.act(nrm[:], p[0:64, :], AF.Sqrt)
            S.ts(nrm[:], nrm[:], 1e-12, None, ALU.max)
            S.recip(nrm[:], nrm[:])
            S.tt(kk[:], kkr[:], nrm[:], ALU.mult, eng="pool")
            S.ts(tf[:], ag[:], cols[:, 12 + h:13 + h], omka[:, h:h + 1], ALU.mult, ALU.add)
            S.tt(k2[:], tf[:], pk[0:64, :], ALU.mult)
            S.tt(bv[:], kk[:], ag[:], ALU.mult, eng="pool")
            pr = xmT(h * 64, 64, t0)
            S.stt(o6[:, 0, :], kk[:], -1.0, Epe[:], ALU.mult, ALU.mult)
            S.tt(o6[:, 1, :], bv[:], En[:], ALU.mult, eng="pool")
            S.tt(o6[:, 2, :], k2[:], En[:], ALU.mult, eng="pool")
            S.tt(o6[:, 3, :], pr[0:64, :], Ep[:], ALU.mult)
            S.tt(o6[:, 4, :], bv[:], EC[:], ALU.mult, eng="pool")
            S.tt(o6[:, 5, :], k2[:], EC[:], ALU.mult, eng="pool")
            S.tt(rkb[h % 2][:], pr[0:64, :], k2[:], ALU.mult)
            for sub in range(4):
                S.mm(pbn[:, sub, h:h + 1], rkb[h % 2][:, sub * 128:(sub + 1) * 128], rkc[:, h:h + 1],
                     start=True, stop=True)
            S.dma(scr["rwT_d"][h].rearrange("q k t -> k q t")[:, :, t0:t0 + 512], o6[:], q="act")
        S.copy(bon[b][:], pbn[:], eng="act")
        S.dma(scr["rw_b_d"][t0:t0 + 512, :].rearrange("(s p) h -> p s h", p=128), bon[b][:], q="act")
    S.dma(scr["rw_gC_d"][:, :, :], gCs[:], q="act")


def phase_rwkv_b(S, nc, w, l, cst, scr, y_d):
    ident = S.sb("ident", [128, 128], BF16)
    S.dma(ident[:], cst["ident"][:, :], q="pool")
    mlo = S.sb("mlo", [64, 4, 64], F32)
    mup = S.sb("mup", [64, 4, 64], F32)
    mupi = S.sb("mupi", [64, 4, 64], F32)
    I4 = S.sb("I4", [64, 4, 64], F32)
    S.dma(mlo[:], cst["rw_mlo"][:, :, :])
    S.dma(mup[:], cst["rw_mup"][:, :, :])
    S.dma(mupi[:], cst["rw_mupi"][:, :, :])
    S.dma(I4[:], cst["rw_I4"][:, :, :])
    gC = S.sb("gC", [64, 4, 64], F32)
    S.dma(gC[:], scr["rw_gC_d"][:, :, :])
    lng = S.sb("lng", [64, 256], F32)
    lnb = S.sb("lnb", [64, 256], F32)
    S.dma(lng[:], w["rwkv_ln_g"][l:l + 1, :].partition_broadcast(64))
    S.dma(lnb[:], w["rwkv_ln_b"][l:l + 1, :].partition_broadcast(64))
    M = S.sb("M", [64, 4, 64], F32)
    Mbf = S.sb("Mbf", [64, 4, 64], BF16)
    S.memset(M[:], 0.0)
    S.memset(Mbf[:], 0.0)
    NB = 4
    pp_full = [S.ps("pp", [128, 512], F32) for _ in range(7)]
    pp = [p_[0:64, :] for p_ in pp_full]
    ptr_full = S.ps("ptr", [128, 4, 3, 64], BF16)
    ptr = ptr_full[0:64]
    pc = [0]

    def nextp():
        p = pp[pc[0] % 7]
        pc[0] += 1
        return p

    def v4(p, half):
        return p[:, half * 256:(half + 1) * 256].rearrange("p (h s) -> p h s", h=4)

    def mk(name, shape, dt):
        return [S.sb(name, shape, dt) for _ in range(NB)]
    feat = [S.sb("feat", [64, 4, 6, 256], BF16) for _ in range(2)]
    vch = [S.sb("vch", [64, 4, 256], BF16) for _ in range(2)]
    bonb = [S.sb("bonb", [64, 4, 4], F32) for _ in range(2)]
    gtk = [S.sb("gtk", [64, 4, 256], F32) for _ in range(2)]
    tokM = mk("tokM", [64, 4, 2, 64], BF16)
    WZin = mk("WZin", [64, 4, 128], BF16)
    Lb = [mk("L0", [64, 4, 64], BF16), mk("L1", [64, 4, 64], BF16)]
    LTb = [mk("LT0", [64, 4, 64], BF16), mk("LT1", [64, 4, 64], BF16)]
    ILb = [mk("IL0", [64, 4, 64], BF16), mk("IL1", [64, 4, 64], BF16)]
    PTb = [mk("PT0", [64, 4, 64], BF16), mk("PT1", [64, 4, 64], BF16)]
    AakT = mk("AakT", [64, 4, 64], BF16)
    ArbT = mk("ArbT", [64, 4, 64], BF16)
    ArkT = mk("ArkT", [64, 4, 64], BF16)
    WZ = mk("WZ", [64, 4, 128], BF16)
    GT = mk("GT", [64, 4, 64], BF16)
    Dg = mk("Dg", [64, 4, 64], F32)
    Nsb = mk("Nsb", [64, 4, 64], F32)
    QeT = mk("QeT", [64, 4, 64], BF16)
    Ol = mk("Ol", [64, 4, 64], F32)
    osb = mk("osb", [64, 4, 64], F32)
    sqs = mk("sqs", [64, 4, 64], F32)
    bvt = mk("bvt", [64, 4, 64], F32)
    stt_ = mk("stt", [64, 8, 4], F32)
    yb = mk("yb", [64, 256], BF16)
    NBATCH = S_LEN // 256
    import os
    VAR = os.environ.get("RWB_VAR", "Z")
    for bt in range(NBATCH):
        if VAR == "A":
            break
        t0 = bt * 256
        fb = feat[bt % 2]
        vb = vch[bt % 2]
        for h in range(4):
            S.dma(fb[:, h, :, :], scr["rwT_d"][h].rearrange("q k t -> k q t")[:, :, t0:t0 + 256])
        S.dma(vb[:], scr["rw_v_d"][t0:t0 + 256, :].rearrange("(c p) f -> p c f", p=64))
        S.dma(bonb[bt % 2][:], scr["rw_b_d"][t0:t0 + 256, :].rearrange("(c p) f -> p c f", p=64))
        S.dma(gtk[bt % 2][:], scr["rw_g_d"][t0:t0 + 256, :].rearrange("(c p) f -> p c f", p=64))

        def F(h, q, c):
            return fb[:, h, q, c * 64:(c + 1) * 64]

        def V(h, c):
            return vb[:, c, h * 64:(h + 1) * 64]
        if VAR == "B":
            continue
        for c in range(NB):
            for h in range(4):
                for j, q in enumerate((0, 4, 5)):
                    if VAR == "D":
                        continue
                    S.tr(ptr[:, h, j, :], F(h, q, c), ident[0:64, 0:64])
            if VAR != "E":
                S.copy(WZin[c][:, :, 0:64], ptr[:, :, 0, :], eng="dve")
            if VAR != "F":
                S.copy(tokM[c][:], ptr[:, :, 1:3, :], eng="dve")
        import os
        STOP = int(os.environ.get("RWB_STOP", "9"))
        if STOP < 1:
            continue
        for c in range(NB):
            p1, p2, p3 = nextp(), nextp(), nextp()
            for h in range(4):
                S.mm(v4(p1, 0)[:, h, :], F(h, 0, c), F(h, 1, c), start=True, stop=True)
                S.mm(v4(p1, 1)[:, h, :], F(h, 1, c), F(h, 0, c), start=True, stop=True)
                S.mm(v4(p2, 0)[:, h, :], F(h, 2, c), F(h, 0, c), start=True, stop=True)
                S.mm(v4(p2, 1)[:, h, :], F(h, 1, c), F(h, 3, c), start=True, stop=True)
                S.mm(v4(p3, 0)[:, h, :], F(h, 2, c), F(h, 3, c), start=True, stop=True)
            S.tt(Lb[0][c][:], v4(p1, 0), mlo[:], ALU.mult)
            S.tt(LTb[0][c][:], v4(p1, 1), mup[:], ALU.mult)
            S.tt(PTb[0][c][:], LTb[0][c][:], I4[:], ALU.add, eng="pool")
            S.tt(AakT[c][:], v4(p2, 0), mup[:], ALU.mult)
            S.tt(ArbT[c][:], v4(p2, 1), mupi[:], ALU.mult)
            S.tt(ArkT[c][:], v4(p3, 0), mupi[:], ALU.mult)
        if STOP < 2:
            continue
        for c in range(NB):
            p1 = nextp()
            for h in range(4):
                S.mm(v4(p1, 0)[:, h, :], AakT[c][:, h, :], V(h, c), start=True, stop=True)
            S.copy(WZin[c][:, :, 64:128], v4(p1, 0), eng="dve")
        if STOP < 3:
            continue
        for i in range(1, 7):
            cur, prv = i % 2, (i - 1) % 2
            for c in range(NB):
                p1 = nextp()
                p2 = nextp() if i >= 2 else None
                for h in range(4):
                    if i <= 5:
                        S.mm(v4(p1, 0)[:, h, :], LTb[prv][c][:, h, :], Lb[prv][c][:, h, :], start=True, stop=True)
                    if i <= 4:
                        S.mm(v4(p1, 1)[:, h, :], Lb[prv][c][:, h, :], LTb[prv][c][:, h, :], start=True, stop=True)
                    if i >= 2:
                        S.mm(v4(p2, 0)[:, h, :], ILb[prv][c][:, h, :], PTb[i % 2][c][:, h, :], start=True, stop=True)
                if i <= 5:
                    S.copy(Lb[cur][c][:], v4(p1, 0), eng="dve")
                    S.tt(ILb[cur][c][:], v4(p1, 0), I4[:], ALU.add)
                if i <= 4:
                    S.copy(LTb[cur][c][:], v4(p1, 1), eng="dve")
                if i >= 2:
                    S.copy(PTb[(i - 1) % 2][c][:], v4(p2, 0), eng="dve")
        TT = PTb[1]
        if STOP < 4:
            continue
        for c in range(NB):
            p1 = nextp()
            pw = p1.rearrange("p (h s) -> p h s", h=4)
            for h in range(4):
                S.mm(pw[:, h, :], TT[c][:, h, :], WZin[c][:, h, :], start=True, stop=True)
            S.copy(WZ[c][:], pw, eng="dve")
        if STOP < 5:
            continue
        for c in range(NB):
            n = bt * NB + c
            p1, p2 = nextp(), nextp()
            for h in range(4):
                S.mm(v4(p1, 0)[:, h, :], WZ[c][:, h, 0:64], tokM[c][:, h, 0, :], start=True, stop=True)
                S.mm(v4(p1, 1)[:, h, :], tokM[c][:, h, 0, :], WZ[c][:, h, 64:128], start=True, stop=False)
                S.mm(v4(p1, 1)[:, h, :], tokM[c][:, h, 1, :], V(h, c), start=False, stop=True)
            for h in range(4):
                S.mm(v4(p2, 0)[:, h, :], WZ[c][:, h, 0:64], ArbT[c][:, h, :], start=True, stop=True)
                S.mm(v4(p2, 1)[:, h, :], ArbT[c][:, h, :], WZ[c][:, h, 64:128], start=True, stop=False)
                S.mm(v4(p2, 1)[:, h, :], ArkT[c][:, h, :], V(h, c), start=False, stop=True)
            S.tt(Dg[c][:], I4[:], gC[:, :, n:n + 1].to_broadcast([64, 4, 64]), ALU.mult, eng="pool")
            S.tt(GT[c][:], v4(p1, 0), Dg[c][:], ALU.add)
            S.copy(Nsb[c][:], v4(p1, 1), eng="dve")
            S.tt(QeT[c][:], v4(p2, 0), fb[:, :, 3, c * 64:(c + 1) * 64], ALU.add)
            S.copy(Ol[c][:], v4(p2, 1), eng="dve")
        if STOP < 6:
            continue
        for c in range(NB):
            p1 = nextp()
            for h in range(4):
                S.mm(v4(p1, 0)[:, h, :], QeT[c][:, h, :], Mbf[:, h, :], start=True, stop=True)
            for h in range(4):
                S.mm(v4(p1, 1)[:, h, :], GT[c][:, h, :], Mbf[:, h, :], start=True, stop=True)
            S.tt(M[:], v4(p1, 1), Nsb[c][:], ALU.add)
            S.copy(Mbf[:], M[:], eng="act")
            o = osb[c]
            s_ = stt_[c]
            S.tt(o[:], v4(p1, 0), Ol[c][:], ALU.add)
            S.reduce(s_[:, 0, :], o[:], ALU.add)
            S.tt(sqs[c][:], o[:], o[:], ALU.mult, eng="pool")
            S.reduce(s_[:, 1, :], sqs[c][:], ALU.add)
            S.ts(s_[:, 2, :], s_[:, 0, :], 1.0 / 64, None, ALU.mult)
            S.tt(s_[:, 3, :], s_[:, 2, :], s_[:, 2, :], ALU.mult)
            S.stt(s_[:, 4, :], s_[:, 1, :], 1.0 / 64, s_[:, 3, :], ALU.mult, ALU.subtract)
            S.ts(s_[:, 5, :], s_[:, 4, :], 64e-5, None, ALU.add)
            S.act(s_[:, 6, :], s_[:, 5, :], AF.Sqrt)
            S.recip(s_[:, 7, :], s_[:, 6, :])
            S.tt(o[:], o[:], s_[:, 2, :].unsqueeze(2).to_broadcast([64, 4, 64]), ALU.subtract)
            S.tt(o[:], o[:], s_[:, 7, :].unsqueeze(2).to_broadcast([64, 4, 64]), ALU.mult)
            of = o[:].rearrange("p h e -> p (h e)")
            S.tt(of, of, lng[:], ALU.mult, eng="pool")
            S.tt(of, of, lnb[:], ALU.add, eng="pool")
            S.tt(bvt[c][:], vb[:, c, :].rearrange("p (h e) -> p h e", h=4),
                 bonb[bt % 2][:, c, :].unsqueeze(2).to_broadcast([64, 4, 64]), ALU.mult)
            S.tt(o[:], o[:], bvt[c][:], ALU.add)
            S.tt(yb[c][:], of, gtk[bt % 2][:, c, :], ALU.mult)
            S.dma(y_d[t0 + c * 64:t0 + (c + 1) * 64, :], yb[c][:], q="act")


def phase_merge(S, nc, xres, w, l, hT_d, wgate_d, cst, scr):
    Wg = S.sb("Wg", [128, 8, 4096], BF16)
    load_w_bf16(S, Wg, wgate_d[l], 8)
    Wbr = S.sb("Wbr", [128, 10, D], BF16)
    off = 0
    for nm, nch in (("w_br_nsa", 2), ("w_br_ret", 4), ("w_br_rwkv", 2), ("w_br_swa", 2)):
        v = w[nm][l].rearrange("(c p) f -> p c f", p=128)
        for c in range(nch):
            S.dma(Wbr[:, off + c, :], v[:, c, :], q="pool")
        off += nch
    Wo = S.sb("Wo", [128, 8, D], BF16)
    load_w_bf16(S, Wo, w["w_out"][l], 8)
    gpost = S.sb("gpost", [128, D], F32)
    bcast_row(S, gpost[:], w["mix_post_g"][l:l + 1, :])
    ident = S.sb("ident", [128, 128], BF16)
    S.dma(ident[:], cst["ident"][:, :], q="pool")
    ycat = [S.sb("ycat", [128, 1280], BF16) for _ in range(2)]
    yT = [S.sb("yT", [128, 10, 128], BF16) for _ in range(2)]
    hTt = [S.sb("hTt", [128, 8, 128], BF16) for _ in range(2)]
    xb = [S.sb("xb", [128, D], F32) for _ in range(2)]
    sg = [S.sb("sg", [128, 512], F32) for _ in range(2)]
    tmpb = [S.sb("tmpb", [128, 512], F32) for _ in range(2)]
    merged = [S.sb("merged", [128, D], F32) for _ in range(2)]
    mbf = [S.sb("mbf", [128, D], BF16) for _ in range(2)]
    mT = [S.sb("mT", [128, 8, 128], BF16) for _ in range(2)]
    fsb = [S.sb("fsb", [128, D], F32) for _ in range(2)]
    junk = S.sb("junk", [128, D], BF16)
    st = [S.sb("st", [128, 8], F32) for _ in range(2)]
    ptrA = S.ps("ptrA", [128, 5, 128], BF16)
    ptrB = S.ps("ptrB", [128, 5, 128], BF16)
    ptm = S.ps("ptm", [128, 8, 128], BF16)
    pg = [S.ps("pg", [128, 512], F32) for _ in range(2)]
    po = [S.ps("po", [128, 512], F32) for _ in range(2)]
    hv = hT_d.rearrange("(c p) t -> p c t", p=128)
    brch = ((0, 2), (2, 4), (6, 2), (8, 2))
    cnt = 0
    for t in range(NT):
        b2 = t % 2
        rows = slice(t * 128, (t + 1) * 128)
        yc = ycat[b2]
        S.dma(yc[:, 0:256], scr["y_nsa"][rows, :])
        S.dma(yc[:, 256:768], scr["y_ret"][rows, :])
        S.dma(yc[:, 768:1024], scr["y_rwkv"][rows, :])
        S.dma(yc[:, 1024:1280], scr["y_swa"][rows, :])
        S.dma(hTt[b2][:], (hv[:, :, rows], ("hT", t)))
        S.dma(xb[b2][:], (xres[rows, :], ("xres", t)))
        for fc in range(10):
            pt_ = ptrA if fc < 5 else ptrB
            S.tr(pt_[:, fc % 5, :], yc[:, fc * 128:(fc + 1) * 128], ident[:])
        S.copy(yT[b2][:, 0:5, :], ptrA[:], eng="act")
        S.copy(yT[b2][:, 5:10, :], ptrB[:], eng="dve")
        mg = merged[b2]
        for br in range(4):
            f0, nf = brch[br]
            for half in range(2):
                pgt = pg[cnt % 2]
                pot = po[cnt % 2]
                sgt = sg[cnt % 2]
                tb_ = tmpb[cnt % 2]
                cnt += 1
                c0 = br * 1024 + half * 512
                for dc in range(8):
                    S.mm(pgt[:, :], hTt[b2][:, dc, :], Wg[:, dc, c0:c0 + 512], start=(dc == 0), stop=(dc == 7))
                for k in range(nf):
                    S.mm(pot[:, :], yT[b2][:, f0 + k, :], Wbr[:, f0 + k, half * 512:(half + 1) * 512],
                         start=(k == 0), stop=(k == nf - 1))
                S.act(sgt[:], pgt[:, :], AF.Sigmoid)
                mslice = mg[:, half * 512:(half + 1) * 512]
                if br == 0:
                    S.tt(mslice, sgt[:], pot[:, :], ALU.mult)
                else:
                    S.tt(tb_[:], sgt[:], pot[:, :], ALU.mult)
                    S.tt(mslice, mslice, tb_[:], ALU.add, eng="pool")
        S.copy(mbf[b2][:], mg[:], eng="act")
        for dc in range(8):
            S.tr(ptm[:, dc, :], mbf[b2][:, dc * 128:(dc + 1) * 128], ident[:])
        S.copy(mT[b2][:], ptm[:], eng="act")
        f = fsb[b2]
        s_ = st[b2]
        for half in range(2):
            pgt = pg[half]
            for dc in range(8):
                S.mm(pgt[:, :], mT[b2][:, dc, :], Wo[:, dc, half * 512:(half + 1) * 512],
                     start=(dc == 0), stop=(dc == 7))
            S.act(f[:, half * 512:(half + 1) * 512], pgt[:, :], AF.Copy)
            S.act(junk[:, half * 512:(half + 1) * 512], pgt[:, :], AF.Square, accum_out=s_[:, half:half + 1])
        S.tt(s_[:, 2:3], s_[:, 0:1], s_[:, 1:2], ALU.add)
        S.ts(s_[:, 3:4], s_[:, 2:3], 1.0 / D, RMS_EPS, ALU.mult, ALU.add)
        S.act(s_[:, 4:5], s_[:, 3:4], AF.Sqrt)
        S.recip(s_[:, 5:6], s_[:, 4:5])
        S.stt(f[:], f[:], s_[:, 5:6], gpost[:], ALU.mult, ALU.mult)
        S.tt(xb[b2][:], xb[b2][:], f[:], ALU.add, eng="pool")
        S.dma((xres[rows, :], ("xres", t)), xb[b2][:], q="act")


WNAMES = ['ffn1_pre_g', 'ffn1_post_g', 'ffn1_w_gate', 'ffn1_w_up', 'ffn1_w_down', 'mix_pre_g', 'mix_post_g',
          'w_in', 'nsa_cmp_pos_k', 'nsa_cmp_pos_v', 'nsa_cmp_k_w1', 'nsa_cmp_k_w2', 'nsa_cmp_v_w1',
          'nsa_cmp_v_w2', 'ret_gn_g', 'rwkv_mu', 'rwkv_w0', 'rwkv_w2', 'rwkv_a0', 'rwkv_a2', 'rwkv_g2',
          'rwkv_k_k', 'rwkv_k_a', 'rwkv_r_k', 'rwkv_ln_g', 'rwkv_ln_b', 'swa_sinks', 'w_br_nsa', 'w_br_ret',
          'w_br_rwkv', 'w_br_swa', 'w_out', 'ffn2_pre_g', 'ffn2_post_g', 'ffn2_w_gate', 'ffn2_w_up',
          'ffn2_w_down']

_CONSTS = None


def band_masks(window, rels):
    p = np.arange(128)[:, None]
    ql = np.arange(512)[None, :]
    out = []
    for r in rels:
        d = ql - (r * 128 + p)
        out.append(((d >= 0) & (d < window)).astype(np.float32))
    return np.stack(out, 0)


def host_consts():
    global _CONSTS
    if _CONSTS is not None:
        return _CONSTS
    c = {}
    c["ident"] = np.eye(128, dtype=np.float32)
    pos = np.arange(S_LEN, dtype=np.float32)
    inv = np.power(np.float32(10000.0), -np.arange(32, dtype=np.float32) * 2.0 / 64).astype(np.float32)
    ang = pos[None, :] * inv[:, None]
    cos = np.cos(ang).astype(np.float32)
    sin = np.sin(ang).astype(np.float32)
    c["cosT"] = np.ascontiguousarray(np.concatenate([cos, cos, cos, cos], 0))
    c["sinT"] = np.ascontiguousarray(np.concatenate([-sin, sin, -sin, sin], 0))
    c["mask_swa"] = band_masks(128, range(-1, 4))
    ii = np.arange(256)
    qq = np.arange(S_LEN)
    c["cmpmaskT"] = (((16 * ii[:, None] + 31) <= qq[None, :]) & (ii[:, None] < 255)).astype(np.float32)
    jj = np.arange(64)
    c["overlap"] = (((16 * ii[:, None]) <= (64 * jj[None, :] + 63)) & ((16 * ii[:, None] + 31) >= 64 * jj[None, :])
                    & (ii[:, None] < 255)).astype(np.float32)
    cur = (qq // 64)[:, None]
    forced = (jj[None, :] == 0) | (jj[None, :] == cur) | (jj[None, :] == cur - 1)
    valid = jj[None, :] <= cur
    c["selbias"] = np.where(forced, 1e9, np.where(valid, 0.0, -1e9)).astype(np.float32)
    c["mask_win"] = band_masks(512, range(-4, 4))
    c["mask_causal"] = band_masks(10 ** 7, range(0, 4))
    kt_ = np.arange(NT)[None, :, None]
    pp = np.arange(128)[None, None, :]
    c["Eall"] = (jj[:, None, None] == (2 * kt_ + pp // 64)).astype(np.float32)
    c["rw_reset"] = np.ascontiguousarray(np.broadcast_to((np.arange(512) % 64 != 0).astype(np.float32)[None, :], (64, 512)))
    tt_ = np.arange(64)[:, None, None]
    ss_ = np.arange(64)[None, None, :]
    one4 = np.ones((1, 4, 1), dtype=np.float32)
    c["rw_mlo"] = np.ascontiguousarray((ss_ < tt_).astype(np.float32) * one4)
    c["rw_mup"] = np.ascontiguousarray((ss_ > tt_).astype(np.float32) * one4)
    c["rw_mupi"] = np.ascontiguousarray((ss_ >= tt_).astype(np.float32) * one4)
    c["rw_I4"] = np.ascontiguousarray((ss_ == tt_).astype(np.float32) * one4)
    gam = (1.0 - np.power(2.0, -5.0 - np.arange(4, dtype=np.float64)))
    m = np.arange(128)[:, None, None]
    cc = np.arange(128)[None, None, :]
    gg = gam[None, :, None]
    dm = np.where(cc >= m, np.power(gg, np.maximum(cc - m, 0)), 0.0) * 0.125
    c["ret_dmaskT"] = np.ascontiguousarray(dm.astype(np.float32))
    c["ret_zeta"] = np.ascontiguousarray((np.power(gam[None, :], 127 - np.arange(128)[:, None]) * 0.125).astype(np.float32))
    xi = np.power(gam[None, :, None], np.arange(128)[None, None, :] + 1.0)
    c["ret_xiT"] = np.ascontiguousarray(np.broadcast_to(xi, (64, 4, 128)).astype(np.float32))
    c["ret_gch"] = np.ascontiguousarray(np.broadcast_to(np.power(gam, 128.0)[None, :], (64, 4)).astype(np.float32))
    _CONSTS = c
    return c


def derived_weights(inputs):
    idx = w_in_index_sets()
    out = {}
    w_in = np.asarray(inputs["w_in"], dtype=np.float32)
    for n, ix in idx.items():
        out["w" + n] = np.ascontiguousarray(w_in[:, :, ix])
    pk = np.asarray(inputs["nsa_cmp_pos_k"], dtype=np.float32)
    pv = np.asarray(inputs["nsa_cmp_pos_v"], dtype=np.float32)
    t64 = lambda n: np.asarray(inputs[n], dtype=np.float32).reshape(DEPTH, 4, 64).transpose(0, 2, 1)
    out["rwkv_cols"] = np.ascontiguousarray(np.concatenate(
        [t64("rwkv_w0"), t64("rwkv_a0"), t64("rwkv_k_k"), t64("rwkv_k_a"), t64("rwkv_r_k")], axis=2))
    out["nsa_posT"] = np.ascontiguousarray(np.stack([pk.transpose(0, 2, 1), pv.transpose(0, 2, 1)], axis=1))
    return out


SCRATCH = {
    "hT_d": ([D, S_LEN], BF16),
    "y_swa": ([S_LEN, 256], BF16),
    "y_ret": ([S_LEN, 512], BF16),
    "y_nsa": ([S_LEN, 256], BF16),
    "y_rwkv": ([S_LEN, 256], BF16),
    "rwT_d": ([4, 6, 64, S_LEN], BF16),
    "rw_gC_d": ([64, 4, 64], F32),
    "rw_v_d": ([S_LEN, 256], BF16),
    "rw_g_d": ([S_LEN, 256], F32),
    "rw_b_d": ([S_LEN, 4], F32),
    "nsaT_d": ([12, 64, S_LEN], BF16),
    "nsa_v_d": ([S_LEN, 128], BF16),
    "nsa_g_d": ([S_LEN, 12], F32),
    "ocmp_d": ([S_LEN, 256], BF16),
    "selT_d": ([64, S_LEN], BF16),
}


def default_phases():
    pl = [("copyin", None)]
    for l in range(DEPTH):
        pl += [("ffn1", l), ("mixpre", l), ("swa", l), ("ret", l), ("nsa_a", l), ("nsa_b", l), ("nsa_c", l),
               ("rwkv_a", l), ("rwkv_b", l), ("merge", l), ("ffn2", l)]
    return pl


def build(shapes, phases=None, dbg=()):
    nc = bass.Bass("TRN2", target_bir_lowering=False)
    x_in = nc.dram_tensor("x", [S_LEN, D], F32, kind="ExternalInput").ap()
    w = {}
    for n in shapes:
        w[n] = nc.dram_tensor(n, list(shapes[n]), F32, kind="ExternalInput").ap()
    cst = {}
    for n, a in host_consts().items():
        cst[n] = nc.dram_tensor("c_" + n, list(a.shape), F32, kind="ExternalInput").ap()
    y = nc.dram_tensor("y", [S_LEN, D], F32, kind="ExternalOutput").ap()
    scr = {}
    for n, (shp, dt_) in SCRATCH.items():
        if n in dbg:
            scr[n] = nc.dram_tensor(n, shp, dt_, kind="ExternalOutput").ap()
        else:
            scr[n] = nc.dram_tensor(n, shp, dt_).ap()
    xres = y
    plist = default_phases() if phases is None else phases
    with ExitStack() as gst:
        S = Sched(nc, gst)
        for pi, (pn, l) in enumerate(plist):
            with ExitStack() as pst:
                S.stack = pst
                if pn == "copyin":
                    for t in range(0, NT, 4):
                        S.dma((xres[t * 128:(t + 4) * 128, :], ("xres", t), ("xres", t + 1), ("xres", t + 2),
                               ("xres", t + 3)), x_in[t * 128:(t + 4) * 128, :], q="sp")
                elif pn in ("ffn1", "ffn2"):
                    phase_ffn(S, nc, xres, w, l, pn, cst["ident"])
                elif pn == "mixpre":
                    phase_mixpre(S, nc, xres, w, l, scr["hT_d"], cst["ident"])
                elif pn == "swa":
                    phase_swa(S, nc, w, l, scr["hT_d"], w["wswa"], cst, scr["y_swa"])
                elif pn == "nsa_a":
                    phase_nsa_a(S, nc, w, l, scr["hT_d"], w["wnsa"], cst, scr)
                elif pn == "nsa_b":
                    phase_nsa_b(S, nc, w, l, cst, scr)
                elif pn == "nsa_c":
                    phase_nsa_c(S, nc, w, l, cst, scr, scr["y_nsa"])
                elif pn == "rwkv_a":
                    phase_rwkv_a(S, nc, w, l, scr["hT_d"], w["wr"], cst, scr)
                elif pn == "rwkv_b":
                    phase_rwkv_b(S, nc, w, l, cst, scr, scr["y_rwkv"])
                elif pn == "merge":
                    phase_merge(S, nc, xres, w, l, scr["hT_d"], w["wgate"], cst, scr)
                elif pn == "ret":
                    phase_ret(S, nc, w, l, scr["hT_d"], w["wret"], cst, scr["y_ret"], cst["ident"])
                else:
                    raise ValueError(pn)
                S.barrier()
                S.emit(final=(pi == len(plist) - 1))
    return nc


def make_in_maps(inputs, cores):
    base = {k: np.ascontiguousarray(inputs[k], dtype=np.float32) for k in WNAMES}
    base.update(derived_weights(inputs))
    shapes = {k: v.shape for k, v in base.items()}
    for k, a in host_consts().items():
        base["c_" + k] = a
    x = np.asarray(inputs["x"], dtype=np.float32)
    in_maps = []
    for i in cores:
        m = dict(base)
        m["x"] = np.ascontiguousarray(x[i])
        in_maps.append(m)
    return shapes, in_maps


def kernel(**inputs):
    n = 8
    shapes, in_maps = make_in_maps(inputs, list(range(n)))
    nc = build(shapes)
    res = run_bass_kernel_spmd(nc, in_maps, core_ids=list(range(n)))
    return np.stack([r["y"] for r in res.results], axis=0)
```

```python
import numpy as np
from contextlib import ExitStack
import concourse.bass as bass
import concourse.mybir as mybir
from concourse.bass_utils import run_bass_kernel_spmd

F32 = mybir.dt.float32
BF16 = mybir.dt.bfloat16
AF = mybir.ActivationFunctionType
ALU = mybir.AluOpType
AX = mybir.AxisListType

S_LEN = 4096
D = 1024
DFF = 2816
NT = S_LEN // 128
DEPTH = 2
RMS_EPS = 1e-6

ENGS = ("pe", "act", "dve", "pool", "sp")
DMA_K = 8


def _kref(x):
    if isinstance(x, tuple):
        return x[0], tuple(x[1:])
    return x, (x.tensor.name,)


class Sched:
    def __init__(self, nc, stack):
        self.nc = nc
        self.gstack = stack
        self.esem = {e: stack.enter_context(nc.semaphore("es_" + e)) for e in ENGS if e != "sp"}
        self.dsem = {q: [stack.enter_context(nc.semaphore("ds_%s%d" % (q, i))) for i in range(DMA_K)]
                     for q in ("sp", "act", "pool")}
        self.ccount = {e: 0 for e in ENGS}
        self.qcount = {q: 0 for q in ("sp", "act", "pool")}
        self.wm = {e: {} for e in ENGS}
        self.pending = {e: [] for e in ENGS}
        self.keys = {}
        self.post_barrier = {e: set() for e in ENGS}
        self.stack = None
        self.uid = 0
        self.phase = 0

    def sb(self, name, shape, dtype):
        self.uid += 1
        return self.stack.enter_context(self.nc.sbuf_tensor("%s_ph%d_%d" % (name, self.phase, self.uid), list(shape), dtype))

    def ps(self, name, shape, dtype=F32):
        self.uid += 1
        return self.stack.enter_context(self.nc.psum_tensor("%s_ph%d_%d" % (name, self.phase, self.uid), list(shape), dtype))

    def _st(self, key):
        st = self.keys.get(key)
        if st is None:
            st = {"W": {}, "R": {}, "Wd": {}, "Rd": {}}
            self.keys[key] = st
        return st

    def op(self, eng, fn, reads=(), writes=(), dma=False):
        deps = set()
        rkeys, wkeys = [], []
        for r in reads:
            if r is None:
                continue
            _, ks = _kref(r)
            rkeys.extend(ks)
        for w in writes:
            if w is None:
                continue
            _, ks = _kref(w)
            wkeys.extend(ks)
        for k in rkeys:
            st = self._st(k)
            for e, c in st["W"].items():
                deps.add((e, c))
            for q, js in st["Wd"].items():
                for j in js:
                    deps.add(("dma", q, j))
        for k in wkeys:
            st = self._st(k)
            for e, c in st["W"].items():
                deps.add((e, c))
            for e, c in st["R"].items():
                if e == eng and not dma:
                    continue
                deps.add((e, c))
            for q, js in st["Wd"].items():
                for j in js:
                    deps.add(("dma", q, j))
            for q, js in st["Rd"].items():
                for j in js:
                    deps.add(("dma", q, j))
        if eng == "pe":
            deps = {d for d in deps if d[0] != "pe"}
        deps |= self.post_barrier[eng]
        self.post_barrier[eng] = set()
        if dma:
            j = self.qcount[eng]
            self.qcount[eng] += 1
            rec = ("dma", eng, j)
            for k in rkeys:
                l = self._st(k)["Rd"].setdefault(eng, [])
                l.append(j)
                if len(l) > DMA_K:
                    del l[0]
            for k in wkeys:
                l = self._st(k)["Wd"].setdefault(eng, [])
                l.append(j)
                if len(l) > DMA_K:
                    del l[0]
            self.pending[eng].append((fn, deps, True, j))
        else:
            self.ccount[eng] += 1
            c = self.ccount[eng]
            for k in rkeys:
                self._st(k)["R"][eng] = c
            for k in wkeys:
                self._st(k)["W"][eng] = c
            self.pending[eng].append((fn, deps, False, c))

    def barrier(self):
        allc = set()
        for e in ENGS:
            if e != "sp" and self.ccount[e] > 0:
                allc.add((e, self.ccount[e]))
        for q in ("sp", "act", "pool"):
            n = self.qcount[q]
            for j in range(max(0, n - DMA_K), n):
                allc.add(("dma", q, j))
        for e in ENGS:
            self.post_barrier[e] |= allc
        self.keys = {}

    def _emit_engine(self, ename, eng, final=False):
        wm = self.wm[ename]
        for fn, deps, is_dma, idx in self.pending[ename]:
            waits = {}
            dmax = {}
            for d in deps:
                if d[0] == "dma":
                    dmax[d[1]] = max(dmax.get(d[1], -1), d[2])
            for d in deps:
                if d[0] == "dma":
                    q, j = d[1], d[2]
                    if j <= dmax[q] - DMA_K:
                        continue
                    sem = self.dsem[q][j % DMA_K]
                    val = 16 * (j // DMA_K + 1)
                else:
                    sem = self.esem[d[0]]
                    val = d[1]
                key = id(sem)
                if key not in waits or waits[key][1] < val:
                    waits[key] = (sem, val)
            if is_dma and idx >= DMA_K:
                sem = self.dsem[ename][idx % DMA_K]
                val = 16 * (idx // DMA_K)
                key = id(sem)
                if key not in waits or waits[key][1] < val:
                    waits[key] = (sem, val)
            for key, (sem, val) in waits.items():
                if wm.get(key, 0) < val:
                    eng.wait_ge(sem, val)
                    wm[key] = val
            ins = fn(eng)
            if is_dma:
                ins.then_inc(self.dsem[ename][idx % DMA_K], 16)
            else:
                ins.then_inc(self.esem[ename], 1)
        self.pending[ename] = []
        if final and ename in self.dsem:
            n = self.qcount[ename]
            for j in range(max(0, n - DMA_K), n):
                sem = self.dsem[ename][j % DMA_K]
                val = 16 * (j // DMA_K + 1)
                if wm.get(id(sem), 0) < val:
                    eng.wait_ge(sem, val)
                    wm[id(sem)] = val

    def emit(self, final=False):
        with self.nc.Block() as block:
            @block.tensor
            def _(e):
                self._emit_engine("pe", e, final)

            @block.scalar
            def _(e):
                self._emit_engine("act", e, final)

            @block.vector
            def _(e):
                self._emit_engine("dve", e, final)

            @block.gpsimd
            def _(e):
                self._emit_engine("pool", e, final)

            @block.sync
            def _(e):
                self._emit_engine("sp", e, final)

    def dma(self, out, in_, q="sp"):
        o, i = _kref(out)[0], _kref(in_)[0]
        self.op(q, lambda e: e.dma_start(out=o, in_=i), reads=[in_], writes=[out], dma=True)

    def mm(self, out, lhsT, rhs, start=True, stop=True):
        o, l, r = _kref(out)[0], _kref(lhsT)[0], _kref(rhs)[0]
        self.op("pe", lambda e: e.matmul(o, l, r, start=start, stop=stop), reads=[lhsT, rhs], writes=[out])

    def tr(self, out, in_, ident):
        o, i, d = _kref(out)[0], _kref(in_)[0], _kref(ident)[0]
        self.op("pe", lambda e: e.transpose(o, i, d), reads=[in_, ident], writes=[out])

    def act(self, out, in_, func, bias=None, scale=None, accum_out=None):
        o, i = _kref(out)[0], _kref(in_)[0]
        kw = {}
        rd = [in_]
        wr = [out]
        if bias is not None:
            if isinstance(bias, (int, float)):
                kw["bias"] = bias
            else:
                kw["bias"] = _kref(bias)[0]
                rd.append(bias)
        if scale is not None:
            if isinstance(scale, (int, float)):
                kw["scale"] = scale
            else:
                kw["scale"] = _kref(scale)[0]
                rd.append(scale)
        if accum_out is not None:
            kw["accum_out"] = _kref(accum_out)[0]
            wr.append(accum_out)
        self.op("act", lambda e: e.activation(o, i, func, **kw), reads=rd, writes=wr)

    def tt(self, out, in0, in1, op, eng="dve"):
        o, a, b = _kref(out)[0], _kref(in0)[0], _kref(in1)[0]
        self.op(eng, lambda e: e.tensor_tensor(o, a, b, op), reads=[in0, in1], writes=[out])

    def ts(self, out, in0, s1, s2, op0, op1=None, eng="dve", accum_out=None):
        o, a = _kref(out)[0], _kref(in0)[0]
        rd = [in0]
        wr = [out]

        def sc(s):
            if s is None or isinstance(s, (int, float)):
                return s
            rd.append(s)
            return _kref(s)[0]
        v1, v2 = sc(s1), sc(s2)
        kw = {}
        if op1 is not None:
            kw["op1"] = op1
        if accum_out is not None:
            kw["accum_out"] = _kref(accum_out)[0]
            wr.append(accum_out)
        self.op(eng, lambda e: e.tensor_scalar(o, a, v1, v2, op0, **kw), reads=rd, writes=wr)

    def stt(self, out, in0, scalar, in1, op0, op1, accum_out=None):
        o, a, b = _kref(out)[0], _kref(in0)[0], _kref(in1)[0]
        rd = [in0, in1]
        wr = [out]
        if isinstance(scalar, (int, float)):
            s = scalar
        else:
            s = _kref(scalar)[0]
            rd.append(scalar)
        kw = {}
        if accum_out is not None:
            kw["accum_out"] = _kref(accum_out)[0]
            wr.append(accum_out)
        self.op("dve", lambda e: e.scalar_tensor_tensor(o, a, s, b, op0, op1, **kw), reads=rd, writes=wr)

    def copy(self, out, in_, eng="dve"):
        o, i = _kref(out)[0], _kref(in_)[0]
        if eng == "act":
            self.op("act", lambda e: e.copy(o, i), reads=[in_], writes=[out])
        else:
            self.op(eng, lambda e: e.tensor_copy(o, i), reads=[in_], writes=[out])

    def recip(self, out, in_):
        o, i = _kref(out)[0], _kref(in_)[0]
        self.op("dve", lambda e: e.reciprocal(o, i), reads=[in_], writes=[out])

    def memset(self, out, val, eng="dve"):
        o = _kref(out)[0]
        self.op(eng, lambda e: e.memset(o, val), reads=[], writes=[out])

    def reduce(self, out, in_, op, axis=None, eng="dve"):
        o, i = _kref(out)[0], _kref(in_)[0]
        ax = AX.X if axis is None else axis
        self.op(eng, lambda e: e.tensor_reduce(o, i, ax, op), reads=[in_], writes=[out])


def load_w_bf16(S, dst, src_dram, nchunk, q="pool"):
    v = src_dram.rearrange("(c p) f -> p c f", p=128)
    for c in range(nchunk):
        S.dma(dst[:, c, :], v[:, c, :], q=q)


def bcast_row(S, dst, src_row_ap, q="sp"):
    S.dma(dst, src_row_ap.partition_broadcast(128), q=q)


def phase_ffn(S, nc, xres, w, l, pre, ident_d):
    TB = 256
    NSUB = TB // 128
    NFC = DFF // 128
    GRP = (6, 6, 5, 5)
    G0 = (0, 6, 12, 17)
    wg = [S.sb("wg", [128, 8, n * 128], BF16) for n in GRP]
    wu = [S.sb("wu", [128, 8, n * 128], BF16) for n in GRP]
    wd = [S.sb("wd", [128, 11, D], BF16) for _ in range(2)]
    fcmap = []
    for g, n in enumerate(GRP):
        for k in range(n):
            fcmap.append((g, k))
    gpre = S.sb("gpre", [128, D], F32)
    gpost = S.sb("gpost", [128, D], F32)
    ident = S.sb("ident", [128, 128], BF16)
    S.dma(ident[:], ident_d[:, :], q="pool")
    bcast_row(S, gpre[:], w[pre + "_pre_g"][l:l + 1, :])
    bcast_row(S, gpost[:], w[pre + "_post_g"][l:l + 1, :])
    S.ts(gpost[:], gpost[:], 0.5, None, ALU.mult, eng="pool")
    vg = w[pre + "_w_gate"][l].rearrange("(c p) f -> p c f", p=128)
    vu = w[pre + "_w_up"][l].rearrange("(c p) f -> p c f", p=128)
    vd = w[pre + "_w_down"][l].rearrange("(c p) f -> p c f", p=128)
    for g, n in enumerate(GRP):
        c0 = G0[g] * 128
        for c in range(8):
            S.dma(wg[g][:, c, :], vg[:, c, c0:c0 + n * 128], q="pool")
        for c in range(8):
            S.dma(wu[g][:, c, :], vu[:, c, c0:c0 + n * 128], q="pool")
    for c in range(NFC):
        S.dma(wd[c // 11][:, c % 11, :], vd[:, c, :], q="pool")

    xb = [S.sb("xb", [128, D], F32) for _ in range(NSUB * 2)]
    hb = [S.sb("hb", [128, D], BF16) for _ in range(2)]
    junk = S.sb("junk", [128, D], BF16)
    hT = [S.sb("hT", [128, 8, TB], BF16) for _ in range(2)]
    actT = S.sb("actT", [128, NFC, TB], BF16)
    sg = [S.sb("sg", [128, TB], F32) for _ in range(2)]
    fsb = [S.sb("fsb", [128, D], F32) for _ in range(2)]
    st = [S.sb("st", [128, 8], F32) for _ in range(4)]
    ptr = [S.ps("ptr", [128, 8, 128], BF16) for _ in range(2)]
    pg = [S.ps("pg", [128, 512], F32) for _ in range(2)]
    po = [S.ps("po", [128, 512], F32) for _ in range(2)]

    nblk = S_LEN // TB
    cnt = 0
    for b in range(nblk):
        hTb = hT[b % 2]
        for s in range(NSUB):
            t = b * NSUB + s
            x = xb[(b % 2) * NSUB + s]
            stt_ = st[cnt % 4]
            h = hb[cnt % 2]
            p = ptr[cnt % 2]
            cnt += 1
            S.dma(x[:], (xres[t * 128:(t + 1) * 128, :], ("xres", t)))
            S.act(junk[:], x[:], AF.Square, accum_out=stt_[:, 0:1])
            S.ts(stt_[:, 1:2], stt_[:, 0:1], 1.0 / D, RMS_EPS, ALU.mult, ALU.add)
            S.act(stt_[:, 2:3], stt_[:, 1:2], AF.Sqrt)
            S.recip(stt_[:, 3:4], stt_[:, 2:3])
            S.stt(h[:], x[:], stt_[:, 3:4], gpre[:], ALU.mult, ALU.mult)
            for dc in range(8):
                S.tr(p[:, dc, :], h[:, dc * 128:(dc + 1) * 128], ident[:])
            S.copy(hTb[:, :, s * 128:(s + 1) * 128], p[:, :, :], eng="act")
        for fc in range(NFC):
            pgt = pg[fc % 2]
            g_, k_ = fcmap[fc]
            for dc in range(8):
                S.mm(pgt[:, 0:TB], wg[g_][:, dc, k_ * 128:(k_ + 1) * 128], hTb[:, dc, :],
                     start=(dc == 0), stop=(dc == 7))
            for dc in range(8):
                S.mm(pgt[:, TB:2 * TB], wu[g_][:, dc, k_ * 128:(k_ + 1) * 128], hTb[:, dc, :],
                     start=(dc == 0), stop=(dc == 7))
            sgt = sg[fc % 2]
            S.act(sgt[:], pgt[:, 0:TB], AF.Silu)
            S.tt(actT[:, fc, :], sgt[:], pgt[:, TB:2 * TB], ALU.mult)
        for s in range(NSUB):
            t = b * NSUB + s
            x = xb[(b % 2) * NSUB + s]
            f = fsb[s % 2]
            stt_ = st[cnt % 4]
            cnt += 1
            for dh in range(2):
                pot = po[dh]
                for fc in range(NFC):
                    S.mm(pot[:], actT[:, fc, s * 128:(s + 1) * 128], wd[fc // 11][:, fc % 11, dh * 512:(dh + 1) * 512],
                         start=(fc == 0), stop=(fc == NFC - 1))
                S.act(f[:, dh * 512:(dh + 1) * 512], pot[:], AF.Copy)
                S.act(junk[:, dh * 512:(dh + 1) * 512], pot[:], AF.Square, accum_out=stt_[:, dh:dh + 1])
            S.tt(stt_[:, 2:3], stt_[:, 0:1], stt_[:, 1:2], ALU.add)
            S.ts(stt_[:, 3:4], stt_[:, 2:3], 1.0 / D, RMS_EPS, ALU.mult, ALU.add)
            S.act(stt_[:, 4:5], stt_[:, 3:4], AF.Sqrt)
            S.recip(stt_[:, 5:6], stt_[:, 4:5])
            S.stt(f[:], f[:], stt_[:, 5:6], gpost[:], ALU.mult, ALU.mult)
            S.tt(x[:], x[:], f[:], ALU.add, eng="pool")
            S.dma((xres[t * 128:(t + 1) * 128, :], ("xres", t)), x[:], q="act")


HD = 64


def col_layout():
    spec = (('nsa_q', 256), ('nsa_k_cmp', 64), ('nsa_v_cmp', 64), ('nsa_k_slc', 64), ('nsa_v_slc', 64),
            ('nsa_k_win', 64), ('nsa_v_win', 64), ('nsa_gate', 12), ('ret_q', 256), ('ret_k', 256),
            ('ret_v', 512), ('ret_g', 512), ('rwkv', 1024), ('swa_q', 256), ('swa_k', 128), ('swa_v', 128),
            ('branch_gate', 4096))
    lay, s = {}, 0
    for n, wd_ in spec:
        lay[n] = (s, s + wd_)
        s += wd_
    return lay, s


def _partner(cols):
    out = []
    for i in range(0, len(cols), 64):
        blk = cols[i:i + 64]
        out.extend(blk[32:64])
        out.extend(blk[0:32])
    return out


def w_in_index_sets():
    lay, _ = col_layout()
    r = lambda n: list(range(*lay[n]))
    idx = {}
    q = r('swa_q')
    k = r('swa_k')
    kk0 = k[0:64] + k[0:64]
    kk1 = k[64:128] + k[64:128]
    idx['swa'] = q + _partner(q) + kk0 + kk1 + _partner(kk0) + _partner(kk1) + r('swa_v')
    nq, ksl, kwi = r('nsa_q'), r('nsa_k_slc'), r('nsa_k_win')
    idx['nsa'] = (nq + _partner(nq) + r('nsa_k_cmp') + r('nsa_v_cmp') + ksl + _partner(ksl) + kwi + _partner(kwi)
                  + r('nsa_v_slc') + r('nsa_v_win') + r('nsa_gate'))
    idx['r'] = r('rwkv')
    idx['gate'] = r('branch_gate')
    rq, rk = r('ret_q'), r('ret_k')
    idx['ret'] = rq + _partner(rq) + rk + _partner(rk) + r('ret_v') + r('ret_g')
    return idx


def phase_mixpre(S, nc, xres, w, l, hT_d, ident_d):
    g = S.sb("g", [128, D], F32)
    ident = S.sb("ident", [128, 128], BF16)
    S.dma(ident[:], ident_d[:, :], q="pool")
    bcast_row(S, g[:], w["mix_pre_g"][l:l + 1, :])
    xb = [S.sb("xb", [128, D], F32) for _ in range(3)]
    hb = [S.sb("hb", [128, D], BF16) for _ in range(2)]
    junk = S.sb("junk", [128, D], BF16)
    hT = [S.sb("hT", [128, 8, 128], BF16) for _ in range(3)]
    st = [S.sb("st", [128, 8], F32) for _ in range(3)]
    ptr = [S.ps("ptr", [128, 8, 128], BF16) for _ in range(2)]
    hv = hT_d.rearrange("(c p) t -> p c t", p=128)
    for t in range(NT):
        x = xb[t % 3]
        s_ = st[t % 3]
        h = hb[t % 2]
        p = ptr[t % 2]
        o = hT[t % 3]
        S.dma(x[:], (xres[t * 128:(t + 1) * 128, :], ("xres", t)))
        S.act(junk[:], x[:], AF.Square, accum_out=s_[:, 0:1])
        S.ts(s_[:, 1:2], s_[:, 0:1], 1.0 / D, RMS_EPS, ALU.mult, ALU.add)
        S.act(s_[:, 2:3], s_[:, 1:2], AF.Sqrt)
        S.recip(s_[:, 3:4], s_[:, 2:3])
        S.stt(h[:], x[:], s_[:, 3:4], g[:], ALU.mult, ALU.mult)
        for dc in range(8):
            S.tr(p[:, dc, :], h[:, dc * 128:(dc + 1) * 128], ident[:])
        S.copy(o[:], p[:], eng="act")
        S.dma((hv[:, :, t * 128:(t + 1) * 128], ("hT", t)), o[:], q="act")


def proj_feat(S, dst, wsb, col0, hT, pps, rope=None, npart=128):
    for tb in range(8):
        tsl = slice(tb * 512, (tb + 1) * 512)
        p0 = pps[tb % 2]
        for dc in range(8):
            S.mm(p0[0:npart, 0:512], wsb[:, dc, col0:col0 + npart], hT[:, dc, tsl],
                 start=(dc == 0), stop=(dc == 7))
        if rope is None:
            S.copy(dst[0:npart, tsl], p0[0:npart, 0:512], eng="act")
        else:
            pc, cos_d, sin_d, cst, tmp, pps2 = rope
            p1 = pps2[tb % 2]
            for dc in range(8):
                S.mm(p1[0:npart, 0:512], wsb[:, dc, pc:pc + npart], hT[:, dc, tsl],
                     start=(dc == 0), stop=(dc == 7))
            cs = cst[tb % 2]
            S.dma(cs[:, 0, :], cos_d[:, tsl])
            S.dma(cs[:, 1, :], sin_d[:, tsl])
            t1 = tmp[tb % 2]
            S.tt(t1[0:npart, 0, :], p0[0:npart, 0:512], cs[0:npart, 0, :], ALU.mult)
            S.tt(t1[0:npart, 1, :], p1[0:npart, 0:512], cs[0:npart, 1, :], ALU.mult)
            S.tt(dst[0:npart, tsl], t1[0:npart, 0, :], t1[0:npart, 1, :], ALU.add, eng="pool")


def proj_tok(S, dst_fn, wsb, col0, ncol, hT, pps, eng="act", view=None):
    for t in range(NT):
        p0 = pps[t % 2]
        for dc in range(8):
            S.mm(p0[:, 0:ncol], hT[:, dc, t * 128:(t + 1) * 128], wsb[:, dc, col0:col0 + ncol],
                 start=(dc == 0), stop=(dc == 7))
        src = p0[:, 0:ncol]
        if view is not None:
            src = view(src)
        S.copy(dst_fn(t), src, eng=eng)


def attn_qblock(S, nheads, q_of, k_of, v_of, tiles, scale, pss, pacc, pbuf, cnt, skew=2):
    items = [(i, kt, h, mf) for i, (kt, mf) in enumerate(tiles) for h in range(nheads)]
    n = len(items)
    masks = {}
    pbs = {}
    skew = min(skew, len(pss) - 1)

    def front(j):
        i, kt, h, mf = items[j]
        if h == 0:
            masks[i] = mf() if mf is not None else None
        ps = pss[(cnt + j) % len(pss)]
        pb = pbuf[(cnt + j) % len(pbuf)]
        pbs[j] = pb
        S.mm(ps[:, 0:512], k_of(h, kt), q_of(h), start=True, stop=True)
        S.act(pb[:], ps[:, 0:512], AF.Exp, scale=scale)
        if masks[i] is not None:
            S.tt(pb[:], pb[:], masks[i], ALU.mult)

    def back(j):
        i, kt, h, mf = items[j]
        pb = pbs.pop(j)
        for sub in range(4):
            S.mm(pacc[h][:, sub, :], pb[:, sub * 128:(sub + 1) * 128], v_of(h, kt),
                 start=(i == 0 and sub == 0), stop=(i == len(tiles) - 1))

    for j in range(n + skew):
        if j < n:
            front(j)
        if j - skew >= 0:
            back(j - skew)
    return cnt + n


def attn_finish(S, nheads, pacc, ybuf, st, zextra=None, gate_of=None, accumulate=False):
    for h in range(nheads):
        s_ = st[h % len(st)]
        if zextra is not None:
            S.ts(s_[:, 0:4], pacc[h][:, :, 64], zextra(h), None, ALU.add)
        else:
            S.ts(s_[:, 0:4], pacc[h][:, :, 64], 1e-30, None, ALU.max)
        S.recip(s_[:, 4:8], s_[:, 0:4])
        if gate_of is not None:
            S.tt(s_[:, 4:8], s_[:, 4:8], gate_of(h), ALU.mult)
        for sub in range(4):
            yo = ybuf[:, sub, h * 64:(h + 1) * 64]
            if not accumulate:
                S.ts(yo, pacc[h][:, sub, 0:64], s_[:, 4 + sub:5 + sub], None, ALU.mult)
            else:
                S.stt(yo, pacc[h][:, sub, 0:64], s_[:, 4 + sub:5 + sub], yo, ALU.mult, ALU.add)


def phase_swa(S, nc, w, l, hT_d, wswa_d, cst, y_d):
    NCOL = 8 * 128 + 128
    wsb = S.sb("wsb", [128, 8, NCOL], BF16)
    load_w_bf16(S, wsb, wswa_d[l], 8)
    hT = S.sb("hTall", [128, 8, S_LEN], BF16)
    hv = hT_d.rearrange("(c p) t -> p c t", p=128)
    for c in range(8):
        S.dma(hT[:, c, :], (hv[:, c, :],) + tuple(("hT", t) for t in range(NT)))
    qT = [S.sb("qT", [128, S_LEN], BF16) for _ in range(2)]
    kT = [S.sb("kT", [128, S_LEN], BF16) for _ in range(2)]
    vext = S.sb("vext", [128, NT, 2, 65], BF16)
    pps = [S.ps("pp", [128, 512], F32) for _ in range(2)]
    pps2 = [S.ps("pp2", [128, 512], F32) for _ in range(2)]
    cstt = [S.sb("cs", [128, 2, 512], F32) for _ in range(2)]
    tmp = [S.sb("tmp", [128, 2, 512], F32) for _ in range(2)]
    rope = lambda pc: (pc, cst["cosT"], cst["sinT"], cstt, tmp, pps2)
    proj_feat(S, qT[0], wsb, 0, hT, pps, rope(256))
    proj_feat(S, qT[1], wsb, 128, hT, pps, rope(384))
    proj_feat(S, kT[0], wsb, 512, hT, pps, rope(768))
    proj_feat(S, kT[1], wsb, 640, hT, pps, rope(896))
    S.memset(vext[:, :, :, 64:65], 1.0, eng="pool")
    proj_tok(S, lambda t: vext[:, t, :, 0:64], wsb, 1024, 128, hT, pps,
             view=lambda a: a.rearrange("p (a b) -> p a b", a=2))
    masks = S.sb("masks", [128, 5, 512], BF16)
    for r in range(5):
        S.dma(masks[:, r, :], cst["mask_swa"][r], q="pool")
    sk = S.sb("sk", [128, 4], F32)
    S.dma(sk[:], w["swa_sinks"][l:l + 1, :].partition_broadcast(128))
    S.act(sk[:], sk[:], AF.Exp)
    pacc = [S.ps("pacc", [128, 4, 65], F32) for _ in range(4)]
    pbuf = [S.sb("pbuf", [128, 512], BF16) for _ in range(5)]
    ybuf = [S.sb("ybuf", [128, 4, 256], BF16) for _ in range(2)]
    st = [S.sb("st", [128, 8], F32) for _ in range(4)]
    cnt = 0
    for qb in range(8):
        qs = slice(qb * 512, (qb + 1) * 512)
        tiles = [(4 * qb + r, (lambda r=r: masks[:, r + 1, :])) for r in range(-1, 4) if 4 * qb + r >= 0]
        yb = ybuf[qb % 2]
        cnt = attn_qblock(
            S, 4,
            lambda h: qT[h // 2][(h % 2) * 64:(h % 2) * 64 + 64, qs],
            lambda h, kt: kT[h // 2][(h % 2) * 64:(h % 2) * 64 + 64, kt * 128:(kt + 1) * 128],
            lambda h, kt: vext[:, kt, h // 2, :],
            tiles, 0.125, pps + pps2, pacc, pbuf, cnt, skew=3)
        attn_finish(S, 4, pacc, yb, st, zextra=lambda h: sk[:, h:h + 1])
        S.dma(y_d[qb * 512:(qb + 1) * 512, :].rearrange("(s p) f -> p s f", p=128), yb[:], q="act")


def phase_ret(S, nc, w, l, hT_d, wret_d, cst, y_d, ident_d):
    wsb = S.sb("wsb", [128, 8, 2048], BF16)
    load_w_bf16(S, wsb, wret_d[l], 8)
    hT = S.sb("hTall", [128, 8, S_LEN], BF16)
    hv = hT_d.rearrange("(c p) t -> p c t", p=128)
    for c in range(8):
        S.dma(hT[:, c, :], (hv[:, c, :],) + tuple(("hT", t) for t in range(NT)))
    ident = S.sb("ident", [128, 128], BF16)
    S.dma(ident[:], ident_d[:, :], q="pool")
    qT = [S.sb("qT", [64, S_LEN], BF16) for _ in range(4)]
    kT = [S.sb("kT", [64, S_LEN], BF16) for _ in range(4)]
    pps = [S.ps("pp", [128, 512], F32) for _ in range(2)]
    pps2 = [S.ps("pp2", [128, 512], F32) for _ in range(2)]
    cstt = [S.sb("cs", [128, 2, 512], F32) for _ in range(2)]
    tmp = [S.sb("tmp", [128, 2, 512], F32) for _ in range(2)]
    rope = lambda pc: (pc, cst["cosT"], cst["sinT"], cstt, tmp, pps2)
    for h in range(4):
        proj_feat(S, qT[h], wsb, h * 64, hT, pps, rope(256 + h * 64), npart=64)
        proj_feat(S, kT[h], wsb, 512 + h * 64, hT, pps, rope(768 + h * 64), npart=64)
    dmask = S.sb("dmask", [128, 4, 128], F32)
    S.dma(dmask[:], cst["ret_dmaskT"][:, :, :])
    zt = S.sb("zt", [128, 4], F32)
    S.dma(zt[:], cst["ret_zeta"][:, :])
    xiT = S.sb("xiT", [64, 4, 128], F32)
    S.dma(xiT[:], cst["ret_xiT"][:, :, :])
    gch = S.sb("gch", [64, 4], F32)
    S.dma(gch[:], cst["ret_gch"][:, :])
    gng = S.sb("gng", [128, 512], F32)
    bcast_row(S, gng[:], w["ret_gn_g"][l:l + 1, :])
    R = S.sb("R", [64, 4, 128], F32)
    Rbf = S.sb("Rbf", [64, 4, 128], BF16)
    S.memset(R[:], 0.0)
    S.memset(Rbf[:], 0.0)
    po = [S.ps("po", [128, 4, 128], F32) for _ in range(2)]
    ptk = S.ps("ptk", [128, 4, 64], BF16)
    pin = pps2[0][:, :].rearrange("p (h c) -> p h c", h=4)
    pkv = pps2[1][:, :].rearrange("p (h e) -> p h e", h=4)
    vb = [S.sb("vb", [128, 512], BF16) for _ in range(2)]
    sgb = [S.sb("sgb", [128, 512], F32) for _ in range(2)]
    qx = [S.sb("qx", [64, 4, 128], BF16) for _ in range(2)]
    kz = [S.sb("kz", [128, 4, 64], BF16) for _ in range(2)]
    inm = [S.sb("inm", [128, 4, 128], BF16) for _ in range(2)]
    osb = [S.sb("osb", [128, 4, 128], F32) for _ in range(2)]
    sq = [S.sb("sq", [128, 4, 128], F32) for _ in range(2)]
    st = [S.sb("st", [128, 8, 4], F32) for _ in range(2)]
    yb = [S.sb("yb", [128, 512], BF16) for _ in range(2)]
    for t in range(NT):
        ts_ = slice(t * 128, (t + 1) * 128)
        b = t % 2
        for dc in range(8):
            S.mm(pps[0][:, :], hT[:, dc, ts_], wsb[:, dc, 1024:1536], start=(dc == 0), stop=(dc == 7))
        S.copy(vb[b][:], pps[0][:, :], eng="act")
        for dc in range(8):
            S.mm(pps[1][:, :], hT[:, dc, ts_], wsb[:, dc, 1536:2048], start=(dc == 0), stop=(dc == 7))
        S.act(sgb[b][:], pps[1][:, :], AF.Silu)
        for h in range(4):
            S.tr(ptk[:, h, :], kT[h][:, ts_], ident[0:64, 0:64])
        S.tt(kz[b][:], ptk[:, :, :], zt[:, :].unsqueeze(2).to_broadcast([128, 4, 64]), ALU.mult)
        for h in range(4):
            S.tt(qx[b][:, h, :], qT[h][:, ts_], xiT[:, h, :], ALU.mult, eng="pool")
        for h in range(4):
            S.mm(pin[:, h, :], kT[h][:, ts_], qT[h][:, ts_], start=True, stop=True)
        S.tt(inm[b][:], pin, dmask[:], ALU.mult)
        pot = po[b]
        for h in range(4):
            S.mm(pot[:, h, :], inm[b][:, h, :], vb[b][:, h * 128:(h + 1) * 128], start=True, stop=False)
            S.mm(pot[:, h, :], qx[b][:, h, :], Rbf[:, h, :], start=False, stop=True)
        for h in range(4):
            S.mm(pkv[0:64, h, :], kz[b][:, h, :], vb[b][:, h * 128:(h + 1) * 128], start=True, stop=True)
        for h in range(4):
            S.stt(R[:, h, :], R[:, h, :], gch[:, h:h + 1], pkv[0:64, h, :], ALU.mult, ALU.add)
        S.copy(Rbf[:], R[:], eng="act")
        o = osb[b]
        s_ = st[b]
        S.copy(o[:], pot[:], eng="act")
        S.reduce(s_[:, 0, :], o[:], ALU.add)
        S.tt(sq[b][:], o[:], o[:], ALU.mult, eng="pool")
        S.reduce(s_[:, 1, :], sq[b][:], ALU.add)
        S.ts(s_[:, 2, :], s_[:, 0, :], 1.0 / 128, None, ALU.mult)
        S.tt(s_[:, 3, :], s_[:, 2, :], s_[:, 2, :], ALU.mult)
        S.stt(s_[:, 4, :], s_[:, 1, :], 1.0 / 128, s_[:, 3, :], ALU.mult, ALU.subtract)
        S.ts(s_[:, 5, :], s_[:, 4, :], 1e-5, None, ALU.add)
        S.act(s_[:, 6, :], s_[:, 5, :], AF.Sqrt)
        S.recip(s_[:, 7, :], s_[:, 6, :])
        S.tt(o[:], o[:], s_[:, 2, :].unsqueeze(2).to_broadcast([128, 4, 128]), ALU.subtract)
        S.tt(o[:], o[:], s_[:, 7, :].unsqueeze(2).to_broadcast([128, 4, 128]), ALU.mult)
        of = o[:].rearrange("p h e -> p (h e)")
        S.tt(of, of, gng[:], ALU.mult, eng="pool")
        S.tt(yb[b][:], of, sgb[b][:], ALU.mult)
        S.dma(y_d[ts_, :], yb[b][:], q="act")


def load_hT(S, hT_d):
    hT = S.sb("hTall", [128, 8, S_LEN], BF16)
    hv = hT_d.rearrange("(c p) t -> p c t", p=128)
    for c in range(8):
        S.dma(hT[:, c, :], (hv[:, c, :],) + tuple(("hT", t) for t in range(NT)))
    return hT


def phase_nsa_a(S, nc, w, l, hT_d, wnsa_d, cst, scr):
    wsb = S.sb("wsb", [128, 8, 1036], BF16)
    load_w_bf16(S, wsb, wnsa_d[l], 8)
    hT = load_hT(S, hT_d)
    pps = [S.ps("pp", [128, 512], F32) for _ in range(2)]
    pps2 = [S.ps("pp2", [128, 512], F32) for _ in range(2)]
    cstt = [S.sb("cs", [128, 2, 512], F32) for _ in range(2)]
    tmp = [S.sb("tmp", [128, 2, 512], F32) for _ in range(2)]
    rope = lambda pc: (pc, cst["cosT"], cst["sinT"], cstt, tmp, pps2)
    stg = [S.sb("stg", [64, S_LEN], BF16) for _ in range(2)]
    outs = [(h, h * 64, None) for h in range(4)] + [(4 + h, h * 64, 256 + h * 64) for h in range(4)]
    outs += [(8, 512, None), (9, 576, None), (10, 640, 704), (11, 768, 832)]
    for n, (idx, col0, pc) in enumerate(outs):
        st_ = stg[n % 2]
        proj_feat(S, st_, wsb, col0, hT, pps, rope(pc) if pc is not None else None, npart=64)
        S.dma((scr["nsaT_d"][idx], ("nsaT", idx)), st_[:, :], q="act")
    vb = [S.sb("vb", [128, 128], BF16) for _ in range(2)]
    gb = [S.sb("gb", [128, 12], F32) for _ in range(2)]
    for t in range(NT):
        p0 = pps[t % 2]
        for dc in range(8):
            S.mm(p0[:, 0:140], hT[:, dc, t * 128:(t + 1) * 128], wsb[:, dc, 896:1036],
                 start=(dc == 0), stop=(dc == 7))
        S.copy(vb[t % 2][:], p0[:, 0:128], eng="dve")
        S.act(gb[t % 2][:], p0[:, 128:140], AF.Sigmoid)
        S.dma(scr["nsa_v_d"][t * 128:(t + 1) * 128, :], vb[t % 2][:], q="act")
        S.dma(scr["nsa_g_d"][t * 128:(t + 1) * 128, :], gb[t % 2][:], q="act")


def phase_nsa_b(S, nc, w, l, cst, scr):
    kvT = S.sb("kvT", [64, 2, S_LEN], BF16)
    S.dma(kvT[:, 0, :], scr["nsaT_d"][8])
    S.dma(kvT[:, 1, :], scr["nsaT_d"][9])
    posT = S.sb("posT", [64, 2, 32], F32)
    S.dma(posT[:], w["nsa_posT"][l].rearrange("a d l -> d a l"))
    ident = S.sb("ident", [128, 128], BF16)
    S.dma(ident[:], cst["ident"][:, :], q="pool")
    w1 = [S.sb("w1", [64, 32, 256], BF16) for _ in range(2)]
    w2 = [S.sb("w2", [128, 2, 64], BF16) for _ in range(2)]
    for a, nm in enumerate(("k", "v")):
        S.dma(w1[a][:], w["nsa_cmp_%s_w1" % nm][l].rearrange("(l d) j -> d l j", d=64), q="pool")
        S.dma(w2[a][:], w["nsa_cmp_%s_w2" % nm][l].rearrange("(c p) d -> p c d", p=128), q="pool")
    X = S.sb("X", [64, 32, 256], BF16)
    gT = [S.sb("gT", [128, 2, 256], BF16) for _ in range(2)]
    kcT = S.sb("kcT", [64, 256], BF16)
    vcx = S.sb("vcx", [128, 2, 65], BF16)
    ov = S.sb("ov", [128, 2, 64], BF16)
    S.dma(ov[:], cst["overlap"].rearrange("(c p) j -> p c j", p=128), q="pool")
    S.memset(kcT[:], 0.0)
    S.memset(vcx[:], 0.0)
    S.memset(vcx[:, :, 64:65], 1.0)
    for a in range(2):
        S.memset(gT[a][:], 0.0)
    pps = [S.ps("pp", [128, 512], F32) for _ in range(2)]
    hs = [S.sb("hs", [128, 3, 256], F32) for _ in range(2)]
    for a in range(2):
        for l_ in range(32):
            S.ts(X[:, l_, 0:255], kvT[:, a, l_:l_ + 16 * 254 + 1:16], posT[:, a, l_:l_ + 1], None, ALU.add)
        for jc in range(2):
            ph = pps[jc]
            for l_ in range(32):
                S.mm(ph[:, 0:255], w1[a][:, l_, jc * 128:(jc + 1) * 128], X[:, l_, 0:255],
                     start=(l_ == 0), stop=(l_ == 31))
            h_ = hs[jc]
            S.act(h_[:, 0, 0:255], ph[:, 0:255], AF.Square)
            S.ts(h_[:, 0, 0:255], h_[:, 0, 0:255], 0.044715, 1.0, ALU.mult, ALU.add)
            S.tt(h_[:, 1, 0:255], h_[:, 0, 0:255], ph[:, 0:255], ALU.mult)
            S.act(h_[:, 2, 0:255], h_[:, 1, 0:255], AF.Sigmoid, scale=1.5957691216057308)
            S.tt(gT[a][:, jc, 0:255], h_[:, 2, 0:255], ph[:, 0:255], ALU.mult)
    for jc in range(2):
        S.mm(pps[0][0:64, 0:256], w2[0][:, jc, :], gT[0][:, jc, :], start=(jc == 0), stop=(jc == 1))
    S.copy(kcT[:, :], pps[0][0:64, 0:256], eng="act")
    for ic in range(2):
        for jc in range(2):
            S.mm(pps[1][:, ic * 64:(ic + 1) * 64], gT[1][:, jc, ic * 128:(ic + 1) * 128], w2[1][:, jc, :],
                 start=(jc == 0), stop=(jc == 1))
        S.copy(vcx[:, ic, 0:64], pps[1][:, ic * 64:(ic + 1) * 64], eng="act")
    qT = S.sb("qT", [64, 4, S_LEN], BF16)
    for h in range(4):
        S.dma(qT[:, h, :], scr["nsaT_d"][h])
    cmask = S.sb("cmask", [128, 2, S_LEN], BF16)
    S.dma(cmask[:], cst["cmpmaskT"].rearrange("(c p) q -> p c q", p=128), q="pool")
    pss = [S.ps("pss", [128, 512], F32) for _ in range(2)]
    pao = [S.ps("pao", [128, 4, 65], F32) for _ in range(2)]
    pai = S.ps("pai", [128, 4, 64], F32)
    pT = S.ps("pT", [64, 4, 128], BF16)
    pbuf = [S.sb("pbuf", [128, 512], BF16) for _ in range(6)]
    pss4 = pss + pps
    ocb = [S.sb("ocb", [128, 4, 256], BF16) for _ in range(2)]
    imp = [S.sb("imp", [128, 4, 64], F32) for _ in range(2)]
    sbt = [S.sb("sbt", [128, 4, 64], F32) for _ in range(2)]
    score = [S.sb("score", [128, 4, 64], F32) for _ in range(2)]
    work = S.sb("work", [128, 4, 64], F32)
    m8 = S.sb("m8", [128, 4, 8], F32)
    m8b = S.sb("m8b", [128, 4, 8], F32)
    selm = [S.sb("selm", [128, 4, 64], BF16) for _ in range(2)]
    selT = S.sb("selT", [64, S_LEN], BF16)
    st = [S.sb("st", [128, 8], F32) for _ in range(4)]
    cnt = 0
    for qb in range(8):
        qs = slice(qb * 512, (qb + 1) * 512)
        b = qb % 2
        pend = []

        def nsab_back(item, b=b):
            h, po_, pbl = item
            for ic in range(2):
                pb = pbl[ic]
                for sub in range(4):
                    S.mm(po_[:, sub, :], pb[:, sub * 128:(sub + 1) * 128], vcx[:, ic, :],
                         start=(ic == 0 and sub == 0), stop=(ic == 1))
                for sub in range(4):
                    S.mm(pai[:, sub, :], pb[:, sub * 128:(sub + 1) * 128], ov[:, ic, :],
                         start=(ic == 0 and sub == 0), stop=(ic == 1))
            s_ = st[h]
            S.ts(s_[:, 0:4], po_[:, :, 64], 1e-30, None, ALU.max)
            S.recip(s_[:, 4:8], s_[:, 0:4])
            for sub in range(4):
                S.ts(ocb[b][:, sub, h * 64:(h + 1) * 64], po_[:, sub, 0:64], s_[:, 4 + sub:5 + sub], None, ALU.mult)
                if h == 0:
                    S.ts(imp[b][:, sub, :], pai[:, sub, :], s_[:, 4 + sub:5 + sub], None, ALU.mult)
                else:
                    S.stt(imp[b][:, sub, :], pai[:, sub, :], s_[:, 4 + sub:5 + sub], imp[b][:, sub, :],
                          ALU.mult, ALU.add)

        for h in range(4):
            po_ = pao[h % 2]
            pbl = []
            for ic in range(2):
                ps = pss4[cnt % 4]
                pb = pbuf[cnt % 6]
                cnt += 1
                pbl.append(pb)
                S.mm(ps[:, 0:512], kcT[:, ic * 128:(ic + 1) * 128], qT[:, h, qs], start=True, stop=True)
                S.act(pb[:], ps[:, 0:512], AF.Exp, scale=0.125)
                S.tt(pb[:], pb[:], cmask[:, ic, qs], ALU.mult)
            pend.append((h, po_, pbl))
            if len(pend) > 1:
                nsab_back(pend.pop(0))
        nsab_back(pend.pop(0))
        S.dma(scr["ocmp_d"][qs, :].rearrange("(s p) f -> p s f", p=128), ocb[b][:], q="act")
        S.dma(sbt[b][:], cst["selbias"][qs, :].rearrange("(s p) j -> p s j", p=128))
        sc = score[b]
        S.tt(sc[:], imp[b][:], sbt[b][:], ALU.add)
        for sub in range(4):
            a_sc, a_m8, a_wk, a_m8b = sc[:, sub, :], m8[:, sub, :], work[:, sub, :], m8b[:, sub, :]
            S.op("dve", lambda e, o=a_m8, i=a_sc: e.max(o, i), reads=[sc[:]], writes=[m8[:]])
            S.op("dve", lambda e, o=a_wk, r=a_m8, i=a_sc: e.match_replace(o, r, i, -3.0e9),
                 reads=[sc[:], m8[:]], writes=[work[:]])
            S.op("dve", lambda e, o=a_m8b, i=a_wk: e.max(o, i), reads=[work[:]], writes=[m8b[:]])
            S.ts(selm[b][:, sub, :], sc[:, sub, :], m8b[:, sub, 7:8], None, ALU.is_ge)
        for sub in range(4):
            S.tr(pT[:, sub, :], selm[b][:, sub, :], ident[:])
        S.copy(selT[:, qs].rearrange("j (s p) -> j s p", s=4), pT[:, :, :], eng="act")
    S.dma(scr["selT_d"][:, :], selT[:, :], q="act")


def phase_nsa_c(S, nc, w, l, cst, scr, y_d):
    qrT = S.sb("qrT", [64, 4, S_LEN], BF16)
    for h in range(4):
        S.dma(qrT[:, h, :], scr["nsaT_d"][4 + h])
    kT = S.sb("kT", [64, 2, S_LEN], BF16)
    S.dma(kT[:, 0, :], scr["nsaT_d"][10])
    S.dma(kT[:, 1, :], scr["nsaT_d"][11])
    selT = S.sb("selT", [64, S_LEN], BF16)
    S.dma(selT[:, :], scr["selT_d"][:, :])
    vext = S.sb("vext", [128, NT, 2, 65], BF16)
    S.memset(vext[:, :, :, 64:65], 1.0, eng="pool")
    vv = scr["nsa_v_d"].rearrange("(t p) f -> p t f", p=128)
    for a in range(2):
        S.dma(vext[:, :, a, 0:64], vv[:, :, a * 64:(a + 1) * 64])
    gsb = S.sb("gsb", [128, NT, 12], F32)
    S.dma(gsb[:], scr["nsa_g_d"].rearrange("(t p) c -> p t c", p=128))
    mwin = S.sb("mwin", [128, 8, 512], BF16)
    for r in range(8):
        S.dma(mwin[:, r, :], cst["mask_win"][r], q="pool")
    mcau = S.sb("mcau", [128, 4, 512], BF16)
    for r in range(4):
        S.dma(mcau[:, r, :], cst["mask_causal"][r], q="pool")
    Eall = S.sb("Eall", [64, NT, 128], BF16)
    S.dma(Eall[:], cst["Eall"][:, :, :], q="pool")
    pss = [S.ps("pss", [128, 512], F32) for _ in range(3)]
    pacc = [S.ps("pacc", [128, 4, 65], F32) for _ in range(4)]
    pm = [S.ps("pm", [128, 512], F32) for _ in range(1)]
    pbuf = [S.sb("pbuf", [128, 512], BF16) for _ in range(4)]
    mbuf = [S.sb("mbuf", [128, 512], BF16) for _ in range(3)]
    ocb = [S.sb("ocb", [128, 4, 256], BF16) for _ in range(2)]
    ybuf = [S.sb("ybuf", [128, 4, 256], F32) for _ in range(2)]
    yb16 = [S.sb("yb16", [128, 4, 256], BF16) for _ in range(2)]
    st = [S.sb("st", [128, 8], F32) for _ in range(4)]
    cnt = 0
    mcnt = [0]
    for qb in range(8):
        qs = slice(qb * 512, (qb + 1) * 512)
        b = qb % 2
        yb = ybuf[b]
        gq = gsb[:, 4 * qb:4 * qb + 4, :]
        S.dma(ocb[b][:], scr["ocmp_d"][qs, :].rearrange("(s p) f -> p s f", p=128))
        for h in range(4):
            S.tt(yb[:, :, h * 64:(h + 1) * 64], ocb[b][:, :, h * 64:(h + 1) * 64],
                 gq[:, :, 3 * h:3 * h + 1].to_broadcast([128, 4, 64]), ALU.mult)
        tiles = [(4 * qb + r, (lambda r=r: mwin[:, r + 4, :])) for r in range(-4, 4) if 4 * qb + r >= 0]
        cnt = attn_qblock(S, 4, lambda h: qrT[:, h, qs], lambda h, kt: kT[:, 1, kt * 128:(kt + 1) * 128],
                          lambda h, kt: vext[:, kt, 1, :], tiles, 0.125, pss, pacc, pbuf, cnt)
        attn_finish(S, 4, pacc, yb, st, gate_of=lambda h: gq[:, :, 3 * h + 2], accumulate=True)

        def mk_mask(kt):
            def f():
                i = mcnt[0]
                mcnt[0] += 1
                p_ = pm[0]
                m_ = mbuf[i % 3]
                S.mm(p_[:, 0:512], Eall[:, kt, :], selT[:, qs], start=True, stop=True)
                r = kt - 4 * qb
                if r >= 0:
                    S.tt(m_[:], p_[:, 0:512], mcau[:, r, :], ALU.mult)
                else:
                    S.copy(m_[:], p_[:, 0:512], eng="pool" if False else "dve")
                return m_[:]
            return f
        tiles = [(kt, mk_mask(kt)) for kt in range(0, 4 * qb + 4)]
        cnt = attn_qblock(S, 4, lambda h: qrT[:, h, qs], lambda h, kt: kT[:, 0, kt * 128:(kt + 1) * 128],
                          lambda h, kt: vext[:, kt, 0, :], tiles, 0.125, pss, pacc, pbuf, cnt)
        attn_finish(S, 4, pacc, yb, st, gate_of=lambda h: gq[:, :, 3 * h + 1], accumulate=True)
        S.copy(yb16[b][:], yb[:], eng="act")
        S.dma(y_d[qs, :].rearrange("(s p) f -> p s f", p=128), yb16[b][:], q="act")


NEG_EH = -0.6065306597126334


def phase_rwkv_a(S, nc, w, l, hT_d, wr_d, cst, scr):
    mub = S.sb("mub", [128, 1024], F32)
    omub = S.sb("omub", [128, 1024], F32)
    bcast_row(S, mub[:], w["rwkv_mu"][l:l + 1, :])
    S.ts(omub[:], mub[:], -1.0, 1.0, ALU.mult, ALU.add)
    W1 = S.sb("W1", [128, 8, 1024], BF16)
    W2 = S.sb("W2", [128, 8, 1024], BF16)
    wst = [S.sb("wst", [128, 1024], F32) for _ in range(2)]
    wv = wr_d[l].rearrange("(c p) f -> p c f", p=128)
    for c in range(8):
        S.dma(wst[c % 2][:], wv[:, c, :])
        S.tt(W1[:, c, :], wst[c % 2][:], omub[:], ALU.mult)
        S.tt(W2[:, c, :], wst[c % 2][:], mub[:], ALU.mult, eng="pool")
    hTp = S.sb("hTp", [128, 8, S_LEN + 1], BF16)
    S.memset(hTp[:, :, 0:1], 0.0)
    hv = hT_d.rearrange("(c p) t -> p c t", p=128)
    for c in range(8):
        S.dma(hTp[:, c, 1:S_LEN + 1], (hv[:, c, :],) + tuple(("hT", t) for t in range(NT)))
    cols = S.sb("cols", [64, 20], F32)
    S.dma(cols[:], w["rwkv_cols"][l])
    omka = S.sb("omka", [64, 4], F32)
    S.ts(omka[:], cols[:, 12:16], -1.0, 1.0, ALU.mult, ALU.add)
    rkc = S.sb("rkc", [64, 4], BF16)
    S.copy(rkc[:], cols[:, 16:20])
    w2sb = S.sb("w2sb", [64, 256], BF16)
    a2sb = S.sb("a2sb", [64, 256], BF16)
    g2sb = S.sb("g2sb", [128, 256], BF16)
    S.dma(w2sb[:], w["rwkv_w2"][l], q="pool")
    S.dma(a2sb[:], w["rwkv_a2"][l], q="pool")
    S.dma(g2sb[:], w["rwkv_g2"][l], q="pool")
    ones64 = S.sb("ones64", [64, 64], BF16)
    S.memset(ones64[:], 1.0)
    rmask = S.sb("rmask", [64, 512], F32)
    S.dma(rmask[:], cst["rw_reset"][:, :])
    gCs = S.sb("gCs", [64, 4, 64], F32)
    pp = [S.ps("pp", [128, 512], F32) for _ in range(7)]
    pbn = S.ps("pbn", [128, 4, 4], F32)
    pc = [0]

    def nextp():
        p = pp[pc[0] % 7]
        pc[0] += 1
        return p

    def xmT(c0, m, t0, n=512):
        p = nextp()
        for dc in range(8):
            S.mm(p[0:m, 0:n], W1[:, dc, c0:c0 + m], hTp[:, dc, 1 + t0:1 + t0 + n], start=(dc == 0), stop=False)
        for dc in range(8):
            S.mm(p[0:m, 0:n], W2[:, dc, c0:c0 + m], hTp[:, dc, t0:t0 + n], start=False, stop=(dc == 7))
        return p

    twl = [S.sb("twl", [64, 512], BF16) for _ in range(2)]
    tal = [S.sb("tal", [64, 512], BF16) for _ in range(2)]
    sgl = [S.sb("sgl", [128, 512], BF16) for _ in range(2)]
    vtok = [S.sb("vtok", [128, 256], BF16) for _ in range(2)]
    gtok = [S.sb("gtok", [128, 256], F32) for _ in range(2)]
    bon = [S.sb("bon", [128, 4, 4], F32) for _ in range(2)]
    NF = 14
    f32t = [[S.sb("f%d" % i, [64, 512], F32) for i in range(NF)] for _ in range(2)]
    sqb = [S.sb("sqb", [64, 512], BF16) for _ in range(2)]
    rkb = [S.sb("rkb", [64, 512], BF16) for _ in range(2)]
    out6 = [S.sb("out6", [64, 6, 512], BF16) for _ in range(2)]
    for tb in range(8):
        t0 = tb * 512
        b = tb % 2
        p = xmT(768, 64, t0)
        S.act(twl[b][:], p[0:64, :], AF.Tanh)
        p = xmT(832, 64, t0)
        S.copy(tal[b][:], p[0:64, :], eng="dve")
        p = xmT(896, 128, t0)
        S.act(sgl[b][:], p[:, :], AF.Sigmoid)
        for sub in range(4):
            tt0 = t0 + sub * 128
            p = nextp()
            for dc in range(8):
                S.mm(p[:, 0:256], hTp[:, dc, 1 + tt0:1 + tt0 + 128], W1[:, dc, 512:768], start=(dc == 0), stop=False)
            for dc in range(8):
                S.mm(p[:, 0:256], hTp[:, dc, tt0:tt0 + 128], W2[:, dc, 512:768], start=False, stop=(dc == 7))
            vt = vtok[sub % 2]
            S.copy(vt[:], p[:, 0:256], eng="act")
            S.dma(scr["rw_v_d"][tt0:tt0 + 128, :], vt[:], q="act")
            p = nextp()
            S.mm(p[:, 0:256], sgl[b][:, sub * 128:(sub + 1) * 128], g2sb[:, :], start=True, stop=True)
            gt = gtok[sub % 2]
            S.copy(gt[:], p[:, 0:256], eng="dve")
            S.dma(scr["rw_g_d"][tt0:tt0 + 128, :], gt[:], q="act")
        def hfront(h, b=b, t0=t0):
            F = f32t[h % 2]
            lw, cs, Ep, En, cse, Epe, EC, ag, kkr, nrm, kk, tf, k2, bv = F
            o6 = out6[h % 2]
            hs = slice(h * 64, (h + 1) * 64)
            p = nextp()
            S.mm(p[0:64, :], w2sb[:, hs], twl[b][:], start=True, stop=True)
            S.act(lw[:], p[0:64, :], AF.Sigmoid, bias=cols[:, h:h + 1])
            a_cs, a_rm, a_lw = cs[:], rmask[:], lw[:]
            S.op("dve", lambda e, o=a_cs, d0=a_rm, d1=a_lw: e.tensor_tensor_scan(o, d0, d1, 0.0, ALU.mult, ALU.add),
                 reads=[rmask[:], lw[:]], writes=[cs[:]])
            S.act(Ep[:], cs[:], AF.Exp, scale=NEG_EH)
            S.act(En[:], cs[:], AF.Exp, scale=-NEG_EH)
            S.tt(cse[:], cs[:], lw[:], ALU.subtract, eng="pool")
            S.act(Epe[:], cse[:], AF.Exp, scale=NEG_EH)
            S.tt(EC[:].rearrange("p (c t) -> p c t", c=8), En[:].rearrange("p (c t) -> p c t", c=8),
                 Ep[:, 63::64].unsqueeze(2).to_broadcast([64, 8, 64]), ALU.mult, eng="pool")
            S.copy(gCs[:, h, tb * 8:(tb + 1) * 8], Ep[:, 63::64], eng="dve")
            p = nextp()
            S.mm(p[0:64, :], a2sb[:, hs], tal[b][:], start=True, stop=True)
            S.act(ag[:], p[0:64, :], AF.Sigmoid, bias=cols[:, 4 + h:5 + h])
            pk = xmT(256 + h * 64, 64, t0)
            S.ts(kkr[:], pk[0:64, :], cols[:, 8 + h:9 + h], None, ALU.mult)
            S.act(sqb[h % 2][:], kkr[:], AF.Square)
            S.ts(tf[:], ag[:], cols[:, 12 + h:13 + h], omka[:, h:h + 1], ALU.mult, ALU.add)
            S.tt(k2[:], tf[:], pk[0:64, :], ALU.mult)
            pr = xmT(h * 64, 64, t0)
            S.tt(o6[:, 2, :], k2[:], En[:], ALU.mult, eng="pool")
            S.tt(o6[:, 3, :], pr[0:64, :], Ep[:], ALU.mult)
            S.tt(o6[:, 5, :], k2[:], EC[:], ALU.mult, eng="pool")
            S.tt(rkb[h % 2][:], pr[0:64, :], k2[:], ALU.mult)

        def hback(h, b=b, t0=t0):
            F = f32t[h % 2]
            lw, cs, Ep, En, cse, Epe, EC, ag, kkr, nrm, kk, tf, k2, bv = F
            o6 = out6[h % 2]
            p = nextp()
            S.mm(p[0:64, :], ones64[:], sqb[h % 2][:], start=True, stop=True)
            S.act(nrm[:], p[0:64, :], AF.Sqrt)
            S.ts(nrm[:], nrm[:], 1e-12, None, ALU.max)
            S.recip(nrm[:], nrm[:])
            S.tt(kk[:], kkr[:], nrm[:], ALU.mult, eng="pool")
            S.tt(bv[:], kk[:], ag[:], ALU.mult, eng="pool")
            S.stt(o6[:, 0, :], kk[:], -1.0, Epe[:], ALU.mult, ALU.mult)
            S.tt(o6[:, 1, :], bv[:], En[:], ALU.mult, eng="pool")
            S.tt(o6[:, 4, :], bv[:], EC[:], ALU.mult, eng="pool")
            for sub in range(4):
                S.mm(pbn[:, sub, h:h + 1], rkb[h % 2][:, sub * 128:(sub + 1) * 128], rkc[:, h:h + 1],
                     start=True, stop=True)
            S.dma(scr["rwT_d"][h].rearrange("q k t -> k q t")[:, :, t0:t0 + 512], o6[:], q="act")

        for h in range(5):
            if h < 4:
                hfront(h)
            if h >= 1:
                hback(h - 1)
        S.copy(bon[b][:], pbn[:], eng="act")
        S.dma(scr["rw_b_d"][t0:t0 + 512, :].rearrange("(s p) h -> p s h", p=128), bon[b][:], q="act")
    S.dma(scr["rw_gC_d"][:, :, :], gCs[:], q="act")


def phase_rwkv_b(S, nc, w, l, cst, scr, y_d):
    ident = S.sb("ident", [128, 128], BF16)
    S.dma(ident[:], cst["ident"][:, :], q="pool")
    mlo = S.sb("mlo", [64, 4, 64], F32)
    mup = S.sb("mup", [64, 4, 64], F32)
    mupi = S.sb("mupi", [64, 4, 64], F32)
    I4 = S.sb("I4", [64, 4, 64], F32)
    S.dma(mlo[:], cst["rw_mlo"][:, :, :])
    S.dma(mup[:], cst["rw_mup"][:, :, :])
    S.dma(mupi[:], cst["rw_mupi"][:, :, :])
    S.dma(I4[:], cst["rw_I4"][:, :, :])
    gC = S.sb("gC", [64, 4, 64], F32)
    S.dma(gC[:], scr["rw_gC_d"][:, :, :])
    lng = S.sb("lng", [64, 256], F32)
    lnb = S.sb("lnb", [64, 256], F32)
    S.dma(lng[:], w["rwkv_ln_g"][l:l + 1, :].partition_broadcast(64))
    S.dma(lnb[:], w["rwkv_ln_b"][l:l + 1, :].partition_broadcast(64))
    M = S.sb("M", [64, 4, 64], F32)
    Mbf = S.sb("Mbf", [64, 4, 64], BF16)
    S.memset(M[:], 0.0)
    S.memset(Mbf[:], 0.0)
    NB = 4
    pp_full = [S.ps("pp", [128, 512], F32) for _ in range(7)]
    pp = [p_[0:64, :] for p_ in pp_full]
    ptr_full = S.ps("ptr", [128, 4, 3, 64], BF16)
    ptr = ptr_full[0:64]
    pc = [0]

    def nextp():
        p = pp[pc[0] % 7]
        pc[0] += 1
        return p

    def v4(p, half):
        return p[:, half * 256:(half + 1) * 256].rearrange("p (h s) -> p h s", h=4)

    def mk(name, shape, dt):
        return [S.sb(name, shape, dt) for _ in range(NB)]
    feat = [S.sb("feat", [64, 4, 6, 256], BF16) for _ in range(2)]
    vch = [S.sb("vch", [64, 4, 256], BF16) for _ in range(2)]
    bonb = [S.sb("bonb", [64, 4, 4], F32) for _ in range(2)]
    gtk = [S.sb("gtk", [64, 4, 256], F32) for _ in range(2)]
    tokM = mk("tokM", [64, 4, 2, 64], BF16)
    WZin = mk("WZin", [64, 4, 128], BF16)
    Lb = [mk("L0", [64, 4, 64], BF16), mk("L1", [64, 4, 64], BF16)]
    LTb = [mk("LT0", [64, 4, 64], BF16), mk("LT1", [64, 4, 64], BF16)]
    ILb = [mk("IL0", [64, 4, 64], BF16), mk("IL1", [64, 4, 64], BF16)]
    PTb = [mk("PT0", [64, 4, 64], BF16), mk("PT1", [64, 4, 64], BF16)]
    AakT = mk("AakT", [64, 4, 64], BF16)
    ArbT = mk("ArbT", [64, 4, 64], BF16)
    ArkT = mk("ArkT", [64, 4, 64], BF16)
    WZ = mk("WZ", [64, 4, 128], BF16)
    GT = mk("GT", [64, 4, 64], BF16)
    GNs = mk("GNs", [64, 2, 4, 64], F32)
    Dg = mk("Dg", [64, 4, 64], F32)
    Nsb = mk("Nsb", [64, 4, 64], F32)
    QeT = mk("QeT", [64, 4, 64], BF16)
    Ol = mk("Ol", [64, 4, 64], F32)
    osb = mk("osb", [64, 4, 64], F32)
    sqs = mk("sqs", [64, 4, 64], F32)
    bvt = mk("bvt", [64, 4, 64], F32)
    stt_ = mk("stt", [64, 8, 4], F32)
    yb = mk("yb", [64, 256], BF16)
    NBATCH = S_LEN // 256
    import os
    VAR = os.environ.get("RWB_VAR", "Z")
    for bt in range(NBATCH):
        if VAR == "A":
            break
        t0 = bt * 256
        fb = feat[bt % 2]
        vb = vch[bt % 2]
        for h in range(4):
            S.dma(fb[:, h, :, :], scr["rwT_d"][h].rearrange("q k t -> k q t")[:, :, t0:t0 + 256])
        S.dma(vb[:], scr["rw_v_d"][t0:t0 + 256, :].rearrange("(c p) f -> p c f", p=64))
        S.dma(bonb[bt % 2][:], scr["rw_b_d"][t0:t0 + 256, :].rearrange("(c p) f -> p c f", p=64))
        S.dma(gtk[bt % 2][:], scr["rw_g_d"][t0:t0 + 256, :].rearrange("(c p) f -> p c f", p=64))

        def F(h, q, c):
            return fb[:, h, q, c * 64:(c + 1) * 64]

        def V(h, c):
            return vb[:, c, h * 64:(h + 1) * 64]
        if VAR == "B":
            continue
        for c in range(NB):
            for h in range(4):
                for j, q in enumerate((0, 4, 5)):
                    if VAR == "D":
                        continue
                    S.tr(ptr[:, h, j, :], F(h, q, c), ident[0:64, 0:64])
            if VAR != "E":
                S.copy(WZin[c][:, :, 0:64], ptr[:, :, 0, :], eng="dve")
            if VAR != "F":
                S.copy(tokM[c][:], ptr[:, :, 1:3, :], eng="dve")
        import os
        STOP = int(os.environ.get("RWB_STOP", "9"))
        if STOP < 1:
            continue
        for c in range(NB):
            p1, p2, p3 = nextp(), nextp(), nextp()
            for h in range(4):
                S.mm(v4(p1, 0)[:, h, :], F(h, 0, c), F(h, 1, c), start=True, stop=True)
                S.mm(v4(p1, 1)[:, h, :], F(h, 1, c), F(h, 0, c), start=True, stop=True)
                S.mm(v4(p2, 0)[:, h, :], F(h, 2, c), F(h, 0, c), start=True, stop=True)
                S.mm(v4(p2, 1)[:, h, :], F(h, 1, c), F(h, 3, c), start=True, stop=True)
                S.mm(v4(p3, 0)[:, h, :], F(h, 2, c), F(h, 3, c), start=True, stop=True)
            S.tt(Lb[0][c][:], v4(p1, 0), mlo[:], ALU.mult)
            S.tt(LTb[0][c][:], v4(p1, 1), mup[:], ALU.mult)
            S.tt(PTb[0][c][:], LTb[0][c][:], I4[:], ALU.add, eng="pool")
            S.tt(AakT[c][:], v4(p2, 0), mup[:], ALU.mult)
            S.tt(ArbT[c][:], v4(p2, 1), mupi[:], ALU.mult)
            S.tt(ArkT[c][:], v4(p3, 0), mupi[:], ALU.mult)
        if STOP < 2:
            continue
        for c in range(NB):
            p1 = nextp()
            for h in range(4):
                S.mm(v4(p1, 0)[:, h, :], AakT[c][:, h, :], V(h, c), start=True, stop=True)
            S.copy(WZin[c][:, :, 64:128], v4(p1, 0), eng="dve")
        if STOP < 3:
            continue
        for i in range(1, 7):
            cur, prv = i % 2, (i - 1) % 2
            for c in range(NB):
                p1 = nextp()
                p2 = nextp() if i >= 2 else None
                for h in range(4):
                    if i <= 5:
                        S.mm(v4(p1, 0)[:, h, :], LTb[prv][c][:, h, :], Lb[prv][c][:, h, :], start=True, stop=True)
                    if i <= 4:
                        S.mm(v4(p1, 1)[:, h, :], Lb[prv][c][:, h, :], LTb[prv][c][:, h, :], start=True, stop=True)
                    if i >= 2:
                        S.mm(v4(p2, 0)[:, h, :], ILb[prv][c][:, h, :], PTb[i % 2][c][:, h, :], start=True, stop=True)
                if i <= 5:
                    S.copy(Lb[cur][c][:].rearrange("p h s -> p (h s)"), p1[:, 0:256], eng="act")
                    S.tt(ILb[cur][c][:], Lb[cur][c][:], I4[:], ALU.add, eng="pool")
                if i <= 4:
                    S.copy(LTb[cur][c][:].rearrange("p h s -> p (h s)"), p1[:, 256:512], eng="act")
                if i >= 2:
                    S.copy(PTb[(i - 1) % 2][c][:], v4(p2, 0), eng="dve")
        TT = PTb[1]
        if STOP < 4:
            continue
        for c in range(NB):
            p1 = nextp()
            pw = p1.rearrange("p (h s) -> p h s", h=4)
            for h in range(4):
                S.mm(pw[:, h, :], TT[c][:, h, :], WZin[c][:, h, :], start=True, stop=True)
            S.copy(WZ[c][:].rearrange("p h s -> p (h s)"), p1[:, 0:512], eng="act")
        if STOP < 5:
            continue
        for c in range(NB):
            n = bt * NB + c
            p1, p2 = nextp(), nextp()
            for h in range(4):
                S.mm(v4(p1, 0)[:, h, :], WZ[c][:, h, 0:64], tokM[c][:, h, 0, :], start=True, stop=True)
                S.mm(v4(p1, 1)[:, h, :], tokM[c][:, h, 0, :], WZ[c][:, h, 64:128], start=True, stop=False)
                S.mm(v4(p1, 1)[:, h, :], tokM[c][:, h, 1, :], V(h, c), start=False, stop=True)
            for h in range(4):
                S.mm(v4(p2, 0)[:, h, :], WZ[c][:, h, 0:64], ArbT[c][:, h, :], start=True, stop=True)
                S.mm(v4(p2, 1)[:, h, :], ArbT[c][:, h, :], WZ[c][:, h, 64:128], start=True, stop=False)
                S.mm(v4(p2, 1)[:, h, :], ArkT[c][:, h, :], V(h, c), start=False, stop=True)
            S.tt(Dg[c][:], I4[:], gC[:, :, n:n + 1].to_broadcast([64, 4, 64]), ALU.mult, eng="pool")
            S.copy(GNs[c][:].rearrange("p a h s -> p (a h s)"), p1[:, 0:512], eng="act")
            S.tt(GT[c][:], GNs[c][:, 0, :, :], Dg[c][:], ALU.add, eng="pool")
            S.tt(QeT[c][:], v4(p2, 0), fb[:, :, 3, c * 64:(c + 1) * 64], ALU.add)
            S.copy(Ol[c][:], v4(p2, 1), eng="dve")
        if STOP < 6:
            continue
        for c in range(NB):
            p1 = nextp()
            for h in range(4):
                S.mm(v4(p1, 0)[:, h, :], QeT[c][:, h, :], Mbf[:, h, :], start=True, stop=True)
            for h in range(4):
                S.mm(v4(p1, 1)[:, h, :], GT[c][:, h, :], Mbf[:, h, :], start=True, stop=True)
            S.tt(M[:], v4(p1, 1), GNs[c][:, 1, :, :], ALU.add)
            S.copy(Mbf[:], M[:], eng="act")
            o = osb[c]
            s_ = stt_[c]
            S.tt(o[:], v4(p1, 0), Ol[c][:], ALU.add)
            S.reduce(s_[:, 0, :], o[:], ALU.add)
            S.tt(sqs[c][:], o[:], o[:], ALU.mult, eng="pool")
            S.reduce(s_[:, 1, :], sqs[c][:], ALU.add)
            S.ts(s_[:, 2, :], s_[:, 0, :], 1.0 / 64, None, ALU.mult)
            S.tt(s_[:, 3, :], s_[:, 2, :], s_[:, 2, :], ALU.mult)
            S.stt(s_[:, 4, :], s_[:, 1, :], 1.0 / 64, s_[:, 3, :], ALU.mult, ALU.subtract)
            S.ts(s_[:, 5, :], s_[:, 4, :], 64e-5, None, ALU.add)
            S.act(s_[:, 6, :], s_[:, 5, :], AF.Sqrt)
            S.recip(s_[:, 7, :], s_[:, 6, :])
            S.tt(o[:], o[:], s_[:, 2, :].unsqueeze(2).to_broadcast([64, 4, 64]), ALU.subtract)
            S.tt(o[:], o[:], s_[:, 7, :].unsqueeze(2).to_broadcast([64, 4, 64]), ALU.mult)
            of = o[:].rearrange("p h e -> p (h e)")
            S.tt(of, of, lng[:], ALU.mult, eng="pool")
            S.tt(of, of, lnb[:], ALU.add, eng="pool")
            S.tt(bvt[c][:], vb[:, c, :].rearrange("p (h e) -> p h e", h=4),
                 bonb[bt % 2][:, c, :].unsqueeze(2).to_broadcast([64, 4, 64]), ALU.mult)
            S.tt(o[:], o[:], bvt[c][:], ALU.add)
            S.tt(yb[c][:], of, gtk[bt % 2][:, c, :], ALU.mult)
            S.dma(y_d[t0 + c * 64:t0 + (c + 1) * 64, :], yb[c][:], q="act")


def phase_merge(S, nc, xres, w, l, hT_d, wgate_d, cst, scr):
    Wg = S.sb("Wg", [128, 8, 4096], BF16)
    load_w_bf16(S, Wg, wgate_d[l], 8)
    Wbr = S.sb("Wbr", [128, 10, D], BF16)
    off = 0
    for nm, nch in (("w_br_nsa", 2), ("w_br_ret", 4), ("w_br_rwkv", 2), ("w_br_swa", 2)):
        v = w[nm][l].rearrange("(c p) f -> p c f", p=128)
        for c in range(nch):
            S.dma(Wbr[:, off + c, :], v[:, c, :], q="pool")
        off += nch
    Wo = S.sb("Wo", [128, 8, D], BF16)
    load_w_bf16(S, Wo, w["w_out"][l], 8)
    gpost = S.sb("gpost", [128, D], F32)
    bcast_row(S, gpost[:], w["mix_post_g"][l:l + 1, :])
    ident = S.sb("ident", [128, 128], BF16)
    S.dma(ident[:], cst["ident"][:, :], q="pool")
    ycat = [S.sb("ycat", [128, 1280], BF16) for _ in range(2)]
    yT = [S.sb("yT", [128, 10, 128], BF16) for _ in range(2)]
    hTt = [S.sb("hTt", [128, 8, 128], BF16) for _ in range(2)]
    xb = [S.sb("xb", [128, D], F32) for _ in range(2)]
    sg = [S.sb("sg", [128, 512], F32) for _ in range(2)]
    tmpb = [S.sb("tmpb", [128, 512], F32) for _ in range(2)]
    merged = [S.sb("merged", [128, D], F32) for _ in range(2)]
    mbf = [S.sb("mbf", [128, D], BF16) for _ in range(2)]
    mT = [S.sb("mT", [128, 8, 128], BF16) for _ in range(2)]
    fsb = [S.sb("fsb", [128, D], F32) for _ in range(2)]
    junk = S.sb("junk", [128, D], BF16)
    st = [S.sb("st", [128, 8], F32) for _ in range(2)]
    ptrA = S.ps("ptrA", [128, 5, 128], BF16)
    ptrB = S.ps("ptrB", [128, 5, 128], BF16)
    ptm = S.ps("ptm", [128, 8, 128], BF16)
    pg = [S.ps("pg", [128, 512], F32) for _ in range(2)]
    po = [S.ps("po", [128, 512], F32) for _ in range(2)]
    hv = hT_d.rearrange("(c p) t -> p c t", p=128)
    brch = ((0, 2), (2, 4), (6, 2), (8, 2))
    pf = S.ps("pf", [128, 512], F32)
    cnt = [0]

    def front(t):
        b2 = t % 2
        rows = slice(t * 128, (t + 1) * 128)
        yc = ycat[b2]
        S.dma(yc[:, 0:256], scr["y_nsa"][rows, :])
        S.dma(yc[:, 256:768], scr["y_ret"][rows, :])
        S.dma(yc[:, 768:1024], scr["y_rwkv"][rows, :])
        S.dma(yc[:, 1024:1280], scr["y_swa"][rows, :])
        S.dma(hTt[b2][:], (hv[:, :, rows], ("hT", t)))
        S.dma(xb[b2][:], (xres[rows, :], ("xres", t)))
        for fc in range(10):
            pt_ = ptrA if fc < 5 else ptrB
            S.tr(pt_[:, fc % 5, :], yc[:, fc * 128:(fc + 1) * 128], ident[:])
        S.copy(yT[b2][:, 0:5, :], ptrA[:], eng="act")
        S.copy(yT[b2][:, 5:10, :], ptrB[:], eng="dve")
        mg = merged[b2]
        for br in range(4):
            f0, nf = brch[br]
            for half in range(2):
                pgt = pg[cnt[0] % 2]
                pot = po[cnt[0] % 2]
                sgt = sg[cnt[0] % 2]
                tb_ = tmpb[cnt[0] % 2]
                cnt[0] += 1
                c0 = br * 1024 + half * 512
                for dc in range(8):
                    S.mm(pgt[:, :], hTt[b2][:, dc, :], Wg[:, dc, c0:c0 + 512], start=(dc == 0), stop=(dc == 7))
                for k in range(nf):
                    S.mm(pot[:, :], yT[b2][:, f0 + k, :], Wbr[:, f0 + k, half * 512:(half + 1) * 512],
                         start=(k == 0), stop=(k == nf - 1))
                S.act(sgt[:], pgt[:, :], AF.Sigmoid)
                mslice = mg[:, half * 512:(half + 1) * 512]
                if br == 0:
                    S.tt(mslice, sgt[:], pot[:, :], ALU.mult)
                else:
                    S.tt(tb_[:], sgt[:], pot[:, :], ALU.mult)
                    S.tt(mslice, mslice, tb_[:], ALU.add, eng="pool")
        S.copy(mbf[b2][:], mg[:], eng="act")

    def back(t):
        b2 = t % 2
        rows = slice(t * 128, (t + 1) * 128)
        for dc in range(8):
            S.tr(ptm[:, dc, :], mbf[b2][:, dc * 128:(dc + 1) * 128], ident[:])
        S.copy(mT[b2][:], ptm[:], eng="act")
        f = fsb[b2]
        s_ = st[b2]
        for half in range(2):
            for dc in range(8):
                S.mm(pf[:, :], mT[b2][:, dc, :], Wo[:, dc, half * 512:(half + 1) * 512],
                     start=(dc == 0), stop=(dc == 7))
            S.act(f[:, half * 512:(half + 1) * 512], pf[:, :], AF.Copy)
            S.act(junk[:, half * 512:(half + 1) * 512], pf[:, :], AF.Square, accum_out=s_[:, half:half + 1])
        S.tt(s_[:, 2:3], s_[:, 0:1], s_[:, 1:2], ALU.add)
        S.ts(s_[:, 3:4], s_[:, 2:3], 1.0 / D, RMS_EPS, ALU.mult, ALU.add)
        S.act(s_[:, 4:5], s_[:, 3:4], AF.Sqrt)
        S.recip(s_[:, 5:6], s_[:, 4:5])
        S.stt(f[:], f[:], s_[:, 5:6], gpost[:], ALU.mult, ALU.mult)
        S.tt(xb[b2][:], xb[b2][:], f[:], ALU.add, eng="pool")
        S.dma((xres[rows, :], ("xres", t)), xb[b2][:], q="act")

    for t in range(NT + 1):
        if t < NT:
            front(t)
        if t >= 1:
            back(t - 1)


WNAMES = ['ffn1_pre_g', 'ffn1_post_g', 'ffn1_w_gate', 'ffn1_w_up', 'ffn1_w_down', 'mix_pre_g', 'mix_post_g',
          'w_in', 'nsa_cmp_pos_k', 'nsa_cmp_pos_v', 'nsa_cmp_k_w1', 'nsa_cmp_k_w2', 'nsa_cmp_v_w1',
          'nsa_cmp_v_w2', 'ret_gn_g', 'rwkv_mu', 'rwkv_w0', 'rwkv_w2', 'rwkv_a0', 'rwkv_a2', 'rwkv_g2',
          'rwkv_k_k', 'rwkv_k_a', 'rwkv_r_k', 'rwkv_ln_g', 'rwkv_ln_b', 'swa_sinks', 'w_br_nsa', 'w_br_ret',
          'w_br_rwkv', 'w_br_swa', 'w_out', 'ffn2_pre_g', 'ffn2_post_g', 'ffn2_w_gate', 'ffn2_w_up',
          'ffn2_w_down']

_CONSTS = None


def band_masks(window, rels):
    p = np.arange(128)[:, None]
    ql = np.arange(512)[None, :]
    out = []
    for r in rels:
        d = ql - (r * 128 + p)
        out.append(((d >= 0) & (d < window)).astype(np.float32))
    return np.stack(out, 0)


def host_consts():
    global _CONSTS
    if _CONSTS is not None:
        return _CONSTS
    c = {}
    c["ident"] = np.eye(128, dtype=np.float32)
    pos = np.arange(S_LEN, dtype=np.float32)
    inv = np.power(np.float32(10000.0), -np.arange(32, dtype=np.float32) * 2.0 / 64).astype(np.float32)
    ang = pos[None, :] * inv[:, None]
    cos = np.cos(ang).astype(np.float32)
    sin = np.sin(ang).astype(np.float32)
    c["cosT"] = np.ascontiguousarray(np.concatenate([cos, cos, cos, cos], 0))
    c["sinT"] = np.ascontiguousarray(np.concatenate([-sin, sin, -sin, sin], 0))
    c["mask_swa"] = band_masks(128, range(-1, 4))
    ii = np.arange(256)
    qq = np.arange(S_LEN)
    c["cmpmaskT"] = (((16 * ii[:, None] + 31) <= qq[None, :]) & (ii[:, None] < 255)).astype(np.float32)
    jj = np.arange(64)
    c["overlap"] = (((16 * ii[:, None]) <= (64 * jj[None, :] + 63)) & ((16 * ii[:, None] + 31) >= 64 * jj[None, :])
                    & (ii[:, None] < 255)).astype(np.float32)
    cur = (qq // 64)[:, None]
    forced = (jj[None, :] == 0) | (jj[None, :] == cur) | (jj[None, :] == cur - 1)
    valid = jj[None, :] <= cur
    c["selbias"] = np.where(forced, 1e9, np.where(valid, 0.0, -1e9)).astype(np.float32)
    c["mask_win"] = band_masks(512, range(-4, 4))
    c["mask_causal"] = band_masks(10 ** 7, range(0, 4))
    kt_ = np.arange(NT)[None, :, None]
    pp = np.arange(128)[None, None, :]
    c["Eall"] = (jj[:, None, None] == (2 * kt_ + pp // 64)).astype(np.float32)
    c["rw_reset"] = np.ascontiguousarray(np.broadcast_to((np.arange(512) % 64 != 0).astype(np.float32)[None, :], (64, 512)))
    tt_ = np.arange(64)[:, None, None]
    ss_ = np.arange(64)[None, None, :]
    one4 = np.ones((1, 4, 1), dtype=np.float32)
    c["rw_mlo"] = np.ascontiguousarray((ss_ < tt_).astype(np.float32) * one4)
    c["rw_mup"] = np.ascontiguousarray((ss_ > tt_).astype(np.float32) * one4)
    c["rw_mupi"] = np.ascontiguousarray((ss_ >= tt_).astype(np.float32) * one4)
    c["rw_I4"] = np.ascontiguousarray((ss_ == tt_).astype(np.float32) * one4)
    gam = (1.0 - np.power(2.0, -5.0 - np.arange(4, dtype=np.float64)))
    m = np.arange(128)[:, None, None]
    cc = np.arange(128)[None, None, :]
    gg = gam[None, :, None]
    dm = np.where(cc >= m, np.power(gg, np.maximum(cc - m, 0)), 0.0) * 0.125
    c["ret_dmaskT"] = np.ascontiguousarray(dm.astype(np.float32))
    c["ret_zeta"] = np.ascontiguousarray((np.power(gam[None, :], 127 - np.arange(128)[:, None]) * 0.125).astype(np.float32))
    xi = np.power(gam[None, :, None], np.arange(128)[None, None, :] + 1.0)
    c["ret_xiT"] = np.ascontiguousarray(np.broadcast_to(xi, (64, 4, 128)).astype(np.float32))
    c["ret_gch"] = np.ascontiguousarray(np.broadcast_to(np.power(gam, 128.0)[None, :], (64, 4)).astype(np.float32))
    _CONSTS = c
    return c


def derived_weights(inputs):
    idx = w_in_index_sets()
    out = {}
    w_in = np.asarray(inputs["w_in"], dtype=np.float32)
    for n, ix in idx.items():
        out["w" + n] = np.ascontiguousarray(w_in[:, :, ix])
    pk = np.asarray(inputs["nsa_cmp_pos_k"], dtype=np.float32)
    pv = np.asarray(inputs["nsa_cmp_pos_v"], dtype=np.float32)
    t64 = lambda n: np.asarray(inputs[n], dtype=np.float32).reshape(DEPTH, 4, 64).transpose(0, 2, 1)
    out["rwkv_cols"] = np.ascontiguousarray(np.concatenate(
        [t64("rwkv_w0"), t64("rwkv_a0"), t64("rwkv_k_k"), t64("rwkv_k_a"), t64("rwkv_r_k")], axis=2))
    out["nsa_posT"] = np.ascontiguousarray(np.stack([pk.transpose(0, 2, 1), pv.transpose(0, 2, 1)], axis=1))
    return out


SCRATCH = {
    "hT_d": ([D, S_LEN], BF16),
    "y_swa": ([S_LEN, 256], BF16),
    "y_ret": ([S_LEN, 512], BF16),
    "y_nsa": ([S_LEN, 256], BF16),
    "y_rwkv": ([S_LEN, 256], BF16),
    "rwT_d": ([4, 6, 64, S_LEN], BF16),
    "rw_gC_d": ([64, 4, 64], F32),
    "rw_v_d": ([S_LEN, 256], BF16),
    "rw_g_d": ([S_LEN, 256], F32),
    "rw_b_d": ([S_LEN, 4], F32),
    "nsaT_d": ([12, 64, S_LEN], BF16),
    "nsa_v_d": ([S_LEN, 128], BF16),
    "nsa_g_d": ([S_LEN, 12], F32),
    "ocmp_d": ([S_LEN, 256], BF16),
    "selT_d": ([64, S_LEN], BF16),
}


def default_phases():
    pl = [("copyin", None)]
    for l in range(DEPTH):
        pl += [("ffn1", l), ("mixpre", l), ("swa", l), ("ret", l), ("nsa_a", l), ("nsa_b", l), ("nsa_c", l),
               ("rwkv_a", l), ("rwkv_b", l), ("merge", l), ("ffn2", l)]
    return pl


def build(shapes, phases=None, dbg=()):
    nc = bass.Bass("TRN2", target_bir_lowering=False)
    x_in = nc.dram_tensor("x", [S_LEN, D], F32, kind="ExternalInput").ap()
    w = {}
    for n in shapes:
        w[n] = nc.dram_tensor(n, list(shapes[n]), F32, kind="ExternalInput").ap()
    cst = {}
    for n, a in host_consts().items():
        cst[n] = nc.dram_tensor("c_" + n, list(a.shape), F32, kind="ExternalInput").ap()
    y = nc.dram_tensor("y", [S_LEN, D], F32, kind="ExternalOutput").ap()
    scr = {}
    for n, (shp, dt_) in SCRATCH.items():
        if n in dbg:
            scr[n] = nc.dram_tensor(n, shp, dt_, kind="ExternalOutput").ap()
        else:
            scr[n] = nc.dram_tensor(n, shp, dt_).ap()
    xres = y
    plist = default_phases() if phases is None else phases
    with ExitStack() as gst:
        S = Sched(nc, gst)
        for pi, (pn, l) in enumerate(plist):
            with ExitStack() as pst:
                S.stack = pst
                S.phase = pi
                if pn == "copyin":
                    for t in range(0, NT, 4):
                        S.dma((xres[t * 128:(t + 4) * 128, :], ("xres", t), ("xres", t + 1), ("xres", t + 2),
                               ("xres", t + 3)), x_in[t * 128:(t + 4) * 128, :], q="sp")
                elif pn in ("ffn1", "ffn2"):
                    phase_ffn(S, nc, xres, w, l, pn, cst["ident"])
                elif pn == "mixpre":
                    phase_mixpre(S, nc, xres, w, l, scr["hT_d"], cst["ident"])
                elif pn == "swa":
                    phase_swa(S, nc, w, l, scr["hT_d"], w["wswa"], cst, scr["y_swa"])
                elif pn == "nsa_a":
                    phase_nsa_a(S, nc, w, l, scr["hT_d"], w["wnsa"], cst, scr)
                elif pn == "nsa_b":
                    phase_nsa_b(S, nc, w, l, cst, scr)
                elif pn == "nsa_c":
                    phase_nsa_c(S, nc, w, l, cst, scr, scr["y_nsa"])
                elif pn == "rwkv_a":
                    phase_rwkv_a(S, nc, w, l, scr["hT_d"], w["wr"], cst, scr)
                elif pn == "rwkv_b":
                    phase_rwkv_b(S, nc, w, l, cst, scr, scr["y_rwkv"])
                elif pn == "merge":
                    phase_merge(S, nc, xres, w, l, scr["hT_d"], w["wgate"], cst, scr)
                elif pn == "ret":
                    phase_ret(S, nc, w, l, scr["hT_d"], w["wret"], cst, scr["y_ret"], cst["ident"])
                else:
                    raise ValueError(pn)
                S.barrier()
                S.emit(final=(pi == len(plist) - 1))
    return nc


def make_in_maps(inputs, cores):
    base = {k: np.ascontiguousarray(inputs[k], dtype=np.float32) for k in WNAMES}
    base.update(derived_weights(inputs))
    shapes = {k: v.shape for k, v in base.items()}
    for k, a in host_consts().items():
        base["c_" + k] = a
    x = np.asarray(inputs["x"], dtype=np.float32)
    in_maps = []
    for i in cores:
        m = dict(base)
        m["x"] = np.ascontiguousarray(x[i])
        in_maps.append(m)
    return shapes, in_maps


def kernel(**inputs):
    n = 8
    shapes, in_maps = make_in_maps(inputs, list(range(n)))
    nc = build(shapes)
    res = run_bass_kernel_spmd(nc, in_maps, core_ids=list(range(n)))
    return np.stack([r["y"] for r in res.results], axis=0)
```

```python
import numpy as np
from contextlib import ExitStack
import concourse.bass as bass
import concourse.mybir as mybir
from concourse.bass_utils import run_bass_kernel_spmd

F32 = mybir.dt.float32
BF16 = mybir.dt.bfloat16
AF = mybir.ActivationFunctionType
ALU = mybir.AluOpType
AX = mybir.AxisListType

S_LEN = 4096
D = 1024
DFF = 2816
NT = S_LEN // 128
DEPTH = 2
RMS_EPS = 1e-6

ENGS = ("pe", "act", "dve", "pool", "sp")
DMA_K = 16


def _kref(x):
    if isinstance(x, tuple):
        return x[0], tuple(x[1:])
    return x, (x.tensor.name,)


class Sched:
    def __init__(self, nc, stack):
        self.nc = nc
        self.gstack = stack
        self.esem = {e: stack.enter_context(nc.semaphore("es_" + e)) for e in ENGS if e != "sp"}
        self.dsem = {q: [stack.enter_context(nc.semaphore("ds_%s%d" % (q, i))) for i in range(DMA_K)]
                     for q in ("sp", "act", "pool")}
        self.ccount = {e: 0 for e in ENGS}
        self.qcount = {q: 0 for q in ("sp", "act", "pool")}
        self.wm = {e: {} for e in ENGS}
        self.pending = {e: [] for e in ENGS}
        self.keys = {}
        self.post_barrier = {e: set() for e in ENGS}
        self.stack = None
        self.uid = 0
        self.phase = 0

    def sb(self, name, shape, dtype):
        self.uid += 1
        return self.stack.enter_context(self.nc.sbuf_tensor("%s_ph%d_%d" % (name, self.phase, self.uid), list(shape), dtype))

    def ps(self, name, shape, dtype=F32):
        self.uid += 1
        return self.stack.enter_context(self.nc.psum_tensor("%s_ph%d_%d" % (name, self.phase, self.uid), list(shape), dtype))

    def _st(self, key):
        st = self.keys.get(key)
        if st is None:
            st = {"W": {}, "R": {}, "Wd": {}, "Rd": {}}
            self.keys[key] = st
        return st

    def op(self, eng, fn, reads=(), writes=(), dma=False):
        deps = set()
        rkeys, wkeys = [], []
        for r in reads:
            if r is None:
                continue
            _, ks = _kref(r)
            rkeys.extend(ks)
        for w in writes:
            if w is None:
                continue
            _, ks = _kref(w)
            wkeys.extend(ks)
        for k in rkeys:
            st = self._st(k)
            for e, c in st["W"].items():
                deps.add((e, c))
            for q, js in st["Wd"].items():
                for j in js:
                    deps.add(("dma", q, j))
        for k in wkeys:
            st = self._st(k)
            for e, c in st["W"].items():
                deps.add((e, c))
            for e, c in st["R"].items():
                if e == eng and not dma:
                    continue
                deps.add((e, c))
            for q, js in st["Wd"].items():
                for j in js:
                    deps.add(("dma", q, j))
            for q, js in st["Rd"].items():
                for j in js:
                    deps.add(("dma", q, j))
        if eng == "pe":
            deps = {d for d in deps if d[0] != "pe"}
        deps |= self.post_barrier[eng]
        self.post_barrier[eng] = set()
        if dma:
            j = self.qcount[eng]
            self.qcount[eng] += 1
            rec = ("dma", eng, j)
            for k in rkeys:
                l = self._st(k)["Rd"].setdefault(eng, [])
                l.append(j)
                if len(l) > DMA_K:
                    del l[0]
            for k in wkeys:
                l = self._st(k)["Wd"].setdefault(eng, [])
                l.append(j)
                if len(l) > DMA_K:
                    del l[0]
            self.pending[eng].append((fn, deps, True, j))
        else:
            self.ccount[eng] += 1
            c = self.ccount[eng]
            for k in rkeys:
                self._st(k)["R"][eng] = c
            for k in wkeys:
                self._st(k)["W"][eng] = c
            self.pending[eng].append((fn, deps, False, c))

    def barrier(self):
        allc = set()
        for e in ENGS:
            if e != "sp" and self.ccount[e] > 0:
                allc.add((e, self.ccount[e]))
        for q in ("sp", "act", "pool"):
            n = self.qcount[q]
            for j in range(max(0, n - DMA_K), n):
                allc.add(("dma", q, j))
        for e in ENGS:
            self.post_barrier[e] |= allc
        self.keys = {}

    def _emit_engine(self, ename, eng, final=False):
        wm = self.wm[ename]
        for fn, deps, is_dma, idx in self.pending[ename]:
            waits = {}
            dmax = {}
            for d in deps:
                if d[0] == "dma":
                    dmax[d[1]] = max(dmax.get(d[1], -1), d[2])
            for d in deps:
                if d[0] == "dma":
                    q, j = d[1], d[2]
                    if j <= dmax[q] - DMA_K:
                        continue
                    sem = self.dsem[q][j % DMA_K]
                    val = 16 * (j // DMA_K + 1)
                else:
                    sem = self.esem[d[0]]
                    val = d[1]
                key = id(sem)
                if key not in waits or waits[key][1] < val:
                    waits[key] = (sem, val)
            if is_dma and idx >= DMA_K:
                sem = self.dsem[ename][idx % DMA_K]
                val = 16 * (idx // DMA_K)
                key = id(sem)
                if key not in waits or waits[key][1] < val:
                    waits[key] = (sem, val)
            for key, (sem, val) in waits.items():
                if wm.get(key, 0) < val:
                    eng.wait_ge(sem, val)
                    wm[key] = val
            ins = fn(eng)
            if is_dma:
                ins.then_inc(self.dsem[ename][idx % DMA_K], 16)
            else:
                ins.then_inc(self.esem[ename], 1)
        self.pending[ename] = []
        if final and ename in self.dsem:
            n = self.qcount[ename]
            for j in range(max(0, n - DMA_K), n):
                sem = self.dsem[ename][j % DMA_K]
                val = 16 * (j // DMA_K + 1)
                if wm.get(id(sem), 0) < val:
                    eng.wait_ge(sem, val)
                    wm[id(sem)] = val

    def emit(self, final=False):
        with self.nc.Block() as block:
            @block.tensor
            def _(e):
                self._emit_engine("pe", e, final)

            @block.scalar
            def _(e):
                self._emit_engine("act", e, final)

            @block.vector
            def _(e):
                self._emit_engine("dve", e, final)

            @block.gpsimd
            def _(e):
                self._emit_engine("pool", e, final)

            @block.sync
            def _(e):
                self._emit_engine("sp", e, final)

    def dma(self, out, in_, q="sp"):
        o, i = _kref(out)[0], _kref(in_)[0]
        self.op(q, lambda e: e.dma_start(out=o, in_=i), reads=[in_], writes=[out], dma=True)

    def mm(self, out, lhsT, rhs, start=True, stop=True):
        o, l, r = _kref(out)[0], _kref(lhsT)[0], _kref(rhs)[0]
        self.op("pe", lambda e: e.matmul(o, l, r, start=start, stop=stop), reads=[lhsT, rhs], writes=[out])

    def tr(self, out, in_, ident):
        o, i, d = _kref(out)[0], _kref(in_)[0], _kref(ident)[0]
        self.op("pe", lambda e: e.transpose(o, i, d), reads=[in_, ident], writes=[out])

    def act(self, out, in_, func, bias=None, scale=None, accum_out=None):
        o, i = _kref(out)[0], _kref(in_)[0]
        kw = {}
        rd = [in_]
        wr = [out]
        if bias is not None:
            if isinstance(bias, (int, float)):
                kw["bias"] = bias
            else:
                kw["bias"] = _kref(bias)[0]
                rd.append(bias)
        if scale is not None:
            if isinstance(scale, (int, float)):
                kw["scale"] = scale
            else:
                kw["scale"] = _kref(scale)[0]
                rd.append(scale)
        if accum_out is not None:
            kw["accum_out"] = _kref(accum_out)[0]
            wr.append(accum_out)
        self.op("act", lambda e: e.activation(o, i, func, **kw), reads=rd, writes=wr)

    def tt(self, out, in0, in1, op, eng="dve"):
        o, a, b = _kref(out)[0], _kref(in0)[0], _kref(in1)[0]
        self.op(eng, lambda e: e.tensor_tensor(o, a, b, op), reads=[in0, in1], writes=[out])

    def ts(self, out, in0, s1, s2, op0, op1=None, eng="dve", accum_out=None):
        o, a = _kref(out)[0], _kref(in0)[0]
        rd = [in0]
        wr = [out]

        def sc(s):
            if s is None or isinstance(s, (int, float)):
                return s
            rd.append(s)
            return _kref(s)[0]
        v1, v2 = sc(s1), sc(s2)
        kw = {}
        if op1 is not None:
            kw["op1"] = op1
        if accum_out is not None:
            kw["accum_out"] = _kref(accum_out)[0]
            wr.append(accum_out)
        self.op(eng, lambda e: e.tensor_scalar(o, a, v1, v2, op0, **kw), reads=rd, writes=wr)

    def stt(self, out, in0, scalar, in1, op0, op1, accum_out=None):
        o, a, b = _kref(out)[0], _kref(in0)[0], _kref(in1)[0]
        rd = [in0, in1]
        wr = [out]
        if isinstance(scalar, (int, float)):
            s = scalar
        else:
            s = _kref(scalar)[0]
            rd.append(scalar)
        kw = {}
        if accum_out is not None:
            kw["accum_out"] = _kref(accum_out)[0]
            wr.append(accum_out)
        self.op("dve", lambda e: e.scalar_tensor_tensor(o, a, s, b, op0, op1, **kw), reads=rd, writes=wr)

    def copy(self, out, in_, eng="dve"):
        o, i = _kref(out)[0], _kref(in_)[0]
        if eng == "act":
            self.op("act", lambda e: e.copy(o, i), reads=[in_], writes=[out])
        else:
            self.op(eng, lambda e: e.tensor_copy(o, i), reads=[in_], writes=[out])

    def recip(self, out, in_):
        o, i = _kref(out)[0], _kref(in_)[0]
        self.op("dve", lambda e: e.reciprocal(o, i), reads=[in_], writes=[out])

    def memset(self, out, val, eng="dve"):
        o = _kref(out)[0]
        self.op(eng, lambda e: e.memset(o, val), reads=[], writes=[out])

    def reduce(self, out, in_, op, axis=None, eng="dve"):
        o, i = _kref(out)[0], _kref(in_)[0]
        ax = AX.X if axis is None else axis
        self.op(eng, lambda e: e.tensor_reduce(o, i, ax, op), reads=[in_], writes=[out])


def load_w_bf16(S, dst, src_dram, nchunk, q="pool"):
    v = src_dram.rearrange("(c p) f -> p c f", p=128)
    step = 4 if nchunk % 4 == 0 else nchunk
    for c in range(0, nchunk, step):
        S.dma(dst[:, c:c + step, :], v[:, c:c + step, :], q=q)


def bcast_row(S, dst, src_row_ap, q="sp"):
    S.dma(dst, src_row_ap.partition_broadcast(128), q=q)


def phase_ffn(S, nc, xres, w, l, pre, ident_d):
    TB = 512
    NSUB = TB // 128
    NFC = DFF // 128
    GRP = (6, 6, 5, 5)
    G0 = (0, 6, 12, 17)
    wg = [S.sb("wg", [128, 8, n * 128], BF16) for n in GRP]
    wu = [S.sb("wu", [128, 8, n * 128], BF16) for n in GRP]
    wd = [S.sb("wd", [128, 11, D], BF16) for _ in range(2)]
    fcmap = []
    for g, n in enumerate(GRP):
        for k in range(n):
            fcmap.append((g, k))
    gpre = S.sb("gpre", [128, D], F32)
    gpost = S.sb("gpost", [128, D], F32)
    ident = S.sb("ident", [128, 128], BF16)
    S.dma(ident[:], ident_d[:, :], q="pool")
    bcast_row(S, gpre[:], w[pre + "_pre_g"][l:l + 1, :])
    bcast_row(S, gpost[:], w[pre + "_post_g"][l:l + 1, :])
    S.ts(gpost[:], gpost[:], 0.5, None, ALU.mult, eng="pool")
    vg = w[pre + "_w_gate"][l].rearrange("(c p) f -> p c f", p=128)
    vu = w[pre + "_w_up"][l].rearrange("(c p) f -> p c f", p=128)
    vd = w[pre + "_w_down"][l].rearrange("(c p) f -> p c f", p=128)
    for g, n in enumerate(GRP):
        c0 = G0[g] * 128
        S.dma(wg[g][:, :, :], vg[:, :, c0:c0 + n * 128], q="pool")
        S.dma(wu[g][:, :, :], vu[:, :, c0:c0 + n * 128], q="pool")
    for i in range(2):
        S.dma(wd[i][:, :, :], vd[:, i * 11:(i + 1) * 11, :], q="pool")

    xf = [S.sb("xf", [128, D], F32) for _ in range(2)]
    xbk = [S.sb("xbk", [128, D], F32) for _ in range(2)]
    hb = [S.sb("hb", [128, D], BF16) for _ in range(2)]
    junk = S.sb("junk", [128, D], BF16)
    hT = S.sb("hT", [128, 8, TB], BF16)
    actT = S.sb("actT", [128, NFC, TB], BF16)
    sg = [S.sb("sg", [128, TB], F32) for _ in range(2)]
    fsb = S.sb("fsb", [128, D], F32)
    st = [S.sb("st", [128, 8], F32) for _ in range(4)]
    ptr = [S.ps("ptr", [128, 8, 128], BF16) for _ in range(2)]
    pgt_ = [S.ps("pg", [128, 512], F32) for _ in range(2)]
    put_ = [S.ps("pu", [128, 512], F32) for _ in range(2)]
    po = [S.ps("po", [128, 512], F32) for _ in range(2)]
    nblk = S_LEN // TB
    cnt = [0]

    def front(b):
        for s in range(NSUB):
            t = b * NSUB + s
            x = xf[cnt[0] % 2]
            stt_ = st[cnt[0] % 4]
            h = hb[cnt[0] % 2]
            p = ptr[cnt[0] % 2]
            cnt[0] += 1
            S.dma(x[:], (xres[t * 128:(t + 1) * 128, :], ("xres", t)))
            S.act(junk[:], x[:], AF.Square, accum_out=stt_[:, 0:1])
            S.ts(stt_[:, 1:2], stt_[:, 0:1], 1.0 / D, RMS_EPS, ALU.mult, ALU.add)
            S.act(stt_[:, 2:3], stt_[:, 1:2], AF.Sqrt)
            S.recip(stt_[:, 3:4], stt_[:, 2:3])
            S.stt(h[:], x[:], stt_[:, 3:4], gpre[:], ALU.mult, ALU.mult)
            for dc in range(8):
                S.tr(p[:, dc, :], h[:, dc * 128:(dc + 1) * 128], ident[:])
            S.copy(hT[:, :, s * 128:(s + 1) * 128], p[:, :, :], eng="act")

    def gateup(b):
        for fc in range(NFC):
            pg1 = pgt_[fc % 2]
            pu1 = put_[fc % 2]
            g_, k_ = fcmap[fc]
            for dc in range(8):
                S.mm(pg1[:, :], wg[g_][:, dc, k_ * 128:(k_ + 1) * 128], hT[:, dc, :],
                     start=(dc == 0), stop=(dc == 7))
            for dc in range(8):
                S.mm(pu1[:, :], wu[g_][:, dc, k_ * 128:(k_ + 1) * 128], hT[:, dc, :],
                     start=(dc == 0), stop=(dc == 7))
            sgt = sg[fc % 2]
            S.act(sgt[:], pg1[:, :], AF.Silu)
            S.tt(actT[:, fc, :], sgt[:], pu1[:, :], ALU.mult)

    def down(b):
        for s in range(NSUB):
            t = b * NSUB + s
            x = xbk[cnt[0] % 2]
            stt_ = st[cnt[0] % 4]
            cnt[0] += 1
            f = fsb
            S.dma(x[:], (xres[t * 128:(t + 1) * 128, :], ("xres", t)))
            for dh in range(2):
                pot = po[dh]
                for fc in range(NFC):
                    S.mm(pot[:], actT[:, fc, s * 128:(s + 1) * 128], wd[fc // 11][:, fc % 11, dh * 512:(dh + 1) * 512],
                         start=(fc == 0), stop=(fc == NFC - 1))
                S.act(f[:, dh * 512:(dh + 1) * 512], pot[:], AF.Copy)
                S.act(junk[:, dh * 512:(dh + 1) * 512], pot[:], AF.Square, accum_out=stt_[:, dh:dh + 1])
            S.tt(stt_[:, 2:3], stt_[:, 0:1], stt_[:, 1:2], ALU.add)
            S.ts(stt_[:, 3:4], stt_[:, 2:3], 1.0 / D, RMS_EPS, ALU.mult, ALU.add)
            S.act(stt_[:, 4:5], stt_[:, 3:4], AF.Sqrt)
            S.recip(stt_[:, 5:6], stt_[:, 4:5])
            S.stt(f[:], f[:], stt_[:, 5:6], gpost[:], ALU.mult, ALU.mult)
            S.tt(x[:], x[:], f[:], ALU.add, eng="pool")
            S.dma((xres[t * 128:(t + 1) * 128, :], ("xres", t)), x[:], q="act")

    front(0)
    for b in range(nblk):
        gateup(b)
        if b + 1 < nblk:
            front(b + 1)
        down(b)


HD = 64


def col_layout():
    spec = (('nsa_q', 256), ('nsa_k_cmp', 64), ('nsa_v_cmp', 64), ('nsa_k_slc', 64), ('nsa_v_slc', 64),
            ('nsa_k_win', 64), ('nsa_v_win', 64), ('nsa_gate', 12), ('ret_q', 256), ('ret_k', 256),
            ('ret_v', 512), ('ret_g', 512), ('rwkv', 1024), ('swa_q', 256), ('swa_k', 128), ('swa_v', 128),
            ('branch_gate', 4096))
    lay, s = {}, 0
    for n, wd_ in spec:
        lay[n] = (s, s + wd_)
        s += wd_
    return lay, s


def _partner(cols):
    out = []
    for i in range(0, len(cols), 64):
        blk = cols[i:i + 64]
        out.extend(blk[32:64])
        out.extend(blk[0:32])
    return out


def w_in_index_sets():
    lay, _ = col_layout()
    r = lambda n: list(range(*lay[n]))
    idx = {}
    q = r('swa_q')
    k = r('swa_k')
    kk0 = k[0:64] + k[0:64]
    kk1 = k[64:128] + k[64:128]
    idx['swa'] = q + _partner(q) + kk0 + kk1 + _partner(kk0) + _partner(kk1) + r('swa_v')
    nq, ksl, kwi = r('nsa_q'), r('nsa_k_slc'), r('nsa_k_win')
    idx['nsa'] = (nq + _partner(nq) + r('nsa_k_cmp') + r('nsa_v_cmp') + ksl + _partner(ksl) + kwi + _partner(kwi)
                  + r('nsa_v_slc') + r('nsa_v_win') + r('nsa_gate'))
    idx['r'] = r('rwkv')
    idx['gate'] = r('branch_gate')
    rq, rk = r('ret_q'), r('ret_k')
    idx['ret'] = rq + _partner(rq) + rk + _partner(rk) + r('ret_v') + r('ret_g')
    return idx


def phase_mixpre(S, nc, xres, w, l, hT_d, ident_d):
    g = S.sb("g", [128, D], F32)
    ident = S.sb("ident", [128, 128], BF16)
    S.dma(ident[:], ident_d[:, :], q="pool")
    bcast_row(S, g[:], w["mix_pre_g"][l:l + 1, :])
    xb = [S.sb("xb", [128, D], F32) for _ in range(3)]
    hb = [S.sb("hb", [128, D], BF16) for _ in range(2)]
    junk = S.sb("junk", [128, D], BF16)
    hT = [S.sb("hT", [128, 8, 128], BF16) for _ in range(3)]
    st = [S.sb("st", [128, 8], F32) for _ in range(3)]
    ptr = [S.ps("ptr", [128, 8, 128], BF16) for _ in range(2)]
    hv = hT_d.rearrange("(c p) t -> p c t", p=128)
    for t in range(NT):
        x = xb[t % 3]
        s_ = st[t % 3]
        h = hb[t % 2]
        p = ptr[t % 2]
        o = hT[t % 3]
        S.dma(x[:], (xres[t * 128:(t + 1) * 128, :], ("xres", t)))
        S.act(junk[:], x[:], AF.Square, accum_out=s_[:, 0:1])
        S.ts(s_[:, 1:2], s_[:, 0:1], 1.0 / D, RMS_EPS, ALU.mult, ALU.add)
        S.act(s_[:, 2:3], s_[:, 1:2], AF.Sqrt)
        S.recip(s_[:, 3:4], s_[:, 2:3])
        S.stt(h[:], x[:], s_[:, 3:4], g[:], ALU.mult, ALU.mult)
        for dc in range(8):
            S.tr(p[:, dc, :], h[:, dc * 128:(dc + 1) * 128], ident[:])
        S.copy(o[:], p[:], eng="act")
        S.dma((hv[:, :, t * 128:(t + 1) * 128], ("hT", t)), o[:], q="act")


def proj_feat(S, dst, wsb, col0, hT, pps, rope=None, npart=128):
    for tb in range(8):
        tsl = slice(tb * 512, (tb + 1) * 512)
        p0 = pps[tb % 2]
        for dc in range(8):
            S.mm(p0[0:npart, 0:512], wsb[:, dc, col0:col0 + npart], hT[:, dc, tsl],
                 start=(dc == 0), stop=(dc == 7))
        if rope is None:
            S.copy(dst[0:npart, tsl], p0[0:npart, 0:512], eng="act")
        else:
            pc, cos_d, sin_d, cst, tmp, pps2 = rope
            p1 = pps2[tb % 2]
            for dc in range(8):
                S.mm(p1[0:npart, 0:512], wsb[:, dc, pc:pc + npart], hT[:, dc, tsl],
                     start=(dc == 0), stop=(dc == 7))
            cs = cst[tb % 2]
            S.dma(cs[:, 0, :], cos_d[:, tsl])
            S.dma(cs[:, 1, :], sin_d[:, tsl])
            t1 = tmp[tb % 2]
            S.tt(t1[0:npart, 0, :], p0[0:npart, 0:512], cs[0:npart, 0, :], ALU.mult)
            S.tt(t1[0:npart, 1, :], p1[0:npart, 0:512], cs[0:npart, 1, :], ALU.mult)
            S.tt(dst[0:npart, tsl], t1[0:npart, 0, :], t1[0:npart, 1, :], ALU.add, eng="pool")


def proj_tok(S, dst_fn, wsb, col0, ncol, hT, pps, eng="act", view=None):
    for t in range(NT):
        p0 = pps[t % 2]
        for dc in range(8):
            S.mm(p0[:, 0:ncol], hT[:, dc, t * 128:(t + 1) * 128], wsb[:, dc, col0:col0 + ncol],
                 start=(dc == 0), stop=(dc == 7))
        src = p0[:, 0:ncol]
        if view is not None:
            src = view(src)
        S.copy(dst_fn(t), src, eng=eng)


def attn_qblock(S, nheads, q_of, k_of, v_of, tiles, scale, pss, pacc, pbuf, cnt, skew=2):
    items = [(i, kt, h, mf) for i, (kt, mf) in enumerate(tiles) for h in range(nheads)]
    n = len(items)
    masks = {}
    pbs = {}
    skew = min(skew, len(pss) - 1)

    def front(j):
        i, kt, h, mf = items[j]
        if h == 0:
            masks[i] = mf() if mf is not None else None
        ps = pss[(cnt + j) % len(pss)]
        pb = pbuf[(cnt + j) % len(pbuf)]
        pbs[j] = pb
        S.mm(ps[:, 0:512], k_of(h, kt), q_of(h), start=True, stop=True)
        S.act(pb[:], ps[:, 0:512], AF.Exp, scale=scale)
        if masks[i] is not None:
            S.tt(pb[:], pb[:], masks[i], ALU.mult)

    def back(j):
        i, kt, h, mf = items[j]
        pb = pbs.pop(j)
        for sub in range(4):
            S.mm(pacc[h][:, sub, :], pb[:, sub * 128:(sub + 1) * 128], v_of(h, kt),
                 start=(i == 0 and sub == 0), stop=(i == len(tiles) - 1))

    for j in range(n + skew):
        if j < n:
            front(j)
        if j - skew >= 0:
            back(j - skew)
    return cnt + n


def attn_finish(S, nheads, pacc, ybuf, st, zextra=None, gate_of=None, accumulate=False):
    for h in range(nheads):
        s_ = st[h % len(st)]
        if zextra is not None:
            S.ts(s_[:, 0:4], pacc[h][:, :, 64], zextra(h), None, ALU.add)
        else:
            S.ts(s_[:, 0:4], pacc[h][:, :, 64], 1e-30, None, ALU.max)
        S.recip(s_[:, 4:8], s_[:, 0:4])
        if gate_of is not None:
            S.tt(s_[:, 4:8], s_[:, 4:8], gate_of(h), ALU.mult)
        for sub in range(4):
            yo = ybuf[:, sub, h * 64:(h + 1) * 64]
            if not accumulate:
                S.ts(yo, pacc[h][:, sub, 0:64], s_[:, 4 + sub:5 + sub], None, ALU.mult)
            else:
                S.stt(yo, pacc[h][:, sub, 0:64], s_[:, 4 + sub:5 + sub], yo, ALU.mult, ALU.add)


def phase_swa(S, nc, w, l, hT_d, wswa_d, cst, y_d):
    NCOL = 8 * 128 + 128
    wsb = S.sb("wsb", [128, 8, NCOL], BF16)
    load_w_bf16(S, wsb, wswa_d[l], 8)
    hT = S.sb("hTall", [128, 8, S_LEN], BF16)
    hv = hT_d.rearrange("(c p) t -> p c t", p=128)
    for c in range(8):
        S.dma(hT[:, c, :], (hv[:, c, :],) + tuple(("hT", t) for t in range(NT)))
    qT = [S.sb("qT", [128, S_LEN], BF16) for _ in range(2)]
    kT = [S.sb("kT", [128, S_LEN], BF16) for _ in range(2)]
    vext = S.sb("vext", [128, NT, 2, 65], BF16)
    pps = [S.ps("pp", [128, 512], F32) for _ in range(2)]
    pps2 = [S.ps("pp2", [128, 512], F32) for _ in range(2)]
    cstt = [S.sb("cs", [128, 2, 512], F32) for _ in range(2)]
    tmp = [S.sb("tmp", [128, 2, 512], F32) for _ in range(2)]
    rope = lambda pc: (pc, cst["cosT"], cst["sinT"], cstt, tmp, pps2)
    proj_feat(S, qT[0], wsb, 0, hT, pps, rope(256))
    proj_feat(S, qT[1], wsb, 128, hT, pps, rope(384))
    proj_feat(S, kT[0], wsb, 512, hT, pps, rope(768))
    proj_feat(S, kT[1], wsb, 640, hT, pps, rope(896))
    S.memset(vext[:, :, :, 64:65], 1.0, eng="pool")
    proj_tok(S, lambda t: vext[:, t, :, 0:64], wsb, 1024, 128, hT, pps,
             view=lambda a: a.rearrange("p (a b) -> p a b", a=2))
    masks = S.sb("masks", [128, 5, 512], BF16)
    for r in range(5):
        S.dma(masks[:, r, :], cst["mask_swa"][r], q="pool")
    sk = S.sb("sk", [128, 4], F32)
    S.dma(sk[:], w["swa_sinks"][l:l + 1, :].partition_broadcast(128))
    S.act(sk[:], sk[:], AF.Exp)
    pacc = [S.ps("pacc", [128, 4, 65], F32) for _ in range(4)]
    pbuf = [S.sb("pbuf", [128, 512], BF16) for _ in range(5)]
    ybuf = [S.sb("ybuf", [128, 4, 256], BF16) for _ in range(2)]
    st = [S.sb("st", [128, 8], F32) for _ in range(4)]
    cnt = 0
    for qb in range(8):
        qs = slice(qb * 512, (qb + 1) * 512)
        tiles = [(4 * qb + r, (lambda r=r: masks[:, r + 1, :])) for r in range(-1, 4) if 4 * qb + r >= 0]
        yb = ybuf[qb % 2]
        cnt = attn_qblock(
            S, 4,
            lambda h: qT[h // 2][(h % 2) * 64:(h % 2) * 64 + 64, qs],
            lambda h, kt: kT[h // 2][(h % 2) * 64:(h % 2) * 64 + 64, kt * 128:(kt + 1) * 128],
            lambda h, kt: vext[:, kt, h // 2, :],
            tiles, 0.125, pps + pps2, pacc, pbuf, cnt, skew=3)
        attn_finish(S, 4, pacc, yb, st, zextra=lambda h: sk[:, h:h + 1])
        S.dma(y_d[qb * 512:(qb + 1) * 512, :].rearrange("(s p) f -> p s f", p=128), yb[:], q="act")


def phase_ret(S, nc, w, l, hT_d, wret_d, cst, y_d, ident_d):
    wsb = S.sb("wsb", [128, 8, 2048], BF16)
    load_w_bf16(S, wsb, wret_d[l], 8)
    hT = S.sb("hTall", [128, 8, S_LEN], BF16)
    hv = hT_d.rearrange("(c p) t -> p c t", p=128)
    for c in range(8):
        S.dma(hT[:, c, :], (hv[:, c, :],) + tuple(("hT", t) for t in range(NT)))
    ident = S.sb("ident", [128, 128], BF16)
    S.dma(ident[:], ident_d[:, :], q="pool")
    qT = [S.sb("qT", [64, S_LEN], BF16) for _ in range(4)]
    kT = [S.sb("kT", [64, S_LEN], BF16) for _ in range(4)]
    pps = [S.ps("pp", [128, 512], F32) for _ in range(2)]
    pps2 = [S.ps("pp2", [128, 512], F32) for _ in range(2)]
    cstt = [S.sb("cs", [128, 2, 512], F32) for _ in range(2)]
    tmp = [S.sb("tmp", [128, 2, 512], F32) for _ in range(2)]
    rope = lambda pc: (pc, cst["cosT"], cst["sinT"], cstt, tmp, pps2)
    for h in range(4):
        proj_feat(S, qT[h], wsb, h * 64, hT, pps, rope(256 + h * 64), npart=64)
        proj_feat(S, kT[h], wsb, 512 + h * 64, hT, pps, rope(768 + h * 64), npart=64)
    dmask = S.sb("dmask", [128, 4, 128], F32)
    S.dma(dmask[:], cst["ret_dmaskT"][:, :, :])
    zt = S.sb("zt", [128, 4], F32)
    S.dma(zt[:], cst["ret_zeta"][:, :])
    xiT = S.sb("xiT", [64, 4, 128], F32)
    S.dma(xiT[:], cst["ret_xiT"][:, :, :])
    gch = S.sb("gch", [64, 4], F32)
    S.dma(gch[:], cst["ret_gch"][:, :])
    gng = S.sb("gng", [128, 512], F32)
    bcast_row(S, gng[:], w["ret_gn_g"][l:l + 1, :])
    R = S.sb("R", [64, 4, 128], F32)
    Rbf = S.sb("Rbf", [64, 4, 128], BF16)
    S.memset(R[:], 0.0)
    S.memset(Rbf[:], 0.0)
    po = [S.ps("po", [128, 4, 128], F32) for _ in range(2)]
    ptk = S.ps("ptk", [128, 4, 64], BF16)
    pin = pps2[0][:, :].rearrange("p (h c) -> p h c", h=4)
    pkv = pps2[1][:, :].rearrange("p (h e) -> p h e", h=4)
    vb = [S.sb("vb", [128, 512], BF16) for _ in range(2)]
    sgb = [S.sb("sgb", [128, 512], F32) for _ in range(2)]
    qx = [S.sb("qx", [64, 4, 128], BF16) for _ in range(2)]
    kz = [S.sb("kz", [128, 4, 64], BF16) for _ in range(2)]
    inm = [S.sb("inm", [128, 4, 128], BF16) for _ in range(2)]
    osb = [S.sb("osb", [128, 4, 128], F32) for _ in range(2)]
    sq = [S.sb("sq", [128, 4, 128], F32) for _ in range(2)]
    st = [S.sb("st", [128, 8, 4], F32) for _ in range(2)]
    yb = [S.sb("yb", [128, 512], BF16) for _ in range(2)]
    for t in range(NT):
        ts_ = slice(t * 128, (t + 1) * 128)
        b = t % 2
        for dc in range(8):
            S.mm(pps[0][:, :], hT[:, dc, ts_], wsb[:, dc, 1024:1536], start=(dc == 0), stop=(dc == 7))
        S.copy(vb[b][:], pps[0][:, :], eng="act")
        for dc in range(8):
            S.mm(pps[1][:, :], hT[:, dc, ts_], wsb[:, dc, 1536:2048], start=(dc == 0), stop=(dc == 7))
        S.act(sgb[b][:], pps[1][:, :], AF.Silu)
        for h in range(4):
            S.tr(ptk[:, h, :], kT[h][:, ts_], ident[0:64, 0:64])
        S.tt(kz[b][:], ptk[:, :, :], zt[:, :].unsqueeze(2).to_broadcast([128, 4, 64]), ALU.mult)
        for h in range(4):
            S.tt(qx[b][:, h, :], qT[h][:, ts_], xiT[:, h, :], ALU.mult, eng="pool")
        for h in range(4):
            S.mm(pin[:, h, :], kT[h][:, ts_], qT[h][:, ts_], start=True, stop=True)
        S.tt(inm[b][:], pin, dmask[:], ALU.mult)
        pot = po[b]
        for h in range(4):
            S.mm(pot[:, h, :], inm[b][:, h, :], vb[b][:, h * 128:(h + 1) * 128], start=True, stop=False)
            S.mm(pot[:, h, :], qx[b][:, h, :], Rbf[:, h, :], start=False, stop=True)
        for h in range(4):
            S.mm(pkv[0:64, h, :], kz[b][:, h, :], vb[b][:, h * 128:(h + 1) * 128], start=True, stop=True)
        for h in range(4):
            S.stt(R[:, h, :], R[:, h, :], gch[:, h:h + 1], pkv[0:64, h, :], ALU.mult, ALU.add)
        S.copy(Rbf[:], R[:], eng="act")
        o = osb[b]
        s_ = st[b]
        S.copy(o[:], pot[:], eng="act")
        S.reduce(s_[:, 0, :], o[:], ALU.add)
        S.tt(sq[b][:], o[:], o[:], ALU.mult, eng="pool")
        S.reduce(s_[:, 1, :], sq[b][:], ALU.add)
        S.ts(s_[:, 2, :], s_[:, 0, :], 1.0 / 128, None, ALU.mult)
        S.tt(s_[:, 3, :], s_[:, 2, :], s_[:, 2, :], ALU.mult)
        S.stt(s_[:, 4, :], s_[:, 1, :], 1.0 / 128, s_[:, 3, :], ALU.mult, ALU.subtract)
        S.ts(s_[:, 5, :], s_[:, 4, :], 1e-5, None, ALU.add)
        S.act(s_[:, 6, :], s_[:, 5, :], AF.Sqrt)
        S.recip(s_[:, 7, :], s_[:, 6, :])
        S.tt(o[:], o[:], s_[:, 2, :].unsqueeze(2).to_broadcast([128, 4, 128]), ALU.subtract)
        S.tt(o[:], o[:], s_[:, 7, :].unsqueeze(2).to_broadcast([128, 4, 128]), ALU.mult)
        of = o[:].rearrange("p h e -> p (h e)")
        S.tt(of, of, gng[:], ALU.mult, eng="pool")
        S.tt(yb[b][:], of, sgb[b][:], ALU.mult)
        S.dma(y_d[ts_, :], yb[b][:], q="act")


def load_hT(S, hT_d):
    hT = S.sb("hTall", [128, 8, S_LEN], BF16)
    hv = hT_d.rearrange("(c p) t -> p c t", p=128)
    for c in range(8):
        S.dma(hT[:, c, :], (hv[:, c, :],) + tuple(("hT", t) for t in range(NT)))
    return hT


def phase_nsa_a(S, nc, w, l, hT_d, wnsa_d, cst, scr):
    wsb = S.sb("wsb", [128, 8, 1036], BF16)
    load_w_bf16(S, wsb, wnsa_d[l], 8)
    hT = load_hT(S, hT_d)
    pps = [S.ps("pp", [128, 512], F32) for _ in range(2)]
    pps2 = [S.ps("pp2", [128, 512], F32) for _ in range(2)]
    cstt = [S.sb("cs", [128, 2, 512], F32) for _ in range(2)]
    tmp = [S.sb("tmp", [128, 2, 512], F32) for _ in range(2)]
    rope = lambda pc: (pc, cst["cosT"], cst["sinT"], cstt, tmp, pps2)
    stg = [S.sb("stg", [64, S_LEN], BF16) for _ in range(2)]
    outs = [(h, h * 64, None) for h in range(4)] + [(4 + h, h * 64, 256 + h * 64) for h in range(4)]
    outs += [(8, 512, None), (9, 576, None), (10, 640, 704), (11, 768, 832)]
    for n, (idx, col0, pc) in enumerate(outs):
        st_ = stg[n % 2]
        proj_feat(S, st_, wsb, col0, hT, pps, rope(pc) if pc is not None else None, npart=64)
        S.dma((scr["nsaT_d"][idx], ("nsaT", idx)), st_[:, :], q="act")
    vb = [S.sb("vb", [128, 128], BF16) for _ in range(2)]
    gb = [S.sb("gb", [128, 12], F32) for _ in range(2)]
    for t in range(NT):
        p0 = pps[t % 2]
        for dc in range(8):
            S.mm(p0[:, 0:140], hT[:, dc, t * 128:(t + 1) * 128], wsb[:, dc, 896:1036],
                 start=(dc == 0), stop=(dc == 7))
        S.copy(vb[t % 2][:], p0[:, 0:128], eng="dve")
        S.act(gb[t % 2][:], p0[:, 128:140], AF.Sigmoid)
        S.dma(scr["nsa_v_d"][t * 128:(t + 1) * 128, :], vb[t % 2][:], q="act")
        S.dma(scr["nsa_g_d"][t * 128:(t + 1) * 128, :], gb[t % 2][:], q="act")


def phase_nsa_b(S, nc, w, l, cst, scr):
    kvT = S.sb("kvT", [64, 2, S_LEN], BF16)
    S.dma(kvT[:, 0, :], scr["nsaT_d"][8])
    S.dma(kvT[:, 1, :], scr["nsaT_d"][9])
    posT = S.sb("posT", [64, 2, 32], F32)
    S.dma(posT[:], w["nsa_posT"][l].rearrange("a d l -> d a l"))
    ident = S.sb("ident", [128, 128], BF16)
    S.dma(ident[:], cst["ident"][:, :], q="pool")
    w1 = [S.sb("w1", [64, 32, 256], BF16) for _ in range(2)]
    w2 = [S.sb("w2", [128, 2, 64], BF16) for _ in range(2)]
    for a, nm in enumerate(("k", "v")):
        S.dma(w1[a][:], w["nsa_cmp_%s_w1" % nm][l].rearrange("(l d) j -> d l j", d=64), q="pool")
        S.dma(w2[a][:], w["nsa_cmp_%s_w2" % nm][l].rearrange("(c p) d -> p c d", p=128), q="pool")
    X = S.sb("X", [64, 32, 256], BF16)
    gT = [S.sb("gT", [128, 2, 256], BF16) for _ in range(2)]
    kcT = S.sb("kcT", [64, 256], BF16)
    vcx = S.sb("vcx", [128, 2, 65], BF16)
    ov = S.sb("ov", [128, 2, 64], BF16)
    S.dma(ov[:], cst["overlap"].rearrange("(c p) j -> p c j", p=128), q="pool")
    S.memset(kcT[:], 0.0)
    S.memset(vcx[:], 0.0)
    S.memset(vcx[:, :, 64:65], 1.0)
    for a in range(2):
        S.memset(gT[a][:], 0.0)
    pps = [S.ps("pp", [128, 512], F32) for _ in range(2)]
    hs = [S.sb("hs", [128, 3, 256], F32) for _ in range(2)]
    for a in range(2):
        for l_ in range(32):
            S.ts(X[:, l_, 0:255], kvT[:, a, l_:l_ + 16 * 254 + 1:16], posT[:, a, l_:l_ + 1], None, ALU.add)
        for jc in range(2):
            ph = pps[jc]
            for l_ in range(32):
                S.mm(ph[:, 0:255], w1[a][:, l_, jc * 128:(jc + 1) * 128], X[:, l_, 0:255],
                     start=(l_ == 0), stop=(l_ == 31))
            h_ = hs[jc]
            S.act(h_[:, 0, 0:255], ph[:, 0:255], AF.Square)
            S.ts(h_[:, 0, 0:255], h_[:, 0, 0:255], 0.044715, 1.0, ALU.mult, ALU.add)
            S.tt(h_[:, 1, 0:255], h_[:, 0, 0:255], ph[:, 0:255], ALU.mult)
            S.act(h_[:, 2, 0:255], h_[:, 1, 0:255], AF.Sigmoid, scale=1.5957691216057308)
            S.tt(gT[a][:, jc, 0:255], h_[:, 2, 0:255], ph[:, 0:255], ALU.mult)
    for jc in range(2):
        S.mm(pps[0][0:64, 0:256], w2[0][:, jc, :], gT[0][:, jc, :], start=(jc == 0), stop=(jc == 1))
    S.copy(kcT[:, :], pps[0][0:64, 0:256], eng="act")
    for ic in range(2):
        for jc in range(2):
            S.mm(pps[1][:, ic * 64:(ic + 1) * 64], gT[1][:, jc, ic * 128:(ic + 1) * 128], w2[1][:, jc, :],
                 start=(jc == 0), stop=(jc == 1))
        S.copy(vcx[:, ic, 0:64], pps[1][:, ic * 64:(ic + 1) * 64], eng="act")
    qT = S.sb("qT", [64, 4, S_LEN], BF16)
    for h in range(4):
        S.dma(qT[:, h, :], scr["nsaT_d"][h])
    cmask = S.sb("cmask", [128, 2, S_LEN], BF16)
    S.dma(cmask[:], cst["cmpmaskT"].rearrange("(c p) q -> p c q", p=128), q="pool")
    pss = [S.ps("pss", [128, 512], F32) for _ in range(2)]
    pao = [S.ps("pao", [128, 4, 65], F32) for _ in range(2)]
    pai = S.ps("pai", [128, 4, 64], F32)
    pT = S.ps("pT", [64, 4, 128], BF16)
    pbuf = [S.sb("pbuf", [128, 512], BF16) for _ in range(6)]
    pss4 = pss + pps
    ocb = [S.sb("ocb", [128, 4, 256], BF16) for _ in range(2)]
    imp = [S.sb("imp", [128, 4, 64], F32) for _ in range(2)]
    sbt = [S.sb("sbt", [128, 4, 64], F32) for _ in range(2)]
    score = [S.sb("score", [128, 4, 64], F32) for _ in range(2)]
    work = S.sb("work", [128, 4, 64], F32)
    m8 = S.sb("m8", [128, 4, 8], F32)
    m8b = S.sb("m8b", [128, 4, 8], F32)
    selm = [S.sb("selm", [128, 4, 64], BF16) for _ in range(2)]
    selT = S.sb("selT", [64, S_LEN], BF16)
    st = [S.sb("st", [128, 8], F32) for _ in range(4)]
    cnt = 0
    for qb in range(8):
        qs = slice(qb * 512, (qb + 1) * 512)
        b = qb % 2
        pend = []

        def nsab_back(item, b=b):
            h, po_, pbl = item
            for ic in range(2):
                pb = pbl[ic]
                for sub in range(4):
                    S.mm(po_[:, sub, :], pb[:, sub * 128:(sub + 1) * 128], vcx[:, ic, :],
                         start=(ic == 0 and sub == 0), stop=(ic == 1))
                for sub in range(4):
                    S.mm(pai[:, sub, :], pb[:, sub * 128:(sub + 1) * 128], ov[:, ic, :],
                         start=(ic == 0 and sub == 0), stop=(ic == 1))
            s_ = st[h]
            S.ts(s_[:, 0:4], po_[:, :, 64], 1e-30, None, ALU.max)
            S.recip(s_[:, 4:8], s_[:, 0:4])
            for sub in range(4):
                S.ts(ocb[b][:, sub, h * 64:(h + 1) * 64], po_[:, sub, 0:64], s_[:, 4 + sub:5 + sub], None, ALU.mult)
                if h == 0:
                    S.ts(imp[b][:, sub, :], pai[:, sub, :], s_[:, 4 + sub:5 + sub], None, ALU.mult)
                else:
                    S.stt(imp[b][:, sub, :], pai[:, sub, :], s_[:, 4 + sub:5 + sub], imp[b][:, sub, :],
                          ALU.mult, ALU.add)

        for h in range(4):
            po_ = pao[h % 2]
            pbl = []
            for ic in range(2):
                ps = pss4[cnt % 4]
                pb = pbuf[cnt % 6]
                cnt += 1
                pbl.append(pb)
                S.mm(ps[:, 0:512], kcT[:, ic * 128:(ic + 1) * 128], qT[:, h, qs], start=True, stop=True)
                S.act(pb[:], ps[:, 0:512], AF.Exp, scale=0.125)
                S.tt(pb[:], pb[:], cmask[:, ic, qs], ALU.mult)
            pend.append((h, po_, pbl))
            if len(pend) > 1:
                nsab_back(pend.pop(0))
        nsab_back(pend.pop(0))
        S.dma(scr["ocmp_d"][qs, :].rearrange("(s p) f -> p s f", p=128), ocb[b][:], q="act")
        S.dma(sbt[b][:], cst["selbias"][qs, :].rearrange("(s p) j -> p s j", p=128))
        sc = score[b]
        S.tt(sc[:], imp[b][:], sbt[b][:], ALU.add)
        for sub in range(4):
            a_sc, a_m8, a_wk, a_m8b = sc[:, sub, :], m8[:, sub, :], work[:, sub, :], m8b[:, sub, :]
            S.op("dve", lambda e, o=a_m8, i=a_sc: e.max(o, i), reads=[sc[:]], writes=[m8[:]])
            S.op("dve", lambda e, o=a_wk, r=a_m8, i=a_sc: e.match_replace(o, r, i, -3.0e9),
                 reads=[sc[:], m8[:]], writes=[work[:]])
            S.op("dve", lambda e, o=a_m8b, i=a_wk: e.max(o, i), reads=[work[:]], writes=[m8b[:]])
            S.ts(selm[b][:, sub, :], sc[:, sub, :], m8b[:, sub, 7:8], None, ALU.is_ge)
        for sub in range(4):
            S.tr(pT[:, sub, :], selm[b][:, sub, :], ident[:])
        S.copy(selT[:, qs].rearrange("j (s p) -> j s p", s=4), pT[:, :, :], eng="act")
    S.dma(scr["selT_d"][:, :], selT[:, :], q="act")


def phase_nsa_c(S, nc, w, l, cst, scr, y_d):
    qrT = S.sb("qrT", [64, 4, S_LEN], BF16)
    for h in range(4):
        S.dma(qrT[:, h, :], scr["nsaT_d"][4 + h])
    kT = S.sb("kT", [64, 2, S_LEN], BF16)
    S.dma(kT[:, 0, :], scr["nsaT_d"][10])
    S.dma(kT[:, 1, :], scr["nsaT_d"][11])
    selT = S.sb("selT", [64, S_LEN], BF16)
    S.dma(selT[:, :], scr["selT_d"][:, :])
    vext = S.sb("vext", [128, NT, 2, 65], BF16)
    S.memset(vext[:, :, :, 64:65], 1.0, eng="pool")
    vv = scr["nsa_v_d"].rearrange("(t p) f -> p t f", p=128)
    for a in range(2):
        S.dma(vext[:, :, a, 0:64], vv[:, :, a * 64:(a + 1) * 64])
    gsb = S.sb("gsb", [128, NT, 12], F32)
    S.dma(gsb[:], scr["nsa_g_d"].rearrange("(t p) c -> p t c", p=128))
    mwin = S.sb("mwin", [128, 8, 512], BF16)
    for r in range(8):
        S.dma(mwin[:, r, :], cst["mask_win"][r], q="pool")
    mcau = S.sb("mcau", [128, 4, 512], BF16)
    for r in range(4):
        S.dma(mcau[:, r, :], cst["mask_causal"][r], q="pool")
    Eall = S.sb("Eall", [64, NT, 128], BF16)
    S.dma(Eall[:], cst["Eall"][:, :, :], q="pool")
    pss = [S.ps("pss", [128, 512], F32) for _ in range(3)]
    pacc = [S.ps("pacc", [128, 4, 65], F32) for _ in range(4)]
    pm = [S.ps("pm", [128, 512], F32) for _ in range(1)]
    pbuf = [S.sb("pbuf", [128, 512], BF16) for _ in range(4)]
    mbuf = [S.sb("mbuf", [128, 512], BF16) for _ in range(3)]
    ocb = [S.sb("ocb", [128, 4, 256], BF16) for _ in range(2)]
    ybuf = [S.sb("ybuf", [128, 4, 256], F32) for _ in range(2)]
    yb16 = [S.sb("yb16", [128, 4, 256], BF16) for _ in range(2)]
    st = [S.sb("st", [128, 8], F32) for _ in range(4)]
    cnt = 0
    mcnt = [0]
    for qb in range(8):
        qs = slice(qb * 512, (qb + 1) * 512)
        b = qb % 2
        yb = ybuf[b]
        gq = gsb[:, 4 * qb:4 * qb + 4, :]
        S.dma(ocb[b][:], scr["ocmp_d"][qs, :].rearrange("(s p) f -> p s f", p=128))
        for h in range(4):
            S.tt(yb[:, :, h * 64:(h + 1) * 64], ocb[b][:, :, h * 64:(h + 1) * 64],
                 gq[:, :, 3 * h:3 * h + 1].to_broadcast([128, 4, 64]), ALU.mult)
        tiles = [(4 * qb + r, (lambda r=r: mwin[:, r + 4, :])) for r in range(-4, 4) if 4 * qb + r >= 0]
        cnt = attn_qblock(S, 4, lambda h: qrT[:, h, qs], lambda h, kt: kT[:, 1, kt * 128:(kt + 1) * 128],
                          lambda h, kt: vext[:, kt, 1, :], tiles, 0.125, pss, pacc, pbuf, cnt)
        attn_finish(S, 4, pacc, yb, st, gate_of=lambda h: gq[:, :, 3 * h + 2], accumulate=True)

        def mk_mask(kt):
            def f():
                i = mcnt[0]
                mcnt[0] += 1
                p_ = pm[0]
                m_ = mbuf[i % 3]
                S.mm(p_[:, 0:512], Eall[:, kt, :], selT[:, qs], start=True, stop=True)
                r = kt - 4 * qb
                if r >= 0:
                    S.tt(m_[:], p_[:, 0:512], mcau[:, r, :], ALU.mult)
                else:
                    S.copy(m_[:], p_[:, 0:512], eng="pool" if False else "dve")
                return m_[:]
            return f
        tiles = [(kt, mk_mask(kt)) for kt in range(0, 4 * qb + 4)]
        cnt = attn_qblock(S, 4, lambda h: qrT[:, h, qs], lambda h, kt: kT[:, 0, kt * 128:(kt + 1) * 128],
                          lambda h, kt: vext[:, kt, 0, :], tiles, 0.125, pss, pacc, pbuf, cnt)
        attn_finish(S, 4, pacc, yb, st, gate_of=lambda h: gq[:, :, 3 * h + 1], accumulate=True)
        S.copy(yb16[b][:], yb[:], eng="act")
        S.dma(y_d[qs, :].rearrange("(s p) f -> p s f", p=128), yb16[b][:], q="act")


NEG_EH = -0.6065306597126334


def phase_rwkv_a(S, nc, w, l, hT_d, wr_d, cst, scr):
    mub = S.sb("mub", [128, 1024], F32)
    omub = S.sb("omub", [128, 1024], F32)
    bcast_row(S, mub[:], w["rwkv_mu"][l:l + 1, :])
    S.ts(omub[:], mub[:], -1.0, 1.0, ALU.mult, ALU.add)
    W1 = S.sb("W1", [128, 8, 1024], BF16)
    W2 = S.sb("W2", [128, 8, 1024], BF16)
    wst = [S.sb("wst", [128, 1024], F32) for _ in range(2)]
    wv = wr_d[l].rearrange("(c p) f -> p c f", p=128)
    for c in range(8):
        S.dma(wst[c % 2][:], wv[:, c, :])
        S.tt(W1[:, c, :], wst[c % 2][:], omub[:], ALU.mult)
        S.tt(W2[:, c, :], wst[c % 2][:], mub[:], ALU.mult, eng="pool")
    hTp = S.sb("hTp", [128, 8, S_LEN + 1], BF16)
    S.memset(hTp[:, :, 0:1], 0.0)
    hv = hT_d.rearrange("(c p) t -> p c t", p=128)
    for c in range(8):
        S.dma(hTp[:, c, 1:S_LEN + 1], (hv[:, c, :],) + tuple(("hT", t) for t in range(NT)))
    cols = S.sb("cols", [64, 20], F32)
    S.dma(cols[:], w["rwkv_cols"][l])
    omka = S.sb("omka", [64, 4], F32)
    S.ts(omka[:], cols[:, 12:16], -1.0, 1.0, ALU.mult, ALU.add)
    rkc = S.sb("rkc", [64, 4], BF16)
    S.copy(rkc[:], cols[:, 16:20])
    w2sb = S.sb("w2sb", [64, 256], BF16)
    a2sb = S.sb("a2sb", [64, 256], BF16)
    g2sb = S.sb("g2sb", [128, 256], BF16)
    S.dma(w2sb[:], w["rwkv_w2"][l], q="pool")
    S.dma(a2sb[:], w["rwkv_a2"][l], q="pool")
    S.dma(g2sb[:], w["rwkv_g2"][l], q="pool")
    ones64 = S.sb("ones64", [64, 64], BF16)
    S.memset(ones64[:], 1.0)
    rmask = S.sb("rmask", [64, 512], F32)
    S.dma(rmask[:], cst["rw_reset"][:, :])
    gCs = S.sb("gCs", [64, 4, 64], F32)
    pp = [S.ps("pp", [128, 512], F32) for _ in range(7)]
    pbn = S.ps("pbn", [128, 4, 4], F32)
    pc = [0]

    def nextp():
        p = pp[pc[0] % 7]
        pc[0] += 1
        return p

    def xmT(c0, m, t0, n=512):
        p = nextp()
        for dc in range(8):
            S.mm(p[0:m, 0:n], W1[:, dc, c0:c0 + m], hTp[:, dc, 1 + t0:1 + t0 + n], start=(dc == 0), stop=False)
        for dc in range(8):
            S.mm(p[0:m, 0:n], W2[:, dc, c0:c0 + m], hTp[:, dc, t0:t0 + n], start=False, stop=(dc == 7))
        return p

    twl = [S.sb("twl", [64, 512], BF16) for _ in range(2)]
    tal = [S.sb("tal", [64, 512], BF16) for _ in range(2)]
    sgl = [S.sb("sgl", [128, 512], BF16) for _ in range(2)]
    vtok = [S.sb("vtok", [128, 256], BF16) for _ in range(2)]
    gtok = [S.sb("gtok", [128, 256], F32) for _ in range(2)]
    bon = [S.sb("bon", [128, 4, 4], F32) for _ in range(2)]
    NF = 14
    f32t = [[S.sb("f%d" % i, [64, 512], F32) for i in range(NF)] for _ in range(2)]
    sqb = [S.sb("sqb", [64, 512], BF16) for _ in range(2)]
    rkb = [S.sb("rkb", [64, 512], BF16) for _ in range(2)]
    out6 = [S.sb("out6", [64, 6, 512], BF16) for _ in range(2)]
    for tb in range(8):
        t0 = tb * 512
        b = tb % 2
        p = xmT(768, 64, t0)
        S.act(twl[b][:], p[0:64, :], AF.Tanh)
        p = xmT(832, 64, t0)
        S.copy(tal[b][:], p[0:64, :], eng="dve")
        p = xmT(896, 128, t0)
        S.act(sgl[b][:], p[:, :], AF.Sigmoid)
        for sub in range(4):
            tt0 = t0 + sub * 128
            p = nextp()
            for dc in range(8):
                S.mm(p[:, 0:256], hTp[:, dc, 1 + tt0:1 + tt0 + 128], W1[:, dc, 512:768], start=(dc == 0), stop=False)
            for dc in range(8):
                S.mm(p[:, 0:256], hTp[:, dc, tt0:tt0 + 128], W2[:, dc, 512:768], start=False, stop=(dc == 7))
            vt = vtok[sub % 2]
            S.copy(vt[:], p[:, 0:256], eng="act")
            S.dma(scr["rw_v_d"][tt0:tt0 + 128, :], vt[:], q="act")
            p = nextp()
            S.mm(p[:, 0:256], sgl[b][:, sub * 128:(sub + 1) * 128], g2sb[:, :], start=True, stop=True)
            gt = gtok[sub % 2]
            S.copy(gt[:], p[:, 0:256], eng="dve")
            S.dma(scr["rw_g_d"][tt0:tt0 + 128, :], gt[:], q="act")
        def hfront(h, b=b, t0=t0):
            F = f32t[h % 2]
            lw, cs, Ep, En, cse, Epe, EC, ag, kkr, nrm, kk, tf, k2, bv = F
            o6 = out6[h % 2]
            hs = slice(h * 64, (h + 1) * 64)
            p = nextp()
            S.mm(p[0:64, :], w2sb[:, hs], twl[b][:], start=True, stop=True)
            S.act(lw[:], p[0:64, :], AF.Sigmoid, bias=cols[:, h:h + 1])
            a_cs, a_rm, a_lw = cs[:], rmask[:], lw[:]
            S.op("dve", lambda e, o=a_cs, d0=a_rm, d1=a_lw: e.tensor_tensor_scan(o, d0, d1, 0.0, ALU.mult, ALU.add),
                 reads=[rmask[:], lw[:]], writes=[cs[:]])
            S.act(Ep[:], cs[:], AF.Exp, scale=NEG_EH)
            S.act(En[:], cs[:], AF.Exp, scale=-NEG_EH)
            S.tt(cse[:], cs[:], lw[:], ALU.subtract, eng="pool")
            S.act(Epe[:], cse[:], AF.Exp, scale=NEG_EH)
            S.tt(EC[:].rearrange("p (c t) -> p c t", c=8), En[:].rearrange("p (c t) -> p c t", c=8),
                 Ep[:, 63::64].unsqueeze(2).to_broadcast([64, 8, 64]), ALU.mult, eng="pool")
            S.copy(gCs[:, h, tb * 8:(tb + 1) * 8], Ep[:, 63::64], eng="dve")
            p = nextp()
            S.mm(p[0:64, :], a2sb[:, hs], tal[b][:], start=True, stop=True)
            S.act(ag[:], p[0:64, :], AF.Sigmoid, bias=cols[:, 4 + h:5 + h])
            pk = xmT(256 + h * 64, 64, t0)
            S.ts(kkr[:], pk[0:64, :], cols[:, 8 + h:9 + h], None, ALU.mult)
            S.act(sqb[h % 2][:], kkr[:], AF.Square)
            S.ts(tf[:], ag[:], cols[:, 12 + h:13 + h], omka[:, h:h + 1], ALU.mult, ALU.add)
            S.tt(k2[:], tf[:], pk[0:64, :], ALU.mult)
            pr = xmT(h * 64, 64, t0)
            S.tt(o6[:, 2, :], k2[:], En[:], ALU.mult, eng="pool")
            S.tt(o6[:, 3, :], pr[0:64, :], Ep[:], ALU.mult)
            S.tt(o6[:, 5, :], k2[:], EC[:], ALU.mult, eng="pool")
            S.tt(rkb[h % 2][:], pr[0:64, :], k2[:], ALU.mult)

        def hback(h, b=b, t0=t0):
            F = f32t[h % 2]
            lw, cs, Ep, En, cse, Epe, EC, ag, kkr, nrm, kk, tf, k2, bv = F
            o6 = out6[h % 2]
            p = nextp()
            S.mm(p[0:64, :], ones64[:], sqb[h % 2][:], start=True, stop=True)
            S.act(nrm[:], p[0:64, :], AF.Sqrt)
            S.ts(nrm[:], nrm[:], 1e-12, None, ALU.max)
            S.recip(nrm[:], nrm[:])
            S.tt(kk[:], kkr[:], nrm[:], ALU.mult, eng="pool")
            S.tt(bv[:], kk[:], ag[:], ALU.mult, eng="pool")
            S.stt(o6[:, 0, :], kk[:], -1.0, Epe[:], ALU.mult, ALU.mult)
            S.tt(o6[:, 1, :], bv[:], En[:], ALU.mult, eng="pool")
            S.tt(o6[:, 4, :], bv[:], EC[:], ALU.mult, eng="pool")
            for sub in range(4):
                S.mm(pbn[:, sub, h:h + 1], rkb[h % 2][:, sub * 128:(sub + 1) * 128], rkc[:, h:h + 1],
                     start=True, stop=True)
            S.dma(scr["rwT_d"][h].rearrange("q k t -> k q t")[:, :, t0:t0 + 512], o6[:], q="act")

        for h in range(5):
            if h < 4:
                hfront(h)
            if h >= 1:
                hback(h - 1)
        S.copy(bon[b][:], pbn[:], eng="act")
        S.dma(scr["rw_b_d"][t0:t0 + 512, :].rearrange("(s p) h -> p s h", p=128), bon[b][:], q="act")
    S.dma(scr["rw_gC_d"][:, :, :], gCs[:], q="act")


def phase_rwkv_b(S, nc, w, l, cst, scr, y_d):
    ident = S.sb("ident", [128, 128], BF16)
    S.dma(ident[:], cst["ident"][:, :], q="pool")
    mlo = S.sb("mlo", [64, 4, 64], F32)
    mup = S.sb("mup", [64, 4, 64], F32)
    mupi = S.sb("mupi", [64, 4, 64], F32)
    I4 = S.sb("I4", [64, 4, 64], F32)
    S.dma(mlo[:], cst["rw_mlo"][:, :, :])
    S.dma(mup[:], cst["rw_mup"][:, :, :])
    S.dma(mupi[:], cst["rw_mupi"][:, :, :])
    S.dma(I4[:], cst["rw_I4"][:, :, :])
    gC = S.sb("gC", [64, 4, 64], F32)
    S.dma(gC[:], scr["rw_gC_d"][:, :, :])
    lng = S.sb("lng", [64, 256], F32)
    lnb = S.sb("lnb", [64, 256], F32)
    S.dma(lng[:], w["rwkv_ln_g"][l:l + 1, :].partition_broadcast(64))
    S.dma(lnb[:], w["rwkv_ln_b"][l:l + 1, :].partition_broadcast(64))
    M = S.sb("M", [64, 4, 64], F32)
    Mbf = S.sb("Mbf", [64, 4, 64], BF16)
    S.memset(M[:], 0.0)
    S.memset(Mbf[:], 0.0)
    NB = 4
    pp_full = [S.ps("pp", [128, 512], F32) for _ in range(7)]
    pp = [p_[0:64, :] for p_ in pp_full]
    ptr_full = S.ps("ptr", [128, 4, 3, 64], BF16)
    ptr = ptr_full[0:64]
    pc = [0]

    def nextp():
        p = pp[pc[0] % 7]
        pc[0] += 1
        return p

    def v4(p, half):
        return p[:, half * 256:(half + 1) * 256].rearrange("p (h s) -> p h s", h=4)

    def mk(name, shape, dt):
        return [S.sb(name, shape, dt) for _ in range(NB)]
    feat = [S.sb("feat", [64, 4, 6, 256], BF16) for _ in range(2)]
    vch = [S.sb("vch", [64, 4, 256], BF16) for _ in range(2)]
    bonb = [S.sb("bonb", [64, 4, 4], F32) for _ in range(2)]
    gtk = [S.sb("gtk", [64, 4, 256], F32) for _ in range(2)]
    tokM = mk("tokM", [64, 4, 2, 64], BF16)
    WZin = mk("WZin", [64, 4, 128], BF16)
    Lb = [mk("L0", [64, 4, 64], BF16), mk("L1", [64, 4, 64], BF16)]
    LTb = [mk("LT0", [64, 4, 64], BF16), mk("LT1", [64, 4, 64], BF16)]
    ILb = [mk("IL0", [64, 4, 64], BF16), mk("IL1", [64, 4, 64], BF16)]
    PTb = [mk("PT0", [64, 4, 64], BF16), mk("PT1", [64, 4, 64], BF16)]
    AakT = mk("AakT", [64, 4, 64], BF16)
    ArbT = mk("ArbT", [64, 4, 64], BF16)
    ArkT = mk("ArkT", [64, 4, 64], BF16)
    WZ = mk("WZ", [64, 4, 128], BF16)
    GT = mk("GT", [64, 4, 64], BF16)
    GNs = mk("GNs", [64, 2, 4, 64], F32)
    Dg = mk("Dg", [64, 4, 64], F32)
    Nsb = mk("Nsb", [64, 4, 64], F32)
    QeT = mk("QeT", [64, 4, 64], BF16)
    Ol = mk("Ol", [64, 4, 64], F32)
    osb = mk("osb", [64, 4, 64], F32)
    sqs = mk("sqs", [64, 4, 64], F32)
    bvt = mk("bvt", [64, 4, 64], F32)
    stt_ = mk("stt", [64, 8, 4], F32)
    yb = mk("yb", [64, 256], BF16)
    NBATCH = S_LEN // 256
    import os
    VAR = os.environ.get("RWB_VAR", "Z")
    for bt in range(NBATCH):
        if VAR == "A":
            break
        t0 = bt * 256
        fb = feat[bt % 2]
        vb = vch[bt % 2]
        for h in range(4):
            S.dma(fb[:, h, :, :], scr["rwT_d"][h].rearrange("q k t -> k q t")[:, :, t0:t0 + 256])
        S.dma(vb[:], scr["rw_v_d"][t0:t0 + 256, :].rearrange("(c p) f -> p c f", p=64))
        S.dma(bonb[bt % 2][:], scr["rw_b_d"][t0:t0 + 256, :].rearrange("(c p) f -> p c f", p=64))
        S.dma(gtk[bt % 2][:], scr["rw_g_d"][t0:t0 + 256, :].rearrange("(c p) f -> p c f", p=64))

        def F(h, q, c):
            return fb[:, h, q, c * 64:(c + 1) * 64]

        def V(h, c):
            return vb[:, c, h * 64:(h + 1) * 64]
        if VAR == "B":
            continue
        for c in range(NB):
            for h in range(4):
                for j, q in enumerate((0, 4, 5)):
                    if VAR == "D":
                        continue
                    S.tr(ptr[:, h, j, :], F(h, q, c), ident[0:64, 0:64])
            if VAR != "E":
                S.copy(WZin[c][:, :, 0:64], ptr[:, :, 0, :], eng="dve")
            if VAR != "F":
                S.copy(tokM[c][:], ptr[:, :, 1:3, :], eng="dve")
        import os
        STOP = int(os.environ.get("RWB_STOP", "9"))
        if STOP < 1:
            continue
        for c in range(NB):
            p1, p2, p3 = nextp(), nextp(), nextp()
            for h in range(4):
                S.mm(v4(p1, 0)[:, h, :], F(h, 0, c), F(h, 1, c), start=True, stop=True)
                S.mm(v4(p1, 1)[:, h, :], F(h, 1, c), F(h, 0, c), start=True, stop=True)
                S.mm(v4(p2, 0)[:, h, :], F(h, 2, c), F(h, 0, c), start=True, stop=True)
                S.mm(v4(p2, 1)[:, h, :], F(h, 1, c), F(h, 3, c), start=True, stop=True)
                S.mm(v4(p3, 0)[:, h, :], F(h, 2, c), F(h, 3, c), start=True, stop=True)
            S.tt(Lb[0][c][:], v4(p1, 0), mlo[:], ALU.mult)
            S.tt(LTb[0][c][:], v4(p1, 1), mup[:], ALU.mult)
            S.tt(PTb[0][c][:], LTb[0][c][:], I4[:], ALU.add, eng="pool")
            S.tt(AakT[c][:], v4(p2, 0), mup[:], ALU.mult)
            S.tt(ArbT[c][:], v4(p2, 1), mupi[:], ALU.mult)
            S.tt(ArkT[c][:], v4(p3, 0), mupi[:], ALU.mult)
        if STOP < 2:
            continue
        for c in range(NB):
            p1 = nextp()
            for h in range(4):
                S.mm(v4(p1, 0)[:, h, :], AakT[c][:, h, :], V(h, c), start=True, stop=True)
            S.copy(WZin[c][:, :, 64:128], v4(p1, 0), eng="dve")
        if STOP < 3:
            continue
        for i in range(1, 7):
            cur, prv = i % 2, (i - 1) % 2
            for c in range(NB):
                p1 = nextp()
                p2 = nextp() if i >= 2 else None
                for h in range(4):
                    if i <= 5:
                        S.mm(v4(p1, 0)[:, h, :], LTb[prv][c][:, h, :], Lb[prv][c][:, h, :], start=True, stop=True)
                    if i <= 4:
                        S.mm(v4(p1, 1)[:, h, :], Lb[prv][c][:, h, :], LTb[prv][c][:, h, :], start=True, stop=True)
                    if i >= 2:
                        S.mm(v4(p2, 0)[:, h, :], ILb[prv][c][:, h, :], PTb[i % 2][c][:, h, :], start=True, stop=True)
                if i <= 5:
                    S.copy(Lb[cur][c][:].rearrange("p h s -> p (h s)"), p1[:, 0:256], eng="act")
                    S.tt(ILb[cur][c][:], Lb[cur][c][:], I4[:], ALU.add, eng="pool")
                if i <= 4:
                    S.copy(LTb[cur][c][:].rearrange("p h s -> p (h s)"), p1[:, 256:512], eng="act")
                if i >= 2:
                    S.copy(PTb[(i - 1) % 2][c][:], v4(p2, 0), eng="dve")
        TT = PTb[1]
        if STOP < 4:
            continue
        for c in range(NB):
            p1 = nextp()
            pw = p1.rearrange("p (h s) -> p h s", h=4)
            for h in range(4):
                S.mm(pw[:, h, :], TT[c][:, h, :], WZin[c][:, h, :], start=True, stop=True)
            S.copy(WZ[c][:].rearrange("p h s -> p (h s)"), p1[:, 0:512], eng="act")
        if STOP < 5:
            continue
        for c in range(NB):
            n = bt * NB + c
            p1, p2 = nextp(), nextp()
            for h in range(4):
                S.mm(v4(p1, 0)[:, h, :], WZ[c][:, h, 0:64], tokM[c][:, h, 0, :], start=True, stop=True)
                S.mm(v4(p1, 1)[:, h, :], tokM[c][:, h, 0, :], WZ[c][:, h, 64:128], start=True, stop=False)
                S.mm(v4(p1, 1)[:, h, :], tokM[c][:, h, 1, :], V(h, c), start=False, stop=True)
            for h in range(4):
                S.mm(v4(p2, 0)[:, h, :], WZ[c][:, h, 0:64], ArbT[c][:, h, :], start=True, stop=True)
                S.mm(v4(p2, 1)[:, h, :], ArbT[c][:, h, :], WZ[c][:, h, 64:128], start=True, stop=False)
                S.mm(v4(p2, 1)[:, h, :], ArkT[c][:, h, :], V(h, c), start=False, stop=True)
            S.tt(Dg[c][:], I4[:], gC[:, :, n:n + 1].to_broadcast([64, 4, 64]), ALU.mult, eng="pool")
            S.copy(GNs[c][:].rearrange("p a h s -> p (a h s)"), p1[:, 0:512], eng="act")
            S.tt(GT[c][:], GNs[c][:, 0, :, :], Dg[c][:], ALU.add, eng="pool")
            S.tt(QeT[c][:], v4(p2, 0), fb[:, :, 3, c * 64:(c + 1) * 64], ALU.add)
            S.copy(Ol[c][:], v4(p2, 1), eng="dve")
        if STOP < 6:
            continue
        for c in range(NB):
            p1 = nextp()
            for h in range(4):
                S.mm(v4(p1, 0)[:, h, :], QeT[c][:, h, :], Mbf[:, h, :], start=True, stop=True)
            for h in range(4):
                S.mm(v4(p1, 1)[:, h, :], GT[c][:, h, :], Mbf[:, h, :], start=True, stop=True)
            S.tt(M[:], v4(p1, 1), GNs[c][:, 1, :, :], ALU.add)
            S.copy(Mbf[:], M[:], eng="act")
            o = osb[c]
            s_ = stt_[c]
            S.tt(o[:], v4(p1, 0), Ol[c][:], ALU.add)
            S.reduce(s_[:, 0, :], o[:], ALU.add)
            S.tt(sqs[c][:], o[:], o[:], ALU.mult, eng="pool")
            S.reduce(s_[:, 1, :], sqs[c][:], ALU.add)
            S.ts(s_[:, 2, :], s_[:, 0, :], 1.0 / 64, None, ALU.mult)
            S.tt(s_[:, 3, :], s_[:, 2, :], s_[:, 2, :], ALU.mult)
            S.stt(s_[:, 4, :], s_[:, 1, :], 1.0 / 64, s_[:, 3, :], ALU.mult, ALU.subtract)
            S.ts(s_[:, 5, :], s_[:, 4, :], 64e-5, None, ALU.add)
            S.act(s_[:, 6, :], s_[:, 5, :], AF.Sqrt)
            S.recip(s_[:, 7, :], s_[:, 6, :])
            S.tt(o[:], o[:], s_[:, 2, :].unsqueeze(2).to_broadcast([64, 4, 64]), ALU.subtract)
            S.tt(o[:], o[:], s_[:, 7, :].unsqueeze(2).to_broadcast([64, 4, 64]), ALU.mult)
            of = o[:].rearrange("p h e -> p (h e)")
            S.tt(of, of, lng[:], ALU.mult, eng="pool")
            S.tt(of, of, lnb[:], ALU.add, eng="pool")
            S.tt(bvt[c][:], vb[:, c, :].rearrange("p (h e) -> p h e", h=4),
                 bonb[bt % 2][:, c, :].unsqueeze(2).to_broadcast([64, 4, 64]), ALU.mult)
            S.tt(o[:], o[:], bvt[c][:], ALU.add)
            S.tt(yb[c][:], of, gtk[bt % 2][:, c, :], ALU.mult)
            S.dma(y_d[t0 + c * 64:t0 + (c + 1) * 64, :], yb[c][:], q="act")


def phase_merge(S, nc, xres, w, l, hT_d, wgate_d, cst, scr):
    Wg = S.sb("Wg", [128, 8, 4096], BF16)
    load_w_bf16(S, Wg, wgate_d[l], 8)
    Wbr = S.sb("Wbr", [128, 10, D], BF16)
    off = 0
    for nm, nch in (("w_br_nsa", 2), ("w_br_ret", 4), ("w_br_rwkv", 2), ("w_br_swa", 2)):
        v = w[nm][l].rearrange("(c p) f -> p c f", p=128)
        for c in range(nch):
            S.dma(Wbr[:, off + c, :], v[:, c, :], q="pool")
        off += nch
    Wo = S.sb("Wo", [128, 8, D], BF16)
    load_w_bf16(S, Wo, w["w_out"][l], 8)
    gpost = S.sb("gpost", [128, D], F32)
    bcast_row(S, gpost[:], w["mix_post_g"][l:l + 1, :])
    ident = S.sb("ident", [128, 128], BF16)
    S.dma(ident[:], cst["ident"][:, :], q="pool")
    ycat = [S.sb("ycat", [128, 1280], BF16) for _ in range(2)]
    yT = [S.sb("yT", [128, 10, 128], BF16) for _ in range(2)]
    hTt = [S.sb("hTt", [128, 8, 128], BF16) for _ in range(2)]
    xb = [S.sb("xb", [128, D], F32) for _ in range(2)]
    sg = [S.sb("sg", [128, 512], F32) for _ in range(2)]
    tmpb = [S.sb("tmpb", [128, 512], F32) for _ in range(2)]
    merged = [S.sb("merged", [128, D], F32) for _ in range(2)]
    mbf = [S.sb("mbf", [128, D], BF16) for _ in range(2)]
    mT = [S.sb("mT", [128, 8, 128], BF16) for _ in range(2)]
    fsb = [S.sb("fsb", [128, D], F32) for _ in range(2)]
    junk = S.sb("junk", [128, D], BF16)
    st = [S.sb("st", [128, 8], F32) for _ in range(2)]
    ptrA = S.ps("ptrA", [128, 5, 128], BF16)
    ptrB = S.ps("ptrB", [128, 5, 128], BF16)
    ptm = S.ps("ptm", [128, 8, 128], BF16)
    pg = [S.ps("pg", [128, 512], F32) for _ in range(2)]
    po = [S.ps("po", [128, 512], F32) for _ in range(2)]
    hv = hT_d.rearrange("(c p) t -> p c t", p=128)
    brch = ((0, 2), (2, 4), (6, 2), (8, 2))
    pf = S.ps("pf", [128, 512], F32)
    cnt = [0]

    def front(t):
        b2 = t % 2
        rows = slice(t * 128, (t + 1) * 128)
        yc = ycat[b2]
        S.dma(yc[:, 0:256], scr["y_nsa"][rows, :])
        S.dma(yc[:, 256:768], scr["y_ret"][rows, :])
        S.dma(yc[:, 768:1024], scr["y_rwkv"][rows, :])
        S.dma(yc[:, 1024:1280], scr["y_swa"][rows, :])
        S.dma(hTt[b2][:], (hv[:, :, rows], ("hT", t)))
        S.dma(xb[b2][:], (xres[rows, :], ("xres", t)))
        for fc in range(10):
            pt_ = ptrA if fc < 5 else ptrB
            S.tr(pt_[:, fc % 5, :], yc[:, fc * 128:(fc + 1) * 128], ident[:])
        S.copy(yT[b2][:, 0:5, :], ptrA[:], eng="act")
        S.copy(yT[b2][:, 5:10, :], ptrB[:], eng="dve")
        mg = merged[b2]
        for br in range(4):
            f0, nf = brch[br]
            for half in range(2):
                pgt = pg[cnt[0] % 2]
                pot = po[cnt[0] % 2]
                sgt = sg[cnt[0] % 2]
                tb_ = tmpb[cnt[0] % 2]
                cnt[0] += 1
                c0 = br * 1024 + half * 512
                for dc in range(8):
                    S.mm(pgt[:, :], hTt[b2][:, dc, :], Wg[:, dc, c0:c0 + 512], start=(dc == 0), stop=(dc == 7))
                for k in range(nf):
                    S.mm(pot[:, :], yT[b2][:, f0 + k, :], Wbr[:, f0 + k, half * 512:(half + 1) * 512],
                         start=(k == 0), stop=(k == nf - 1))
                S.act(sgt[:], pgt[:, :], AF.Sigmoid)
                mslice = mg[:, half * 512:(half + 1) * 512]
                if br == 0:
                    S.tt(mslice, sgt[:], pot[:, :], ALU.mult)
                else:
                    S.tt(tb_[:], sgt[:], pot[:, :], ALU.mult)
                    S.tt(mslice, mslice, tb_[:], ALU.add, eng="pool")
        S.copy(mbf[b2][:], mg[:], eng="act")

    def back(t):
        b2 = t % 2
        rows = slice(t * 128, (t + 1) * 128)
        for dc in range(8):
            S.tr(ptm[:, dc, :], mbf[b2][:, dc * 128:(dc + 1) * 128], ident[:])
        S.copy(mT[b2][:], ptm[:], eng="act")
        f = fsb[b2]
        s_ = st[b2]
        for half in range(2):
            for dc in range(8):
                S.mm(pf[:, :], mT[b2][:, dc, :], Wo[:, dc, half * 512:(half + 1) * 512],
                     start=(dc == 0), stop=(dc == 7))
            S.act(f[:, half * 512:(half + 1) * 512], pf[:, :], AF.Copy)
            S.act(junk[:, half * 512:(half + 1) * 512], pf[:, :], AF.Square, accum_out=s_[:, half:half + 1])
        S.tt(s_[:, 2:3], s_[:, 0:1], s_[:, 1:2], ALU.add)
        S.ts(s_[:, 3:4], s_[:, 2:3], 1.0 / D, RMS_EPS, ALU.mult, ALU.add)
        S.act(s_[:, 4:5], s_[:, 3:4], AF.Sqrt)
        S.recip(s_[:, 5:6], s_[:, 4:5])
        S.stt(f[:], f[:], s_[:, 5:6], gpost[:], ALU.mult, ALU.mult)
        S.tt(xb[b2][:], xb[b2][:], f[:], ALU.add, eng="pool")
        S.dma((xres[rows, :], ("xres", t)), xb[b2][:], q="act")

    for t in range(NT + 1):
        if t < NT:
            front(t)
        if t >= 1:
            back(t - 1)


WNAMES = ['ffn1_pre_g', 'ffn1_post_g', 'ffn1_w_gate', 'ffn1_w_up', 'ffn1_w_down', 'mix_pre_g', 'mix_post_g',
          'w_in', 'nsa_cmp_pos_k', 'nsa_cmp_pos_v', 'nsa_cmp_k_w1', 'nsa_cmp_k_w2', 'nsa_cmp_v_w1',
          'nsa_cmp_v_w2', 'ret_gn_g', 'rwkv_mu', 'rwkv_w0', 'rwkv_w2', 'rwkv_a0', 'rwkv_a2', 'rwkv_g2',
          'rwkv_k_k', 'rwkv_k_a', 'rwkv_r_k', 'rwkv_ln_g', 'rwkv_ln_b', 'swa_sinks', 'w_br_nsa', 'w_br_ret',
          'w_br_rwkv', 'w_br_swa', 'w_out', 'ffn2_pre_g', 'ffn2_post_g', 'ffn2_w_gate', 'ffn2_w_up',
          'ffn2_w_down']

_CONSTS = None


def band_masks(window, rels):
    p = np.arange(128)[:, None]
    ql = np.arange(512)[None, :]
    out = []
    for r in rels:
        d = ql - (r * 128 + p)
        out.append(((d >= 0) & (d < window)).astype(np.float32))
    return np.stack(out, 0)


def host_consts():
    global _CONSTS
    if _CONSTS is not None:
        return _CONSTS
    c = {}
    c["ident"] = np.eye(128, dtype=np.float32)
    pos = np.arange(S_LEN, dtype=np.float32)
    inv = np.power(np.float32(10000.0), -np.arange(32, dtype=np.float32) * 2.0 / 64).astype(np.float32)
    ang = pos[None, :] * inv[:, None]
    cos = np.cos(ang).astype(np.float32)
    sin = np.sin(ang).astype(np.float32)
    c["cosT"] = np.ascontiguousarray(np.concatenate([cos, cos, cos, cos], 0))
    c["sinT"] = np.ascontiguousarray(np.concatenate([-sin, sin, -sin, sin], 0))
    c["mask_swa"] = band_masks(128, range(-1, 4))
    ii = np.arange(256)
    qq = np.arange(S_LEN)
    c["cmpmaskT"] = (((16 * ii[:, None] + 31) <= qq[None, :]) & (ii[:, None] < 255)).astype(np.float32)
    jj = np.arange(64)
    c["overlap"] = (((16 * ii[:, None]) <= (64 * jj[None, :] + 63)) & ((16 * ii[:, None] + 31) >= 64 * jj[None, :])
                    & (ii[:, None] < 255)).astype(np.float32)
    cur = (qq // 64)[:, None]
    forced = (jj[None, :] == 0) | (jj[None, :] == cur) | (jj[None, :] == cur - 1)
    valid = jj[None, :] <= cur
    c["selbias"] = np.where(forced, 1e9, np.where(valid, 0.0, -1e9)).astype(np.float32)
    c["mask_win"] = band_masks(512, range(-4, 4))
    c["mask_causal"] = band_masks(10 ** 7, range(0, 4))
    kt_ = np.arange(NT)[None, :, None]
    pp = np.arange(128)[None, None, :]
    c["Eall"] = (jj[:, None, None] == (2 * kt_ + pp // 64)).astype(np.float32)
    c["rw_reset"] = np.ascontiguousarray(np.broadcast_to((np.arange(512) % 64 != 0).astype(np.float32)[None, :], (64, 512)))
    tt_ = np.arange(64)[:, None, None]
    ss_ = np.arange(64)[None, None, :]
    one4 = np.ones((1, 4, 1), dtype=np.float32)
    c["rw_mlo"] = np.ascontiguousarray((ss_ < tt_).astype(np.float32) * one4)
    c["rw_mup"] = np.ascontiguousarray((ss_ > tt_).astype(np.float32) * one4)
    c["rw_mupi"] = np.ascontiguousarray((ss_ >= tt_).astype(np.float32) * one4)
    c["rw_I4"] = np.ascontiguousarray((ss_ == tt_).astype(np.float32) * one4)
    gam = (1.0 - np.power(2.0, -5.0 - np.arange(4, dtype=np.float64)))
    m = np.arange(128)[:, None, None]
    cc = np.arange(128)[None, None, :]
    gg = gam[None, :, None]
    dm = np.where(cc >= m, np.power(gg, np.maximum(cc - m, 0)), 0.0) * 0.125
    c["ret_dmaskT"] = np.ascontiguousarray(dm.astype(np.float32))
    c["ret_zeta"] = np.ascontiguousarray((np.power(gam[None, :], 127 - np.arange(128)[:, None]) * 0.125).astype(np.float32))
    xi = np.power(gam[None, :, None], np.arange(128)[None, None, :] + 1.0)
    c["ret_xiT"] = np.ascontiguousarray(np.broadcast_to(xi, (64, 4, 128)).astype(np.float32))
    c["ret_gch"] = np.ascontiguousarray(np.broadcast_to(np.power(gam, 128.0)[None, :], (64, 4)).astype(np.float32))
    _CONSTS = c
    return c


def derived_weights(inputs):
    idx = w_in_index_sets()
    out = {}
    w_in = np.asarray(inputs["w_in"], dtype=np.float32)
    for n, ix in idx.items():
        out["w" + n] = np.ascontiguousarray(w_in[:, :, ix])
    pk = np.asarray(inputs["nsa_cmp_pos_k"], dtype=np.float32)
    pv = np.asarray(inputs["nsa_cmp_pos_v"], dtype=np.float32)
    t64 = lambda n: np.asarray(inputs[n], dtype=np.float32).reshape(DEPTH, 4, 64).transpose(0, 2, 1)
    out["rwkv_cols"] = np.ascontiguousarray(np.concatenate(
        [t64("rwkv_w0"), t64("rwkv_a0"), t64("rwkv_k_k"), t64("rwkv_k_a"), t64("rwkv_r_k")], axis=2))
    out["nsa_posT"] = np.ascontiguousarray(np.stack([pk.transpose(0, 2, 1), pv.transpose(0, 2, 1)], axis=1))
    return out


SCRATCH = {
    "hT_d": ([D, S_LEN], BF16),
    "y_swa": ([S_LEN, 256], BF16),
    "y_ret": ([S_LEN, 512], BF16),
    "y_nsa": ([S_LEN, 256], BF16),
    "y_rwkv": ([S_LEN, 256], BF16),
    "rwT_d": ([4, 6, 64, S_LEN], BF16),
    "rw_gC_d": ([64, 4, 64], F32),
    "rw_v_d": ([S_LEN, 256], BF16),
    "rw_g_d": ([S_LEN, 256], F32),
    "rw_b_d": ([S_LEN, 4], F32),
    "nsaT_d": ([12, 64, S_LEN], BF16),
    "nsa_v_d": ([S_LEN, 128], BF16),
    "nsa_g_d": ([S_LEN, 12], F32),
    "ocmp_d": ([S_LEN, 256], BF16),
    "selT_d": ([64, S_LEN], BF16),
}


def default_phases():
    pl = [("copyin", None)]
    for l in range(DEPTH):
        pl += [("ffn1", l), ("mixpre", l), ("swa", l), ("ret", l), ("nsa_a", l), ("nsa_b", l), ("nsa_c", l),
               ("rwkv_a", l), ("rwkv_b", l), ("merge", l), ("ffn2", l)]
    return pl


def build(shapes, phases=None, dbg=()):
    nc = bass.Bass("TRN2", target_bir_lowering=False)
    x_in = nc.dram_tensor("x", [S_LEN, D], F32, kind="ExternalInput").ap()
    w = {}
    for n in shapes:
        w[n] = nc.dram_tensor(n, list(shapes[n]), F32, kind="ExternalInput").ap()
    cst = {}
    for n, a in host_consts().items():
        cst[n] = nc.dram_tensor("c_" + n, list(a.shape), F32, kind="ExternalInput").ap()
    y = nc.dram_tensor("y", [S_LEN, D], F32, kind="ExternalOutput").ap()
    scr = {}
    for n, (shp, dt_) in SCRATCH.items():
        if n in dbg:
            scr[n] = nc.dram_tensor(n, shp, dt_, kind="ExternalOutput").ap()
        else:
            scr[n] = nc.dram_tensor(n, shp, dt_).ap()
    xres = y
    plist = default_phases() if phases is None else phases
    with ExitStack() as gst:
        S = Sched(nc, gst)
        for pi, (pn, l) in enumerate(plist):
            with ExitStack() as pst:
                S.stack = pst
                S.phase = pi
                if pn == "copyin":
                    for t in range(0, NT, 4):
                        S.dma((xres[t * 128:(t + 4) * 128, :], ("xres", t), ("xres", t + 1), ("xres", t + 2),
                               ("xres", t + 3)), x_in[t * 128:(t + 4) * 128, :], q="sp")
                elif pn in ("ffn1", "ffn2"):
                    phase_ffn(S, nc, xres, w, l, pn, cst["ident"])
                elif pn == "mixpre":
                    phase_mixpre(S, nc, xres, w, l, scr["hT_d"], cst["ident"])
                elif pn == "swa":
                    phase_swa(S, nc, w, l, scr["hT_d"], w["wswa"], cst, scr["y_swa"])
                elif pn == "nsa_a":
                    phase_nsa_a(S, nc, w, l, scr["hT_d"], w["wnsa"], cst, scr)
                elif pn == "nsa_b":
                    phase_nsa_b(S, nc, w, l, cst, scr)
                elif pn == "nsa_c":
                    phase_nsa_c(S, nc, w, l, cst, scr, scr["y_nsa"])
                elif pn == "rwkv_a":
                    phase_rwkv_a(S, nc, w, l, scr["hT_d"], w["wr"], cst, scr)
                elif pn == "rwkv_b":
                    phase_rwkv_b(S, nc, w, l, cst, scr, scr["y_rwkv"])
                elif pn == "merge":
                    phase_merge(S, nc, xres, w, l, scr["hT_d"], w["wgate"], cst, scr)
                elif pn == "ret":
                    phase_ret(S, nc, w, l, scr["hT_d"], w["wret"], cst, scr["y_ret"], cst["ident"])
                else:
                    raise ValueError(pn)
                S.barrier()
                S.emit(final=(pi == len(plist) - 1))
    return nc


def make_in_maps(inputs, cores):
    base = {k: np.ascontiguousarray(inputs[k], dtype=np.float32) for k in WNAMES}
    base.update(derived_weights(inputs))
    shapes = {k: v.shape for k, v in base.items()}
    for k, a in host_consts().items():
        base["c_" + k] = a
    x = np.asarray(inputs["x"], dtype=np.float32)
    in_maps = []
    for i in cores:
        m = dict(base)
        m["x"] = np.ascontiguousarray(x[i])
        in_maps.append(m)
    return shapes, in_maps


def kernel(**inputs):
    n = 8
    shapes, in_maps = make_in_maps(inputs, list(range(n)))
    nc = build(shapes)
    res = run_bass_kernel_spmd(nc, in_maps, core_ids=list(range(n)))
    return np.stack([r["y"] for r in res.results], axis=0)
```

```python
import numpy as np
from contextlib import ExitStack
import concourse.bass as bass
import concourse.mybir as mybir
from concourse.bass_utils import run_bass_kernel_spmd

F32 = mybir.dt.float32
BF16 = mybir.dt.bfloat16
AF = mybir.ActivationFunctionType
ALU = mybir.AluOpType
AX = mybir.AxisListType

S_LEN = 4096
D = 1024
DFF = 2816
NT = S_LEN // 128
DEPTH = 2
RMS_EPS = 1e-6

ENGS = ("pe", "act", "dve", "pool", "sp")
DMA_K = 16


def _kref(x):
    if isinstance(x, tuple):
        return x[0], tuple(x[1:])
    return x, (x.tensor.name,)


class Sched:
    def __init__(self, nc, stack):
        self.nc = nc
        self.gstack = stack
        self.esem = {e: stack.enter_context(nc.semaphore("es_" + e)) for e in ENGS if e != "sp"}
        self.dsem = {q: [stack.enter_context(nc.semaphore("ds_%s%d" % (q, i))) for i in range(DMA_K)]
                     for q in ("sp", "act", "pool")}
        self.ccount = {e: 0 for e in ENGS}
        self.qcount = {q: 0 for q in ("sp", "act", "pool")}
        self.wm = {e: {} for e in ENGS}
        self.pending = {e: [] for e in ENGS}
        self.keys = {}
        self.post_barrier = {e: set() for e in ENGS}
        self.stack = None
        self.uid = 0
        self.phase = 0

    def sb(self, name, shape, dtype):
        self.uid += 1
        return self.stack.enter_context(self.nc.sbuf_tensor("%s_ph%d_%d" % (name, self.phase, self.uid), list(shape), dtype))

    def ps(self, name, shape, dtype=F32):
        self.uid += 1
        return self.stack.enter_context(self.nc.psum_tensor("%s_ph%d_%d" % (name, self.phase, self.uid), list(shape), dtype))

    def _st(self, key):
        st = self.keys.get(key)
        if st is None:
            st = {"W": {}, "R": {}, "Wd": {}, "Rd": {}}
            self.keys[key] = st
        return st

    def op(self, eng, fn, reads=(), writes=(), dma=False):
        deps = set()
        rkeys, wkeys = [], []
        for r in reads:
            if r is None:
                continue
            _, ks = _kref(r)
            rkeys.extend(ks)
        for w in writes:
            if w is None:
                continue
            _, ks = _kref(w)
            wkeys.extend(ks)
        for k in rkeys:
            st = self._st(k)
            for e, c in st["W"].items():
                deps.add((e, c))
            for q, js in st["Wd"].items():
                for j in js:
                    deps.add(("dma", q, j))
        for k in wkeys:
            st = self._st(k)
            for e, c in st["W"].items():
                deps.add((e, c))
            for e, c in st["R"].items():
                if e == eng and not dma:
                    continue
                deps.add((e, c))
            for q, js in st["Wd"].items():
                for j in js:
                    deps.add(("dma", q, j))
            for q, js in st["Rd"].items():
                for j in js:
                    deps.add(("dma", q, j))
        if eng == "pe":
            deps = {d for d in deps if d[0] != "pe"}
        deps |= self.post_barrier[eng]
        self.post_barrier[eng] = set()
        if dma:
            j = self.qcount[eng]
            self.qcount[eng] += 1
            rec = ("dma", eng, j)
            for k in rkeys:
                l = self._st(k)["Rd"].setdefault(eng, [])
                l.append(j)
                if len(l) > DMA_K:
                    del l[0]
            for k in wkeys:
                l = self._st(k)["Wd"].setdefault(eng, [])
                l.append(j)
                if len(l) > DMA_K:
                    del l[0]
            self.pending[eng].append((fn, deps, True, j))
        else:
            self.ccount[eng] += 1
            c = self.ccount[eng]
            for k in rkeys:
                self._st(k)["R"][eng] = c
            for k in wkeys:
                self._st(k)["W"][eng] = c
            self.pending[eng].append((fn, deps, False, c))

    def barrier(self):
        allc = set()
        for e in ENGS:
            if e != "sp" and self.ccount[e] > 0:
                allc.add((e, self.ccount[e]))
        for q in ("sp", "act", "pool"):
            n = self.qcount[q]
            for j in range(max(0, n - DMA_K), n):
                allc.add(("dma", q, j))
        for e in ENGS:
            self.post_barrier[e] |= allc
        self.keys = {}

    def _emit_engine(self, ename, eng, final=False):
        wm = self.wm[ename]
        for fn, deps, is_dma, idx in self.pending[ename]:
            waits = {}
            dmax = {}
            for d in deps:
                if d[0] == "dma":
                    dmax[d[1]] = max(dmax.get(d[1], -1), d[2])
            for d in deps:
                if d[0] == "dma":
                    q, j = d[1], d[2]
                    if j <= dmax[q] - DMA_K:
                        continue
                    sem = self.dsem[q][j % DMA_K]
                    val = 16 * (j // DMA_K + 1)
                else:
                    sem = self.esem[d[0]]
                    val = d[1]
                key = id(sem)
                if key not in waits or waits[key][1] < val:
                    waits[key] = (sem, val)
            if is_dma and idx >= DMA_K:
                sem = self.dsem[ename][idx % DMA_K]
                val = 16 * (idx // DMA_K)
                key = id(sem)
                if key not in waits or waits[key][1] < val:
                    waits[key] = (sem, val)
            for key, (sem, val) in waits.items():
                if wm.get(key, 0) < val:
                    eng.wait_ge(sem, val)
                    wm[key] = val
            ins = fn(eng)
            if is_dma:
                ins.then_inc(self.dsem[ename][idx % DMA_K], 16)
            else:
                ins.then_inc(self.esem[ename], 1)
        self.pending[ename] = []
        if final and ename in self.dsem:
            n = self.qcount[ename]
            for j in range(max(0, n - DMA_K), n):
                sem = self.dsem[ename][j % DMA_K]
                val = 16 * (j // DMA_K + 1)
                if wm.get(id(sem), 0) < val:
                    eng.wait_ge(sem, val)
                    wm[id(sem)] = val

    def emit(self, final=False):
        with self.nc.Block() as block:
            @block.tensor
            def _(e):
                self._emit_engine("pe", e, final)

            @block.scalar
            def _(e):
                self._emit_engine("act", e, final)

            @block.vector
            def _(e):
                self._emit_engine("dve", e, final)

            @block.gpsimd
            def _(e):
                self._emit_engine("pool", e, final)

            @block.sync
            def _(e):
                self._emit_engine("sp", e, final)

    def dma(self, out, in_, q="sp"):
        o, i = _kref(out)[0], _kref(in_)[0]
        self.op(q, lambda e: e.dma_start(out=o, in_=i), reads=[in_], writes=[out], dma=True)

    def mm(self, out, lhsT, rhs, start=True, stop=True, skip=False):
        o, l, r = _kref(out)[0], _kref(lhsT)[0], _kref(rhs)[0]
        if skip:
            self.op("pe", lambda e: e.matmul(o, l, r, start=start, stop=stop, skip_group_check=True),
                    reads=[lhsT, rhs], writes=[out])
        else:
            self.op("pe", lambda e: e.matmul(o, l, r, start=start, stop=stop), reads=[lhsT, rhs], writes=[out])

    def tr(self, out, in_, ident):
        o, i, d = _kref(out)[0], _kref(in_)[0], _kref(ident)[0]
        self.op("pe", lambda e: e.transpose(o, i, d), reads=[in_, ident], writes=[out])

    def act(self, out, in_, func, bias=None, scale=None, accum_out=None):
        o, i = _kref(out)[0], _kref(in_)[0]
        kw = {}
        rd = [in_]
        wr = [out]
        if bias is not None:
            if isinstance(bias, (int, float)):
                kw["bias"] = bias
            else:
                kw["bias"] = _kref(bias)[0]
                rd.append(bias)
        if scale is not None:
            if isinstance(scale, (int, float)):
                kw["scale"] = scale
            else:
                kw["scale"] = _kref(scale)[0]
                rd.append(scale)
        if accum_out is not None:
            kw["accum_out"] = _kref(accum_out)[0]
            wr.append(accum_out)
        self.op("act", lambda e: e.activation(o, i, func, **kw), reads=rd, writes=wr)

    def tt(self, out, in0, in1, op, eng="dve"):
        o, a, b = _kref(out)[0], _kref(in0)[0], _kref(in1)[0]
        self.op(eng, lambda e: e.tensor_tensor(o, a, b, op), reads=[in0, in1], writes=[out])

    def ts(self, out, in0, s1, s2, op0, op1=None, eng="dve", accum_out=None):
        o, a = _kref(out)[0], _kref(in0)[0]
        rd = [in0]
        wr = [out]

        def sc(s):
            if s is None or isinstance(s, (int, float)):
                return s
            rd.append(s)
            return _kref(s)[0]
        v1, v2 = sc(s1), sc(s2)
        kw = {}
        if op1 is not None:
            kw["op1"] = op1
        if accum_out is not None:
            kw["accum_out"] = _kref(accum_out)[0]
            wr.append(accum_out)
        self.op(eng, lambda e: e.tensor_scalar(o, a, v1, v2, op0, **kw), reads=rd, writes=wr)

    def stt(self, out, in0, scalar, in1, op0, op1, accum_out=None):
        o, a, b = _kref(out)[0], _kref(in0)[0], _kref(in1)[0]
        rd = [in0, in1]
        wr = [out]
        if isinstance(scalar, (int, float)):
            s = scalar
        else:
            s = _kref(scalar)[0]
            rd.append(scalar)
        kw = {}
        if accum_out is not None:
            kw["accum_out"] = _kref(accum_out)[0]
            wr.append(accum_out)
        self.op("dve", lambda e: e.scalar_tensor_tensor(o, a, s, b, op0, op1, **kw), reads=rd, writes=wr)

    def copy(self, out, in_, eng="dve"):
        o, i = _kref(out)[0], _kref(in_)[0]
        if eng == "act":
            self.op("act", lambda e: e.copy(o, i), reads=[in_], writes=[out])
        else:
            self.op(eng, lambda e: e.tensor_copy(o, i), reads=[in_], writes=[out])

    def recip(self, out, in_):
        o, i = _kref(out)[0], _kref(in_)[0]
        self.op("dve", lambda e: e.reciprocal(o, i), reads=[in_], writes=[out])

    def memset(self, out, val, eng="dve"):
        o = _kref(out)[0]
        self.op(eng, lambda e: e.memset(o, val), reads=[], writes=[out])

    def reduce(self, out, in_, op, axis=None, eng="dve"):
        o, i = _kref(out)[0], _kref(in_)[0]
        ax = AX.X if axis is None else axis
        self.op(eng, lambda e: e.tensor_reduce(o, i, ax, op), reads=[in_], writes=[out])


def load_w_bf16(S, dst, src_dram, nchunk, q="pool"):
    v = src_dram.rearrange("(c p) f -> p c f", p=128)
    step = 4 if nchunk % 4 == 0 else nchunk
    for c in range(0, nchunk, step):
        S.dma(dst[:, c:c + step, :], v[:, c:c + step, :], q=q)


def bcast_row(S, dst, src_row_ap, q="sp"):
    S.dma(dst, src_row_ap.partition_broadcast(128), q=q)


def phase_ffn(S, nc, xres, w, l, pre, ident_d):
    TB = 512
    NSUB = TB // 128
    NFC = DFF // 128
    GRP = (6, 6, 5, 5)
    G0 = (0, 6, 12, 17)
    wg = [S.sb("wg", [128, 8, n * 128], BF16) for n in GRP]
    wu = [S.sb("wu", [128, 8, n * 128], BF16) for n in GRP]
    wd = [S.sb("wd", [128, 11, D], BF16) for _ in range(2)]
    fcmap = []
    for g, n in enumerate(GRP):
        for k in range(n):
            fcmap.append((g, k))
    gpre = S.sb("gpre", [128, D], F32)
    gpost = S.sb("gpost", [128, D], F32)
    ident = S.sb("ident", [128, 128], BF16)
    S.dma(ident[:], ident_d[:, :], q="pool")
    bcast_row(S, gpre[:], w[pre + "_pre_g"][l:l + 1, :])
    bcast_row(S, gpost[:], w[pre + "_post_g"][l:l + 1, :])
    S.ts(gpost[:], gpost[:], 0.5, None, ALU.mult, eng="pool")
    vg = w[pre + "_w_gate"][l].rearrange("(c p) f -> p c f", p=128)
    vu = w[pre + "_w_up"][l].rearrange("(c p) f -> p c f", p=128)
    vd = w[pre + "_w_down"][l].rearrange("(c p) f -> p c f", p=128)
    for g, n in enumerate(GRP):
        c0 = G0[g] * 128
        S.dma(wg[g][:, :, :], vg[:, :, c0:c0 + n * 128], q="pool")
        S.dma(wu[g][:, :, :], vu[:, :, c0:c0 + n * 128], q="pool")
    for i in range(2):
        S.dma(wd[i][:, :, :], vd[:, i * 11:(i + 1) * 11, :], q="pool")

    xf = [S.sb("xf", [128, D], F32) for _ in range(2)]
    xbk = [S.sb("xbk", [128, D], F32) for _ in range(2)]
    hb = [S.sb("hb", [128, D], BF16) for _ in range(2)]
    junk = S.sb("junk", [128, D], BF16)
    hT = S.sb("hT", [128, 8, TB], BF16)
    actT = S.sb("actT", [128, NFC, TB], BF16)
    sg = [S.sb("sg", [128, TB], F32) for _ in range(2)]
    fsb = S.sb("fsb", [128, D], F32)
    st = [S.sb("st", [128, 8], F32) for _ in range(4)]
    ptr = [S.ps("ptr", [128, 8, 128], BF16) for _ in range(2)]
    pgt_ = [S.ps("pg", [128, 512], F32) for _ in range(2)]
    put_ = [S.ps("pu", [128, 512], F32) for _ in range(2)]
    po = [S.ps("po", [128, 512], F32) for _ in range(2)]
    nblk = S_LEN // TB
    cnt = [0]

    def front(b):
        for s in range(NSUB):
            t = b * NSUB + s
            x = xf[cnt[0] % 2]
            stt_ = st[cnt[0] % 4]
            h = hb[cnt[0] % 2]
            p = ptr[cnt[0] % 2]
            cnt[0] += 1
            S.dma(x[:], (xres[t * 128:(t + 1) * 128, :], ("xres", t)))
            S.act(junk[:], x[:], AF.Square, accum_out=stt_[:, 0:1])
            S.ts(stt_[:, 1:2], stt_[:, 0:1], 1.0 / D, RMS_EPS, ALU.mult, ALU.add)
            S.act(stt_[:, 2:3], stt_[:, 1:2], AF.Sqrt)
            S.recip(stt_[:, 3:4], stt_[:, 2:3])
            S.stt(h[:], x[:], stt_[:, 3:4], gpre[:], ALU.mult, ALU.mult)
            for dc in range(8):
                S.tr(p[:, dc, :], h[:, dc * 128:(dc + 1) * 128], ident[:])
            S.copy(hT[:, :, s * 128:(s + 1) * 128], p[:, :, :], eng="act")

    def gateup(b):
        for fc in range(NFC):
            pg1 = pgt_[fc % 2]
            pu1 = put_[fc % 2]
            g_, k_ = fcmap[fc]
            for dc in range(8):
                S.mm(pg1[:, :], wg[g_][:, dc, k_ * 128:(k_ + 1) * 128], hT[:, dc, :],
                     start=(dc == 0), stop=(dc == 7))
            for dc in range(8):
                S.mm(pu1[:, :], wu[g_][:, dc, k_ * 128:(k_ + 1) * 128], hT[:, dc, :],
                     start=(dc == 0), stop=(dc == 7))
            sgt = sg[fc % 2]
            S.act(sgt[:], pg1[:, :], AF.Silu)
            S.tt(actT[:, fc, :], sgt[:], pu1[:, :], ALU.mult)

    def down(b):
        for s in range(NSUB):
            t = b * NSUB + s
            x = xbk[cnt[0] % 2]
            stt_ = st[cnt[0] % 4]
            cnt[0] += 1
            f = fsb
            S.dma(x[:], (xres[t * 128:(t + 1) * 128, :], ("xres", t)))
            for dh in range(2):
                pot = po[dh]
                for fc in range(NFC):
                    S.mm(pot[:], actT[:, fc, s * 128:(s + 1) * 128], wd[fc // 11][:, fc % 11, dh * 512:(dh + 1) * 512],
                         start=(fc == 0), stop=(fc == NFC - 1))
                S.act(f[:, dh * 512:(dh + 1) * 512], pot[:], AF.Copy)
                S.act(junk[:, dh * 512:(dh + 1) * 512], pot[:], AF.Square, accum_out=stt_[:, dh:dh + 1])
            S.tt(stt_[:, 2:3], stt_[:, 0:1], stt_[:, 1:2], ALU.add)
            S.ts(stt_[:, 3:4], stt_[:, 2:3], 1.0 / D, RMS_EPS, ALU.mult, ALU.add)
            S.act(stt_[:, 4:5], stt_[:, 3:4], AF.Sqrt)
            S.recip(stt_[:, 5:6], stt_[:, 4:5])
            S.stt(f[:], f[:], stt_[:, 5:6], gpost[:], ALU.mult, ALU.mult)
            S.tt(x[:], x[:], f[:], ALU.add, eng="pool")
            S.dma((xres[t * 128:(t + 1) * 128, :], ("xres", t)), x[:], q="act")

    front(0)
    for b in range(nblk):
        gateup(b)
        if b + 1 < nblk:
            front(b + 1)
        down(b)


HD = 64


def col_layout():
    spec = (('nsa_q', 256), ('nsa_k_cmp', 64), ('nsa_v_cmp', 64), ('nsa_k_slc', 64), ('nsa_v_slc', 64),
            ('nsa_k_win', 64), ('nsa_v_win', 64), ('nsa_gate', 12), ('ret_q', 256), ('ret_k', 256),
            ('ret_v', 512), ('ret_g', 512), ('rwkv', 1024), ('swa_q', 256), ('swa_k', 128), ('swa_v', 128),
            ('branch_gate', 4096))
    lay, s = {}, 0
    for n, wd_ in spec:
        lay[n] = (s, s + wd_)
        s += wd_
    return lay, s


def _partner(cols):
    out = []
    for i in range(0, len(cols), 64):
        blk = cols[i:i + 64]
        out.extend(blk[32:64])
        out.extend(blk[0:32])
    return out


def w_in_index_sets():
    lay, _ = col_layout()
    r = lambda n: list(range(*lay[n]))
    idx = {}
    q = r('swa_q')
    k = r('swa_k')
    kk0 = k[0:64] + k[0:64]
    kk1 = k[64:128] + k[64:128]
    idx['swa'] = q + _partner(q) + kk0 + kk1 + _partner(kk0) + _partner(kk1) + r('swa_v')
    nq, ksl, kwi = r('nsa_q'), r('nsa_k_slc'), r('nsa_k_win')
    idx['nsa'] = (nq + _partner(nq) + r('nsa_k_cmp') + r('nsa_v_cmp') + ksl + _partner(ksl) + kwi + _partner(kwi)
                  + r('nsa_v_slc') + r('nsa_v_win') + r('nsa_gate'))
    idx['r'] = r('rwkv')
    idx['gate'] = r('branch_gate')
    rq, rk = r('ret_q'), r('ret_k')
    idx['ret'] = rq + _partner(rq) + rk + _partner(rk) + r('ret_v') + r('ret_g')
    return idx


def phase_mixpre(S, nc, xres, w, l, hT_d, ident_d):
    g = S.sb("g", [128, D], F32)
    ident = S.sb("ident", [128, 128], BF16)
    S.dma(ident[:], ident_d[:, :], q="pool")
    bcast_row(S, g[:], w["mix_pre_g"][l:l + 1, :])
    xb = [S.sb("xb", [128, D], F32) for _ in range(3)]
    hb = [S.sb("hb", [128, D], BF16) for _ in range(2)]
    junk = S.sb("junk", [128, D], BF16)
    hT = [S.sb("hT", [128, 8, 128], BF16) for _ in range(3)]
    st = [S.sb("st", [128, 8], F32) for _ in range(3)]
    ptr = [S.ps("ptr", [128, 8, 128], BF16) for _ in range(2)]
    hv = hT_d.rearrange("(c p) t -> p c t", p=128)
    for t in range(NT):
        x = xb[t % 3]
        s_ = st[t % 3]
        h = hb[t % 2]
        p = ptr[t % 2]
        o = hT[t % 3]
        S.dma(x[:], (xres[t * 128:(t + 1) * 128, :], ("xres", t)))
        S.act(junk[:], x[:], AF.Square, accum_out=s_[:, 0:1])
        S.ts(s_[:, 1:2], s_[:, 0:1], 1.0 / D, RMS_EPS, ALU.mult, ALU.add)
        S.act(s_[:, 2:3], s_[:, 1:2], AF.Sqrt)
        S.recip(s_[:, 3:4], s_[:, 2:3])
        S.stt(h[:], x[:], s_[:, 3:4], g[:], ALU.mult, ALU.mult)
        for dc in range(8):
            S.tr(p[:, dc, :], h[:, dc * 128:(dc + 1) * 128], ident[:])
        S.copy(o[:], p[:], eng="act")
        S.dma((hv[:, :, t * 128:(t + 1) * 128], ("hT", t)), o[:], q="act")


def proj_feat(S, dst, wsb, col0, hT, pps, rope=None, npart=128):
    for tb in range(8):
        tsl = slice(tb * 512, (tb + 1) * 512)
        p0 = pps[tb % 2]
        for dc in range(8):
            S.mm(p0[0:npart, 0:512], wsb[:, dc, col0:col0 + npart], hT[:, dc, tsl],
                 start=(dc == 0), stop=(dc == 7))
        if rope is None:
            S.copy(dst[0:npart, tsl], p0[0:npart, 0:512], eng="act")
        else:
            pc, cos_d, sin_d, cst, tmp, pps2 = rope
            p1 = pps2[tb % 2]
            for dc in range(8):
                S.mm(p1[0:npart, 0:512], wsb[:, dc, pc:pc + npart], hT[:, dc, tsl],
                     start=(dc == 0), stop=(dc == 7))
            cs = cst[tb % 2]
            S.dma(cs[:, 0, :], cos_d[:, tsl])
            S.dma(cs[:, 1, :], sin_d[:, tsl])
            t1 = tmp[tb % 2]
            S.tt(t1[0:npart, 0, :], p0[0:npart, 0:512], cs[0:npart, 0, :], ALU.mult)
            S.tt(t1[0:npart, 1, :], p1[0:npart, 0:512], cs[0:npart, 1, :], ALU.mult)
            S.tt(dst[0:npart, tsl], t1[0:npart, 0, :], t1[0:npart, 1, :], ALU.add, eng="pool")


def proj_tok(S, dst_fn, wsb, col0, ncol, hT, pps, eng="act", view=None):
    for t in range(NT):
        p0 = pps[t % 2]
        for dc in range(8):
            S.mm(p0[:, 0:ncol], hT[:, dc, t * 128:(t + 1) * 128], wsb[:, dc, col0:col0 + ncol],
                 start=(dc == 0), stop=(dc == 7))
        src = p0[:, 0:ncol]
        if view is not None:
            src = view(src)
        S.copy(dst_fn(t), src, eng=eng)


def attn_qblock(S, nheads, q_of, k_of, v_of, tiles, scale, pss, pacc, pbuf, cnt, skew=2):
    items = [(i, kt, h, mf) for i, (kt, mf) in enumerate(tiles) for h in range(nheads)]
    n = len(items)
    masks = {}
    pbs = {}
    skew = min(skew, len(pss) - 1)

    def front(j):
        i, kt, h, mf = items[j]
        if h == 0:
            masks[i] = mf() if mf is not None else None
        ps = pss[(cnt + j) % len(pss)]
        pb = pbuf[(cnt + j) % len(pbuf)]
        pbs[j] = pb
        S.mm(ps[:, 0:512], k_of(h, kt), q_of(h), start=True, stop=True)
        S.act(pb[:], ps[:, 0:512], AF.Exp, scale=scale)
        if masks[i] is not None:
            S.tt(pb[:], pb[:], masks[i], ALU.mult, eng=("pool" if (j % 3 == 2) else "dve"))

    def back(j):
        i, kt, h, mf = items[j]
        pb = pbs.pop(j)
        for sub in range(4):
            S.mm(pacc[h][:, sub, :], pb[:, sub * 128:(sub + 1) * 128], v_of(h, kt),
                 start=(i == 0 and sub == 0), stop=(i == len(tiles) - 1), skip=True)

    for j in range(n + skew):
        if j < n:
            front(j)
        if j - skew >= 0:
            back(j - skew)
    return cnt + n


def attn_finish(S, nheads, pacc, ybuf, st, zextra=None, gate_of=None, accumulate=False):
    for h in range(nheads):
        s_ = st[h % len(st)]
        if zextra is not None:
            S.ts(s_[:, 0:4], pacc[h][:, :, 64], zextra(h), None, ALU.add)
        else:
            S.ts(s_[:, 0:4], pacc[h][:, :, 64], 1e-30, None, ALU.max)
        S.recip(s_[:, 4:8], s_[:, 0:4])
        if gate_of is not None:
            S.tt(s_[:, 4:8], s_[:, 4:8], gate_of(h), ALU.mult)
        for sub in range(4):
            yo = ybuf[:, sub, h * 64:(h + 1) * 64]
            if not accumulate:
                S.ts(yo, pacc[h][:, sub, 0:64], s_[:, 4 + sub:5 + sub], None, ALU.mult)
            else:
                S.stt(yo, pacc[h][:, sub, 0:64], s_[:, 4 + sub:5 + sub], yo, ALU.mult, ALU.add)


def phase_swa(S, nc, w, l, hT_d, wswa_d, cst, y_d):
    NCOL = 8 * 128 + 128
    wsb = S.sb("wsb", [128, 8, NCOL], BF16)
    load_w_bf16(S, wsb, wswa_d[l], 8)
    hT = S.sb("hTall", [128, 8, S_LEN], BF16)
    hv = hT_d.rearrange("(c p) t -> p c t", p=128)
    for c in range(8):
        S.dma(hT[:, c, :], (hv[:, c, :],) + tuple(("hT", t) for t in range(NT)))
    qT = [S.sb("qT", [128, S_LEN], BF16) for _ in range(2)]
    kT = [S.sb("kT", [128, S_LEN], BF16) for _ in range(2)]
    vext = S.sb("vext", [128, NT, 2, 65], BF16)
    pps = [S.ps("pp", [128, 512], F32) for _ in range(2)]
    pps2 = [S.ps("pp2", [128, 512], F32) for _ in range(2)]
    cstt = [S.sb("cs", [128, 2, 512], F32) for _ in range(2)]
    tmp = [S.sb("tmp", [128, 2, 512], F32) for _ in range(2)]
    rope = lambda pc: (pc, cst["cosT"], cst["sinT"], cstt, tmp, pps2)
    proj_feat(S, qT[0], wsb, 0, hT, pps, rope(256))
    proj_feat(S, qT[1], wsb, 128, hT, pps, rope(384))
    proj_feat(S, kT[0], wsb, 512, hT, pps, rope(768))
    proj_feat(S, kT[1], wsb, 640, hT, pps, rope(896))
    S.memset(vext[:, :, :, 64:65], 1.0, eng="pool")
    proj_tok(S, lambda t: vext[:, t, :, 0:64], wsb, 1024, 128, hT, pps,
             view=lambda a: a.rearrange("p (a b) -> p a b", a=2))
    masks = S.sb("masks", [128, 5, 512], BF16)
    for r in range(5):
        S.dma(masks[:, r, :], cst["mask_swa"][r], q="pool")
    sk = S.sb("sk", [128, 4], F32)
    S.dma(sk[:], w["swa_sinks"][l:l + 1, :].partition_broadcast(128))
    S.act(sk[:], sk[:], AF.Exp)
    pacc = [S.ps("pacc", [128, 4, 65], F32) for _ in range(4)]
    pbuf = [S.sb("pbuf", [128, 512], BF16) for _ in range(5)]
    ybuf = [S.sb("ybuf", [128, 4, 256], BF16) for _ in range(2)]
    st = [S.sb("st", [128, 8], F32) for _ in range(4)]
    cnt = 0
    for qb in range(8):
        qs = slice(qb * 512, (qb + 1) * 512)
        tiles = [(4 * qb + r, (lambda r=r: masks[:, r + 1, :])) for r in range(-1, 4) if 4 * qb + r >= 0]
        yb = ybuf[qb % 2]
        cnt = attn_qblock(
            S, 4,
            lambda h: qT[h // 2][(h % 2) * 64:(h % 2) * 64 + 64, qs],
            lambda h, kt: kT[h // 2][(h % 2) * 64:(h % 2) * 64 + 64, kt * 128:(kt + 1) * 128],
            lambda h, kt: vext[:, kt, h // 2, :],
            tiles, 0.125, pps + pps2, pacc, pbuf, cnt, skew=3)
        attn_finish(S, 4, pacc, yb, st, zextra=lambda h: sk[:, h:h + 1])
        S.dma(y_d[qb * 512:(qb + 1) * 512, :].rearrange("(s p) f -> p s f", p=128), yb[:], q="act")


def phase_ret(S, nc, w, l, hT_d, wret_d, cst, y_d, ident_d):
    wsb = S.sb("wsb", [128, 8, 2048], BF16)
    load_w_bf16(S, wsb, wret_d[l], 8)
    hT = S.sb("hTall", [128, 8, S_LEN], BF16)
    hv = hT_d.rearrange("(c p) t -> p c t", p=128)
    for c in range(8):
        S.dma(hT[:, c, :], (hv[:, c, :],) + tuple(("hT", t) for t in range(NT)))
    ident = S.sb("ident", [128, 128], BF16)
    S.dma(ident[:], ident_d[:, :], q="pool")
    qT = [S.sb("qT", [64, S_LEN], BF16) for _ in range(4)]
    kT = [S.sb("kT", [64, S_LEN], BF16) for _ in range(4)]
    pps = [S.ps("pp", [128, 512], F32) for _ in range(2)]
    pps2 = [S.ps("pp2", [128, 512], F32) for _ in range(2)]
    cstt = [S.sb("cs", [128, 2, 512], F32) for _ in range(2)]
    tmp = [S.sb("tmp", [128, 2, 512], F32) for _ in range(2)]
    rope = lambda pc: (pc, cst["cosT"], cst["sinT"], cstt, tmp, pps2)
    for h in range(4):
        proj_feat(S, qT[h], wsb, h * 64, hT, pps, rope(256 + h * 64), npart=64)
        proj_feat(S, kT[h], wsb, 512 + h * 64, hT, pps, rope(768 + h * 64), npart=64)
    dmask = S.sb("dmask", [128, 4, 128], F32)
    S.dma(dmask[:], cst["ret_dmaskT"][:, :, :])
    zt = S.sb("zt", [128, 4], F32)
    S.dma(zt[:], cst["ret_zeta"][:, :])
    xiT = S.sb("xiT", [64, 4, 128], F32)
    S.dma(xiT[:], cst["ret_xiT"][:, :, :])
    gch = S.sb("gch", [64, 4], F32)
    S.dma(gch[:], cst["ret_gch"][:, :])
    gng = S.sb("gng", [128, 512], F32)
    bcast_row(S, gng[:], w["ret_gn_g"][l:l + 1, :])
    R = S.sb("R", [64, 4, 128], F32)
    Rbf = S.sb("Rbf", [64, 4, 128], BF16)
    S.memset(R[:], 0.0)
    S.memset(Rbf[:], 0.0)
    po = [S.ps("po", [128, 4, 128], F32) for _ in range(2)]
    ptk = S.ps("ptk", [128, 4, 64], BF16)
    pin = pps2[0][:, :].rearrange("p (h c) -> p h c", h=4)
    pkv = pps2[1][:, :].rearrange("p (h e) -> p h e", h=4)
    vb = [S.sb("vb", [128, 512], BF16) for _ in range(2)]
    sgb = [S.sb("sgb", [128, 512], F32) for _ in range(2)]
    qx = [S.sb("qx", [64, 4, 128], BF16) for _ in range(2)]
    kz = [S.sb("kz", [128, 4, 64], BF16) for _ in range(2)]
    inm = [S.sb("inm", [128, 4, 128], BF16) for _ in range(2)]
    osb = [S.sb("osb", [128, 4, 128], F32) for _ in range(2)]
    sq = [S.sb("sq", [128, 4, 128], F32) for _ in range(2)]
    st = [S.sb("st", [128, 8, 4], F32) for _ in range(2)]
    yb = [S.sb("yb", [128, 512], BF16) for _ in range(2)]
    def rfront(t):
        ts_ = slice(t * 128, (t + 1) * 128)
        b = t % 2
        for dc in range(8):
            S.mm(pps[0][:, :], hT[:, dc, ts_], wsb[:, dc, 1024:1536], start=(dc == 0), stop=(dc == 7))
        S.copy(vb[b][:], pps[0][:, :], eng="act")
        for dc in range(8):
            S.mm(pps[1][:, :], hT[:, dc, ts_], wsb[:, dc, 1536:2048], start=(dc == 0), stop=(dc == 7))
        S.act(sgb[b][:], pps[1][:, :], AF.Silu)
        for h in range(4):
            S.tr(ptk[:, h, :], kT[h][:, ts_], ident[0:64, 0:64])
        S.tt(kz[b][:], ptk[:, :, :], zt[:, :].unsqueeze(2).to_broadcast([128, 4, 64]), ALU.mult)
        for h in range(4):
            S.tt(qx[b][:, h, :], qT[h][:, ts_], xiT[:, h, :], ALU.mult, eng="pool")
        for h in range(4):
            S.mm(pin[:, h, :], kT[h][:, ts_], qT[h][:, ts_], start=True, stop=True)
        S.tt(inm[b][:], pin, dmask[:], ALU.mult)

    def rback(t):
        ts_ = slice(t * 128, (t + 1) * 128)
        b = t % 2
        pot = po[b]
        for h in range(4):
            S.mm(pot[:, h, :], inm[b][:, h, :], vb[b][:, h * 128:(h + 1) * 128], start=True, stop=False)
            S.mm(pot[:, h, :], qx[b][:, h, :], Rbf[:, h, :], start=False, stop=True)
        for h in range(4):
            S.mm(pkv[0:64, h, :], kz[b][:, h, :], vb[b][:, h * 128:(h + 1) * 128], start=True, stop=True)
        for h in range(4):
            S.stt(R[:, h, :], R[:, h, :], gch[:, h:h + 1], pkv[0:64, h, :], ALU.mult, ALU.add)
        S.copy(Rbf[:], R[:], eng="act")
        o = osb[b]
        s_ = st[b]
        S.copy(o[:], pot[:], eng="act")
        S.reduce(s_[:, 0, :], o[:], ALU.add)
        S.tt(sq[b][:], o[:], o[:], ALU.mult, eng="pool")
        S.reduce(s_[:, 1, :], sq[b][:], ALU.add)
        S.ts(s_[:, 2, :], s_[:, 0, :], 1.0 / 128, None, ALU.mult)
        S.tt(s_[:, 3, :], s_[:, 2, :], s_[:, 2, :], ALU.mult)
        S.stt(s_[:, 4, :], s_[:, 1, :], 1.0 / 128, s_[:, 3, :], ALU.mult, ALU.subtract)
        S.ts(s_[:, 5, :], s_[:, 4, :], 1e-5, None, ALU.add)
        S.act(s_[:, 6, :], s_[:, 5, :], AF.Sqrt)
        S.recip(s_[:, 7, :], s_[:, 6, :])
        S.tt(o[:], o[:], s_[:, 2, :].unsqueeze(2).to_broadcast([128, 4, 128]), ALU.subtract)
        S.tt(o[:], o[:], s_[:, 7, :].unsqueeze(2).to_broadcast([128, 4, 128]), ALU.mult)
        of = o[:].rearrange("p h e -> p (h e)")
        S.tt(of, of, gng[:], ALU.mult, eng="pool")
        S.tt(yb[b][:], of, sgb[b][:], ALU.mult)
        S.dma(y_d[ts_, :], yb[b][:], q="act")

    for t in range(NT + 1):
        if t < NT:
            rfront(t)
        if t >= 1:
            rback(t - 1)


def load_hT(S, hT_d):
    hT = S.sb("hTall", [128, 8, S_LEN], BF16)
    hv = hT_d.rearrange("(c p) t -> p c t", p=128)
    for c in range(8):
        S.dma(hT[:, c, :], (hv[:, c, :],) + tuple(("hT", t) for t in range(NT)))
    return hT


def phase_nsa_a(S, nc, w, l, hT_d, wnsa_d, cst, scr):
    wsb = S.sb("wsb", [128, 8, 1036], BF16)
    load_w_bf16(S, wsb, wnsa_d[l], 8)
    hT = load_hT(S, hT_d)
    pps = [S.ps("pp", [128, 512], F32) for _ in range(2)]
    pps2 = [S.ps("pp2", [128, 512], F32) for _ in range(2)]
    cstt = [S.sb("cs", [128, 2, 512], F32) for _ in range(2)]
    tmp = [S.sb("tmp", [128, 2, 512], F32) for _ in range(2)]
    rope = lambda pc: (pc, cst["cosT"], cst["sinT"], cstt, tmp, pps2)
    stg = [S.sb("stg", [64, S_LEN], BF16) for _ in range(2)]
    outs = [(h, h * 64, None) for h in range(4)] + [(4 + h, h * 64, 256 + h * 64) for h in range(4)]
    outs += [(8, 512, None), (9, 576, None), (10, 640, 704), (11, 768, 832)]
    for n, (idx, col0, pc) in enumerate(outs):
        st_ = stg[n % 2]
        proj_feat(S, st_, wsb, col0, hT, pps, rope(pc) if pc is not None else None, npart=64)
        S.dma((scr["nsaT_d"][idx], ("nsaT", idx)), st_[:, :], q="act")
    vb = [S.sb("vb", [128, 128], BF16) for _ in range(2)]
    gb = [S.sb("gb", [128, 12], F32) for _ in range(2)]
    for t in range(NT):
        p0 = pps[t % 2]
        for dc in range(8):
            S.mm(p0[:, 0:140], hT[:, dc, t * 128:(t + 1) * 128], wsb[:, dc, 896:1036],
                 start=(dc == 0), stop=(dc == 7))
        S.copy(vb[t % 2][:], p0[:, 0:128], eng="dve")
        S.act(gb[t % 2][:], p0[:, 128:140], AF.Sigmoid)
        S.dma(scr["nsa_v_d"][t * 128:(t + 1) * 128, :], vb[t % 2][:], q="act")
        S.dma(scr["nsa_g_d"][t * 128:(t + 1) * 128, :], gb[t % 2][:], q="act")


def phase_nsa_b(S, nc, w, l, cst, scr):
    kvT = S.sb("kvT", [64, 2, S_LEN], BF16)
    S.dma(kvT[:, 0, :], scr["nsaT_d"][8])
    S.dma(kvT[:, 1, :], scr["nsaT_d"][9])
    posT = S.sb("posT", [64, 2, 32], F32)
    S.dma(posT[:], w["nsa_posT"][l].rearrange("a d l -> d a l"))
    ident = S.sb("ident", [128, 128], BF16)
    S.dma(ident[:], cst["ident"][:, :], q="pool")
    w1 = [S.sb("w1", [64, 32, 256], BF16) for _ in range(2)]
    w2 = [S.sb("w2", [128, 2, 64], BF16) for _ in range(2)]
    for a, nm in enumerate(("k", "v")):
        S.dma(w1[a][:], w["nsa_cmp_%s_w1" % nm][l].rearrange("(l d) j -> d l j", d=64), q="pool")
        S.dma(w2[a][:], w["nsa_cmp_%s_w2" % nm][l].rearrange("(c p) d -> p c d", p=128), q="pool")
    X = S.sb("X", [64, 32, 256], BF16)
    gT = [S.sb("gT", [128, 2, 256], BF16) for _ in range(2)]
    kcT = S.sb("kcT", [64, 256], BF16)
    vcx = S.sb("vcx", [128, 2, 65], BF16)
    ov = S.sb("ov", [128, 2, 64], BF16)
    S.dma(ov[:], cst["overlap"].rearrange("(c p) j -> p c j", p=128), q="pool")
    S.memset(kcT[:], 0.0)
    S.memset(vcx[:], 0.0)
    S.memset(vcx[:, :, 64:65], 1.0)
    for a in range(2):
        S.memset(gT[a][:], 0.0)
    pps = [S.ps("pp", [128, 512], F32) for _ in range(2)]
    hs = [S.sb("hs", [128, 3, 256], F32) for _ in range(2)]
    for a in range(2):
        for l_ in range(32):
            S.ts(X[:, l_, 0:255], kvT[:, a, l_:l_ + 16 * 254 + 1:16], posT[:, a, l_:l_ + 1], None, ALU.add)
        for jc in range(2):
            ph = pps[jc]
            for l_ in range(32):
                S.mm(ph[:, 0:255], w1[a][:, l_, jc * 128:(jc + 1) * 128], X[:, l_, 0:255],
                     start=(l_ == 0), stop=(l_ == 31))
            h_ = hs[jc]
            S.act(h_[:, 0, 0:255], ph[:, 0:255], AF.Square)
            S.ts(h_[:, 0, 0:255], h_[:, 0, 0:255], 0.044715, 1.0, ALU.mult, ALU.add)
            S.tt(h_[:, 1, 0:255], h_[:, 0, 0:255], ph[:, 0:255], ALU.mult)
            S.act(h_[:, 2, 0:255], h_[:, 1, 0:255], AF.Sigmoid, scale=1.5957691216057308)
            S.tt(gT[a][:, jc, 0:255], h_[:, 2, 0:255], ph[:, 0:255], ALU.mult)
    for jc in range(2):
        S.mm(pps[0][0:64, 0:256], w2[0][:, jc, :], gT[0][:, jc, :], start=(jc == 0), stop=(jc == 1))
    S.copy(kcT[:, :], pps[0][0:64, 0:256], eng="act")
    for ic in range(2):
        for jc in range(2):
            S.mm(pps[1][:, ic * 64:(ic + 1) * 64], gT[1][:, jc, ic * 128:(ic + 1) * 128], w2[1][:, jc, :],
                 start=(jc == 0), stop=(jc == 1))
        S.copy(vcx[:, ic, 0:64], pps[1][:, ic * 64:(ic + 1) * 64], eng="act")
    qT = S.sb("qT", [64, 4, S_LEN], BF16)
    for h in range(4):
        S.dma(qT[:, h, :], scr["nsaT_d"][h])
    cmask = S.sb("cmask", [128, 2, S_LEN], BF16)
    S.dma(cmask[:], cst["cmpmaskT"].rearrange("(c p) q -> p c q", p=128), q="pool")
    pss = [S.ps("pss", [128, 512], F32) for _ in range(2)]
    pao = [S.ps("pao", [128, 4, 65], F32) for _ in range(2)]
    pai = S.ps("pai", [128, 4, 64], F32)
    pT = S.ps("pT", [64, 4, 128], BF16)
    pbuf = [S.sb("pbuf", [128, 512], BF16) for _ in range(6)]
    pss4 = pss + pps
    ocb = [S.sb("ocb", [128, 4, 256], BF16) for _ in range(2)]
    imp = [S.sb("imp", [128, 4, 64], F32) for _ in range(2)]
    sbt = [S.sb("sbt", [128, 4, 64], F32) for _ in range(2)]
    score = [S.sb("score", [128, 4, 64], F32) for _ in range(2)]
    work = S.sb("work", [128, 4, 64], F32)
    m8 = S.sb("m8", [128, 4, 8], F32)
    m8b = S.sb("m8b", [128, 4, 8], F32)
    selm = [S.sb("selm", [128, 4, 64], BF16) for _ in range(2)]
    selT = S.sb("selT", [64, S_LEN], BF16)
    st = [S.sb("st", [128, 8], F32) for _ in range(4)]
    cnt = 0
    for qb in range(8):
        qs = slice(qb * 512, (qb + 1) * 512)
        b = qb % 2
        pend = []

        def nsab_back(item, b=b):
            h, po_, pbl = item
            for ic in range(2):
                pb = pbl[ic]
                for sub in range(4):
                    S.mm(po_[:, sub, :], pb[:, sub * 128:(sub + 1) * 128], vcx[:, ic, :],
                         start=(ic == 0 and sub == 0), stop=(ic == 1), skip=True)
                for sub in range(4):
                    S.mm(pai[:, sub, :], pb[:, sub * 128:(sub + 1) * 128], ov[:, ic, :],
                         start=(ic == 0 and sub == 0), stop=(ic == 1), skip=True)
            s_ = st[h]
            S.ts(s_[:, 0:4], po_[:, :, 64], 1e-30, None, ALU.max)
            S.recip(s_[:, 4:8], s_[:, 0:4])
            for sub in range(4):
                S.ts(ocb[b][:, sub, h * 64:(h + 1) * 64], po_[:, sub, 0:64], s_[:, 4 + sub:5 + sub], None, ALU.mult)
                if h == 0:
                    S.ts(imp[b][:, sub, :], pai[:, sub, :], s_[:, 4 + sub:5 + sub], None, ALU.mult)
                else:
                    S.stt(imp[b][:, sub, :], pai[:, sub, :], s_[:, 4 + sub:5 + sub], imp[b][:, sub, :],
                          ALU.mult, ALU.add)

        for h in range(4):
            po_ = pao[h % 2]
            pbl = []
            for ic in range(2):
                ps = pss4[cnt % 4]
                pb = pbuf[cnt % 6]
                cnt += 1
                pbl.append(pb)
                S.mm(ps[:, 0:512], kcT[:, ic * 128:(ic + 1) * 128], qT[:, h, qs], start=True, stop=True)
                S.act(pb[:], ps[:, 0:512], AF.Exp, scale=0.125)
                S.tt(pb[:], pb[:], cmask[:, ic, qs], ALU.mult)
            pend.append((h, po_, pbl))
            if len(pend) > 1:
                nsab_back(pend.pop(0))
        nsab_back(pend.pop(0))
        S.dma(scr["ocmp_d"][qs, :].rearrange("(s p) f -> p s f", p=128), ocb[b][:], q="act")
        S.dma(sbt[b][:], cst["selbias"][qs, :].rearrange("(s p) j -> p s j", p=128))
        sc = score[b]
        S.tt(sc[:], imp[b][:], sbt[b][:], ALU.add)
        for sub in range(4):
            a_sc, a_m8, a_wk, a_m8b = sc[:, sub, :], m8[:, sub, :], work[:, sub, :], m8b[:, sub, :]
            S.op("dve", lambda e, o=a_m8, i=a_sc: e.max(o, i), reads=[sc[:]], writes=[m8[:]])
            S.op("dve", lambda e, o=a_wk, r=a_m8, i=a_sc: e.match_replace(o, r, i, -3.0e9),
                 reads=[sc[:], m8[:]], writes=[work[:]])
            S.op("dve", lambda e, o=a_m8b, i=a_wk: e.max(o, i), reads=[work[:]], writes=[m8b[:]])
            S.ts(selm[b][:, sub, :], sc[:, sub, :], m8b[:, sub, 7:8], None, ALU.is_ge)
        for sub in range(4):
            S.tr(pT[:, sub, :], selm[b][:, sub, :], ident[:])
        S.copy(selT[:, qs].rearrange("j (s p) -> j s p", s=4), pT[:, :, :], eng="act")
    S.dma(scr["selT_d"][:, :], selT[:, :], q="act")


def phase_nsa_c(S, nc, w, l, cst, scr, y_d):
    qrT = S.sb("qrT", [64, 4, S_LEN], BF16)
    for h in range(4):
        S.dma(qrT[:, h, :], scr["nsaT_d"][4 + h])
    kT = S.sb("kT", [64, 2, S_LEN], BF16)
    S.dma(kT[:, 0, :], scr["nsaT_d"][10])
    S.dma(kT[:, 1, :], scr["nsaT_d"][11])
    selT = S.sb("selT", [64, S_LEN], BF16)
    S.dma(selT[:, :], scr["selT_d"][:, :])
    vext = S.sb("vext", [128, NT, 2, 65], BF16)
    S.memset(vext[:, :, :, 64:65], 1.0, eng="pool")
    vv = scr["nsa_v_d"].rearrange("(t p) f -> p t f", p=128)
    for a in range(2):
        S.dma(vext[:, :, a, 0:64], vv[:, :, a * 64:(a + 1) * 64])
    gsb = S.sb("gsb", [128, NT, 12], F32)
    S.dma(gsb[:], scr["nsa_g_d"].rearrange("(t p) c -> p t c", p=128))
    mwin = S.sb("mwin", [128, 8, 512], BF16)
    for r in range(8):
        S.dma(mwin[:, r, :], cst["mask_win"][r], q="pool")
    mcau = S.sb("mcau", [128, 4, 512], BF16)
    for r in range(4):
        S.dma(mcau[:, r, :], cst["mask_causal"][r], q="pool")
    Eall = S.sb("Eall", [64, NT, 128], BF16)
    S.dma(Eall[:], cst["Eall"][:, :, :], q="pool")
    pss = [S.ps("pss", [128, 512], F32) for _ in range(3)]
    pacc = [S.ps("pacc", [128, 4, 65], F32) for _ in range(4)]
    pm = [S.ps("pm", [128, 512], F32) for _ in range(1)]
    pbuf = [S.sb("pbuf", [128, 512], BF16) for _ in range(4)]
    mbuf = [S.sb("mbuf", [128, 512], BF16) for _ in range(3)]
    ocb = [S.sb("ocb", [128, 4, 256], BF16) for _ in range(2)]
    ybuf = [S.sb("ybuf", [128, 4, 256], F32) for _ in range(2)]
    yb16 = [S.sb("yb16", [128, 4, 256], BF16) for _ in range(2)]
    st = [S.sb("st", [128, 8], F32) for _ in range(4)]
    cnt = 0
    mcnt = [0]
    for qb in range(8):
        qs = slice(qb * 512, (qb + 1) * 512)
        b = qb % 2
        yb = ybuf[b]
        gq = gsb[:, 4 * qb:4 * qb + 4, :]
        S.dma(ocb[b][:], scr["ocmp_d"][qs, :].rearrange("(s p) f -> p s f", p=128))
        for h in range(4):
            S.tt(yb[:, :, h * 64:(h + 1) * 64], ocb[b][:, :, h * 64:(h + 1) * 64],
                 gq[:, :, 3 * h:3 * h + 1].to_broadcast([128, 4, 64]), ALU.mult)
        tiles = [(4 * qb + r, (lambda r=r: mwin[:, r + 4, :])) for r in range(-4, 4) if 4 * qb + r >= 0]
        cnt = attn_qblock(S, 4, lambda h: qrT[:, h, qs], lambda h, kt: kT[:, 1, kt * 128:(kt + 1) * 128],
                          lambda h, kt: vext[:, kt, 1, :], tiles, 0.125, pss, pacc, pbuf, cnt)
        attn_finish(S, 4, pacc, yb, st, gate_of=lambda h: gq[:, :, 3 * h + 2], accumulate=True)

        def mk_mask(kt):
            def f():
                i = mcnt[0]
                mcnt[0] += 1
                p_ = pm[0]
                m_ = mbuf[i % 3]
                S.mm(p_[:, 0:512], Eall[:, kt, :], selT[:, qs], start=True, stop=True)
                r = kt - 4 * qb
                if r >= 0:
                    S.tt(m_[:], p_[:, 0:512], mcau[:, r, :], ALU.mult)
                else:
                    S.copy(m_[:], p_[:, 0:512], eng="pool" if False else "dve")
                return m_[:]
            return f
        tiles = [(kt, mk_mask(kt)) for kt in range(0, 4 * qb + 4)]
        cnt = attn_qblock(S, 4, lambda h: qrT[:, h, qs], lambda h, kt: kT[:, 0, kt * 128:(kt + 1) * 128],
                          lambda h, kt: vext[:, kt, 0, :], tiles, 0.125, pss, pacc, pbuf, cnt)
        attn_finish(S, 4, pacc, yb, st, gate_of=lambda h: gq[:, :, 3 * h + 1], accumulate=True)
        S.copy(yb16[b][:], yb[:], eng="act")
        S.dma(y_d[qs, :].rearrange("(s p) f -> p s f", p=128), yb16[b][:], q="act")


NEG_EH = -0.6065306597126334


def phase_rwkv_a(S, nc, w, l, hT_d, wr_d, cst, scr):
    mub = S.sb("mub", [128, 1024], F32)
    omub = S.sb("omub", [128, 1024], F32)
    bcast_row(S, mub[:], w["rwkv_mu"][l:l + 1, :])
    S.ts(omub[:], mub[:], -1.0, 1.0, ALU.mult, ALU.add)
    W1 = S.sb("W1", [128, 8, 1024], BF16)
    W2 = S.sb("W2", [128, 8, 1024], BF16)
    wst = [S.sb("wst", [128, 1024], F32) for _ in range(2)]
    wv = wr_d[l].rearrange("(c p) f -> p c f", p=128)
    for c in range(8):
        S.dma(wst[c % 2][:], wv[:, c, :])
        S.tt(W1[:, c, :], wst[c % 2][:], omub[:], ALU.mult)
        S.tt(W2[:, c, :], wst[c % 2][:], mub[:], ALU.mult, eng="pool")
    hTp = S.sb("hTp", [128, 8, S_LEN + 1], BF16)
    S.memset(hTp[:, :, 0:1], 0.0)
    hv = hT_d.rearrange("(c p) t -> p c t", p=128)
    for c in range(8):
        S.dma(hTp[:, c, 1:S_LEN + 1], (hv[:, c, :],) + tuple(("hT", t) for t in range(NT)))
    cols = S.sb("cols", [64, 20], F32)
    S.dma(cols[:], w["rwkv_cols"][l])
    omka = S.sb("omka", [64, 4], F32)
    S.ts(omka[:], cols[:, 12:16], -1.0, 1.0, ALU.mult, ALU.add)
    rkc = S.sb("rkc", [64, 4], BF16)
    S.copy(rkc[:], cols[:, 16:20])
    w2sb = S.sb("w2sb", [64, 256], BF16)
    a2sb = S.sb("a2sb", [64, 256], BF16)
    g2sb = S.sb("g2sb", [128, 256], BF16)
    S.dma(w2sb[:], w["rwkv_w2"][l], q="pool")
    S.dma(a2sb[:], w["rwkv_a2"][l], q="pool")
    S.dma(g2sb[:], w["rwkv_g2"][l], q="pool")
    ones64 = S.sb("ones64", [64, 64], BF16)
    S.memset(ones64[:], 1.0)
    rmask = S.sb("rmask", [64, 512], F32)
    S.dma(rmask[:], cst["rw_reset"][:, :])
    gCs = S.sb("gCs", [64, 4, 64], F32)
    pp = [S.ps("pp", [128, 512], F32) for _ in range(7)]
    pbn = S.ps("pbn", [128, 4, 4], F32)
    pc = [0]

    def nextp():
        p = pp[pc[0] % 7]
        pc[0] += 1
        return p

    def xmT(c0, m, t0, n=512):
        p = nextp()
        for dc in range(8):
            S.mm(p[0:m, 0:n], W1[:, dc, c0:c0 + m], hTp[:, dc, 1 + t0:1 + t0 + n], start=(dc == 0), stop=False)
        for dc in range(8):
            S.mm(p[0:m, 0:n], W2[:, dc, c0:c0 + m], hTp[:, dc, t0:t0 + n], start=False, stop=(dc == 7))
        return p

    twl = [S.sb("twl", [64, 512], BF16) for _ in range(2)]
    tal = [S.sb("tal", [64, 512], BF16) for _ in range(2)]
    sgl = [S.sb("sgl", [128, 512], BF16) for _ in range(2)]
    vtok = [S.sb("vtok", [128, 256], BF16) for _ in range(2)]
    gtok = [S.sb("gtok", [128, 256], F32) for _ in range(2)]
    bon = [S.sb("bon", [128, 4, 4], F32) for _ in range(2)]
    NF = 14
    f32t = [[S.sb("f%d" % i, [64, 512], F32) for i in range(NF)] for _ in range(2)]
    sqb = [S.sb("sqb", [64, 512], BF16) for _ in range(2)]
    rkb = [S.sb("rkb", [64, 512], BF16) for _ in range(2)]
    out6 = [S.sb("out6", [64, 6, 512], BF16) for _ in range(2)]
    for tb in range(8):
        t0 = tb * 512
        b = tb % 2
        p = xmT(768, 64, t0)
        S.act(twl[b][:], p[0:64, :], AF.Tanh)
        p = xmT(832, 64, t0)
        S.copy(tal[b][:], p[0:64, :], eng="dve")
        p = xmT(896, 128, t0)
        S.act(sgl[b][:], p[:, :], AF.Sigmoid)
        for sub in range(4):
            tt0 = t0 + sub * 128
            p = nextp()
            for dc in range(8):
                S.mm(p[:, 0:256], hTp[:, dc, 1 + tt0:1 + tt0 + 128], W1[:, dc, 512:768], start=(dc == 0), stop=False)
            for dc in range(8):
                S.mm(p[:, 0:256], hTp[:, dc, tt0:tt0 + 128], W2[:, dc, 512:768], start=False, stop=(dc == 7))
            vt = vtok[sub % 2]
            S.copy(vt[:], p[:, 0:256], eng="act")
            S.dma(scr["rw_v_d"][tt0:tt0 + 128, :], vt[:], q="act")
            p = nextp()
            S.mm(p[:, 0:256], sgl[b][:, sub * 128:(sub + 1) * 128], g2sb[:, :], start=True, stop=True)
            gt = gtok[sub % 2]
            S.copy(gt[:], p[:, 0:256], eng="dve")
            S.dma(scr["rw_g_d"][tt0:tt0 + 128, :], gt[:], q="act")
        def hfront(h, b=b, t0=t0):
            F = f32t[h % 2]
            lw, cs, Ep, En, cse, Epe, EC, ag, kkr, nrm, kk, tf, k2, bv = F
            o6 = out6[h % 2]
            hs = slice(h * 64, (h + 1) * 64)
            p = nextp()
            S.mm(p[0:64, :], w2sb[:, hs], twl[b][:], start=True, stop=True)
            S.act(lw[:], p[0:64, :], AF.Sigmoid, bias=cols[:, h:h + 1])
            a_cs, a_rm, a_lw = cs[:], rmask[:], lw[:]
            S.op("dve", lambda e, o=a_cs, d0=a_rm, d1=a_lw: e.tensor_tensor_scan(o, d0, d1, 0.0, ALU.mult, ALU.add),
                 reads=[rmask[:], lw[:]], writes=[cs[:]])
            S.act(Ep[:], cs[:], AF.Exp, scale=NEG_EH)
            S.act(En[:], cs[:], AF.Exp, scale=-NEG_EH)
            S.tt(cse[:], cs[:], lw[:], ALU.subtract, eng="pool")
            S.act(Epe[:], cse[:], AF.Exp, scale=NEG_EH)
            S.tt(EC[:].rearrange("p (c t) -> p c t", c=8), En[:].rearrange("p (c t) -> p c t", c=8),
                 Ep[:, 63::64].unsqueeze(2).to_broadcast([64, 8, 64]), ALU.mult, eng="pool")
            S.copy(gCs[:, h, tb * 8:(tb + 1) * 8], Ep[:, 63::64], eng="dve")
            p = nextp()
            S.mm(p[0:64, :], a2sb[:, hs], tal[b][:], start=True, stop=True)
            S.act(ag[:], p[0:64, :], AF.Sigmoid, bias=cols[:, 4 + h:5 + h])
            pk = xmT(256 + h * 64, 64, t0)
            S.ts(kkr[:], pk[0:64, :], cols[:, 8 + h:9 + h], None, ALU.mult)
            S.act(sqb[h % 2][:], kkr[:], AF.Square)
            S.ts(tf[:], ag[:], cols[:, 12 + h:13 + h], omka[:, h:h + 1], ALU.mult, ALU.add)
            S.tt(k2[:], tf[:], pk[0:64, :], ALU.mult)
            pr = xmT(h * 64, 64, t0)
            S.tt(o6[:, 2, :], k2[:], En[:], ALU.mult, eng="pool")
            S.tt(o6[:, 3, :], pr[0:64, :], Ep[:], ALU.mult)
            S.tt(o6[:, 5, :], k2[:], EC[:], ALU.mult, eng="pool")
            S.tt(rkb[h % 2][:], pr[0:64, :], k2[:], ALU.mult)

        def hback(h, b=b, t0=t0):
            F = f32t[h % 2]
            lw, cs, Ep, En, cse, Epe, EC, ag, kkr, nrm, kk, tf, k2, bv = F
            o6 = out6[h % 2]
            p = nextp()
            S.mm(p[0:64, :], ones64[:], sqb[h % 2][:], start=True, stop=True)
            S.act(nrm[:], p[0:64, :], AF.Sqrt)
            S.ts(nrm[:], nrm[:], 1e-12, None, ALU.max)
            S.recip(nrm[:], nrm[:])
            S.tt(kk[:], kkr[:], nrm[:], ALU.mult, eng="pool")
            S.tt(bv[:], kk[:], ag[:], ALU.mult, eng="pool")
            S.stt(o6[:, 0, :], kk[:], -1.0, Epe[:], ALU.mult, ALU.mult)
            S.tt(o6[:, 1, :], bv[:], En[:], ALU.mult, eng="pool")
            S.tt(o6[:, 4, :], bv[:], EC[:], ALU.mult, eng="pool")
            for sub in range(4):
                S.mm(pbn[:, sub, h:h + 1], rkb[h % 2][:, sub * 128:(sub + 1) * 128], rkc[:, h:h + 1],
                     start=True, stop=True)
            S.dma(scr["rwT_d"][h].rearrange("q k t -> k q t")[:, :, t0:t0 + 512], o6[:], q="act")

        for h in range(5):
            if h < 4:
                hfront(h)
            if h >= 1:
                hback(h - 1)
        S.copy(bon[b][:], pbn[:], eng="act")
        S.dma(scr["rw_b_d"][t0:t0 + 512, :].rearrange("(s p) h -> p s h", p=128), bon[b][:], q="act")
    S.dma(scr["rw_gC_d"][:, :, :], gCs[:], q="act")


def phase_rwkv_b(S, nc, w, l, cst, scr, y_d):
    ident = S.sb("ident", [128, 128], BF16)
    S.dma(ident[:], cst["ident"][:, :], q="pool")
    mlo = S.sb("mlo", [64, 4, 64], F32)
    mup = S.sb("mup", [64, 4, 64], F32)
    mupi = S.sb("mupi", [64, 4, 64], F32)
    I4 = S.sb("I4", [64, 4, 64], F32)
    S.dma(mlo[:], cst["rw_mlo"][:, :, :])
    S.dma(mup[:], cst["rw_mup"][:, :, :])
    S.dma(mupi[:], cst["rw_mupi"][:, :, :])
    S.dma(I4[:], cst["rw_I4"][:, :, :])
    gC = S.sb("gC", [64, 4, 64], F32)
    S.dma(gC[:], scr["rw_gC_d"][:, :, :])
    lng = S.sb("lng", [64, 256], F32)
    lnb = S.sb("lnb", [64, 256], F32)
    S.dma(lng[:], w["rwkv_ln_g"][l:l + 1, :].partition_broadcast(64))
    S.dma(lnb[:], w["rwkv_ln_b"][l:l + 1, :].partition_broadcast(64))
    M = S.sb("M", [64, 4, 64], F32)
    Mbf = S.sb("Mbf", [64, 4, 64], BF16)
    S.memset(M[:], 0.0)
    S.memset(Mbf[:], 0.0)
    NB = 4
    pp_full = [S.ps("pp", [128, 512], F32) for _ in range(7)]
    pp = [p_[0:64, :] for p_ in pp_full]
    ptr_full = S.ps("ptr", [128, 4, 3, 64], BF16)
    ptr = ptr_full[0:64]
    pc = [0]

    def nextp():
        p = pp[pc[0] % 7]
        pc[0] += 1
        return p

    def v4(p, half):
        return p[:, half * 256:(half + 1) * 256].rearrange("p (h s) -> p h s", h=4)

    def mk(name, shape, dt):
        return [S.sb(name, shape, dt) for _ in range(NB)]
    feat = [S.sb("feat", [64, 4, 6, 256], BF16) for _ in range(2)]
    vch = [S.sb("vch", [64, 4, 256], BF16) for _ in range(2)]
    bonb = [S.sb("bonb", [64, 4, 4], F32) for _ in range(2)]
    gtk = [S.sb("gtk", [64, 4, 256], F32) for _ in range(2)]
    tokM = mk("tokM", [64, 4, 2, 64], BF16)
    WZin = mk("WZin", [64, 4, 128], BF16)
    Lb = [mk("L0", [64, 4, 64], BF16), mk("L1", [64, 4, 64], BF16)]
    LTb = [mk("LT0", [64, 4, 64], BF16), mk("LT1", [64, 4, 64], BF16)]
    ILb = [mk("IL0", [64, 4, 64], BF16), mk("IL1", [64, 4, 64], BF16)]
    PTb = [mk("PT0", [64, 4, 64], BF16), mk("PT1", [64, 4, 64], BF16)]
    AakT = mk("AakT", [64, 4, 64], BF16)
    ArbT = mk("ArbT", [64, 4, 64], BF16)
    ArkT = mk("ArkT", [64, 4, 64], BF16)
    WZ = mk("WZ", [64, 4, 128], BF16)
    GT = mk("GT", [64, 4, 64], BF16)
    GNs = mk("GNs", [64, 2, 4, 64], F32)
    Dg = mk("Dg", [64, 4, 64], F32)
    Nsb = mk("Nsb", [64, 4, 64], F32)
    QeT = mk("QeT", [64, 4, 64], BF16)
    Ol = mk("Ol", [64, 4, 64], F32)
    osb = mk("osb", [64, 4, 64], F32)
    sqs = mk("sqs", [64, 4, 64], F32)
    bvt = mk("bvt", [64, 4, 64], F32)
    stt_ = mk("stt", [64, 8, 4], F32)
    yb = mk("yb", [64, 256], BF16)
    NBATCH = S_LEN // 256
    import os
    VAR = os.environ.get("RWB_VAR", "Z")
    for bt in range(NBATCH):
        if VAR == "A":
            break
        t0 = bt * 256
        fb = feat[bt % 2]
        vb = vch[bt % 2]
        for h in range(4):
            S.dma(fb[:, h, :, :], scr["rwT_d"][h].rearrange("q k t -> k q t")[:, :, t0:t0 + 256])
        S.dma(vb[:], scr["rw_v_d"][t0:t0 + 256, :].rearrange("(c p) f -> p c f", p=64))
        S.dma(bonb[bt % 2][:], scr["rw_b_d"][t0:t0 + 256, :].rearrange("(c p) f -> p c f", p=64))
        S.dma(gtk[bt % 2][:], scr["rw_g_d"][t0:t0 + 256, :].rearrange("(c p) f -> p c f", p=64))

        def F(h, q, c):
            return fb[:, h, q, c * 64:(c + 1) * 64]

        def V(h, c):
            return vb[:, c, h * 64:(h + 1) * 64]
        if VAR == "B":
            continue
        for c in range(NB):
            for h in range(4):
                for j, q in enumerate((0, 4, 5)):
                    if VAR == "D":
                        continue
                    S.tr(ptr[:, h, j, :], F(h, q, c), ident[0:64, 0:64])
            if VAR != "E":
                S.copy(WZin[c][:, :, 0:64], ptr[:, :, 0, :], eng="dve")
            if VAR != "F":
                S.copy(tokM[c][:], ptr[:, :, 1:3, :], eng="dve")
        import os
        STOP = int(os.environ.get("RWB_STOP", "9"))
        if STOP < 1:
            continue
        for c in range(NB):
            p1, p2, p3 = nextp(), nextp(), nextp()
            for h in range(4):
                S.mm(v4(p1, 0)[:, h, :], F(h, 0, c), F(h, 1, c), start=True, stop=True)
                S.mm(v4(p1, 1)[:, h, :], F(h, 1, c), F(h, 0, c), start=True, stop=True)
                S.mm(v4(p2, 0)[:, h, :], F(h, 2, c), F(h, 0, c), start=True, stop=True)
                S.mm(v4(p2, 1)[:, h, :], F(h, 1, c), F(h, 3, c), start=True, stop=True)
                S.mm(v4(p3, 0)[:, h, :], F(h, 2, c), F(h, 3, c), start=True, stop=True)
            S.tt(Lb[0][c][:], v4(p1, 0), mlo[:], ALU.mult)
            S.tt(LTb[0][c][:], v4(p1, 1), mup[:], ALU.mult)
            S.tt(PTb[0][c][:], LTb[0][c][:], I4[:], ALU.add, eng="pool")
            S.tt(AakT[c][:], v4(p2, 0), mup[:], ALU.mult)
            S.tt(ArbT[c][:], v4(p2, 1), mupi[:], ALU.mult)
            S.tt(ArkT[c][:], v4(p3, 0), mupi[:], ALU.mult)
        if STOP < 2:
            continue
        for c in range(NB):
            p1 = nextp()
            for h in range(4):
                S.mm(v4(p1, 0)[:, h, :], AakT[c][:, h, :], V(h, c), start=True, stop=True)
            S.copy(WZin[c][:, :, 64:128], v4(p1, 0), eng="dve")
        if STOP < 3:
            continue
        for i in range(1, 7):
            cur, prv = i % 2, (i - 1) % 2
            for c in range(NB):
                p1 = nextp()
                p2 = nextp() if i >= 2 else None
                for h in range(4):
                    if i <= 5:
                        S.mm(v4(p1, 0)[:, h, :], LTb[prv][c][:, h, :], Lb[prv][c][:, h, :], start=True, stop=True)
                    if i <= 4:
                        S.mm(v4(p1, 1)[:, h, :], Lb[prv][c][:, h, :], LTb[prv][c][:, h, :], start=True, stop=True)
                    if i >= 2:
                        S.mm(v4(p2, 0)[:, h, :], ILb[prv][c][:, h, :], PTb[i % 2][c][:, h, :], start=True, stop=True)
                if i <= 5:
                    S.copy(Lb[cur][c][:].rearrange("p h s -> p (h s)"), p1[:, 0:256], eng="act")
                    S.tt(ILb[cur][c][:], Lb[cur][c][:], I4[:], ALU.add, eng="pool")
                if i <= 4:
                    S.copy(LTb[cur][c][:].rearrange("p h s -> p (h s)"), p1[:, 256:512], eng="act")
                if i >= 2:
                    S.copy(PTb[(i - 1) % 2][c][:], v4(p2, 0), eng="dve")
        TT = PTb[1]
        if STOP < 4:
            continue
        for c in range(NB):
            p1 = nextp()
            pw = p1.rearrange("p (h s) -> p h s", h=4)
            for h in range(4):
                S.mm(pw[:, h, :], TT[c][:, h, :], WZin[c][:, h, :], start=True, stop=True)
            S.copy(WZ[c][:].rearrange("p h s -> p (h s)"), p1[:, 0:512], eng="act")
        if STOP < 5:
            continue
        for c in range(NB):
            n = bt * NB + c
            p1, p2 = nextp(), nextp()
            for h in range(4):
                S.mm(v4(p1, 0)[:, h, :], WZ[c][:, h, 0:64], tokM[c][:, h, 0, :], start=True, stop=True)
                S.mm(v4(p1, 1)[:, h, :], tokM[c][:, h, 0, :], WZ[c][:, h, 64:128], start=True, stop=False)
                S.mm(v4(p1, 1)[:, h, :], tokM[c][:, h, 1, :], V(h, c), start=False, stop=True)
            for h in range(4):
                S.mm(v4(p2, 0)[:, h, :], WZ[c][:, h, 0:64], ArbT[c][:, h, :], start=True, stop=True)
                S.mm(v4(p2, 1)[:, h, :], ArbT[c][:, h, :], WZ[c][:, h, 64:128], start=True, stop=False)
                S.mm(v4(p2, 1)[:, h, :], ArkT[c][:, h, :], V(h, c), start=False, stop=True)
            S.tt(Dg[c][:], I4[:], gC[:, :, n:n + 1].to_broadcast([64, 4, 64]), ALU.mult, eng="pool")
            S.copy(GNs[c][:].rearrange("p a h s -> p (a h s)"), p1[:, 0:512], eng="act")
            S.tt(GT[c][:], GNs[c][:, 0, :, :], Dg[c][:], ALU.add, eng="pool")
            S.tt(QeT[c][:], v4(p2, 0), fb[:, :, 3, c * 64:(c + 1) * 64], ALU.add)
            S.copy(Ol[c][:], v4(p2, 1), eng="dve")
        if STOP < 6:
            continue
        for c in range(NB):
            p1 = nextp()
            for h in range(4):
                S.mm(v4(p1, 0)[:, h, :], QeT[c][:, h, :], Mbf[:, h, :], start=True, stop=True)
            for h in range(4):
                S.mm(v4(p1, 1)[:, h, :], GT[c][:, h, :], Mbf[:, h, :], start=True, stop=True)
            S.tt(M[:], v4(p1, 1), GNs[c][:, 1, :, :], ALU.add)
            S.copy(Mbf[:], M[:], eng="act")
            o = osb[c]
            s_ = stt_[c]
            S.tt(o[:], v4(p1, 0), Ol[c][:], ALU.add)
            S.reduce(s_[:, 0, :], o[:], ALU.add)
            S.tt(sqs[c][:], o[:], o[:], ALU.mult, eng="pool")
            S.reduce(s_[:, 1, :], sqs[c][:], ALU.add)
            S.ts(s_[:, 2, :], s_[:, 0, :], 1.0 / 64, None, ALU.mult)
            S.tt(s_[:, 3, :], s_[:, 2, :], s_[:, 2, :], ALU.mult)
            S.stt(s_[:, 4, :], s_[:, 1, :], 1.0 / 64, s_[:, 3, :], ALU.mult, ALU.subtract)
            S.ts(s_[:, 5, :], s_[:, 4, :], 64e-5, None, ALU.add)
            S.act(s_[:, 6, :], s_[:, 5, :], AF.Sqrt)
            S.recip(s_[:, 7, :], s_[:, 6, :])
            S.tt(o[:], o[:], s_[:, 2, :].unsqueeze(2).to_broadcast([64, 4, 64]), ALU.subtract)
            S.tt(o[:], o[:], s_[:, 7, :].unsqueeze(2).to_broadcast([64, 4, 64]), ALU.mult)
            of = o[:].rearrange("p h e -> p (h e)")
            S.tt(of, of, lng[:], ALU.mult, eng="pool")
            S.tt(of, of, lnb[:], ALU.add, eng="pool")
            S.tt(bvt[c][:], vb[:, c, :].rearrange("p (h e) -> p h e", h=4),
                 bonb[bt % 2][:, c, :].unsqueeze(2).to_broadcast([64, 4, 64]), ALU.mult)
            S.tt(o[:], o[:], bvt[c][:], ALU.add)
            S.tt(yb[c][:], of, gtk[bt % 2][:, c, :], ALU.mult)
            S.dma(y_d[t0 + c * 64:t0 + (c + 1) * 64, :], yb[c][:], q="act")


def phase_merge(S, nc, xres, w, l, hT_d, wgate_d, cst, scr):
    Wg = S.sb("Wg", [128, 8, 4096], BF16)
    load_w_bf16(S, Wg, wgate_d[l], 8)
    Wbr = S.sb("Wbr", [128, 10, D], BF16)
    off = 0
    for nm, nch in (("w_br_nsa", 2), ("w_br_ret", 4), ("w_br_rwkv", 2), ("w_br_swa", 2)):
        v = w[nm][l].rearrange("(c p) f -> p c f", p=128)
        for c in range(nch):
            S.dma(Wbr[:, off + c, :], v[:, c, :], q="pool")
        off += nch
    Wo = S.sb("Wo", [128, 8, D], BF16)
    load_w_bf16(S, Wo, w["w_out"][l], 8)
    gpost = S.sb("gpost", [128, D], F32)
    bcast_row(S, gpost[:], w["mix_post_g"][l:l + 1, :])
    ident = S.sb("ident", [128, 128], BF16)
    S.dma(ident[:], cst["ident"][:, :], q="pool")
    ycat = [S.sb("ycat", [128, 1280], BF16) for _ in range(2)]
    yT = [S.sb("yT", [128, 10, 128], BF16) for _ in range(2)]
    hTt = [S.sb("hTt", [128, 8, 128], BF16) for _ in range(2)]
    xb = [S.sb("xb", [128, D], F32) for _ in range(2)]
    sg = [S.sb("sg", [128, 512], F32) for _ in range(2)]
    tmpb = [S.sb("tmpb", [128, 512], F32) for _ in range(2)]
    merged = [S.sb("merged", [128, D], F32) for _ in range(2)]
    mbf = [S.sb("mbf", [128, D], BF16) for _ in range(2)]
    mT = [S.sb("mT", [128, 8, 128], BF16) for _ in range(2)]
    fsb = [S.sb("fsb", [128, D], F32) for _ in range(2)]
    junk = S.sb("junk", [128, D], BF16)
    st = [S.sb("st", [128, 8], F32) for _ in range(2)]
    ptrA = S.ps("ptrA", [128, 5, 128], BF16)
    ptrB = S.ps("ptrB", [128, 5, 128], BF16)
    ptm = S.ps("ptm", [128, 8, 128], BF16)
    pg = [S.ps("pg", [128, 512], F32) for _ in range(2)]
    po = [S.ps("po", [128, 512], F32) for _ in range(2)]
    hv = hT_d.rearrange("(c p) t -> p c t", p=128)
    brch = ((0, 2), (2, 4), (6, 2), (8, 2))
    pf = S.ps("pf", [128, 512], F32)
    cnt = [0]

    def front(t):
        b2 = t % 2
        rows = slice(t * 128, (t + 1) * 128)
        yc = ycat[b2]
        S.dma(yc[:, 0:256], scr["y_nsa"][rows, :])
        S.dma(yc[:, 256:768], scr["y_ret"][rows, :])
        S.dma(yc[:, 768:1024], scr["y_rwkv"][rows, :])
        S.dma(yc[:, 1024:1280], scr["y_swa"][rows, :])
        S.dma(hTt[b2][:], (hv[:, :, rows], ("hT", t)))
        S.dma(xb[b2][:], (xres[rows, :], ("xres", t)))
        for fc in range(10):
            pt_ = ptrA if fc < 5 else ptrB
            S.tr(pt_[:, fc % 5, :], yc[:, fc * 128:(fc + 1) * 128], ident[:])
        S.copy(yT[b2][:, 0:5, :], ptrA[:], eng="act")
        S.copy(yT[b2][:, 5:10, :], ptrB[:], eng="dve")
        mg = merged[b2]
        for br in range(4):
            f0, nf = brch[br]
            for half in range(2):
                pgt = pg[cnt[0] % 2]
                pot = po[cnt[0] % 2]
                sgt = sg[cnt[0] % 2]
                tb_ = tmpb[cnt[0] % 2]
                cnt[0] += 1
                c0 = br * 1024 + half * 512
                for dc in range(8):
                    S.mm(pgt[:, :], hTt[b2][:, dc, :], Wg[:, dc, c0:c0 + 512], start=(dc == 0), stop=(dc == 7))
                for k in range(nf):
                    S.mm(pot[:, :], yT[b2][:, f0 + k, :], Wbr[:, f0 + k, half * 512:(half + 1) * 512],
                         start=(k == 0), stop=(k == nf - 1))
                S.act(sgt[:], pgt[:, :], AF.Sigmoid)
                mslice = mg[:, half * 512:(half + 1) * 512]
                if br == 0:
                    S.tt(mslice, sgt[:], pot[:, :], ALU.mult)
                else:
                    S.tt(tb_[:], sgt[:], pot[:, :], ALU.mult)
                    S.tt(mslice, mslice, tb_[:], ALU.add, eng="pool")
        S.copy(mbf[b2][:], mg[:], eng="act")

    def back(t):
        b2 = t % 2
        rows = slice(t * 128, (t + 1) * 128)
        for dc in range(8):
            S.tr(ptm[:, dc, :], mbf[b2][:, dc * 128:(dc + 1) * 128], ident[:])
        S.copy(mT[b2][:], ptm[:], eng="act")
        f = fsb[b2]
        s_ = st[b2]
        for half in range(2):
            for dc in range(8):
                S.mm(pf[:, :], mT[b2][:, dc, :], Wo[:, dc, half * 512:(half + 1) * 512],
                     start=(dc == 0), stop=(dc == 7))
            S.act(f[:, half * 512:(half + 1) * 512], pf[:, :], AF.Copy)
            S.act(junk[:, half * 512:(half + 1) * 512], pf[:, :], AF.Square, accum_out=s_[:, half:half + 1])
        S.tt(s_[:, 2:3], s_[:, 0:1], s_[:, 1:2], ALU.add)
        S.ts(s_[:, 3:4], s_[:, 2:3], 1.0 / D, RMS_EPS, ALU.mult, ALU.add)
        S.act(s_[:, 4:5], s_[:, 3:4], AF.Sqrt)
        S.recip(s_[:, 5:6], s_[:, 4:5])
        S.stt(f[:], f[:], s_[:, 5:6], gpost[:], ALU.mult, ALU.mult)
        S.tt(xb[b2][:], xb[b2][:], f[:], ALU.add, eng="pool")
        S.dma((xres[rows, :], ("xres", t)), xb[b2][:], q="act")

    for t in range(NT + 1):
        if t < NT:
            front(t)
        if t >= 1:
            back(t - 1)


WNAMES = ['ffn1_pre_g', 'ffn1_post_g', 'ffn1_w_gate', 'ffn1_w_up', 'ffn1_w_down', 'mix_pre_g', 'mix_post_g',
          'w_in', 'nsa_cmp_pos_k', 'nsa_cmp_pos_v', 'nsa_cmp_k_w1', 'nsa_cmp_k_w2', 'nsa_cmp_v_w1',
          'nsa_cmp_v_w2', 'ret_gn_g', 'rwkv_mu', 'rwkv_w0', 'rwkv_w2', 'rwkv_a0', 'rwkv_a2', 'rwkv_g2',
          'rwkv_k_k', 'rwkv_k_a', 'rwkv_r_k', 'rwkv_ln_g', 'rwkv_ln_b', 'swa_sinks', 'w_br_nsa', 'w_br_ret',
          'w_br_rwkv', 'w_br_swa', 'w_out', 'ffn2_pre_g', 'ffn2_post_g', 'ffn2_w_gate', 'ffn2_w_up',
          'ffn2_w_down']

_CONSTS = None


def band_masks(window, rels):
    p = np.arange(128)[:, None]
    ql = np.arange(512)[None, :]
    out = []
    for r in rels:
        d = ql - (r * 128 + p)
        out.append(((d >= 0) & (d < window)).astype(np.float32))
    return np.stack(out, 0)


def host_consts():
    global _CONSTS
    if _CONSTS is not None:
        return _CONSTS
    c = {}
    c["ident"] = np.eye(128, dtype=np.float32)
    pos = np.arange(S_LEN, dtype=np.float32)
    inv = np.power(np.float32(10000.0), -np.arange(32, dtype=np.float32) * 2.0 / 64).astype(np.float32)
    ang = pos[None, :] * inv[:, None]
    cos = np.cos(ang).astype(np.float32)
    sin = np.sin(ang).astype(np.float32)
    c["cosT"] = np.ascontiguousarray(np.concatenate([cos, cos, cos, cos], 0))
    c["sinT"] = np.ascontiguousarray(np.concatenate([-sin, sin, -sin, sin], 0))
    c["mask_swa"] = band_masks(128, range(-1, 4))
    ii = np.arange(256)
    qq = np.arange(S_LEN)
    c["cmpmaskT"] = (((16 * ii[:, None] + 31) <= qq[None, :]) & (ii[:, None] < 255)).astype(np.float32)
    jj = np.arange(64)
    c["overlap"] = (((16 * ii[:, None]) <= (64 * jj[None, :] + 63)) & ((16 * ii[:, None] + 31) >= 64 * jj[None, :])
                    & (ii[:, None] < 255)).astype(np.float32)
    cur = (qq // 64)[:, None]
    forced = (jj[None, :] == 0) | (jj[None, :] == cur) | (jj[None, :] == cur - 1)
    valid = jj[None, :] <= cur
    c["selbias"] = np.where(forced, 1e9, np.where(valid, 0.0, -1e9)).astype(np.float32)
    c["mask_win"] = band_masks(512, range(-4, 4))
    c["mask_causal"] = band_masks(10 ** 7, range(0, 4))
    kt_ = np.arange(NT)[None, :, None]
    pp = np.arange(128)[None, None, :]
    c["Eall"] = (jj[:, None, None] == (2 * kt_ + pp // 64)).astype(np.float32)
    c["rw_reset"] = np.ascontiguousarray(np.broadcast_to((np.arange(512) % 64 != 0).astype(np.float32)[None, :], (64, 512)))
    tt_ = np.arange(64)[:, None, None]
    ss_ = np.arange(64)[None, None, :]
    one4 = np.ones((1, 4, 1), dtype=np.float32)
    c["rw_mlo"] = np.ascontiguousarray((ss_ < tt_).astype(np.float32) * one4)
    c["rw_mup"] = np.ascontiguousarray((ss_ > tt_).astype(np.float32) * one4)
    c["rw_mupi"] = np.ascontiguousarray((ss_ >= tt_).astype(np.float32) * one4)
    c["rw_I4"] = np.ascontiguousarray((ss_ == tt_).astype(np.float32) * one4)
    gam = (1.0 - np.power(2.0, -5.0 - np.arange(4, dtype=np.float64)))
    m = np.arange(128)[:, None, None]
    cc = np.arange(128)[None, None, :]
    gg = gam[None, :, None]
    dm = np.where(cc >= m, np.power(gg, np.maximum(cc - m, 0)), 0.0) * 0.125
    c["ret_dmaskT"] = np.ascontiguousarray(dm.astype(np.float32))
    c["ret_zeta"] = np.ascontiguousarray((np.power(gam[None, :], 127 - np.arange(128)[:, None]) * 0.125).astype(np.float32))
    xi = np.power(gam[None, :, None], np.arange(128)[None, None, :] + 1.0)
    c["ret_xiT"] = np.ascontiguousarray(np.broadcast_to(xi, (64, 4, 128)).astype(np.float32))
    c["ret_gch"] = np.ascontiguousarray(np.broadcast_to(np.power(gam, 128.0)[None, :], (64, 4)).astype(np.float32))
    _CONSTS = c
    return c


def derived_weights(inputs):
    idx = w_in_index_sets()
    out = {}
    w_in = np.asarray(inputs["w_in"], dtype=np.float32)
    for n, ix in idx.items():
        out["w" + n] = np.ascontiguousarray(w_in[:, :, ix])
    pk = np.asarray(inputs["nsa_cmp_pos_k"], dtype=np.float32)
    pv = np.asarray(inputs["nsa_cmp_pos_v"], dtype=np.float32)
    t64 = lambda n: np.asarray(inputs[n], dtype=np.float32).reshape(DEPTH, 4, 64).transpose(0, 2, 1)
    out["rwkv_cols"] = np.ascontiguousarray(np.concatenate(
        [t64("rwkv_w0"), t64("rwkv_a0"), t64("rwkv_k_k"), t64("rwkv_k_a"), t64("rwkv_r_k")], axis=2))
    out["nsa_posT"] = np.ascontiguousarray(np.stack([pk.transpose(0, 2, 1), pv.transpose(0, 2, 1)], axis=1))
    return out


SCRATCH = {
    "hT_d": ([D, S_LEN], BF16),
    "y_swa": ([S_LEN, 256], BF16),
    "y_ret": ([S_LEN, 512], BF16),
    "y_nsa": ([S_LEN, 256], BF16),
    "y_rwkv": ([S_LEN, 256], BF16),
    "rwT_d": ([4, 6, 64, S_LEN], BF16),
    "rw_gC_d": ([64, 4, 64], F32),
    "rw_v_d": ([S_LEN, 256], BF16),
    "rw_g_d": ([S_LEN, 256], F32),
    "rw_b_d": ([S_LEN, 4], F32),
    "nsaT_d": ([12, 64, S_LEN], BF16),
    "nsa_v_d": ([S_LEN, 128], BF16),
    "nsa_g_d": ([S_LEN, 12], F32),
    "ocmp_d": ([S_LEN, 256], BF16),
    "selT_d": ([64, S_LEN], BF16),
}


def default_phases():
    pl = [("copyin", None)]
    for l in range(DEPTH):
        pl += [("ffn1", l), ("mixpre", l), ("swa", l), ("ret", l), ("nsa_a", l), ("nsa_b", l), ("nsa_c", l),
               ("rwkv_a", l), ("rwkv_b", l), ("merge", l), ("ffn2", l)]
    return pl


def build(shapes, phases=None, dbg=()):
    nc = bass.Bass("TRN2", target_bir_lowering=False)
    x_in = nc.dram_tensor("x", [S_LEN, D], F32, kind="ExternalInput").ap()
    w = {}
    for n in shapes:
        w[n] = nc.dram_tensor(n, list(shapes[n]), F32, kind="ExternalInput").ap()
    cst = {}
    for n, a in host_consts().items():
        cst[n] = nc.dram_tensor("c_" + n, list(a.shape), F32, kind="ExternalInput").ap()
    y = nc.dram_tensor("y", [S_LEN, D], F32, kind="ExternalOutput").ap()
    scr = {}
    for n, (shp, dt_) in SCRATCH.items():
        if n in dbg:
            scr[n] = nc.dram_tensor(n, shp, dt_, kind="ExternalOutput").ap()
        else:
            scr[n] = nc.dram_tensor(n, shp, dt_).ap()
    xres = y
    plist = default_phases() if phases is None else phases
    with ExitStack() as gst:
        S = Sched(nc, gst)
        for pi, (pn, l) in enumerate(plist):
            with ExitStack() as pst:
                S.stack = pst
                S.phase = pi
                if pn == "copyin":
                    for t in range(0, NT, 4):
                        S.dma((xres[t * 128:(t + 4) * 128, :], ("xres", t), ("xres", t + 1), ("xres", t + 2),
                               ("xres", t + 3)), x_in[t * 128:(t + 4) * 128, :], q="sp")
                elif pn in ("ffn1", "ffn2"):
                    phase_ffn(S, nc, xres, w, l, pn, cst["ident"])
                elif pn == "mixpre":
                    phase_mixpre(S, nc, xres, w, l, scr["hT_d"], cst["ident"])
                elif pn == "swa":
                    phase_swa(S, nc, w, l, scr["hT_d"], w["wswa"], cst, scr["y_swa"])
                elif pn == "nsa_a":
                    phase_nsa_a(S, nc, w, l, scr["hT_d"], w["wnsa"], cst, scr)
                elif pn == "nsa_b":
                    phase_nsa_b(S, nc, w, l, cst, scr)
                elif pn == "nsa_c":
                    phase_nsa_c(S, nc, w, l, cst, scr, scr["y_nsa"])
                elif pn == "rwkv_a":
                    phase_rwkv_a(S, nc, w, l, scr["hT_d"], w["wr"], cst, scr)
                elif pn == "rwkv_b":
                    phase_rwkv_b(S, nc, w, l, cst, scr, scr["y_rwkv"])
                elif pn == "merge":
                    phase_merge(S, nc, xres, w, l, scr["hT_d"], w["wgate"], cst, scr)
                elif pn == "ret":
                    phase_ret(S, nc, w, l, scr["hT_d"], w["wret"], cst, scr["y_ret"], cst["ident"])
                else:
                    raise ValueError(pn)
                S.barrier()
                S.emit(final=(pi == len(plist) - 1))
    return nc


def make_in_maps(inputs, cores):
    base = {k: np.ascontiguousarray(inputs[k], dtype=np.float32) for k in WNAMES}
    base.update(derived_weights(inputs))
    shapes = {k: v.shape for k, v in base.items()}
    for k, a in host_consts().items():
        base["c_" + k] = a
    x = np.asarray(inputs["x"], dtype=np.float32)
    in_maps = []
    for i in cores:
        m = dict(base)
        m["x"] = np.ascontiguousarray(x[i])
        in_maps.append(m)
    return shapes, in_maps


def kernel(**inputs):
    n = 8
    shapes, in_maps = make_in_maps(inputs, list(range(n)))
    nc = build(shapes)
    res = run_bass_kernel_spmd(nc, in_maps, core_ids=list(range(n)))
    return np.stack([r["y"] for r in res.results], axis=0)
```
